# Optimizing a Trainium2 kernel written in Bass

```python
import jax, jax.numpy as jnp
from jax import lax
import numpy as np

D_MODEL = 1024
BATCH = 4
SEQ = 8192
DEPTH = 1
DEC_BATCH = 32
DEC_SEQ = 1
PAST_LEN = 16384
PAGE_SIZE = 128

HEAD_DIM = 64
HEADS_PER_GROUP = 4
GROUPS = ((128, 1), (512, 4), (2048, 16))
N_GROUPS = len(GROUPS)
N_HEADS = HEADS_PER_GROUP * N_GROUPS
ATTN_WIDTH = N_HEADS * HEAD_DIM
ATTN_OUT_WIDTH = HEADS_PER_GROUP * HEAD_DIM
ROT_DIM = HEAD_DIM // 4
ROPE_THETA = 500000.0
C_CONV = D_MODEL
CONV_WIDTH = 31
D_FF = 2816
FFN_CONV_WIDTH = 3
EPS = 1e-6
NEG = -1e30
IN_WIDTH = 2 * C_CONV + 3 * ATTN_WIDTH + 2 * D_MODEL
SPLITS = (C_CONV, 2 * C_CONV, 2 * C_CONV + ATTN_WIDTH, 2 * C_CONV + 2 * ATTN_WIDTH,
          2 * C_CONV + 3 * ATTN_WIDTH, 2 * C_CONV + 3 * ATTN_WIDTH + D_MODEL)

kernel_name = 'hybrid_conformer_conv_dilated_swa_convffn_step'


def rms_norm(x, g):
    xf = x.astype(jnp.float32)
    y = xf * lax.rsqrt(jnp.mean(xf * xf, axis=-1, keepdims=True) + EPS)
    return (y * g.astype(jnp.float32)).astype(x.dtype)


def layer_norm(x, g, b):
    xf = x.astype(jnp.float32)
    mu = jnp.mean(xf, axis=-1, keepdims=True)
    var = jnp.mean(jnp.square(xf - mu), axis=-1, keepdims=True)
    y = (xf - mu) * lax.rsqrt(var + EPS) * g.astype(jnp.float32) + b.astype(jnp.float32)
    return y.astype(x.dtype)


def causal_dwconv(hist, x, w, b):
    xe = jnp.concatenate([hist.astype(x.dtype), x], axis=1)
    y = lax.conv_general_dilated(xe, w[:, None, :].astype(x.dtype), (1,), 'VALID',
                                 dimension_numbers=('NWC', 'WIO', 'NWC'),
                                 feature_group_count=x.shape[-1])
    return y + b.astype(x.dtype), xe[:, -(w.shape[0] - 1):]


def rope(x, pos):
    half = ROT_DIM // 2
    inv = ROPE_THETA ** (-jnp.arange(half, dtype=jnp.float32) / half)
    ang = pos.astype(jnp.float32)[:, None] * inv[None, :]
    cos = jnp.cos(ang)[None, :, None, :]
    sin = jnp.sin(ang)[None, :, None, :]
    xr = x[..., :ROT_DIM].astype(jnp.float32)
    x1, x2 = xr[..., :half], xr[..., half:]
    rot = jnp.concatenate([x1 * cos - x2 * sin, x2 * cos + x1 * sin], axis=-1).astype(x.dtype)
    return jnp.concatenate([rot, x[..., ROT_DIM:]], axis=-1)


def dilated_group_prompt(q, k, v, dilation, n_keys):
    b_sz, s_len, n_h, d_h = q.shape
    m_len = s_len // dilation
    nb = -(-m_len // n_keys)
    mp = nb * n_keys

    def split(t):
        t = t.reshape(b_sz, m_len, dilation, n_h, d_h).transpose(0, 2, 1, 3, 4)
        return jnp.pad(t, ((0, 0), (0, 0), (0, mp - m_len), (0, 0), (0, 0)))

    def band_keys(t):
        t = jnp.pad(split(t), ((0, 0), (0, 0), (n_keys, 0), (0, 0), (0, 0)))
        t = t.reshape(b_sz, dilation, nb + 1, n_keys, n_h, d_h)
        return jnp.concatenate([t[:, :, :-1], t[:, :, 1:]], axis=3)

    qb = split(q).reshape(b_sz, dilation, nb, n_keys, n_h, d_h)
    kb = band_keys(k)
    vb = band_keys(v)
    s = jnp.einsum('brnqhd,brnkhd->brnhqk', qb, kb).astype(jnp.float32)
    qi = jnp.arange(n_keys)[:, None]
    kj = jnp.arange(2 * n_keys)[None, :]
    dist = qi + n_keys - kj
    band = (dist >= 0) & (dist <= n_keys)
    blk = jnp.arange(nb)[:, None, None]
    valid = band[None] & (blk * n_keys + kj[None] - n_keys >= 0)
    s = jnp.where(valid[None, None, :, None], s, NEG)
    mx = jnp.max(s, axis=-1, keepdims=True)
    p = jnp.exp(s - mx)
    den = jnp.sum(p, axis=-1, keepdims=True)
    o = jnp.einsum('brnhqk,brnkhd->brnqhd', p, vb.astype(jnp.float32))
    o = o / jnp.swapaxes(den[..., 0], -1, -2)[..., None]
    lse = jnp.swapaxes(mx[..., 0] + jnp.log(den[..., 0]), -1, -2)
    o = o.reshape(b_sz, dilation, mp, n_h, d_h)[:, :, :m_len].transpose(0, 2, 1, 3, 4)
    lse = lse.reshape(b_sz, dilation, mp, n_h)[:, :, :m_len].transpose(0, 2, 1, 3)
    return o.reshape(b_sz, s_len, n_h, d_h), lse.reshape(b_sz, s_len, n_h)


def dilated_group_sample(q, k_ext, v_ext, dilation, n_keys, hist_len, pos0):
    t_len = q.shape[1]
    t = jnp.arange(t_len)[:, None]
    back = jnp.arange(n_keys + 1)[None, :] * dilation
    idx = hist_len + t - back
    valid = (idx >= 0) & (pos0 + t - back >= 0)
    idx = jnp.maximum(idx, 0)
    kg = jnp.take(k_ext, idx, axis=1)
    vg = jnp.take(v_ext, idx, axis=1)
    s = jnp.einsum('bthd,btjhd->bthj', q, kg).astype(jnp.float32)
    s = jnp.where(valid[None, :, None, :], s, NEG)
    mx = jnp.max(s, axis=-1, keepdims=True)
    p = jnp.exp(s - mx)
    den = jnp.sum(p, axis=-1, keepdims=True)
    o = jnp.einsum('bthj,btjhd->bthd', p, vg.astype(jnp.float32)) / den
    return o, (mx + jnp.log(den))[..., 0]


def _layer(x, pos0, hist_conv, hist_kv, hist_ffn, prompt, g_mix, w_in, w_dw, b_dw, ln_g, ln_b,
           w_conv_out, w_attn_out, w_out, g_ffn, w_up, w_fdw, b_fdw, w_down):
    n_seq, t_len, _ = x.shape
    pos = pos0 + jnp.arange(t_len, dtype=jnp.int32)
    h = rms_norm(x, g_mix)
    z = h @ w_in
    a_lin, a_gate, q, k, v, g_a, g_b = jnp.split(z, SPLITS, axis=-1)
    u = a_lin * jax.nn.sigmoid(a_gate)
    c, new_conv = causal_dwconv(hist_conv, u, w_dw, b_dw)
    out_a = jax.nn.silu(layer_norm(c, ln_g, ln_b)) @ w_conv_out
    q = rope(q.reshape(n_seq, t_len, N_HEADS, HEAD_DIM), pos) * (HEAD_DIM ** -0.5)
    k = rope(k.reshape(n_seq, t_len, N_HEADS, HEAD_DIM), pos)
    v = v.reshape(n_seq, t_len, N_HEADS, HEAD_DIM)
    outs, lses, new_kv = [], [], []
    for gi, (window, dil) in enumerate(GROUPS):
        hs = slice(gi * HEADS_PER_GROUP, (gi + 1) * HEADS_PER_GROUP)
        qg, kg, vg = q[:, :, hs], k[:, :, hs], v[:, :, hs]
        n_keys = window // dil
        if prompt:
            o, l = dilated_group_prompt(qg, kg, vg, dil, n_keys)
            ke, ve = kg, vg
        else:
            kh, vh = hist_kv[gi]
            ke = jnp.concatenate([kh.astype(kg.dtype), kg], axis=1)
            ve = jnp.concatenate([vh.astype(vg.dtype), vg], axis=1)
            o, l = dilated_group_sample(qg, ke, ve, dil, n_keys, kh.shape[1], pos0)
        keep = min(window, pos0 + t_len)
        new_kv.append((ke[:, -keep:], ve[:, -keep:]))
        outs.append(o)
        lses.append(l)
    w_grp = jax.nn.softmax(jnp.stack(lses, axis=0), axis=0)
    o = jnp.sum(w_grp[..., None] * jnp.stack(outs, axis=0), axis=0)
    out_b = o.reshape(n_seq, t_len, ATTN_OUT_WIDTH).astype(x.dtype) @ w_attn_out
    mix = jax.nn.sigmoid(g_a) * out_a + jax.nn.sigmoid(g_b) * out_b
    x = x + mix @ w_out
    h2 = rms_norm(x, g_ffn)
    up, new_ffn = causal_dwconv(hist_ffn, h2 @ w_up, w_fdw, b_fdw)
    gate, val = jnp.split(up, 2, axis=-1)
    x = x + (jax.nn.silu(gate) * val) @ w_down
    return x, new_conv, new_kv, new_ffn


def setup_inputs(seed: int = 0) -> dict:
    key = jax.random.key(seed)
    ks = iter(jax.random.split(key, 32))

    def nrm(shape, scale):
        return scale * jax.random.normal(next(ks), shape, jnp.float32)

    L = DEPTH
    w0, w1, w2 = GROUPS[0][0], GROUPS[1][0], GROUPS[2][0]
    kvs = lambda w: (L, DEC_BATCH, min(w, PAST_LEN), HEADS_PER_GROUP, HEAD_DIM)
    return {
        'x_prompt': nrm((BATCH, SEQ, D_MODEL), 1.0),
        'x_sample': nrm((DEC_BATCH, DEC_SEQ, D_MODEL), 1.0),
        'state_conv': nrm((L, DEC_BATCH, CONV_WIDTH - 1, C_CONV), 0.5),
        'cache_k_w128': nrm(kvs(w0), 1.0),
        'cache_v_w128': nrm(kvs(w0), 1.0),
        'cache_k_w512': nrm(kvs(w1), 1.0),
        'cache_v_w512': nrm(kvs(w1), 1.0),
        'cache_k_w2048': nrm(kvs(w2), 1.0),
        'cache_v_w2048': nrm(kvs(w2), 1.0),
        'state_ffn_conv': nrm((L, DEC_BATCH, FFN_CONV_WIDTH - 1, 2 * D_FF), 1.0),
        'g_mix': 1.0 + nrm((L, D_MODEL), 0.05),
        'w_in': nrm((L, D_MODEL, IN_WIDTH), D_MODEL ** -0.5),
        'w_dw': nrm((L, CONV_WIDTH, C_CONV), CONV_WIDTH ** -0.5),
        'b_dw': nrm((L, C_CONV), 0.02),
        'ln_g': 1.0 + nrm((L, C_CONV), 0.05),
        'ln_b': nrm((L, C_CONV), 0.02),
        'w_conv_out': nrm((L, C_CONV, D_MODEL), C_CONV ** -0.5),
        'w_attn_out': nrm((L, ATTN_OUT_WIDTH, D_MODEL), ATTN_OUT_WIDTH ** -0.5),
        'w_out': nrm((L, D_MODEL, D_MODEL), D_MODEL ** -0.5),
        'g_ffn': 1.0 + nrm((L, D_MODEL), 0.05),
        'w_up': nrm((L, D_MODEL, 2 * D_FF), D_MODEL ** -0.5),
        'w_fdw': nrm((L, FFN_CONV_WIDTH, 2 * D_FF), FFN_CONV_WIDTH ** -0.5),
        'b_fdw': nrm((L, 2 * D_FF), 0.02),
        'w_down': nrm((L, D_FF, D_MODEL), D_FF ** -0.5),
        'g_final': 1.0 + nrm((D_MODEL,), 0.05),
    }


def reference(x_prompt, x_sample, state_conv, cache_k_w128, cache_v_w128, cache_k_w512, cache_v_w512,
              cache_k_w2048, cache_v_w2048, state_ffn_conv, g_mix, w_in, w_dw, b_dw, ln_g, ln_b,
              w_conv_out, w_attn_out, w_out, g_ffn, w_up, w_fdw, b_fdw, w_down, g_final):
    xp, xs = x_prompt, x_sample
    cp, cs, kvp, kvs, fp, fs = [], [], [], [], [], []
    for l in range(DEPTH):
        w = (g_mix[l], w_in[l], w_dw[l], b_dw[l], ln_g[l], ln_b[l], w_conv_out[l], w_attn_out[l],
             w_out[l], g_ffn[l], w_up[l], w_fdw[l], b_fdw[l], w_down[l])
        zc = jnp.zeros((xp.shape[0], CONV_WIDTH - 1, C_CONV), xp.dtype)
        zf = jnp.zeros((xp.shape[0], FFN_CONV_WIDTH - 1, 2 * D_FF), xp.dtype)
        xp, c1, kv1, f1 = _layer(xp, 0, zc, None, zf, True, *w)
        hist = ((cache_k_w128[l], cache_v_w128[l]), (cache_k_w512[l], cache_v_w512[l]),
                (cache_k_w2048[l], cache_v_w2048[l]))
        xs, c2, kv2, f2 = _layer(xs, PAST_LEN, state_conv[l], hist, state_ffn_conv[l], False, *w)
        cp.append(c1); cs.append(c2); kvp.append(kv1); kvs.append(kv2); fp.append(f1); fs.append(f2)

    def st(lst, gi, j):
        return jnp.stack([e[gi][j] for e in lst], axis=0)

    y_prompt = rms_norm(xp, g_final)
    y_sample = rms_norm(xs, g_final)
    return (y_prompt, y_sample,
            jnp.stack(cp, axis=0), jnp.stack(cs, axis=0),
            st(kvp, 0, 0), st(kvp, 0, 1), st(kvs, 0, 0), st(kvs, 0, 1),
            st(kvp, 1, 0), st(kvp, 1, 1), st(kvs, 1, 0), st(kvs, 1, 1),
            st(kvp, 2, 0), st(kvp, 2, 1), st(kvs, 2, 0), st(kvs, 2, 1),
            jnp.stack(fp, axis=0), jnp.stack(fs, axis=0))
```

```python
import contextlib
import types
import numpy as np
import concourse.bass as bass
import concourse.mybir as mybir
from concourse.bass_utils import run_bass_kernel_spmd

F32 = mybir.dt.float32
BF16 = mybir.dt.bfloat16
ALU = mybir.AluOpType
AF = mybir.ActivationFunctionType
AX = mybir.AxisListType

D = 1024
DFF = 2816
NS = 4
NT = 512
GROUPS = ((128, 1), (512, 4), (2048, 16))
EPS = 1e-6
ENGS = ("pe", "act", "dve", "pool", "sp")


class Sched:
    def __init__(self, nc, st, nsem=100):
        self.nc = nc
        self.ops = {e: [] for e in ENGS}
        self.cnt = {}
        self.res_w = {}
        self.res_r = {}
        self.waited = {e: {} for e in ENGS}
        self.pool = [st.enter_context(nc.semaphore("sm%d" % i)) for i in range(nsem)]
        self.sem = {}
        self.alias = {}

    def _sk(self, k):
        if k not in self.cnt:
            self.cnt[k] = 0
            assert len(self.sem) < len(self.pool), "out of semaphores"
            self.sem[k] = self.pool[len(self.sem)]
        return k

    @staticmethod
    def _freeze(fn):
        if fn.__closure__ is None:
            return fn
        cells = []
        for c in fn.__closure__:
            try:
                cells.append(types.CellType(c.cell_contents))
            except ValueError:
                cells.append(c)
        return types.FunctionType(fn.__code__, fn.__globals__, fn.__name__, fn.__defaults__, tuple(cells))

    def op(self, eng, fn, reads=(), writes=(), dma=None, sig=True):
        fn = self._freeze(fn)
        waits = {}
        reads = [x for r in reads for x in [r] + self.alias.get(r, [])]
        writes = [x for r in writes for x in [r] + self.alias.get(r, [])]
        writes = writes + [r for r in reads if r.startswith("ps") or r.startswith("ptb")]

        def need(w):
            if w[1] > waits.get(w[0], 0):
                waits[w[0]] = w[1]

        for r in reads:
            if r in self.res_w:
                need(self.res_w[r])
        for r in writes:
            if r in self.res_w:
                need(self.res_w[r])
            for sk, v in self.res_r.get(r, {}).items():
                need((sk, v))
        if dma is None:
            sk = self._sk(eng)
            inc = 1 if sig else 0
        else:
            sk = self._sk("d:" + str(dma))
            inc = 16
        self.cnt[sk] += inc
        val = self.cnt[sk] if inc else self.cnt[sk] + 1
        wl = []
        for k, v in waits.items():
            if k == "pe" and eng == "pe" and dma is None:
                continue
            if self.waited[eng].get(k, 0) >= v:
                continue
            self.waited[eng][k] = v
            wl.append((k, v))
        for r in writes:
            self.res_w[r] = (sk, val)
            self.res_r[r] = {}
        for r in reads:
            d = self.res_r.setdefault(r, {})
            if d.get(sk, 0) < val:
                d[sk] = val
        self.ops[eng].append((wl, fn, sk, inc))

    def barrier(self, engs=ENGS):
        for e in engs:
            wl = []
            for k, v in self.cnt.items():
                if v > 0 and self.waited[e].get(k, 0) < v:
                    self.waited[e][k] = v
                    wl.append((k, v))
            if wl:
                self.ops[e].append((wl, None, None, 0))

    def emit(self):
        nc = self.nc
        with nc.Block() as block:
            def run(e, eng):
                for wl, fn, sk, inc in self.ops[eng]:
                    for k, v in wl:
                        e.wait_ge(self.sem[k], v)
                    if fn is not None:
                        ins = fn(e)
                        if inc:
                            ins.then_inc(self.sem[sk], inc)

            @block.tensor
            def _(e):
                run(e, "pe")

            @block.scalar
            def _(e):
                run(e, "act")

            @block.vector
            def _(e):
                run(e, "dve")

            @block.gpsimd
            def _(e):
                run(e, "pool")

            @block.sync
            def _(e):
                run(e, "sp")
        self.ops = {e: [] for e in ENGS}


PP = {}
_o = 0
for _n, _w in (("gmix", 8), ("bdw", 8), ("lng", 8), ("lnb", 8), ("gffn", 8), ("wdw", 8 * 31), ("wfdw", 44 * 3),
               ("bfdw", 44)):
    PP[_n] = (_o, _w)
    _o += _w
NPP = _o


HALO = 4096


def build(MAIN):
    S_TOK = HALO + MAIN
    NSB = S_TOK // 2048
    NTILE = S_TOK // 128
    NG = MAIN // NT
    nc = bass.Bass("TRN2", target_bir_lowering=False)

    def din(name, shape, dt=F32):
        return nc.dram_tensor(name, list(shape), dt, kind="ExternalInput").ap()

    def dout(name, shape, dt=F32):
        return nc.dram_tensor(name, list(shape), dt, kind="ExternalOutput").ap()

    def dscr(name, shape, dt):
        return nc.dram_tensor(name, list(shape), dt, kind="Internal").ap()

    xp = din("xp", [S_TOK, D])
    xs = din("xs", [NS, D])
    sconv = din("sconv", [NS, 30, D])
    cks = [din("ck%d" % g, [NS, GROUPS[g][0], 256]) for g in range(3)]
    cvs = [din("cv%d" % g, [NS, GROUPS[g][0], 256]) for g in range(3)]
    sffn = din("sffn", [NS, 2, 2 * DFF])
    w_in = din("w_in", [D, 6400])
    w_co = din("w_co", [D, D])
    w_ao = din("w_ao", [256, D])
    w_out = din("w_out", [D, D])
    w_up = din("w_up", [D, 2 * DFF])
    w_dn = din("w_dn", [DFF, D])
    pp_d = din("pp", [128, NPP])
    gfin_d = din("gfin", [128, D])
    ident_d = din("ident", [128, 128])
    mask_d = din("mask2", [128, 2, 128])
    csp_d = din("csp", [128, NTILE, 16])
    snp_d = din("snp", [128, NTILE, 16])
    css_d = din("css", [NS, 16])
    sns_d = din("sns", [NS, 16])
    sel_d = din("sel", [NS, NS, 128])
    vld_d = [din("vld%d" % g, [128, NSB, GROUPS[g][1], 16 // GROUPS[g][1]]) for g in range(3)]
    hv_d = din("hv", [128, 1])

    wb_in = dscr("wb_in", [D, 6400], BF16)
    wb_co = dscr("wb_co", [D, D], BF16)
    wb_ao = dscr("wb_ao", [256, D], BF16)
    wb_out = dscr("wb_out", [D, D], BF16)
    wb_up = dscr("wb_up", [D, 2 * DFF], BF16)
    wb_dn = dscr("wb_dn", [DFF, D], BF16)
    import os as _os0
    DBG = bool(_os0.environ.get("KDBG"))
    oT_d = (dout if DBG else dscr)("oT_d", [64, 4, S_TOK], BF16)
    if DBG:
        dbg_sT = dout("dbg_sT", [128, 8, NT], BF16)
        dbg_mix = dout("dbg_mix", [128, 8, NT], BF16)
        dbg_xmid = dout("dbg_xmid", [NT, D], F32)
        dbg_acc = dout("dbg_acc", [128, 8, NT], F32)

    y_p = dout("y_p", [MAIN, D])
    y_s = dout("y_s", [NS, D])
    conv_p = dout("conv_p", [30, D])
    conv_s = dout("conv_s", [NS, 30, D])
    kp = [dout("k%d_p" % g, [min(GROUPS[g][0], S_TOK), 256]) for g in range(3)]
    vp = [dout("v%d_p" % g, [min(GROUPS[g][0], S_TOK), 256]) for g in range(3)]
    ks_o = [dout("k%d_s" % g, [NS, GROUPS[g][0], 256]) for g in range(3)]
    vs_o = [dout("v%d_s" % g, [NS, GROUPS[g][0], 256]) for g in range(3)]
    ffn_p = dout("ffn_p", [2, 2 * DFF])
    ffn_s = dout("ffn_s", [NS, 2, 2 * DFF])

    with contextlib.ExitStack() as gst:
        S = Sched(nc, gst)

        def sbuf(st, name, shape, dt):
            return st.enter_context(nc.sbuf_tensor("sb_" + name, list(shape), dt))

        NPS = 6
        PSB = [gst.enter_context(nc.psum_tensor("psb%d" % i, [128, 512], F32)) for i in range(NPS)]
        PTB = gst.enter_context(nc.psum_tensor("ptb", [128, 1024], BF16))
        PTB2 = gst.enter_context(nc.psum_tensor("ptb2", [128, 1024], BF16))
        ps_i = [0]

        def psum():
            i = ps_i[0] % NPS
            ps_i[0] += 1
            return PSB[i], "ps%d" % i

        pp = sbuf(gst, "pp", [128, NPP], F32)
        identf = sbuf(gst, "identf", [128, 128], F32)
        identb = sbuf(gst, "identb", [128, 128], BF16)
        onesb = sbuf(gst, "onesb", [128, 128], BF16)
        onesf = sbuf(gst, "onesf", [128, 128], F32)
        epst = sbuf(gst, "epst", [128, 1], F32)
        stat = sbuf(gst, "stat", [128, 3, 4 * NTILE + 16], F32)
        xsb = [sbuf(gst, "xsb%d" % i, [128, D], BF16) for i in range(2)]
        oTs = sbuf(gst, "oTs", [128, 2, NS], BF16)
        sts = sbuf(gst, "sts", [NS, 2304], F32)
        stat_i = [0]

        ps45 = [0]

        def psum45():
            i = 4 + ps45[0] % 2
            ps45[0] += 1
            return PSB[i], "ps%d" % i

        def P(name, c=None):
            o, w = PP[name]
            if c is None:
                return pp[:, o:o + w]
            return pp[:, o + c:o + c + 1]

        def dma(eng, out, in_, reads=(), writes=(), key=None):
            S.op(eng, lambda e: e.dma_start(out=out, in_=in_), reads=reads, writes=writes, dma=key)

        dma("sp", pp[:], pp_d, writes=["pp"], key="pp")
        dma("sp", identf[:], ident_d, writes=["identf"], key="identf")
        S.op("dve", lambda e: e.tensor_copy(out=identb[:], in_=identf[:]), reads=["identf"], writes=["identb"])
        S.op("dve", lambda e: e.memset(onesb[:], 1.0), writes=["onesb"])
        S.op("dve", lambda e: e.memset(onesf[:], 1.0), writes=["onesf"])
        S.op("dve", lambda e: e.memset(epst[:], EPS), writes=["epst"])
        S.op("dve", lambda e: e.memset(stat[:], 0.0), writes=["stat"])
        def conv_w(dst, src, rows, c0, c1, key, after=()):
            for r0 in range(0, rows, 256):
                r1 = min(rows, r0 + 256)
                dma("pool", dst[r0:r1, c0:c1], src[r0:r1, c0:c1], reads=list(after), writes=[key], key=key)
        for cg in range(5):
            conv_w(wb_in, w_in, D, 2048 + 512 * cg, 2048 + min(512 * (cg + 1), 2304), "wb_qkv%d" % cg)
        QKV_ALL = ["wb_qkv%d" % cg for cg in range(5)]
        conv_w(wb_in, w_in, D, 0, 2048, "wb_in1", after=QKV_ALL)
        conv_w(wb_in, w_in, D, 4352, 6400, "wb_in3")
        conv_w(wb_co, w_co, D, 0, D, "wb_co")
        conv_w(wb_ao, w_ao, 256, 0, D, "wb_ao")
        conv_w(wb_out, w_out, D, 0, D, "wb_out")
        conv_w(wb_up, w_up, D, 0, 2 * DFF, "wb_up")
        conv_w(wb_dn, w_dn, DFF, 0, D, "wb_dn")
        def flat16(ap):
            return ap.rearrange("w c -> (w c)").rearrange("(a b) -> a b", a=16)
        for g in range(3):
            W = GROUPS[g][0]
            for (src, dst, nm) in ((cks[g], ks_o[g], "k"), (cvs[g], vs_o[g], "v")):
                for s in range(NS):
                    dma("pool", flat16(dst[s, 0:W - 1, :]), flat16(src[s, 1:W, :]), key="cshift")
        for s in range(NS):
            dma("pool", flat16(conv_s[s, 0:29, :]), flat16(sconv[s, 1:30, :]), key="cshift")
            dma("pool", flat16(ffn_s[s, 0:1, :]), flat16(sffn[s, 1:2, :]), key="cshift")

        import os as _os
        KSTOP = int(_os.environ.get("KSTOP", "9"))
        if KSTOP == 0:
            S.barrier()
            S.emit()
            return nc
        def norm_T(src_ap, rd, TT, dst3, dst_res, gain_name, xi):
            col = stat_i[0]
            stat_i[0] += 1
            xb = xsb[xi % 2]
            xr = "xsb%d" % (xi % 2)
            S.op("act", lambda e: e.activation(out=xb[0:TT, :], in_=src_ap, func=AF.Square,
                                               accum_out=stat[0:TT, 0, col:col + 1]),
                 reads=[rd], writes=[xr, "stat%d" % col])
            S.op("act", lambda e: e.activation(out=stat[0:TT, 1, col:col + 1], in_=stat[0:TT, 0, col:col + 1],
                                               func=AF.Sqrt, scale=1.0 / D, bias=epst[0:TT, 0:1]),
                 reads=["stat%d" % col, "epst"], writes=["stat%d" % col])
            S.op("dve", lambda e: e.reciprocal(out=stat[0:TT, 2, col:col + 1], in_=stat[0:TT, 1, col:col + 1]),
                 reads=["stat%d" % col], writes=["stat%d" % col])
            S.op("act", lambda e: e.activation(out=xb[0:TT, :], in_=src_ap, func=AF.Copy,
                                               scale=stat[0:TT, 2, col:col + 1]),
                 reads=[rd, "stat%d" % col], writes=[xr])
            for c in range(8):
                S.op("pe", lambda e, c=c: e.transpose(PTB[:, c * 128:c * 128 + TT], xb[0:TT, c * 128:(c + 1) * 128],
                                                      identb[0:TT, 0:TT]),
                     reads=[xr, "identb"], writes=["ptb"], sig=(c == 7))
            o, w = PP[gain_name]
            S.op("dve", lambda e: e.tensor_tensor(
                out=dst3, in0=PTB[:, :].rearrange("p (c t) -> p c t", c=8)[:, :, 0:TT],
                in1=pp[:, o:o + 8].unsqueeze(2).to_broadcast([128, 8, TT]), op=ALU.mult),
                reads=["ptb", "pp"], writes=[dst_res])
            return col

        with contextlib.ExitStack() as st1:
            hT1 = sbuf(st1, "hT1", [128, 8, 2048], BF16)
            hTs = sbuf(st1, "hTs", [128, 8, NS], BF16)
            QT = [sbuf(st1, "QT%d" % g, [128, 2, GROUPS[g][1], 2048 // GROUPS[g][1]], BF16) for g in range(3)]
            KT = [sbuf(st1, "KT%d" % g, [128, 2, GROUPS[g][1], 128 + 2048 // GROUPS[g][1]], BF16) for g in range(3)]
            NB = [1 + 16 // GROUPS[g][1] for g in range(3)]
            VV = [sbuf(st1, "VV%d" % g, [128, GROUPS[g][1], NB[g], 260], BF16) for g in range(3)]
            Wg = [sbuf(st1, "Wg%d" % i, [128, 8, 512], BF16) for i in range(1)]
            xt = [sbuf(st1, "xt%d" % i, [128, D], F32) for i in range(2)]
            stg = [sbuf(st1, "stg%d" % i, [128, 512], F32) for i in range(2)]
            qkb = [sbuf(st1, "qkb%d" % i, [128, 512], BF16) for i in range(2)]
            rtmp = [sbuf(st1, "rtmp%d" % i, [128, 24, 16], F32) for i in range(2)]
            vst = [sbuf(st1, "vst%d" % i, [128, 256], F32) for i in range(2)]
            PT2 = sbuf(st1, "PT2", [128, 16, 2, 128], BF16)
            PTs = [sbuf(st1, "PTs%d" % i, [128, 2, 2, 128], BF16) for i in range(2)]
            csp = sbuf(st1, "csp", [128, 16, 16], F32)
            snp = sbuf(st1, "snp", [128, 16, 16], F32)
            css = sbuf(st1, "css", [NS, 16], F32)
            sns = sbuf(st1, "sns", [NS, 16], F32)
            maskf = sbuf(st1, "maskf", [128, 2, 128], F32)
            vld = [sbuf(st1, "vld%d" % g, [128, GROUPS[g][1], 16 // GROUPS[g][1]], F32) for g in range(3)]
            maskb = sbuf(st1, "maskb", [128, 2, 128], BF16)
            rec = [sbuf(st1, "rec%d" % i, [64, 512], F32) for i in range(1)]
            oTsb = sbuf(st1, "oTsb", [64, 2048], BF16)
            selb = sbuf(st1, "selb", [NS, NS, 128], F32)
            Kc = [sbuf(st1, "Kc%d" % i, [128, 256], F32) for i in range(3)]
            Vc = [sbuf(st1, "Vc%d" % i, [128, 256], F32) for i in range(3)]
            prod = sbuf(st1, "prod", [128, 256], F32)
            sT4 = sbuf(st1, "sT4", [128, 24], F32)
            sm = sbuf(st1, "sm", [NS, 768], F32)
            pcur = sbuf(st1, "pcur", [NS, 12], F32)
            pcm = sbuf(st1, "pcm", [NS, NS, 12], F32)
            id4 = sbuf(st1, "id4", [NS, NS], F32)
            oTf = sbuf(st1, "oTf", [128, 2, NS], F32)

            for g in range(3):
                S.op("pool", lambda e, g=g: e.memset(VV[g][:, :, :, :], 1.0), writes=["VV%d" % g])
            dma("sp", css[:], css_d, writes=["css"], key="css")
            dma("sp", sns[:], sns_d, writes=["sns"], key="sns")
            dma("sp", maskf[:], mask_d, writes=["maskf"], key="maskf")
            dma("sp", selb[:], sel_d, writes=["selb"], key="selb")
            S.op("dve", lambda e: e.tensor_copy(out=maskb[:], in_=maskf[:]), reads=["maskf"], writes=["maskb"])
            S.op("dve", lambda e: e.tensor_copy(out=id4[:], in_=identf[0:NS, 0:NS]), reads=["identf"], writes=["id4"])

            def rope(stv, rd, TT, nh, cs_ap, sn_ap, tab_res, ri):
                rt = rtmp[ri % 2]
                rr = "rtmp%d" % (ri % 2)
                t1 = rt[0:TT, 0:nh, :]
                csb = cs_ap.unsqueeze(1).to_broadcast([TT, nh, 16])
                S.op("dve", lambda e: e.tensor_tensor(out=t1, in0=stv[:, :, 0:16], in1=csb, op=ALU.mult),
                     reads=[rd] + tab_res, writes=[rr])
                S.op("dve", lambda e: e.tensor_tensor(out=stv[:, :, 0:8], in0=stv[:, :, 0:8],
                                                      in1=sn_ap[:, 8:16].unsqueeze(1).to_broadcast([TT, nh, 8]),
                                                      op=ALU.mult), reads=[rd] + tab_res, writes=[rd])
                S.op("dve", lambda e: e.tensor_tensor(out=stv[:, :, 8:16], in0=stv[:, :, 8:16],
                                                      in1=sn_ap[:, 0:8].unsqueeze(1).to_broadcast([TT, nh, 8]),
                                                      op=ALU.mult), reads=[rd] + tab_res, writes=[rd])
                S.op("dve", lambda e: e.tensor_tensor(out=t1[:, :, 0:8], in0=t1[:, :, 0:8], in1=stv[:, :, 8:16],
                                                      op=ALU.add), reads=[rd, rr], writes=[rr])
                S.op("dve", lambda e: e.tensor_tensor(out=t1[:, :, 8:16], in0=t1[:, :, 8:16], in1=stv[:, :, 0:8],
                                                      op=ALU.add), reads=[rd, rr], writes=[rr])
                S.op("dve", lambda e: e.tensor_copy(out=stv[:, :, 0:16], in_=t1), reads=[rr], writes=[rd])

            xi = 0
            ri = 0
            ev = 0
            for _k in range(int(_os.environ.get("KDUMMY", "0"))):
                if _os.environ.get("KDUMMYT") == "memset":
                    S.op("dve", lambda e: e.memset(prod[:, :], 0.0), writes=["prod"])
                else:
                    S.op("dve", lambda e: e.tensor_tensor(out=prod[:, :], in0=prod[:, :], in1=prod[:, :], op=ALU.mult),
                         writes=["prod"])
            for sb in range(NSB):
                T0 = sb * 2048
                last = (sb == NSB - 1)
                dma("sp", csp[:], csp_d[:, 16 * sb:16 * (sb + 1), :], writes=["csp"], key="csp")
                for g in range(3):
                    dma("sp", vld[g][:, :, :], vld_d[g][:, sb, :, :], writes=["vld%d" % g], key="vld%d" % g)
                dma("sp", snp[:], snp_d[:, 16 * sb:16 * (sb + 1), :], writes=["snp"], key="snp")
                for i in range(16):
                    b = xi % 2
                    dma("sp", xt[b][:], xp[T0 + 128 * i:T0 + 128 * (i + 1), :], writes=["xt%d" % b], key="xt%d" % b)
                    norm_T(xt[b][:], "xt%d" % b, 128, hT1[:, :, 128 * i:128 * (i + 1)], "hT1_%d" % i, "gmix", xi)
                    xi += 1
                if sb == 1:
                    b = xi % 2
                    dma("sp", xt[b][0:NS, :], xs, writes=["xt%d" % b], key="xt%d" % b)
                    norm_T(xt[b][0:NS, :], "xt%d" % b, NS, hTs[:, :, 0:NS], "hTs", "gmix", xi)
                    xi += 1
                if KSTOP == 10:
                    S.barrier()
                    S.emit()
                    return nc
                hT_all = ["hT1_%d" % i for i in range(16)]
                if sb > 0:
                    for g in range(3):
                        M = 2048 // GROUPS[g][1]
                        S.op("pool", lambda e, g=g, M=M: e.tensor_copy(out=KT[g][:, :, :, 0:128],
                                                                        in_=KT[g][:, :, :, M:M + 128]),
                             reads=["KT%d" % g], writes=["KT%d" % g])
                        S.op("pool", lambda e, g=g: e.tensor_copy(out=VV[g][:, :, 0, :], in_=VV[g][:, :, NB[g] - 1, :]),
                             reads=["VV%d" % g], writes=["VV%d" % g])
                for cg in range(5):
                    if sb == 0 and cg in (0, 1, 3):
                        continue
                    wcols = 512 if cg < 4 else 256
                    wb = Wg[0]
                    wr = "Wg0"
                    dma("sp", wb[:, :, 0:wcols],
                        wb_in[:, 2048 + 512 * cg:2048 + 512 * cg + wcols].rearrange("(kc p) n -> p kc n", p=128),
                        reads=["wb_qkv%d" % cg], writes=[wr], key=wr)
                    if sb == 1:
                        pst, pr = psum()
                        for h0 in range(0, wcols, 256):
                            for kc in range(8):
                                S.op("pe", lambda e, kc=kc, pst=pst, h0=h0: e.matmul(
                                    pst[0:NS, h0:h0 + 256], lhsT=hTs[:, kc, 0:NS], rhs=wb[:, kc, h0:h0 + 256],
                                    start=(kc == 0), stop=(kc == 7)), reads=["hTs", wr], writes=[pr], sig=(kc == 7))
                        for (c0, c1) in ((0, 256), (256, 512)):
                            if c0 >= wcols:
                                continue
                            gcol = 512 * cg + c0
                            sc = 0.125 if gcol < 768 else 1.0
                            S.op("act", lambda e, c0=c0, c1=c1, pst=pst, sc=sc, gcol=gcol: e.activation(
                                out=sts[0:NS, gcol:gcol + 256], in_=pst[0:NS, c0:c1], func=AF.Copy, scale=sc),
                                reads=[pr], writes=["sts"])
                    if KSTOP == 20:
                        S.barrier()
                        S.emit()
                        return nc
                    if cg < 3:
                        for i in range(16):
                            pst, pr = psum()
                            for kc in range(8):
                                S.op("pe", lambda e, kc=kc, pst=pst, i=i: e.matmul(
                                    pst[:, 0:512], lhsT=hT1[:, kc, 128 * i:128 * (i + 1)], rhs=wb[:, kc, 0:512],
                                    start=(kc == 0), stop=(kc == 7)), reads=["hT1_%d" % i, wr], writes=[pr], sig=(kc == 7))
                            sg = stg[ev % 2]
                            sr = "stg%d" % (ev % 2)
                            qb_ = qkb[ev % 2]
                            qr = "qkb%d" % (ev % 2)
                            ev += 1
                            for (c0, c1) in ((0, 256), (256, 512)):
                                sc = 0.125 if (512 * cg + c0) < 768 else 1.0
                                S.op("act", lambda e, c0=c0, c1=c1, pst=pst, sc=sc, sg=sg: e.activation(
                                    out=sg[:, c0:c1], in_=pst[:, c0:c1], func=AF.Copy, scale=sc),
                                    reads=[pr], writes=[sr])
                            ti = sb * 16 + i
                            if not _os.environ.get("KNOROPE"):
                              rope(sg[:, :].rearrange("p (h d) -> p h d", d=64), sr, 128, 8, csp[:, i, :], snp[:, i, :],
                                 ["csp", "snp"], ri)
                            ri += 1
                            S.op("pool", lambda e, sg=sg, qb_=qb_: e.tensor_copy(out=qb_[:, :], in_=sg[:, :]),
                                 reads=[sr], writes=[qr])
                            if last and KSTOP != 24:
                                for g in range(3):
                                    W = min(GROUPS[g][0], S_TOK)
                                    kcol = 768 + 256 * g
                                    if not (512 * cg <= kcol < 512 * cg + 512):
                                        continue
                                    tpos = 2048 - 128 * (16 - i)
                                    if S_TOK - (T0 + 128 * i) > W:
                                        continue
                                    row0 = W - (S_TOK - (T0 + 128 * i))
                                    lc = kcol - 512 * cg
                                    dma("sp", kp[g][row0:row0 + 128, :], sg[:, lc:lc + 256], reads=[sr], key="o" + sr)
                            for j in range(4):
                                S.op("pe", lambda e, j=j, qb_=qb_: e.transpose((PTB if j < 2 else PTB2)[:, j * 128:(j + 1) * 128],
                                                                                qb_[:, j * 128:(j + 1) * 128], identb[:, :]),
                                     reads=[qr, "identb"], writes=["ptb" if j < 2 else "ptb2"], sig=(j % 2 == 1))
                            for j in range(4):
                                if _os.environ.get("KNOEVAC"):
                                    continue
                                cc = cg * 4 + j
                                isq = cc < 6
                                g = (cc % 6) // 2
                                half = cc % 2
                                d = GROUPS[g][1]
                                mlo = 128 * i // d
                                cnt = 128 // d
                                if isq:
                                    dst = QT[g][:, half, :, mlo:mlo + cnt]
                                    dres = "QT%d" % g
                                else:
                                    dst = KT[g][:, half, :, 128 + mlo:128 + mlo + cnt]
                                    dres = "KT%d" % g
                                src = (PTB if j < 2 else PTB2)[:, j * 128:(j + 1) * 128].rearrange("p (m r) -> p r m", r=d)
                                if j < 2:
                                    S.op("act", lambda e, dst=dst, src=src: e.activation(out=dst, in_=src, func=AF.Copy),
                                         reads=["ptb"], writes=[dres])
                                else:
                                    S.op("dve", lambda e, dst=dst, src=src: e.tensor_copy(out=dst, in_=src),
                                         reads=["ptb2"], writes=[dres])
                            if KSTOP == 21:
                                S.barrier()
                                S.emit()
                                return nc
                            if (KSTOP in (22, 24) and cg == 2 and i == 15) or (KSTOP == 25 and i == 1) or (KSTOP == 33 and cg == 1 and i == 14) or (KSTOP == 31 and cg == 1 and i == 1) or (KSTOP == 32 and cg == 1 and i == 7) or (KSTOP == 29 and cg == 1 and i == 15) or (KSTOP == 30 and cg == 2 and i == 0) or (KSTOP == 28 and cg == 1 and i == 0) or (KSTOP == 26 and i == 7) or (KSTOP == 27 and cg == 0 and i == 15):
                                S.barrier()
                                S.emit()
                                return nc
                    else:
                        jobs = []
                        if cg == 3:
                            for blk in range(16):
                                jobs.append((0, 0, blk, 0, hT1[:, :, 128 * blk:128 * (blk + 1)], ["hT1_%d" % blk]))
                            for r in range(4):
                                for blk in range(4):
                                    jobs.append((1, r, blk, 256, hT1[:, :, 512 * blk + r:512 * (blk + 1):4],
                                                 ["hT1_%d" % t for t in range(4 * blk, 4 * blk + 4)]))
                        else:
                            for r in range(16):
                                jobs.append((2, r, 0, 0, hT1[:, :, r:2048:16], hT_all))
                        for (g, r, blk, c0, lh, lres) in jobs:
                            pst, pr = psum()
                            for kc in range(8):
                                S.op("pe", lambda e, kc=kc, pst=pst, lh=lh, c0=c0: e.matmul(
                                    pst[:, 0:256], lhsT=lh[:, kc, :], rhs=wb[:, kc, c0:c0 + 256],
                                    start=(kc == 0), stop=(kc == 7)), reads=lres + [wr], writes=[pr], sig=(kc == 7))
                            S.op("act", lambda e, pst=pst, g=g, r=r, blk=blk: e.activation(
                                out=VV[g][:, r, 1 + blk, :].rearrange("p (s e) -> p s e", e=65)[:, :, 0:64],
                                in_=pst[:, 0:256].rearrange("p (s e) -> p s e", e=64), func=AF.Copy,
                                scale=vld[g][:, r, blk:blk + 1]),
                                reads=[pr, "vld%d" % g], writes=["VV%d" % g])
                            S.op("pool", lambda e, g=g, r=r, blk=blk: e.tensor_copy(
                                out=VV[g][:, r, 1 + blk, :].rearrange("p (s e) -> p s e", e=65)[:, :, 64:65],
                                in_=vld[g][:, r, blk:blk + 1].unsqueeze(1).to_broadcast([128, 4, 1])),
                                reads=["vld%d" % g], writes=["VV%d" % g])
                            if last:
                                W = min(GROUPS[g][0], S_TOK)
                                d = GROUPS[g][1]
                                nblk = 16 // d
                                first_tok = T0 + r + d * 128 * blk
                                if S_TOK - (T0 + d * 128 * blk) <= W:
                                    row0 = W - (S_TOK - first_tok)
                                    vb = vst[ev % 2]
                                    vr = "vst%d" % (ev % 2)
                                    ev += 1
                                    S.op("dve", lambda e, pst=pst, vb=vb: e.tensor_copy(out=vb[:, :], in_=pst[:, 0:256]),
                                         reads=[pr], writes=[vr])
                                    dma("sp", vp[g][row0:W:d, :], vb[:, :], reads=[vr], key="o" + vr)
                if KSTOP == 23 and False:
                    S.barrier()
                    S.emit()
                    return nc
                if KSTOP == 11:
                    S.barrier()
                    S.emit()
                    return nc
                if sb == 1:
                    rope(sts[0:NS, 0:1536].rearrange("p (h d) -> p h d", d=64), "sts", NS, 24, css[:, :], sns[:, :],
                         ["css", "sns"], ri)
                    ri += 1
                    for g in range(3):
                        W = GROUPS[g][0]
                        dma("sp", ks_o[g][:, W - 1, :], sts[0:NS, 768 + 256 * g:768 + 256 * (g + 1)], reads=["sts"],
                            key="osts")
                        dma("sp", vs_o[g][:, W - 1, :], sts[0:NS, 1536 + 256 * g:1536 + 256 * (g + 1)], reads=["sts"],
                            key="osts")
                    S.op("dve", lambda e: e.tensor_tensor(out=sm[0:NS, 0:768], in0=sts[0:NS, 0:768],
                                                          in1=sts[0:NS, 768:1536], op=ALU.mult),
                         reads=["sts"], writes=["sm"])
                    S.op("dve", lambda e: e.tensor_reduce(out=pcur[:, :],
                                                          in_=sm[0:NS, 0:768].rearrange("p (h d) -> p h d", d=64),
                                                          axis=AX.X, op=ALU.add), reads=["sm"], writes=["pcur"])
                    S.op("act", lambda e: e.activation(out=pcur[:, :], in_=pcur[:, :], func=AF.Exp),
                         reads=["pcur"], writes=["pcur"])
                    S.op("dve", lambda e: e.tensor_tensor(out=pcm[:, :, :],
                                                          in0=pcur[:, :].unsqueeze(1).to_broadcast([NS, NS, 12]),
                                                          in1=id4[:, :].unsqueeze(2).to_broadcast([NS, NS, 12]),
                                                          op=ALU.mult), reads=["pcur", "id4"], writes=["pcm"])
                    for s in range(NS):
                        pnum, pnr = psum()
                        for g in range(3):
                            W, d = GROUPS[g]
                            kb_, vb_ = Kc[g], Vc[g]
                            kr, vr = "Kc%d" % g, "Vc%d" % g
                            dma("sp", kb_[:, :], cks[g][s, 0:W:d, :], writes=[kr], key=kr)
                            dma("sp", vb_[:, :], cvs[g][s, 0:W:d, :], writes=[vr], key=vr)
                            pq, pqr = psum()
                            S.op("pe", lambda e, pq=pq, s=s, g=g: e.matmul(pq[:, 0:256], lhsT=selb[0:NS, s, :],
                                                                           rhs=sts[0:NS, 256 * g:256 * (g + 1)],
                                                                           start=True, stop=True),
                                 reads=["selb", "sts"], writes=[pqr])
                            S.op("dve", lambda e, pq=pq, kb_=kb_: e.tensor_tensor(out=prod[:, :], in0=kb_[:, :],
                                                                                    in1=pq[:, 0:256], op=ALU.mult),
                                 reads=[kr, pqr], writes=["prod"])
                            S.op("dve", lambda e, g=g: e.tensor_reduce(out=sT4[:, 4 * g:4 * g + 4],
                                                                  in_=prod[:, :].rearrange("p (h d) -> p h d", d=64),
                                                                  axis=AX.X, op=ALU.add), reads=["prod"], writes=["sT4"])
                        S.op("act", lambda e: e.activation(out=sT4[:, 12:24], in_=sT4[:, 0:12], func=AF.Exp),
                             reads=["sT4"], writes=["sT4"])
                        for c in range(3):
                            for g in range(3):
                                pT_ = sT4[:, 12 + 4 * g:16 + 4 * g]
                                pc_ = pcm[0:NS, s, 4 * g:4 * g + 4]
                                if c < 2:
                                    l1 = Vc[g][:, 128 * c:128 * (c + 1)]
                                    l2 = sts[0:NS, 1536 + 256 * g + 128 * c:1536 + 256 * g + 128 * (c + 1)]
                                    r1 = ["Vc%d" % g, "sT4"]
                                else:
                                    l1 = onesf[:, :]
                                    l2 = onesf[0:NS, :]
                                    r1 = ["onesf", "sT4"]
                                S.op("pe", lambda e, c=c, g=g, pnum=pnum, l1=l1, pT_=pT_: e.matmul(
                                    pnum[:, 4 * c:4 * c + 4], lhsT=l1, rhs=pT_, start=(g == 0), stop=False),
                                    reads=r1, writes=[pnr])
                                S.op("pe", lambda e, c=c, g=g, pnum=pnum, l2=l2, pc_=pc_: e.matmul(
                                    pnum[:, 4 * c:4 * c + 4], lhsT=l2, rhs=pc_, start=False, stop=(g == 2)),
                                    reads=["sts", "pcm", "onesf"], writes=[pnr], sig=(g == 2))
                        S.op("dve", lambda e, pnum=pnum: e.reciprocal(out=sT4[:, 0:4], in_=pnum[:, 8:12]),
                             reads=[pnr], writes=["sT4"])
                        for c in range(2):
                            for hf in range(2):
                                slot = 2 * c + hf
                                S.op("dve", lambda e, c=c, hf=hf, slot=slot, pnum=pnum, s=s: e.tensor_tensor(
                                    out=oTf[64 * hf:64 * (hf + 1), c, s:s + 1],
                                    in0=pnum[64 * hf:64 * (hf + 1), 4 * c + slot:4 * c + slot + 1],
                                    in1=sT4[64 * hf:64 * (hf + 1), slot:slot + 1], op=ALU.mult),
                                    reads=[pnr, "sT4"], writes=["oTf"])
                    S.op("dve", lambda e: e.tensor_copy(out=oTs[:, :, :], in_=oTf[:, :, :]), reads=["oTf"], writes=["oTs"])
                    if DBG:
                        d1 = dout("dbg_oTf", [128, 2, NS], F32)
                        dma("sp", d1, oTf[:, :, :], reads=["oTf"], key="dbg")
                        d2 = dout("dbg_sts", [NS, 2304], F32)
                        dma("sp", d2, sts[:, :], reads=["sts"], key="dbg")
                        d3 = dout("dbg_pcur", [NS, 12], F32)
                        dma("sp", d3, pcur[:, :], reads=["pcur"], key="dbg")

                if KSTOP == 12:
                    S.barrier()
                    S.emit()
                    return nc
                if DBG and sb == 0:
                    for g in range(3):
                        dq = dout("dbg_QT%d" % g, [128, 2, GROUPS[g][1], 2048 // GROUPS[g][1]], BF16)
                        dk = dout("dbg_KT%d" % g, [128, 2, GROUPS[g][1], 128 + 2048 // GROUPS[g][1]], BF16)
                        dv = dout("dbg_VV%d" % g, [128, GROUPS[g][1], NB[g], 260], BF16)
                        dma("sp", dq, QT[g][:, :, :, :], reads=["QT%d" % g], key="dbg")
                        dma("sp", dk, KT[g][:, :, :, :], reads=["KT%d" % g], key="dbg")
                        dma("sp", dv, VV[g][:, :, :, :], reads=["VV%d" % g], key="dbg")
                pti = 0
                if sb == 0:
                    continue
                ch_list = (3,) if sb == 1 else (0, 1, 2, 3)
                for slot in range(4):
                    c2 = slot // 2
                    pb = 64 * (slot % 2)
                    for rp in range(8):
                        pss, psr = psum45()
                        for q in range(2):
                            r = 2 * rp + q
                            if sb > 0:
                                S.op("pe", lambda e, pss=pss, q=q, r=r: e.matmul(
                                    pss[:, 256 * q:256 * q + 128], lhsT=KT[2][pb:pb + 64, c2, r, 0:128],
                                    rhs=QT[2][pb:pb + 64, c2, r, 0:128], start=True, stop=True),
                                    reads=["KT2", "QT2"], writes=[psr])
                            S.op("pe", lambda e, pss=pss, q=q, r=r: e.matmul(
                                pss[:, 256 * q + 128:256 * q + 256], lhsT=KT[2][pb:pb + 64, c2, r, 128:256],
                                rhs=QT[2][pb:pb + 64, c2, r, 0:128], start=True, stop=True),
                                reads=["KT2", "QT2"], writes=[psr])
                        k0 = 0 if sb > 0 else 1
                        S.op("act", lambda e, pss=pss, rp=rp, k0=k0: e.activation(
                            out=PT2[:, 2 * rp:2 * rp + 2, k0:2, :],
                            in_=pss[:, :].rearrange("p (q k t) -> p q k t", q=2, k=2)[:, :, k0:2, :], func=AF.Exp),
                            reads=[psr], writes=["PT2"])
                        S.op("pool", lambda e, rp=rp, k0=k0: e.tensor_tensor(
                            out=PT2[:, 2 * rp:2 * rp + 2, k0:2, :], in0=PT2[:, 2 * rp:2 * rp + 2, k0:2, :],
                            in1=maskb[:, k0:2, :].unsqueeze(1).to_broadcast([128, 2, 2 - k0, 128]), op=ALU.mult),
                            reads=["PT2", "maskb"], writes=["PT2"])
                    if DBG and sb == 0 and slot == 0:
                        dp2 = dout("dbg_PT2", [128, 16, 2, 128], BF16)
                        dma("sp", dp2, PT2[:, :, :, :], reads=["PT2"], key="dbg")
                    for ch in ch_list:
                        Bk = [(PSB[i_], "ps%d" % i_) for i_ in range(3)]
                        for g in range(2):
                            for pair in range(2):
                                pss, psr = psum45()
                                ptb_ = PTs[pti % 2]
                                ptr = "PTs%d" % (pti % 2)
                                pti += 1
                                info = []
                                for q in range(2):
                                    u = 2 * pair + q
                                    if g == 0:
                                        qb_i = 4 * ch + u
                                        hasp = (sb > 0) or (qb_i > 0)
                                        kprev = KT[0][pb:pb + 64, c2, 0, 128 * qb_i:128 * qb_i + 128]
                                        kcur = KT[0][pb:pb + 64, c2, 0, 128 + 128 * qb_i:256 + 128 * qb_i]
                                        qq = QT[0][pb:pb + 64, c2, 0, 128 * qb_i:128 * qb_i + 128]
                                        vprev = VV[0][:, 0, qb_i, 65 * slot:65 * (slot + 1)]
                                        vcur = VV[0][:, 0, qb_i + 1, 65 * slot:65 * (slot + 1)]
                                        ocols = slice(128 * u, 128 * (u + 1))
                                    else:
                                        hasp = (sb > 0) or (ch > 0)
                                        kprev = KT[1][pb:pb + 64, c2, u, 128 * ch:128 * ch + 128]
                                        kcur = KT[1][pb:pb + 64, c2, u, 128 + 128 * ch:256 + 128 * ch]
                                        qq = QT[1][pb:pb + 64, c2, u, 128 * ch:128 * ch + 128]
                                        vprev = VV[1][:, u, ch, 65 * slot:65 * (slot + 1)]
                                        vcur = VV[1][:, u, ch + 1, 65 * slot:65 * (slot + 1)]
                                        ocols = slice(128 * u, 128 * (u + 1))
                                    if hasp:
                                        S.op("pe", lambda e, pss=pss, q=q, kprev=kprev, qq=qq: e.matmul(
                                            pss[:, 256 * q:256 * q + 128], lhsT=kprev, rhs=qq, start=True, stop=True),
                                            reads=["KT%d" % g, "QT%d" % g], writes=[psr])
                                    S.op("pe", lambda e, pss=pss, q=q, kcur=kcur, qq=qq: e.matmul(
                                        pss[:, 256 * q + 128:256 * q + 256], lhsT=kcur, rhs=qq, start=True, stop=True),
                                        reads=["KT%d" % g, "QT%d" % g], writes=[psr])
                                    info.append((hasp, vprev, vcur, ocols))
                                allp = info[0][0] and info[1][0]
                                k0 = 0 if allp else 1
                                if (not allp) and (info[0][0] or info[1][0]):
                                    S.op("act", lambda e, pss=pss, ptb_=ptb_: e.activation(
                                        out=ptb_[:, 1, 0, :], in_=pss[:, 256:384], func=AF.Exp), reads=[psr], writes=[ptr])
                                    S.op("pool", lambda e, ptb_=ptb_: e.tensor_tensor(
                                        out=ptb_[:, 1, 0, :], in0=ptb_[:, 1, 0, :], in1=maskb[:, 0, :], op=ALU.mult),
                                        reads=[ptr, "maskb"], writes=[ptr])
                                S.op("act", lambda e, pss=pss, ptb_=ptb_, k0=k0: e.activation(
                                    out=ptb_[:, :, k0:2, :],
                                    in_=pss[:, :].rearrange("p (q k t) -> p q k t", q=2, k=2)[:, :, k0:2, :], func=AF.Exp),
                                    reads=[psr], writes=[ptr])
                                S.op("pool", lambda e, ptb_=ptb_, k0=k0: e.tensor_tensor(
                                    out=ptb_[:, :, k0:2, :], in0=ptb_[:, :, k0:2, :],
                                    in1=maskb[:, k0:2, :].unsqueeze(1).to_broadcast([128, 2, 2 - k0, 128]), op=ALU.mult),
                                    reads=[ptr, "maskb"], writes=[ptr])
                                if DBG and sb == 0 and slot == 0 and ch == 0:
                                    dpt = dout("dbg_PT%d_%d" % (g, pair), [128, 2, 2, 128], BF16)
                                    dma("sp", dpt, ptb_[:, :, :, :], reads=[ptr], key="dbg")
                                for q in range(2):
                                    hasp, vprev, vcur, ocols = info[q]
                                    pO, pOr = Bk[g]
                                    kbl = (0, 1) if hasp else (1,)
                                    for kb in kbl:
                                        vv = vprev if kb == 0 else vcur
                                        S.op("pe", lambda e, pO=pO, vv=vv, ptb_=ptb_, q=q, kb=kb, ocols=ocols, kbl=kbl: e.matmul(
                                            pO[0:65, ocols], lhsT=vv, rhs=ptb_[:, q, kb, :], start=(kb == kbl[0]),
                                            stop={"0": False, "1": True}.get(_os.environ.get("KSTOPF", ""), kb == 1)),
                                            reads=["VV%d" % g, ptr], writes=[pOr])
                        pO, pOr = Bk[2]
                        for r in range(16):
                            kbs = (0, 1) if sb > 0 else (1,)
                            for kb in kbs:
                                S.op("pe", lambda e, pO=pO, r=r, kb=kb, kbs=kbs: e.matmul(
                                    pO[0:65, 32 * r:32 * (r + 1)], lhsT=VV[2][:, r, kb, 65 * slot:65 * (slot + 1)],
                                    rhs=PT2[:, r, kb, 32 * ch:32 * (ch + 1)], start=(kb == kbs[0]), stop=(kb == 1)),
                                    reads=["VV2", "PT2"], writes=[pOr], sig=(kb == 1))
                        if DBG and sb == 0 and slot == 0 and ch == 0:
                            for gq in range(3):
                                db = dout("dbg_B%d" % gq, [65, 512], F32)
                                S.op("dve", lambda e, gq=gq: e.tensor_copy(out=stg[1][0:65, :], in_=Bk[gq][0][0:65, :]),
                                     reads=[Bk[gq][1]], writes=["stg1"])
                                dma("sp", db, stg[1][0:65, :], reads=["stg1"], key="dbg")
                        tmpb = stg[0]
                        S.op("act", lambda e, tmpb=tmpb, b0=Bk[0][0]: e.activation(out=tmpb[0:65, :], in_=b0[0:65, :], func=AF.Copy),
                             reads=[Bk[0][1]], writes=["stg0"])
                        S.op("dve", lambda e, tmpb=tmpb, b1=Bk[1][0]: e.tensor_tensor(
                            out=tmpb[0:65, :].rearrange("p (j r) -> p j r", r=4),
                            in0=tmpb[0:65, :].rearrange("p (j r) -> p j r", r=4),
                            in1=b1[0:65, :].rearrange("p (r j) -> p j r", r=4), op=ALU.add),
                            reads=[Bk[1][1], "stg0"], writes=["stg0"])
                        S.op("dve", lambda e, tmpb=tmpb, b2=Bk[2][0]: e.tensor_tensor(
                            out=tmpb[0:65, :].rearrange("p (j r) -> p j r", r=16),
                            in0=tmpb[0:65, :].rearrange("p (j r) -> p j r", r=16),
                            in1=b2[0:65, :].rearrange("p (r j) -> p j r", r=16), op=ALU.add),
                            reads=[Bk[2][1], "stg0"], writes=["stg0"])
                        S.op("dve", lambda e, tmpb=tmpb: e.tensor_scalar(out=tmpb[64:65, :], in0=tmpb[64:65, :], scalar1=1e-30,
                                                                        scalar2=None, op0=ALU.add),
                             reads=["stg0"], writes=["stg0"])
                        S.op("dve", lambda e, tmpb=tmpb: e.reciprocal(out=stg[1][64:65, :], in_=tmpb[64:65, :]),
                             reads=["stg0"], writes=["stg1"])
                        pD, pDr = PSB[3], "ps3"
                        S.op("pe", lambda e, pD=pD: e.matmul(pD[0:64, :], lhsT=onesf[64:65, 0:64], rhs=stg[1][64:65, :],
                                                             start=True, stop=True), reads=["onesf", "stg1"], writes=[pDr])
                        pO, pOr = None, "stg0"
                        rc = rec[0]
                        rcr = "rec0"
                        S.op("dve", lambda e, pD=pD, tmpb=tmpb, slot=slot, ch=ch: e.tensor_tensor(
                            out=oTsb[:, 512 * ch:512 * (ch + 1)], in0=tmpb[0:64, :], in1=pD[0:64, :], op=ALU.mult),
                            reads=[pDr, "stg0"], writes=["oTsb"])
                    dma("sp", oT_d[:, slot, T0:T0 + 2048], oTsb[:, :], reads=["oTsb"], writes=["oT_d"], key="oT_d")
            S.barrier()
            S.emit()
        if KSTOP == 1:
            return nc

        with contextlib.ExitStack() as st2:
            xg = sbuf(st2, "xg", [128, NT // 128, D], F32)
            hT = sbuf(st2, "hT", [128, 8, NT], BF16)
            uext = sbuf(st2, "uext", [128, 8, 30 + NT], BF16)
            dgb = [sbuf(st2, "dgb%d" % i, [128, 31, 128], BF16) for i in range(2)]
            big2 = sbuf(st2, "big2", [128, 12 * NT], F32)
            acc = big2[:, 0:8 * NT].rearrange("p (c t) -> p c t", c=8)
            lnm = big2[:, 8 * NT:12 * NT].rearrange("p (c t) -> p c t", c=4)
            aT = big2[:, 0:11 * NT].bitcast(BF16).rearrange("p (c t) -> p c t", c=22)
            R3 = sbuf(st2, "R3", [128, 11264], F32)
            wdnb = R3[:, :].bitcast(BF16).rearrange("p (c t) -> p c t", c=22)
            sT = R3[:, 0:2048].bitcast(BF16).rearrange("p (c t) -> p c t", c=8)
            mixT = R3[:, 2048:4096].bitcast(BF16).rearrange("p (c t) -> p c t", c=8)
            woutb = R3[:, 4096:8192].bitcast(BF16).rearrange("p (c t) -> p c t", c=8)
            oTg = R3[0:64, 8192:9216].bitcast(BF16).rearrange("p (c t) -> p c t", c=4)
            cb16 = [R3[:, 9216 + 256 * i:9472 + 256 * i].bitcast(BF16) for i in range(2)]
            csq16 = [R3[:, 9728 + 256 * i:9984 + 256 * i].bitcast(BF16) for i in range(2)]
            tt = [R3[:, 10240 + 512 * i:10752 + 512 * i] for i in range(2)]
            S.alias["wdnb"] = ["sT", "mixT", "woutb", "oTg", "cb16_0", "cb16_1", "csq16_0", "csq16_1", "tt0", "tt1"]
            S.alias["aT"] = ["acc%d" % c for c in range(8)] + ["lnm"]
            S.alias["uh"] = ["uext"] + ["uext%d" % c for c in range(8)]
            S.alias["uprod"] = S.alias["uh"]
            wao64 = sbuf(st2, "wao64", [64, 4, D], BF16)
            wao128 = sbuf(st2, "wao128", [128, 2, D], BF16)
            NWB = 3
            wt = [sbuf(st2, "wt%d" % i, [128, 8, 256], BF16) for i in range(NWB)]
            sg_ = [sbuf(st2, "sg%d" % i, [128, NT], F32) for i in range(3)]
            upx = [sbuf(st2, "upx%d" % i, [128, NT + 2], F32) for i in range(2)]
            cgb = [sbuf(st2, "cgb%d" % i, [128, NT], F32) for i in range(2)]
            fh = sbuf(st2, "fh", [128, 44, 2], F32)
            gfin = sbuf(st2, "gfin", [128, D], F32)
            yt = [sbuf(st2, "yt%d" % i, [128, D], F32) for i in range(1)]
            uflat = uext[:, :, :].rearrange("p c t -> p (c t)")[:, 0:4336].bitcast(F32)
            uh = uflat[:, 0:8 * NS * 31].rearrange("p (c s j) -> p c s j", c=8, s=NS)
            uprod = uflat[:, 8 * NS * 31:16 * NS * 31].rearrange("p (c s j) -> p c s j", c=8, s=NS)
            fhs = sbuf(st2, "fhs", [128, 44, 2 * NS], F32)
            upn = sbuf(st2, "upn", [128, 44, NS], F32)
            orow = sbuf(st2, "orow", [30, D], F32)

            dma("sp", gfin[:], gfin_d, writes=["gfin"], key="gfin")
            dma("sp", wao64[:], wb_ao.rearrange("(s d) n -> d s n", d=64), reads=["wb_ao"], writes=["wao"], key="wao64")
            dma("sp", wao128[:], wb_ao.rearrange("(c p) n -> p c n", p=128), reads=["wb_ao"], writes=["wao"], key="wao128")
            S.op("pool", lambda e: e.memset(uext[:, :, 0:30], 0.0), writes=["uext"])
            S.op("pool", lambda e: e.memset(fh[:], 0.0), writes=["fh"])

            wi = [0]

            def wtile(src_ap, rd):
                i = wi[0] % NWB
                wi[0] += 1
                dma("sp", wt[i][:, :, :], src_ap.rearrange("(kc p) n -> p kc n", p=128), reads=[rd],
                    writes=["wt%d" % i], key="wt%d" % i)
                return wt[i], "wt%d" % i

            tgl = [0]

            def alt(a, b):
                tgl[0] += 1
                return a if tgl[0] % 2 else b

            ygi = [0]

            prevN = [NT]

            def group(t0, TT_list, N, sample, first=False, lastg=False, out0=None):
                gi = 1
                for (i, TT) in TT_list:
                    src = xs if sample else xp[t0 + 128 * i:t0 + 128 * i + TT, :]
                    dma("sp", xg[0:TT, i, :], src, writes=["xg%d" % i], key="xg%d" % i)
                    norm_T(xg[0:TT, i, :], "xg%d" % i, TT, hT[:, :, 128 * i:128 * i + TT], "hT", "gmix", ygi[0])
                    ygi[0] += 1
                if not sample:
                    dma("sp", oTg[:, :, 0:N], oT_d[:, :, t0:t0 + N], reads=["oT_d"], writes=["oTg"], key="oTg")
                dma("sp", woutb[:, :, :], wb_out.rearrange("(kc p) n -> p kc n", p=128), reads=["wb_out"],
                    writes=["woutb"], key="woutb")
                if sample:
                    for s in range(NS):
                        dma("sp", orow[:, :], sconv[s, :, :], writes=["orow"], key="scv")
                        for c in range(8):
                            pst, pr = psum()
                            S.op("pe", lambda e, c=c, pst=pst: e.transpose(pst[:, 0:30], orow[0:30, 128 * c:128 * (c + 1)],
                                                                            identf[0:30, 0:30]),
                                 reads=["orow", "identf"], writes=[pr])
                            S.op("act", lambda e, c=c, pst=pst, s=s: e.activation(out=uh[:, c, s, 0:30], in_=pst[:, 0:30],
                                                                                  func=AF.Copy), reads=[pr], writes=["uh"])
                elif not first:
                    pN = prevN[0]
                    S.op("pool", lambda e: e.tensor_copy(out=uext[:, :, 0:30], in_=uext[:, :, pN:pN + 30]),
                         reads=["uext"], writes=["uext"])
                if not sample:
                    prevN[0] = N
                for c in range(8):
                    w, wr = wtile(wb_in[:, 256 * c:256 * (c + 1)], "wb_in1")
                    pl, plr = psum()
                    pg, pgr = psum()
                    for kc in range(8):
                        S.op("pe", lambda e, kc=kc, w=w, pl=pl: e.matmul(pl[:, 0:N], lhsT=w[:, kc, 0:128], rhs=hT[:, kc, 0:N],
                                                                         start=(kc == 0), stop=(kc == 7)),
                             reads=[wr, "hT"], writes=[plr], sig=(kc == 7))
                    for kc in range(8):
                        S.op("pe", lambda e, kc=kc, w=w, pg=pg: e.matmul(pg[:, 0:N], lhsT=w[:, kc, 128:256], rhs=hT[:, kc, 0:N],
                                                                         start=(kc == 0), stop=(kc == 7)),
                             reads=[wr, "hT"], writes=[pgr], sig=(kc == 7))
                    sgb = sg_[c % 2]
                    sgr = "sg%d" % (c % 2)
                    S.op("act", lambda e, pg=pg, sgb=sgb: e.activation(out=sgb[:, 0:N], in_=pg[:, 0:N], func=AF.Sigmoid),
                         reads=[pgr], writes=[sgr])
                    if sample:
                        S.op("dve", lambda e, c=c, pl=pl, sgb=sgb: e.tensor_tensor(out=uh[:, c, :, 30], in0=pl[:, 0:N],
                                                                                   in1=sgb[:, 0:N], op=ALU.mult),
                             reads=[plr, sgr], writes=["uh"])
                    else:
                        S.op("dve", lambda e, c=c, pl=pl, sgb=sgb: e.tensor_tensor(out=uext[:, c, 30:30 + N], in0=pl[:, 0:N],
                                                                                   in1=sgb[:, 0:N], op=ALU.mult),
                             reads=[plr, sgr], writes=["uext%d" % c])
                o_w = PP["wdw"][0]
                if sample:
                    S.op("dve", lambda e: e.tensor_tensor(
                        out=uprod[:, :, :, :], in0=uh[:, :, :, :],
                        in1=pp[:, o_w:o_w + 248].rearrange("p (c j) -> p c j", j=31).unsqueeze(2).to_broadcast([128, 8, NS, 31]),
                        op=ALU.mult), reads=["uh", "pp"], writes=["uprod"])
                    S.op("dve", lambda e: e.tensor_reduce(out=acc[:, :, 0:NS], in_=uprod[:, :, :, :], axis=AX.X, op=ALU.add),
                         reads=["uprod"], writes=["acc%d" % c for c in range(8)])
                    o_b = PP["bdw"][0]
                    S.op("dve", lambda e: e.tensor_tensor(out=acc[:, :, 0:NS], in0=acc[:, :, 0:NS],
                                                          in1=pp[:, o_b:o_b + 8].unsqueeze(2).to_broadcast([128, 8, NS]),
                                                          op=ALU.add), reads=["acc%d" % c for c in range(8)] + ["pp"],
                         writes=["acc%d" % c for c in range(8)])
                else:
                    for c in range(8):
                        dg = dgb[c % 2]
                        dgr = "dgb%d" % (c % 2)
                        S.op("pool", lambda e, c=c, dg=dg: e.tensor_tensor(
                            out=dg[:, :, :], in0=identf[:, :].unsqueeze(1).to_broadcast([128, 31, 128]),
                            in1=pp[:, o_w + 31 * c:o_w + 31 * c + 31].unsqueeze(2).to_broadcast([128, 31, 128]),
                            op=ALU.mult), reads=["identf", "pp"], writes=[dgr])
                        pcv, pcr = psum()
                        for j in range(31):
                            S.op("pe", lambda e, c=c, j=j, dg=dg, pcv=pcv: e.matmul(
                                pcv[:, 0:N], lhsT=dg[:, j, :], rhs=uext[:, c, j:j + N], start=(j == 0), stop=(j == 30)),
                                reads=[dgr, "uext%d" % c, "uext"], writes=[pcr], sig=(j == 30))
                        S.op("act", lambda e, c=c, pcv=pcv: e.activation(out=acc[:, c, 0:N], in_=pcv[:, 0:N], func=AF.Identity,
                                                                         bias=P("bdw", c)),
                             reads=[pcr, "pp"], writes=["acc%d" % c])
                if sample or lastg:
                    for c in range(8):
                        pst, pr = psum()
                        if sample:
                            S.op("pe", lambda e, c=c, pst=pst: e.transpose(pst[0:NS, 0:128], uh[:, c, :, 30], identf[:, :]),
                                 reads=["uh", "identf"], writes=[pr])
                            S.op("act", lambda e, c=c, pst=pst: e.activation(out=orow[0:NS, 128 * c:128 * (c + 1)],
                                                                             in_=pst[0:NS, 0:128], func=AF.Copy),
                                 reads=[pr], writes=["orow"])
                        else:
                            S.op("pe", lambda e, c=c: e.transpose(PTB[0:30, 128 * c:128 * (c + 1)], uext[:, c, NT:NT + 30],
                                                                   identb[:, :]),
                                 reads=["uext%d" % c, "identb"], writes=["ptb"])
                            S.op("act", lambda e, c=c: e.activation(out=orow[0:30, 128 * c:128 * (c + 1)],
                                                                    in_=PTB[0:30, 128 * c:128 * (c + 1)], func=AF.Copy),
                                 reads=["ptb"], writes=["orow"])
                    if sample:
                        dma("sp", conv_s[:, 29, :], orow[0:NS, 0:D], reads=["orow"], key="oorow")
                    else:
                        dma("sp", conv_p[:, :], orow[0:30, 0:D], reads=["orow"], key="oorow")
                if DBG and t0 == HALO and not sample:
                    dma("sp", dbg_acc, acc[:, :, :], reads=["acc%d" % c for c in range(8)], key="dbg")
                p1, p1r = psum()
                p2, p2r = psum()
                for c in range(8):
                    b1 = cb16[c % 2]
                    b2 = csq16[c % 2]
                    S.op("pool", lambda e, c=c, b1=b1: e.tensor_copy(out=b1[:, 0:N], in_=acc[:, c, 0:N]),
                         reads=["acc%d" % c], writes=["cb16_%d" % (c % 2)])
                    S.op("act", lambda e, c=c, b2=b2: e.activation(out=b2[:, 0:N], in_=acc[:, c, 0:N], func=AF.Square),
                         reads=["acc%d" % c], writes=["csq16_%d" % (c % 2)])
                    S.op("pe", lambda e, c=c, b1=b1: e.matmul(p1[:, 0:N], lhsT=onesb[:, :], rhs=b1[:, 0:N], start=(c == 0),
                                                              stop=(c == 7)), reads=["onesb", "cb16_%d" % (c % 2)], writes=[p1r])
                    S.op("pe", lambda e, c=c, b2=b2: e.matmul(p2[:, 0:N], lhsT=onesb[:, :], rhs=b2[:, 0:N], start=(c == 0),
                                                              stop=(c == 7)), reads=["onesb", "csq16_%d" % (c % 2)], writes=[p2r])
                S.op("dve", lambda e: e.tensor_scalar(out=lnm[:, 0, 0:N], in0=p1[:, 0:N], scalar1=1.0 / D, scalar2=None,
                                                      op0=ALU.mult), reads=[p1r], writes=["lnm"])
                S.op("dve", lambda e: e.tensor_tensor(out=lnm[:, 1, 0:N], in0=lnm[:, 0, 0:N], in1=lnm[:, 0, 0:N], op=ALU.mult),
                     reads=["lnm"], writes=["lnm"])
                S.op("dve", lambda e: e.scalar_tensor_tensor(out=lnm[:, 1, 0:N], in0=p2[:, 0:N], scalar=1.0 / D,
                                                             in1=lnm[:, 1, 0:N], op0=ALU.mult, op1=ALU.subtract),
                     reads=[p2r, "lnm"], writes=["lnm"])
                S.op("act", lambda e: e.activation(out=lnm[:, 2, 0:N], in_=lnm[:, 1, 0:N], func=AF.Sqrt, bias=epst[:, 0:1]),
                     reads=["lnm", "epst"], writes=["lnm"])
                S.op("dve", lambda e: e.reciprocal(out=lnm[:, 3, 0:N], in_=lnm[:, 2, 0:N]), reads=["lnm"], writes=["lnm"])
                for c in range(8):
                    tb = tt[c % 2]
                    tr = "tt%d" % (c % 2)
                    S.op("dve", lambda e, c=c, tb=tb: e.tensor_tensor(out=tb[:, 0:N], in0=acc[:, c, 0:N], in1=lnm[:, 0, 0:N],
                                                                      op=ALU.subtract),
                         reads=["acc%d" % c, "lnm"], writes=[tr])
                    S.op("pool", lambda e, tb=tb: e.tensor_tensor(out=tb[:, 0:N], in0=tb[:, 0:N], in1=lnm[:, 3, 0:N],
                                                                  op=ALU.mult), reads=[tr, "lnm"], writes=[tr])
                    S.op("act", lambda e, c=c, tb=tb: e.activation(out=sT[:, c, 0:N], in_=tb[:, 0:N], func=AF.Silu,
                                                                   scale=P("lng", c), bias=P("lnb", c)),
                         reads=[tr, "pp"], writes=["sT"])
                if DBG and t0 == HALO and not sample:
                    dma("sp", dbg_sT, sT[:, :, :], reads=["sT"], key="dbg")
                for c in range(8):
                    if c % 2 == 0:
                        wco_t, wco_r = wtile(wb_co[:, 128 * c:128 * c + 256], "wb_co")
                    wg_t, wg_r = wtile(wb_in[:, 4352 + 256 * c:4352 + 256 * (c + 1)], "wb_in3")
                    pa, par = psum()
                    pbb, pbr = psum()
                    pga, pgar = psum()
                    pgb, pgbr = psum()
                    co = 128 * (c % 2)
                    for kc in range(8):
                        S.op("pe", lambda e, kc=kc, pa=pa, wco_t=wco_t, co=co: e.matmul(
                            pa[:, 0:N], lhsT=wco_t[:, kc, co:co + 128], rhs=sT[:, kc, 0:N], start=(kc == 0), stop=(kc == 7)),
                            reads=[wco_r, "sT"], writes=[par], sig=(kc == 7))
                    if sample:
                        for k2 in range(2):
                            S.op("pe", lambda e, k2=k2, pbb=pbb, c=c: e.matmul(
                                pbb[:, 0:N], lhsT=wao128[:, k2, 128 * c:128 * (c + 1)], rhs=oTs[:, k2, 0:N],
                                start=(k2 == 0), stop=(k2 == 1)), reads=["wao", "oTs"], writes=[pbr], sig=(k2 == 1))
                    else:
                        for sl in range(4):
                            S.op("pe", lambda e, sl=sl, pbb=pbb, c=c: e.matmul(
                                pbb[:, 0:N], lhsT=wao64[:, sl, 128 * c:128 * (c + 1)], rhs=oTg[:, sl, 0:N],
                                start=(sl == 0), stop=(sl == 3)), reads=["wao", "oTg"], writes=[pbr], sig=(sl == 3))
                    for kc in range(8):
                        S.op("pe", lambda e, kc=kc, pga=pga, wg_t=wg_t: e.matmul(
                            pga[:, 0:N], lhsT=wg_t[:, kc, 0:128], rhs=hT[:, kc, 0:N], start=(kc == 0), stop=(kc == 7)),
                            reads=[wg_r, "hT"], writes=[pgar], sig=(kc == 7))
                    for kc in range(8):
                        S.op("pe", lambda e, kc=kc, pgb=pgb, wg_t=wg_t: e.matmul(
                            pgb[:, 0:N], lhsT=wg_t[:, kc, 128:256], rhs=hT[:, kc, 0:N], start=(kc == 0), stop=(kc == 7)),
                            reads=[wg_r, "hT"], writes=[pgbr], sig=(kc == 7))
                    sa, sar = sg_[0], "sg0"
                    sb_, sbr = sg_[1], "sg1"
                    S.op("act", lambda e, pga=pga: e.activation(out=sa[:, 0:N], in_=pga[:, 0:N], func=AF.Sigmoid),
                         reads=[pgar], writes=[sar])
                    S.op("act", lambda e, pgb=pgb: e.activation(out=sb_[:, 0:N], in_=pgb[:, 0:N], func=AF.Sigmoid),
                         reads=[pgbr], writes=[sbr])
                    S.op("dve", lambda e, pa=pa: e.tensor_tensor(out=sa[:, 0:N], in0=pa[:, 0:N], in1=sa[:, 0:N], op=ALU.mult),
                         reads=[par, sar], writes=[sar])
                    S.op("dve", lambda e, pbb=pbb: e.tensor_tensor(out=sb_[:, 0:N], in0=pbb[:, 0:N], in1=sb_[:, 0:N],
                                                                   op=ALU.mult), reads=[pbr, sbr], writes=[sbr])
                    S.op("pool", lambda e, c=c: e.tensor_tensor(out=mixT[:, c, 0:N], in0=sa[:, 0:N], in1=sb_[:, 0:N],
                                                                op=ALU.add), reads=[sar, sbr], writes=["mixT"])
                if DBG and t0 == HALO and not sample:
                    dma("sp", dbg_mix, mixT[:, :, :], reads=["mixT"], key="dbg")
                for (i, TT) in TT_list:
                    WN = 512 if TT == 128 else 256
                    for n in range(D // WN):
                        po, por = psum()
                        for kc in range(8):
                            S.op("pe", lambda e, kc=kc, po=po, i=i, TT=TT, n=n, WN=WN: e.matmul(
                                po[0:TT, 0:WN], lhsT=mixT[:, kc, 128 * i:128 * i + TT], rhs=woutb[:, kc, WN * n:WN * (n + 1)],
                                start=(kc == 0), stop=(kc == 7)), reads=["mixT", "woutb"], writes=[por], sig=(kc == 7))
                        S.op("dve", lambda e, po=po, i=i, TT=TT, n=n, WN=WN: e.tensor_tensor(
                            out=xg[0:TT, i, WN * n:WN * (n + 1)], in0=po[0:TT, 0:WN], in1=xg[0:TT, i, WN * n:WN * (n + 1)],
                            op=ALU.add), reads=[por, "xg%d" % i], writes=["xg%d" % i])
                if DBG and t0 == HALO and not sample:
                    for (i, TT) in TT_list:
                        dma("sp", dbg_xmid[128 * i:128 * (i + 1), :], xg[0:TT, i, :], reads=["xg%d" % i], key="dbg")
                dma("sp", wdnb[:, :, :], wb_dn.rearrange("(kc p) n -> p kc n", p=128), reads=["wb_dn"], writes=["wdnb"],
                    key="wdnb")
                for (i, TT) in TT_list:
                    norm_T(xg[0:TT, i, :], "xg%d" % i, TT, hT[:, :, 128 * i:128 * i + TT], "hT", "gffn", ygi[0])
                    ygi[0] += 1
                if sample:
                    for q in range(44):
                        if q % 8 == 0:
                            wpc = min(1024, 2 * DFF - 128 * q)
                            dma("sp", orow[0:2 * NS, 0:wpc], sffn.rearrange("s j n -> (s j) n")[:, 128 * q:128 * q + wpc],
                                writes=["orow"], key="scv")
                        pst, pr = psum()
                        S.op("pe", lambda e, q=q, pst=pst: e.transpose(
                            pst[:, 0:2 * NS], orow[0:2 * NS, 128 * (q % 8):128 * (q % 8 + 1)], identf[0:2 * NS, 0:2 * NS]),
                             reads=["orow", "identf"], writes=[pr])
                        S.op("act", lambda e, q=q, pst=pst: e.activation(out=fhs[:, q, :], in_=pst[:, 0:2 * NS], func=AF.Copy),
                             reads=[pr], writes=["fhs"])
                o_f = PP["wfdw"][0]
                for j in range(22):
                    w, wr = wtile(wb_up[:, 256 * j:256 * (j + 1)], "wb_up")
                    pgv = []
                    for hv in range(2):
                        pz, pzr = psum()
                        for kc in range(8):
                            S.op("pe", lambda e, kc=kc, w=w, pz=pz, hv=hv: e.matmul(
                                pz[:, 0:N], lhsT=w[:, kc, 128 * hv:128 * (hv + 1)], rhs=hT[:, kc, 0:N], start=(kc == 0),
                                stop=(kc == 7)), reads=[wr, "hT"], writes=[pzr], sig=(kc == 7))
                        pgv.append((pz, pzr))
                    for hv in range(2):
                        q = 2 * j + hv
                        pz, pzr = pgv[hv]
                        ub, ur = upx[hv], "upx%d" % hv
                        cb, cr = cgb[hv], "cgb%d" % hv
                        w0 = pp[:, o_f + 3 * q:o_f + 3 * q + 1]
                        w1 = pp[:, o_f + 3 * q + 1:o_f + 3 * q + 2]
                        w2 = pp[:, o_f + 3 * q + 2:o_f + 3 * q + 3]
                        S.op("act", lambda e, pz=pz, cb=cb, w2=w2, q=q: e.activation(
                            out=cb[:, 0:N], in_=pz[:, 0:N], func=AF.Identity, scale=w2, bias=P("bfdw", q)),
                            reads=[pzr, "pp"], writes=[cr])
                        if sample:
                            S.op("act", lambda e, pz=pz, q=q: e.activation(out=upn[:, q, :], in_=pz[:, 0:NS], func=AF.Copy),
                                 reads=[pzr], writes=["upn"])
                            fv = fhs[:, q, :].rearrange("p (s j) -> p j s", j=2)
                            S.op("dve", lambda e, cb=cb, fv=fv, w1=w1: e.scalar_tensor_tensor(
                                out=cb[:, 0:N], in0=fv[:, 1, :], scalar=w1, in1=cb[:, 0:N], op0=ALU.mult, op1=ALU.add),
                                reads=["fhs", cr, "pp"], writes=[cr])
                            S.op("dve", lambda e, cb=cb, fv=fv, w0=w0: e.scalar_tensor_tensor(
                                out=cb[:, 0:N], in0=fv[:, 0, :], scalar=w0, in1=cb[:, 0:N], op0=ALU.mult, op1=ALU.add),
                                reads=["fhs", cr, "pp"], writes=[cr])
                        else:
                            S.op("pool", lambda e, ub=ub, q=q: e.tensor_copy(out=ub[:, 0:2], in_=fh[:, q, :]),
                                 reads=["fh"], writes=[ur])
                            S.op("act", lambda e, pz=pz, ub=ub: e.activation(out=ub[:, 2:2 + N], in_=pz[:, 0:N], func=AF.Copy),
                                 reads=[pzr], writes=[ur])
                            S.op("pool", lambda e, ub=ub, q=q: e.tensor_copy(out=fh[:, q, :], in_=ub[:, N:N + 2]),
                                 reads=[ur], writes=["fh"])
                            S.op("dve", lambda e, cb=cb, ub=ub, w1=w1: e.scalar_tensor_tensor(
                                out=cb[:, 0:N], in0=ub[:, 1:1 + N], scalar=w1, in1=cb[:, 0:N], op0=ALU.mult, op1=ALU.add),
                                reads=[ur, cr, "pp"], writes=[cr])
                            S.op("dve", lambda e, cb=cb, ub=ub, w0=w0: e.scalar_tensor_tensor(
                                out=cb[:, 0:N], in0=ub[:, 0:N], scalar=w0, in1=cb[:, 0:N], op0=ALU.mult, op1=ALU.add),
                                reads=[ur, cr, "pp"], writes=[cr])
                    sgb, sgr = sg_[2], "sg2"
                    S.op("act", lambda e, sgb=sgb: e.activation(out=sgb[:, 0:N], in_=cgb[0][:, 0:N], func=AF.Silu),
                         reads=["cgb0"], writes=[sgr])
                    S.op(alt("pool", "dve"), lambda e, j=j, sgb=sgb: e.tensor_tensor(out=aT[:, j, 0:N], in0=sgb[:, 0:N],
                                                                                    in1=cgb[1][:, 0:N], op=ALU.mult),
                         reads=[sgr, "cgb1"], writes=["aT"])
                if sample or lastg:
                    nrow = NS if sample else 2
                    for q in range(44):
                        pst, pr = psum()
                        srcT = upn[:, q, :] if sample else fh[:, q, :]
                        S.op("pe", lambda e, pst=pst, srcT=srcT, nrow=nrow: e.transpose(pst[0:nrow, 0:128], srcT, identf[:, :]),
                             reads=["upn" if sample else "fh", "identf"], writes=[pr])
                        S.op("act", lambda e, q=q, pst=pst, nrow=nrow: e.activation(
                            out=orow[0:nrow, 128 * (q % 8):128 * (q % 8 + 1)], in_=pst[0:nrow, 0:128], func=AF.Copy),
                            reads=[pr], writes=["orow"])
                        if q % 8 == 7 or q == 43:
                            q0 = 8 * (q // 8)
                            wpc = 128 * (q - q0 + 1)
                            if sample:
                                dma("sp", ffn_s[:, 1, 128 * q0:128 * q0 + wpc], orow[0:NS, 0:wpc], reads=["orow"], key="oorow")
                            else:
                                dma("sp", ffn_p[:, 128 * q0:128 * q0 + wpc], orow[0:2, 0:wpc], reads=["orow"], key="oorow")
                for (i, TT) in TT_list:
                    WN = 512 if TT == 128 else 256
                    for n in range(D // WN):
                        po, por = psum()
                        for kc in range(22):
                            S.op("pe", lambda e, kc=kc, po=po, i=i, TT=TT, n=n, WN=WN: e.matmul(
                                po[0:TT, 0:WN], lhsT=aT[:, kc, 128 * i:128 * i + TT], rhs=wdnb[:, kc, WN * n:WN * (n + 1)],
                                start=(kc == 0), stop=(kc == 21)), reads=["aT", "wdnb"], writes=[por], sig=(kc == 21))
                        S.op("dve", lambda e, po=po, i=i, TT=TT, n=n, WN=WN: e.tensor_tensor(
                            out=xg[0:TT, i, WN * n:WN * (n + 1)], in0=po[0:TT, 0:WN], in1=xg[0:TT, i, WN * n:WN * (n + 1)],
                            op=ALU.add), reads=[por, "xg%d" % i], writes=["xg%d" % i])
                for (i, TT) in TT_list:
                    col = stat_i[0]
                    stat_i[0] += 1
                    src = xg[0:TT, i, :]
                    S.op("act", lambda e, src=src, TT=TT, col=col: e.activation(
                        out=yt[0][0:TT, :], in_=src, func=AF.Square, accum_out=stat[0:TT, 0, col:col + 1]),
                        reads=["xg%d" % i], writes=["yt0", "stat%d" % col])
                    S.op("act", lambda e, TT=TT, col=col: e.activation(
                        out=stat[0:TT, 1, col:col + 1], in_=stat[0:TT, 0, col:col + 1], func=AF.Sqrt, scale=1.0 / D,
                        bias=epst[0:TT, 0:1]), reads=["stat%d" % col, "epst"], writes=["stat%d" % col])
                    S.op("dve", lambda e, TT=TT, col=col: e.reciprocal(out=stat[0:TT, 2, col:col + 1],
                                                                       in_=stat[0:TT, 1, col:col + 1]),
                         reads=["stat%d" % col], writes=["stat%d" % col])
                    yb = yt[0]
                    yr = "yt0"
                    S.op("dve", lambda e, src=src, TT=TT, col=col, yb=yb: e.scalar_tensor_tensor(
                        out=yb[0:TT, :], in0=src, scalar=stat[0:TT, 2, col:col + 1], in1=gfin[0:TT, :], op0=ALU.mult,
                        op1=ALU.mult), reads=["xg%d" % i, "stat%d" % col, "gfin"], writes=[yr])
                    if sample:
                        dma("sp", y_s[:, :], yb[0:TT, :], reads=[yr], key="o" + yr)
                    elif out0 is not None:
                        dma("sp", y_p[out0 + 128 * i:out0 + 128 * i + TT, :], yb[0:TT, :], reads=[yr], key="o" + yr)

            hvt = sbuf(st2, "hvt", [128, 1], F32)
            dma("sp", hvt[:], hv_d, writes=["hvt"], key="hvt")
            group(HALO - 256, [(0, 128), (1, 128)], 256, False, first=True)
            S.op("dve", lambda e: e.tensor_scalar(out=fh[:, :, :], in0=fh[:, :, :], scalar1=hvt[:, 0:1], scalar2=None,
                                                  op0=ALU.mult), reads=["fh", "hvt"], writes=["fh"])
            for gi in range(NG):
                group(HALO + gi * NT, [(i, 128) for i in range(NT // 128)], NT, False, lastg=(gi == NG - 1),
                      out0=gi * NT)
            group(0, [(0, NS)], NS, True)
            S.barrier()
            S.emit()
    return nc


def _perm_in():
    idx = []
    for c in range(8):
        idx += list(range(128 * c, 128 * (c + 1)))
        idx += list(range(1024 + 128 * c, 1024 + 128 * (c + 1)))
    idx += list(range(2048, 4352))
    for c in range(8):
        idx += list(range(4352 + 128 * c, 4352 + 128 * (c + 1)))
        idx += list(range(5376 + 128 * c, 5376 + 128 * (c + 1)))
    return np.array(idx)


def _perm_up():
    idx = []
    for j in range(22):
        idx += list(range(128 * j, 128 * (j + 1)))
        idx += list(range(DFF + 128 * j, DFF + 128 * (j + 1)))
    return np.array(idx)


def _fm(v, nch):
    return np.ascontiguousarray(v.reshape(nch, 128).T)


def make_shared(inp):
    f = np.float32
    pin = _perm_in()
    pup = _perm_up()
    ppv = np.zeros((128, NPP), f)

    def put(name, arr):
        o, w = PP[name]
        ppv[:, o:o + w] = arr.reshape(128, w)

    put("gmix", _fm(inp["g_mix"][0], 8))
    put("bdw", _fm(inp["b_dw"][0], 8))
    put("lng", _fm(inp["ln_g"][0], 8))
    put("lnb", _fm(inp["ln_b"][0], 8))
    put("gffn", _fm(inp["g_ffn"][0], 8))
    wdw = inp["w_dw"][0]
    put("wdw", np.ascontiguousarray(wdw.T.reshape(8, 128, 31).transpose(1, 0, 2)))
    wf = inp["w_fdw"][0][:, pup]
    put("wfdw", np.ascontiguousarray(wf.T.reshape(44, 128, 3).transpose(1, 0, 2)))
    put("bfdw", _fm(inp["b_fdw"][0][pup], 44))
    inv = (np.float32(500000.0) ** (-np.arange(8, dtype=f) / np.float32(8))).astype(f)
    angs = (np.full((NS, 1), 16384.0, f) * inv[None, :]).astype(f)
    css = np.concatenate([np.cos(angs), np.cos(angs)], 1).astype(f)
    sns = np.concatenate([-np.sin(angs), np.sin(angs)], 1).astype(f)
    j = np.arange(128)[:, None]
    i = np.arange(128)[None, :]
    mask2 = np.stack([(j >= i), (j <= i)], 1).astype(f)
    sel = np.zeros((NS, NS, 128), f)
    for s in range(NS):
        sel[s, s, :] = 1.0
    return {
        "w_in": np.ascontiguousarray(inp["w_in"][0][:, pin]),
        "w_co": np.ascontiguousarray(inp["w_conv_out"][0]),
        "w_ao": np.ascontiguousarray(inp["w_attn_out"][0]),
        "w_out": np.ascontiguousarray(inp["w_out"][0]),
        "w_up": np.ascontiguousarray(inp["w_up"][0][:, pup]),
        "w_dn": np.ascontiguousarray(inp["w_down"][0]),
        "pp": ppv,
        "gfin": np.ascontiguousarray(np.broadcast_to(inp["g_final"][None, :], (128, D))),
        "ident": np.eye(128, dtype=f),
        "mask2": mask2, "css": css, "sns": sns, "sel": sel,
    }


def make_core(inp, b, half, MAIN, s0, pup):
    f = np.float32
    LS = HALO + MAIN
    start = half * MAIN
    absp = start - HALO + np.arange(LS)
    valid = absp >= 0
    x = inp["x_prompt"][b]
    xl = np.zeros((LS, D), f)
    xl[valid] = x[absp[valid]]
    inv = (np.float32(500000.0) ** (-np.arange(8, dtype=f) / np.float32(8))).astype(f)
    pos = np.maximum(absp, 0).astype(f)
    ang = (pos[:, None] * inv[None, :]).astype(f)
    cos, sin = np.cos(ang).astype(f), np.sin(ang).astype(f)
    ntile = LS // 128
    csp = np.ascontiguousarray(np.concatenate([cos, cos], 1).reshape(ntile, 128, 16).transpose(1, 0, 2))
    snp = np.ascontiguousarray(np.concatenate([-sin, sin], 1).reshape(ntile, 128, 16).transpose(1, 0, 2))
    m = {
        "xp": xl, "csp": csp, "snp": snp,
        "hv": np.full((128, 1), 1.0 if start > 0 else 0.0, f),
        "xs": np.ascontiguousarray(inp["x_sample"][s0:s0 + NS, 0]),
        "sconv": np.ascontiguousarray(inp["state_conv"][0, s0:s0 + NS]),
        "sffn": np.ascontiguousarray(inp["state_ffn_conv"][0, s0:s0 + NS][:, :, pup]),
    }
    vf = valid.astype(f)
    nsb = LS // 2048
    for g, (W, d) in enumerate(GROUPS):
        m["vld%d" % g] = np.ascontiguousarray(vf.reshape(nsb, 16 // d, 128, d).transpose(2, 0, 3, 1))
    caches = ((inp["cache_k_w128"], inp["cache_v_w128"]), (inp["cache_k_w512"], inp["cache_v_w512"]),
              (inp["cache_k_w2048"], inp["cache_v_w2048"]))
    for g, W in enumerate((128, 512, 2048)):
        m["ck%d" % g] = np.ascontiguousarray(caches[g][0][0, s0:s0 + NS].reshape(NS, W, 256))
        m["cv%d" % g] = np.ascontiguousarray(caches[g][1][0, s0:s0 + NS].reshape(NS, W, 256))
    return m


_NC_CACHE = {}


def run(inp, n_cores=8):
    inp = {k: np.asarray(v) for k, v in inp.items()}
    B, S_FULL, _ = inp["x_prompt"].shape
    nsamp = inp["x_sample"].shape[0]
    MAIN = S_FULL // 2
    assert n_cores == 2 * B
    if MAIN not in _NC_CACHE:
        _NC_CACHE[MAIN] = build(MAIN)
    nc = _NC_CACHE[MAIN]
    shared = make_shared(inp)
    pup = _perm_up()
    in_maps = []
    for c in range(n_cores):
        m = dict(shared)
        m.update(make_core(inp, c // 2, c % 2, MAIN, (NS * c) % nsamp, pup))
        in_maps.append(m)
    res = run_bass_kernel_spmd(nc, in_maps, core_ids=list(range(n_cores))).results
    global LAST_RES
    LAST_RES = res
    ipup = np.argsort(pup)
    f = np.float32
    y_p = np.stack([np.concatenate([res[2 * b]["y_p"], res[2 * b + 1]["y_p"]], 0) for b in range(B)], 0)
    nsc = nsamp // NS
    y_s = np.concatenate([res[c]["y_s"] for c in range(nsc)], 0)[:, None, :]
    hi = [2 * b + 1 for b in range(B)]
    conv_p = np.stack([res[c]["conv_p"] for c in hi], 0)[None]
    conv_s = np.concatenate([res[c]["conv_s"] for c in range(nsc)], 0)[None]
    outs = [y_p.astype(f), y_s.astype(f), conv_p.astype(f), conv_s.astype(f)]
    for g, W in enumerate((128, 512, 2048)):
        outs.append(np.stack([res[c]["k%d_p" % g] for c in hi], 0).reshape(1, B, W, 4, 64).astype(f))
        outs.append(np.stack([res[c]["v%d_p" % g] for c in hi], 0).reshape(1, B, W, 4, 64).astype(f))
        outs.append(np.concatenate([res[c]["k%d_s" % g] for c in range(nsc)], 0).reshape(1, nsamp, W, 4, 64).astype(f))
        outs.append(np.concatenate([res[c]["v%d_s" % g] for c in range(nsc)], 0).reshape(1, nsamp, W, 4, 64).astype(f))
    ffn_p = np.stack([res[c]["ffn_p"] for c in hi], 0)[:, :, ipup][None]
    ffn_s = np.concatenate([res[c]["ffn_s"] for c in range(nsc)], 0)[:, :, ipup][None]
    outs += [ffn_p.astype(f), ffn_s.astype(f)]
    return tuple(outs)


def kernel(**inputs):
    return run(inputs, 8)
```

```python
import contextlib
import types
import numpy as np
import concourse.bass as bass
import concourse.mybir as mybir
from concourse.bass_utils import run_bass_kernel_spmd

F32 = mybir.dt.float32
BF16 = mybir.dt.bfloat16
ALU = mybir.AluOpType
AF = mybir.ActivationFunctionType
AX = mybir.AxisListType

D = 1024
DFF = 2816
NS = 4
NT = 512
GROUPS = ((128, 1), (512, 4), (2048, 16))
EPS = 1e-6
ENGS = ("pe", "act", "dve", "pool", "sp")


class Sched:
    def __init__(self, nc, st, nsem=100):
        self.nc = nc
        self.ops = {e: [] for e in ENGS}
        self.cnt = {}
        self.res_w = {}
        self.res_r = {}
        self.waited = {e: {} for e in ENGS}
        self.pool = [st.enter_context(nc.semaphore("sm%d" % i)) for i in range(nsem)]
        self.sem = {}
        self.alias = {}

    def _sk(self, k):
        if k not in self.cnt:
            self.cnt[k] = 0
            assert len(self.sem) < len(self.pool), "out of semaphores"
            self.sem[k] = self.pool[len(self.sem)]
        return k

    @staticmethod
    def _freeze(fn):
        if fn.__closure__ is None:
            return fn
        cells = []
        for c in fn.__closure__:
            try:
                cells.append(types.CellType(c.cell_contents))
            except ValueError:
                cells.append(c)
        return types.FunctionType(fn.__code__, fn.__globals__, fn.__name__, fn.__defaults__, tuple(cells))

    def op(self, eng, fn, reads=(), writes=(), dma=None, sig=True):
        fn = self._freeze(fn)
        waits = {}
        reads = [x for r in reads for x in [r] + self.alias.get(r, [])]
        writes = [x for r in writes for x in [r] + self.alias.get(r, [])]
        writes = writes + [r for r in reads if r.startswith("ps") or r.startswith("ptb")]

        def need(w):
            if w[1] > waits.get(w[0], 0):
                waits[w[0]] = w[1]

        for r in reads:
            if r in self.res_w:
                need(self.res_w[r])
        for r in writes:
            if r in self.res_w:
                need(self.res_w[r])
            for sk, v in self.res_r.get(r, {}).items():
                need((sk, v))
        if dma is None:
            sk = self._sk(eng)
            inc = 1 if sig else 0
        else:
            sk = self._sk("d:" + str(dma))
            inc = 16
        self.cnt[sk] += inc
        val = self.cnt[sk] if inc else self.cnt[sk] + 1
        wl = []
        for k, v in waits.items():
            if k == "pe" and eng == "pe" and dma is None:
                continue
            if self.waited[eng].get(k, 0) >= v:
                continue
            self.waited[eng][k] = v
            wl.append((k, v))
        for r in writes:
            self.res_w[r] = (sk, val)
            self.res_r[r] = {}
        for r in reads:
            d = self.res_r.setdefault(r, {})
            if d.get(sk, 0) < val:
                d[sk] = val
        self.ops[eng].append((wl, fn, sk, inc))

    def barrier(self, engs=ENGS):
        for e in engs:
            wl = []
            for k, v in self.cnt.items():
                if v > 0 and self.waited[e].get(k, 0) < v:
                    self.waited[e][k] = v
                    wl.append((k, v))
            if wl:
                self.ops[e].append((wl, None, None, 0))

    def emit(self):
        nc = self.nc
        with nc.Block() as block:
            def run(e, eng):
                for wl, fn, sk, inc in self.ops[eng]:
                    for k, v in wl:
                        e.wait_ge(self.sem[k], v)
                    if fn is not None:
                        ins = fn(e)
                        if inc:
                            ins.then_inc(self.sem[sk], inc)

            @block.tensor
            def _(e):
                run(e, "pe")

            @block.scalar
            def _(e):
                run(e, "act")

            @block.vector
            def _(e):
                run(e, "dve")

            @block.gpsimd
            def _(e):
                run(e, "pool")

            @block.sync
            def _(e):
                run(e, "sp")
        self.ops = {e: [] for e in ENGS}


PP = {}
_o = 0
for _n, _w in (("gmix", 8), ("bdw", 8), ("lng", 8), ("lnb", 8), ("gffn", 8), ("wdw", 8 * 31), ("wfdw", 44 * 3),
               ("bfdw", 44)):
    PP[_n] = (_o, _w)
    _o += _w
NPP = _o


HALO = 4096


def build(MAIN):
    S_TOK = HALO + MAIN
    NSB = S_TOK // 2048
    NTILE = S_TOK // 128
    NG = MAIN // NT
    nc = bass.Bass("TRN2", target_bir_lowering=False)

    def din(name, shape, dt=F32):
        return nc.dram_tensor(name, list(shape), dt, kind="ExternalInput").ap()

    def dout(name, shape, dt=F32):
        return nc.dram_tensor(name, list(shape), dt, kind="ExternalOutput").ap()

    def dscr(name, shape, dt):
        return nc.dram_tensor(name, list(shape), dt, kind="Internal").ap()

    xp = din("xp", [S_TOK, D])
    xs = din("xs", [NS, D])
    sconv = din("sconv", [NS, 30, D])
    cks = [din("ck%d" % g, [NS, GROUPS[g][0], 256]) for g in range(3)]
    cvs = [din("cv%d" % g, [NS, GROUPS[g][0], 256]) for g in range(3)]
    sffn = din("sffn", [NS, 2, 2 * DFF])
    w_in = din("w_in", [D, 6400])
    w_co = din("w_co", [D, D])
    w_ao = din("w_ao", [256, D])
    w_out = din("w_out", [D, D])
    w_up = din("w_up", [D, 2 * DFF])
    w_dn = din("w_dn", [DFF, D])
    pp_d = din("pp", [128, NPP])
    gfin_d = din("gfin", [128, D])
    ident_d = din("ident", [128, 128])
    mask_d = din("mask2", [128, 2, 128])
    csp_d = din("csp", [128, NTILE, 16])
    snp_d = din("snp", [128, NTILE, 16])
    css_d = din("css", [NS, 16])
    sns_d = din("sns", [NS, 16])
    sel_d = din("sel", [NS, NS, 128])
    vld_d = [din("vld%d" % g, [128, NSB, GROUPS[g][1], 16 // GROUPS[g][1]]) for g in range(3)]
    hv_d = din("hv", [128, 1])

    wb_in = dscr("wb_in", [D, 6400], BF16)
    wb_co = dscr("wb_co", [D, D], BF16)
    wb_ao = dscr("wb_ao", [256, D], BF16)
    wb_out = dscr("wb_out", [D, D], BF16)
    wb_up = dscr("wb_up", [D, 2 * DFF], BF16)
    wb_dn = dscr("wb_dn", [DFF, D], BF16)
    import os as _os0
    DBG = bool(_os0.environ.get("KDBG"))
    oT_d = (dout if DBG else dscr)("oT_d", [64, 4, S_TOK], BF16)
    if DBG:
        dbg_sT = dout("dbg_sT", [128, 8, NT], BF16)
        dbg_mix = dout("dbg_mix", [128, 8, NT], BF16)
        dbg_xmid = dout("dbg_xmid", [NT, D], F32)
        dbg_acc = dout("dbg_acc", [128, 8, NT], F32)

    y_p = dout("y_p", [MAIN, D])
    y_s = dout("y_s", [NS, D])
    conv_p = dout("conv_p", [30, D])
    conv_s = dout("conv_s", [NS, 30, D])
    kp = [dout("k%d_p" % g, [min(GROUPS[g][0], S_TOK), 256]) for g in range(3)]
    vp = [dout("v%d_p" % g, [min(GROUPS[g][0], S_TOK), 256]) for g in range(3)]
    ks_o = [dout("k%d_s" % g, [NS, GROUPS[g][0], 256]) for g in range(3)]
    vs_o = [dout("v%d_s" % g, [NS, GROUPS[g][0], 256]) for g in range(3)]
    ffn_p = dout("ffn_p", [2, 2 * DFF])
    ffn_s = dout("ffn_s", [NS, 2, 2 * DFF])

    with contextlib.ExitStack() as gst:
        S = Sched(nc, gst)

        def sbuf(st, name, shape, dt):
            return st.enter_context(nc.sbuf_tensor("sb_" + name, list(shape), dt))

        NPS = 6
        PSB = [gst.enter_context(nc.psum_tensor("psb%d" % i, [128, 512], F32)) for i in range(NPS)]
        PTB = gst.enter_context(nc.psum_tensor("ptb", [128, 1024], BF16))
        PTB2 = gst.enter_context(nc.psum_tensor("ptb2", [128, 1024], BF16))
        ps_i = [0]

        def psum():
            i = ps_i[0] % NPS
            ps_i[0] += 1
            return PSB[i], "ps%d" % i

        pp = sbuf(gst, "pp", [128, NPP], F32)
        identf = sbuf(gst, "identf", [128, 128], F32)
        identb = sbuf(gst, "identb", [128, 128], BF16)
        onesb = sbuf(gst, "onesb", [128, 128], BF16)
        onesf = sbuf(gst, "onesf", [128, 128], F32)
        epst = sbuf(gst, "epst", [128, 1], F32)
        stat = sbuf(gst, "stat", [128, 3, 4 * NTILE + 16], F32)
        xsb = [sbuf(gst, "xsb%d" % i, [128, D], BF16) for i in range(2)]
        oTs = sbuf(gst, "oTs", [128, 2, NS], BF16)
        sts = sbuf(gst, "sts", [NS, 2304], F32)
        stat_i = [0]

        ps45 = [0]

        def psum45():
            i = 4 + ps45[0] % 2
            ps45[0] += 1
            return PSB[i], "ps%d" % i

        def P(name, c=None):
            o, w = PP[name]
            if c is None:
                return pp[:, o:o + w]
            return pp[:, o + c:o + c + 1]

        def dma(eng, out, in_, reads=(), writes=(), key=None):
            S.op(eng, lambda e: e.dma_start(out=out, in_=in_), reads=reads, writes=writes, dma=key)

        dma("sp", pp[:], pp_d, writes=["pp"], key="pp")
        dma("sp", identf[:], ident_d, writes=["identf"], key="identf")
        S.op("dve", lambda e: e.tensor_copy(out=identb[:], in_=identf[:]), reads=["identf"], writes=["identb"])
        S.op("dve", lambda e: e.memset(onesb[:], 1.0), writes=["onesb"])
        S.op("dve", lambda e: e.memset(onesf[:], 1.0), writes=["onesf"])
        S.op("dve", lambda e: e.memset(epst[:], EPS), writes=["epst"])
        S.op("dve", lambda e: e.memset(stat[:], 0.0), writes=["stat"])
        def conv_w(dst, src, rows, c0, c1, key, after=()):
            for r0 in range(0, rows, 256):
                r1 = min(rows, r0 + 256)
                dma("pool", dst[r0:r1, c0:c1], src[r0:r1, c0:c1], reads=list(after), writes=[key], key=key)
        for cg in (2, 4, 0, 1, 3):
            conv_w(wb_in, w_in, D, 2048 + 512 * cg, 2048 + min(512 * (cg + 1), 2304), "wb_qkv%d" % cg)
        QKV_ALL = ["wb_qkv%d" % cg for cg in range(5)]
        conv_w(wb_in, w_in, D, 0, 2048, "wb_in1", after=QKV_ALL)
        conv_w(wb_in, w_in, D, 4352, 6400, "wb_in3")
        conv_w(wb_co, w_co, D, 0, D, "wb_co")
        conv_w(wb_ao, w_ao, 256, 0, D, "wb_ao")
        conv_w(wb_out, w_out, D, 0, D, "wb_out")
        conv_w(wb_up, w_up, D, 0, 2 * DFF, "wb_up")
        conv_w(wb_dn, w_dn, DFF, 0, D, "wb_dn")
        def flat16(ap):
            return ap.rearrange("w c -> (w c)").rearrange("(a b) -> a b", a=16)
        for g in range(3):
            W = GROUPS[g][0]
            for (src, dst, nm) in ((cks[g], ks_o[g], "k"), (cvs[g], vs_o[g], "v")):
                for s in range(NS):
                    dma("pool", flat16(dst[s, 0:W - 1, :]), flat16(src[s, 1:W, :]), key="cshift")
        for s in range(NS):
            dma("pool", flat16(conv_s[s, 0:29, :]), flat16(sconv[s, 1:30, :]), key="cshift")
            dma("pool", flat16(ffn_s[s, 0:1, :]), flat16(sffn[s, 1:2, :]), key="cshift")

        import os as _os
        KSTOP = int(_os.environ.get("KSTOP", "9"))
        if KSTOP == 0:
            S.barrier()
            S.emit()
            return nc
        def norm_T(src_ap, rd, TT, dst3, dst_res, gain_name, xi):
            col = stat_i[0]
            stat_i[0] += 1
            xb = xsb[xi % 2]
            xr = "xsb%d" % (xi % 2)
            S.op("act", lambda e: e.activation(out=xb[0:TT, :], in_=src_ap, func=AF.Square,
                                               accum_out=stat[0:TT, 0, col:col + 1]),
                 reads=[rd], writes=[xr, "stat%d" % col])
            S.op("act", lambda e: e.activation(out=stat[0:TT, 1, col:col + 1], in_=stat[0:TT, 0, col:col + 1],
                                               func=AF.Sqrt, scale=1.0 / D, bias=epst[0:TT, 0:1]),
                 reads=["stat%d" % col, "epst"], writes=["stat%d" % col])
            S.op("dve", lambda e: e.reciprocal(out=stat[0:TT, 2, col:col + 1], in_=stat[0:TT, 1, col:col + 1]),
                 reads=["stat%d" % col], writes=["stat%d" % col])
            S.op("act", lambda e: e.activation(out=xb[0:TT, :], in_=src_ap, func=AF.Copy,
                                               scale=stat[0:TT, 2, col:col + 1]),
                 reads=[rd, "stat%d" % col], writes=[xr])
            for c in range(8):
                S.op("pe", lambda e, c=c: e.transpose(PTB[:, c * 128:c * 128 + TT], xb[0:TT, c * 128:(c + 1) * 128],
                                                      identb[0:TT, 0:TT]),
                     reads=[xr, "identb"], writes=["ptb"], sig=(c == 7))
            o, w = PP[gain_name]
            S.op("dve", lambda e: e.tensor_tensor(
                out=dst3, in0=PTB[:, :].rearrange("p (c t) -> p c t", c=8)[:, :, 0:TT],
                in1=pp[:, o:o + 8].unsqueeze(2).to_broadcast([128, 8, TT]), op=ALU.mult),
                reads=["ptb", "pp"], writes=[dst_res])
            return col

        with contextlib.ExitStack() as st1:
            hT1 = sbuf(st1, "hT1", [128, 8, 2048], BF16)
            hTs = sbuf(st1, "hTs", [128, 8, NS], BF16)
            QT = [sbuf(st1, "QT%d" % g, [128, 2, GROUPS[g][1], 2048 // GROUPS[g][1]], BF16) for g in range(3)]
            KT = [sbuf(st1, "KT%d" % g, [128, 2, GROUPS[g][1], 128 + 2048 // GROUPS[g][1]], BF16) for g in range(3)]
            NB = [1 + 16 // GROUPS[g][1] for g in range(3)]
            VV = [sbuf(st1, "VV%d" % g, [128, GROUPS[g][1], NB[g], 260], BF16) for g in range(3)]
            Wg = [sbuf(st1, "Wg%d" % i, [128, 8, 512], BF16) for i in range(1)]
            xt = [sbuf(st1, "xt%d" % i, [128, D], F32) for i in range(2)]
            stg = [sbuf(st1, "stg%d" % i, [128, 512], F32) for i in range(2)]
            qkb = [sbuf(st1, "qkb%d" % i, [128, 512], BF16) for i in range(2)]
            rtmp = [sbuf(st1, "rtmp%d" % i, [128, 24, 16], F32) for i in range(2)]
            vst = [sbuf(st1, "vst%d" % i, [128, 256], F32) for i in range(2)]
            PT2 = sbuf(st1, "PT2", [128, 16, 2, 128], BF16)
            PTs = [sbuf(st1, "PTs%d" % i, [128, 2, 2, 128], BF16) for i in range(2)]
            csp = sbuf(st1, "csp", [128, 16, 16], F32)
            snp = sbuf(st1, "snp", [128, 16, 16], F32)
            css = sbuf(st1, "css", [NS, 16], F32)
            sns = sbuf(st1, "sns", [NS, 16], F32)
            maskf = sbuf(st1, "maskf", [128, 2, 128], F32)
            vld = [sbuf(st1, "vld%d" % g, [128, GROUPS[g][1], 16 // GROUPS[g][1]], F32) for g in range(3)]
            maskb = sbuf(st1, "maskb", [128, 2, 128], BF16)
            rec = [sbuf(st1, "rec%d" % i, [64, 512], F32) for i in range(1)]
            oTsb = sbuf(st1, "oTsb", [64, 2048], BF16)
            selb = sbuf(st1, "selb", [NS, NS, 128], F32)
            Kc = [sbuf(st1, "Kc%d" % i, [128, 256], F32) for i in range(3)]
            Vc = [sbuf(st1, "Vc%d" % i, [128, 256], F32) for i in range(3)]
            prod = sbuf(st1, "prod", [128, 256], F32)
            sT4 = sbuf(st1, "sT4", [128, 24], F32)
            sm = sbuf(st1, "sm", [NS, 768], F32)
            pcur = sbuf(st1, "pcur", [NS, 12], F32)
            pcm = sbuf(st1, "pcm", [NS, NS, 12], F32)
            id4 = sbuf(st1, "id4", [NS, NS], F32)
            oTf = sbuf(st1, "oTf", [128, 2, NS], F32)

            for g in range(3):
                S.op("pool", lambda e, g=g: e.memset(VV[g][:, :, :, :], 1.0), writes=["VV%d" % g])
            dma("sp", css[:], css_d, writes=["css"], key="css")
            dma("sp", sns[:], sns_d, writes=["sns"], key="sns")
            dma("sp", maskf[:], mask_d, writes=["maskf"], key="maskf")
            dma("sp", selb[:], sel_d, writes=["selb"], key="selb")
            S.op("dve", lambda e: e.tensor_copy(out=maskb[:], in_=maskf[:]), reads=["maskf"], writes=["maskb"])
            S.op("dve", lambda e: e.tensor_copy(out=id4[:], in_=identf[0:NS, 0:NS]), reads=["identf"], writes=["id4"])

            def rope(stv, rd, TT, nh, cs_ap, sn_ap, tab_res, ri):
                rt = rtmp[ri % 2]
                rr = "rtmp%d" % (ri % 2)
                t1 = rt[0:TT, 0:nh, :]
                csb = cs_ap.unsqueeze(1).to_broadcast([TT, nh, 16])
                S.op("dve", lambda e: e.tensor_tensor(out=t1, in0=stv[:, :, 0:16], in1=csb, op=ALU.mult),
                     reads=[rd] + tab_res, writes=[rr])
                S.op("dve", lambda e: e.tensor_tensor(out=stv[:, :, 0:8], in0=stv[:, :, 0:8],
                                                      in1=sn_ap[:, 8:16].unsqueeze(1).to_broadcast([TT, nh, 8]),
                                                      op=ALU.mult), reads=[rd] + tab_res, writes=[rd])
                S.op("dve", lambda e: e.tensor_tensor(out=stv[:, :, 8:16], in0=stv[:, :, 8:16],
                                                      in1=sn_ap[:, 0:8].unsqueeze(1).to_broadcast([TT, nh, 8]),
                                                      op=ALU.mult), reads=[rd] + tab_res, writes=[rd])
                S.op("dve", lambda e: e.tensor_tensor(out=t1[:, :, 0:8], in0=t1[:, :, 0:8], in1=stv[:, :, 8:16],
                                                      op=ALU.add), reads=[rd, rr], writes=[rr])
                S.op("dve", lambda e: e.tensor_tensor(out=t1[:, :, 8:16], in0=t1[:, :, 8:16], in1=stv[:, :, 0:8],
                                                      op=ALU.add), reads=[rd, rr], writes=[rr])
                S.op("dve", lambda e: e.tensor_copy(out=stv[:, :, 0:16], in_=t1), reads=[rr], writes=[rd])

            xi = 0
            ri = 0
            ev = 0
            bset = [0]
            PTBf = PTB[:, :].bitcast(F32)
            PTB2f = PTB2[:, :].bitcast(F32)
            for _k in range(int(_os.environ.get("KDUMMY", "0"))):
                if _os.environ.get("KDUMMYT") == "memset":
                    S.op("dve", lambda e: e.memset(prod[:, :], 0.0), writes=["prod"])
                else:
                    S.op("dve", lambda e: e.tensor_tensor(out=prod[:, :], in0=prod[:, :], in1=prod[:, :], op=ALU.mult),
                         writes=["prod"])
            for sb in range(NSB):
                T0 = sb * 2048
                last = (sb == NSB - 1)
                dma("sp", csp[:], csp_d[:, 16 * sb:16 * (sb + 1), :], writes=["csp"], key="csp")
                for g in range(3):
                    dma("sp", vld[g][:, :, :], vld_d[g][:, sb, :, :], writes=["vld%d" % g], key="vld%d" % g)
                dma("sp", snp[:], snp_d[:, 16 * sb:16 * (sb + 1), :], writes=["snp"], key="snp")
                for i in range(16):
                    b = xi % 2
                    dma("sp", xt[b][:], xp[T0 + 128 * i:T0 + 128 * (i + 1), :], writes=["xt%d" % b], key="xt%d" % b)
                    norm_T(xt[b][:], "xt%d" % b, 128, hT1[:, :, 128 * i:128 * (i + 1)], "hT1_%d" % i, "gmix", xi)
                    xi += 1
                if sb == 1:
                    b = xi % 2
                    dma("sp", xt[b][0:NS, :], xs, writes=["xt%d" % b], key="xt%d" % b)
                    norm_T(xt[b][0:NS, :], "xt%d" % b, NS, hTs[:, :, 0:NS], "hTs", "gmix", xi)
                    xi += 1
                if KSTOP == 10:
                    S.barrier()
                    S.emit()
                    return nc
                hT_all = ["hT1_%d" % i for i in range(16)]
                if sb > 0:
                    for g in range(3):
                        M = 2048 // GROUPS[g][1]
                        S.op("pool", lambda e, g=g, M=M: e.tensor_copy(out=KT[g][:, :, :, 0:128],
                                                                        in_=KT[g][:, :, :, M:M + 128]),
                             reads=["KT%d" % g], writes=["KT%d" % g])
                        S.op("pool", lambda e, g=g: e.tensor_copy(out=VV[g][:, :, 0, :], in_=VV[g][:, :, NB[g] - 1, :]),
                             reads=["VV%d" % g], writes=["VV%d" % g])
                for cg in range(5):
                    if sb == 0 and cg in (0, 1, 3):
                        continue
                    wcols = 512 if cg < 4 else 256
                    wb = Wg[0]
                    wr = "Wg0"
                    dma("sp", wb[:, :, 0:wcols],
                        wb_in[:, 2048 + 512 * cg:2048 + 512 * cg + wcols].rearrange("(kc p) n -> p kc n", p=128),
                        reads=["wb_qkv%d" % cg], writes=[wr], key=wr)
                    if sb == 1:
                        pst, pr = psum()
                        for h0 in range(0, wcols, 256):
                            for kc in range(8):
                                S.op("pe", lambda e, kc=kc, pst=pst, h0=h0: e.matmul(
                                    pst[0:NS, h0:h0 + 256], lhsT=hTs[:, kc, 0:NS], rhs=wb[:, kc, h0:h0 + 256],
                                    start=(kc == 0), stop=(kc == 7)), reads=["hTs", wr], writes=[pr], sig=(kc == 7))
                        for (c0, c1) in ((0, 256), (256, 512)):
                            if c0 >= wcols:
                                continue
                            gcol = 512 * cg + c0
                            sc = 0.125 if gcol < 768 else 1.0
                            S.op("act", lambda e, c0=c0, c1=c1, pst=pst, sc=sc, gcol=gcol: e.activation(
                                out=sts[0:NS, gcol:gcol + 256], in_=pst[0:NS, c0:c1], func=AF.Copy, scale=sc),
                                reads=[pr], writes=["sts"])
                    if KSTOP == 20:
                        S.barrier()
                        S.emit()
                        return nc
                    if cg < 3:
                        for i in range(16):
                            pst, pr = psum()
                            for kc in range(8):
                                S.op("pe", lambda e, kc=kc, pst=pst, i=i: e.matmul(
                                    pst[:, 0:512], lhsT=hT1[:, kc, 128 * i:128 * (i + 1)], rhs=wb[:, kc, 0:512],
                                    start=(kc == 0), stop=(kc == 7)), reads=["hT1_%d" % i, wr], writes=[pr], sig=(kc == 7))
                            sg = stg[ev % 2]
                            sr = "stg%d" % (ev % 2)
                            qb_ = qkb[ev % 2]
                            qr = "qkb%d" % (ev % 2)
                            ev += 1
                            for (c0, c1) in ((0, 256), (256, 512)):
                                sc = 0.125 if (512 * cg + c0) < 768 else 1.0
                                S.op("act", lambda e, c0=c0, c1=c1, pst=pst, sc=sc, sg=sg: e.activation(
                                    out=sg[:, c0:c1], in_=pst[:, c0:c1], func=AF.Copy, scale=sc),
                                    reads=[pr], writes=[sr])
                            ti = sb * 16 + i
                            if not _os.environ.get("KNOROPE"):
                              rope(sg[:, :].rearrange("p (h d) -> p h d", d=64), sr, 128, 8, csp[:, i, :], snp[:, i, :],
                                 ["csp", "snp"], ri)
                            ri += 1
                            S.op("pool", lambda e, sg=sg, qb_=qb_: e.tensor_copy(out=qb_[:, :], in_=sg[:, :]),
                                 reads=[sr], writes=[qr])
                            if last and KSTOP != 24:
                                for g in range(3):
                                    W = min(GROUPS[g][0], S_TOK)
                                    kcol = 768 + 256 * g
                                    if not (512 * cg <= kcol < 512 * cg + 512):
                                        continue
                                    tpos = 2048 - 128 * (16 - i)
                                    if S_TOK - (T0 + 128 * i) > W:
                                        continue
                                    row0 = W - (S_TOK - (T0 + 128 * i))
                                    lc = kcol - 512 * cg
                                    dma("sp", kp[g][row0:row0 + 128, :], sg[:, lc:lc + 256], reads=[sr], key="o" + sr)
                            for j in range(4):
                                S.op("pe", lambda e, j=j, qb_=qb_: e.transpose((PTB if j < 2 else PTB2)[:, j * 128:(j + 1) * 128],
                                                                                qb_[:, j * 128:(j + 1) * 128], identb[:, :]),
                                     reads=[qr, "identb"], writes=["ptb" if j < 2 else "ptb2"], sig=(j % 2 == 1))
                            for j in range(4):
                                if _os.environ.get("KNOEVAC"):
                                    continue
                                cc = cg * 4 + j
                                isq = cc < 6
                                g = (cc % 6) // 2
                                half = cc % 2
                                d = GROUPS[g][1]
                                mlo = 128 * i // d
                                cnt = 128 // d
                                if isq:
                                    dst = QT[g][:, half, :, mlo:mlo + cnt]
                                    dres = "QT%d" % g
                                else:
                                    dst = KT[g][:, half, :, 128 + mlo:128 + mlo + cnt]
                                    dres = "KT%d" % g
                                src = (PTB if j < 2 else PTB2)[:, j * 128:(j + 1) * 128].rearrange("p (m r) -> p r m", r=d)
                                if j < 2:
                                    S.op("act", lambda e, dst=dst, src=src: e.activation(out=dst, in_=src, func=AF.Copy),
                                         reads=["ptb"], writes=[dres])
                                else:
                                    S.op("dve", lambda e, dst=dst, src=src: e.tensor_copy(out=dst, in_=src),
                                         reads=["ptb2"], writes=[dres])
                            if KSTOP == 21:
                                S.barrier()
                                S.emit()
                                return nc
                            if (KSTOP in (22, 24) and cg == 2 and i == 15) or (KSTOP == 25 and i == 1) or (KSTOP == 33 and cg == 1 and i == 14) or (KSTOP == 31 and cg == 1 and i == 1) or (KSTOP == 32 and cg == 1 and i == 7) or (KSTOP == 29 and cg == 1 and i == 15) or (KSTOP == 30 and cg == 2 and i == 0) or (KSTOP == 28 and cg == 1 and i == 0) or (KSTOP == 26 and i == 7) or (KSTOP == 27 and cg == 0 and i == 15):
                                S.barrier()
                                S.emit()
                                return nc
                    else:
                        jobs = []
                        if cg == 3:
                            for blk in range(16):
                                jobs.append((0, 0, blk, 0, hT1[:, :, 128 * blk:128 * (blk + 1)], ["hT1_%d" % blk]))
                            for r in range(4):
                                for blk in range(4):
                                    jobs.append((1, r, blk, 256, hT1[:, :, 512 * blk + r:512 * (blk + 1):4],
                                                 ["hT1_%d" % t for t in range(4 * blk, 4 * blk + 4)]))
                        else:
                            for r in range(16):
                                jobs.append((2, r, 0, 0, hT1[:, :, r:2048:16], hT_all))
                        for (g, r, blk, c0, lh, lres) in jobs:
                            pst, pr = psum()
                            for kc in range(8):
                                S.op("pe", lambda e, kc=kc, pst=pst, lh=lh, c0=c0: e.matmul(
                                    pst[:, 0:256], lhsT=lh[:, kc, :], rhs=wb[:, kc, c0:c0 + 256],
                                    start=(kc == 0), stop=(kc == 7)), reads=lres + [wr], writes=[pr], sig=(kc == 7))
                            S.op("act", lambda e, pst=pst, g=g, r=r, blk=blk: e.activation(
                                out=VV[g][:, r, 1 + blk, :].rearrange("p (s e) -> p s e", e=65)[:, :, 0:64],
                                in_=pst[:, 0:256].rearrange("p (s e) -> p s e", e=64), func=AF.Copy,
                                scale=vld[g][:, r, blk:blk + 1]),
                                reads=[pr, "vld%d" % g], writes=["VV%d" % g])
                            S.op("pool", lambda e, g=g, r=r, blk=blk: e.tensor_copy(
                                out=VV[g][:, r, 1 + blk, :].rearrange("p (s e) -> p s e", e=65)[:, :, 64:65],
                                in_=vld[g][:, r, blk:blk + 1].unsqueeze(1).to_broadcast([128, 4, 1])),
                                reads=["vld%d" % g], writes=["VV%d" % g])
                            if last:
                                W = min(GROUPS[g][0], S_TOK)
                                d = GROUPS[g][1]
                                nblk = 16 // d
                                first_tok = T0 + r + d * 128 * blk
                                if S_TOK - (T0 + d * 128 * blk) <= W:
                                    row0 = W - (S_TOK - first_tok)
                                    vb = vst[ev % 2]
                                    vr = "vst%d" % (ev % 2)
                                    ev += 1
                                    S.op("dve", lambda e, pst=pst, vb=vb: e.tensor_copy(out=vb[:, :], in_=pst[:, 0:256]),
                                         reads=[pr], writes=[vr])
                                    dma("sp", vp[g][row0:W:d, :], vb[:, :], reads=[vr], key="o" + vr)
                if KSTOP == 23 and False:
                    S.barrier()
                    S.emit()
                    return nc
                if KSTOP == 11:
                    S.barrier()
                    S.emit()
                    return nc
                if sb == 1:
                    rope(sts[0:NS, 0:1536].rearrange("p (h d) -> p h d", d=64), "sts", NS, 24, css[:, :], sns[:, :],
                         ["css", "sns"], ri)
                    ri += 1
                    for g in range(3):
                        W = GROUPS[g][0]
                        dma("sp", ks_o[g][:, W - 1, :], sts[0:NS, 768 + 256 * g:768 + 256 * (g + 1)], reads=["sts"],
                            key="osts")
                        dma("sp", vs_o[g][:, W - 1, :], sts[0:NS, 1536 + 256 * g:1536 + 256 * (g + 1)], reads=["sts"],
                            key="osts")
                    S.op("dve", lambda e: e.tensor_tensor(out=sm[0:NS, 0:768], in0=sts[0:NS, 0:768],
                                                          in1=sts[0:NS, 768:1536], op=ALU.mult),
                         reads=["sts"], writes=["sm"])
                    S.op("dve", lambda e: e.tensor_reduce(out=pcur[:, :],
                                                          in_=sm[0:NS, 0:768].rearrange("p (h d) -> p h d", d=64),
                                                          axis=AX.X, op=ALU.add), reads=["sm"], writes=["pcur"])
                    S.op("act", lambda e: e.activation(out=pcur[:, :], in_=pcur[:, :], func=AF.Exp),
                         reads=["pcur"], writes=["pcur"])
                    S.op("dve", lambda e: e.tensor_tensor(out=pcm[:, :, :],
                                                          in0=pcur[:, :].unsqueeze(1).to_broadcast([NS, NS, 12]),
                                                          in1=id4[:, :].unsqueeze(2).to_broadcast([NS, NS, 12]),
                                                          op=ALU.mult), reads=["pcur", "id4"], writes=["pcm"])
                    for s in range(NS):
                        pnum, pnr = psum()
                        for g in range(3):
                            W, d = GROUPS[g]
                            kb_, vb_ = Kc[g], Vc[g]
                            kr, vr = "Kc%d" % g, "Vc%d" % g
                            dma("sp", kb_[:, :], cks[g][s, 0:W:d, :], writes=[kr], key=kr)
                            dma("sp", vb_[:, :], cvs[g][s, 0:W:d, :], writes=[vr], key=vr)
                            pq, pqr = psum()
                            S.op("pe", lambda e, pq=pq, s=s, g=g: e.matmul(pq[:, 0:256], lhsT=selb[0:NS, s, :],
                                                                           rhs=sts[0:NS, 256 * g:256 * (g + 1)],
                                                                           start=True, stop=True),
                                 reads=["selb", "sts"], writes=[pqr])
                            S.op("dve", lambda e, pq=pq, kb_=kb_: e.tensor_tensor(out=prod[:, :], in0=kb_[:, :],
                                                                                    in1=pq[:, 0:256], op=ALU.mult),
                                 reads=[kr, pqr], writes=["prod"])
                            S.op("dve", lambda e, g=g: e.tensor_reduce(out=sT4[:, 4 * g:4 * g + 4],
                                                                  in_=prod[:, :].rearrange("p (h d) -> p h d", d=64),
                                                                  axis=AX.X, op=ALU.add), reads=["prod"], writes=["sT4"])
                        S.op("act", lambda e: e.activation(out=sT4[:, 12:24], in_=sT4[:, 0:12], func=AF.Exp),
                             reads=["sT4"], writes=["sT4"])
                        for c in range(3):
                            for g in range(3):
                                pT_ = sT4[:, 12 + 4 * g:16 + 4 * g]
                                pc_ = pcm[0:NS, s, 4 * g:4 * g + 4]
                                if c < 2:
                                    l1 = Vc[g][:, 128 * c:128 * (c + 1)]
                                    l2 = sts[0:NS, 1536 + 256 * g + 128 * c:1536 + 256 * g + 128 * (c + 1)]
                                    r1 = ["Vc%d" % g, "sT4"]
                                else:
                                    l1 = onesf[:, :]
                                    l2 = onesf[0:NS, :]
                                    r1 = ["onesf", "sT4"]
                                S.op("pe", lambda e, c=c, g=g, pnum=pnum, l1=l1, pT_=pT_: e.matmul(
                                    pnum[:, 4 * c:4 * c + 4], lhsT=l1, rhs=pT_, start=(g == 0), stop=False),
                                    reads=r1, writes=[pnr])
                                S.op("pe", lambda e, c=c, g=g, pnum=pnum, l2=l2, pc_=pc_: e.matmul(
                                    pnum[:, 4 * c:4 * c + 4], lhsT=l2, rhs=pc_, start=False, stop=(g == 2)),
                                    reads=["sts", "pcm", "onesf"], writes=[pnr], sig=(g == 2))
                        S.op("dve", lambda e, pnum=pnum: e.reciprocal(out=sT4[:, 0:4], in_=pnum[:, 8:12]),
                             reads=[pnr], writes=["sT4"])
                        for c in range(2):
                            for hf in range(2):
                                slot = 2 * c + hf
                                S.op("dve", lambda e, c=c, hf=hf, slot=slot, pnum=pnum, s=s: e.tensor_tensor(
                                    out=oTf[64 * hf:64 * (hf + 1), c, s:s + 1],
                                    in0=pnum[64 * hf:64 * (hf + 1), 4 * c + slot:4 * c + slot + 1],
                                    in1=sT4[64 * hf:64 * (hf + 1), slot:slot + 1], op=ALU.mult),
                                    reads=[pnr, "sT4"], writes=["oTf"])
                    S.op("dve", lambda e: e.tensor_copy(out=oTs[:, :, :], in_=oTf[:, :, :]), reads=["oTf"], writes=["oTs"])
                    if DBG:
                        d1 = dout("dbg_oTf", [128, 2, NS], F32)
                        dma("sp", d1, oTf[:, :, :], reads=["oTf"], key="dbg")
                        d2 = dout("dbg_sts", [NS, 2304], F32)
                        dma("sp", d2, sts[:, :], reads=["sts"], key="dbg")
                        d3 = dout("dbg_pcur", [NS, 12], F32)
                        dma("sp", d3, pcur[:, :], reads=["pcur"], key="dbg")

                if KSTOP == 12:
                    S.barrier()
                    S.emit()
                    return nc
                if DBG and sb == 0:
                    for g in range(3):
                        dq = dout("dbg_QT%d" % g, [128, 2, GROUPS[g][1], 2048 // GROUPS[g][1]], BF16)
                        dk = dout("dbg_KT%d" % g, [128, 2, GROUPS[g][1], 128 + 2048 // GROUPS[g][1]], BF16)
                        dv = dout("dbg_VV%d" % g, [128, GROUPS[g][1], NB[g], 260], BF16)
                        dma("sp", dq, QT[g][:, :, :, :], reads=["QT%d" % g], key="dbg")
                        dma("sp", dk, KT[g][:, :, :, :], reads=["KT%d" % g], key="dbg")
                        dma("sp", dv, VV[g][:, :, :, :], reads=["VV%d" % g], key="dbg")
                pti = 0
                if sb == 0:
                    continue
                ch_list = (3,) if sb == 1 else (0, 1, 2, 3)
                for slot in range(4):
                    c2 = slot // 2
                    pb = 64 * (slot % 2)
                    for rp in range(8):
                        pss, psr = psum45()
                        for q in range(2):
                            r = 2 * rp + q
                            if sb > 0:
                                S.op("pe", lambda e, pss=pss, q=q, r=r: e.matmul(
                                    pss[:, 256 * q:256 * q + 128], lhsT=KT[2][pb:pb + 64, c2, r, 0:128],
                                    rhs=QT[2][pb:pb + 64, c2, r, 0:128], start=True, stop=True),
                                    reads=["KT2", "QT2"], writes=[psr])
                            S.op("pe", lambda e, pss=pss, q=q, r=r: e.matmul(
                                pss[:, 256 * q + 128:256 * q + 256], lhsT=KT[2][pb:pb + 64, c2, r, 128:256],
                                rhs=QT[2][pb:pb + 64, c2, r, 0:128], start=True, stop=True),
                                reads=["KT2", "QT2"], writes=[psr])
                        k0 = 0 if sb > 0 else 1
                        S.op("act", lambda e, pss=pss, rp=rp, k0=k0: e.activation(
                            out=PT2[:, 2 * rp:2 * rp + 2, k0:2, :],
                            in_=pss[:, :].rearrange("p (q k t) -> p q k t", q=2, k=2)[:, :, k0:2, :], func=AF.Exp),
                            reads=[psr], writes=["PT2"])
                        S.op("dve", lambda e, rp=rp, k0=k0: e.tensor_tensor(
                            out=PT2[:, 2 * rp:2 * rp + 2, k0:2, :], in0=PT2[:, 2 * rp:2 * rp + 2, k0:2, :],
                            in1=maskb[:, k0:2, :].unsqueeze(1).to_broadcast([128, 2, 2 - k0, 128]), op=ALU.mult),
                            reads=["PT2", "maskb"], writes=["PT2"])
                    if DBG and sb == 0 and slot == 0:
                        dp2 = dout("dbg_PT2", [128, 16, 2, 128], BF16)
                        dma("sp", dp2, PT2[:, :, :, :], reads=["PT2"], key="dbg")
                    for ch in ch_list:
                        bset[0] += 1
                        if bset[0] % 2:
                            Bk = [(PSB[i_], "ps%d" % i_) for i_ in range(3)]
                        else:
                            Bk = [(PSB[3], "ps3"), (PTBf, "ptb"), (PTB2f, "ptb2")]
                        for g in range(2):
                            for pair in range(2):
                                pss, psr = psum45()
                                ptb_ = PTs[pti % 2]
                                ptr = "PTs%d" % (pti % 2)
                                pti += 1
                                info = []
                                for q in range(2):
                                    u = 2 * pair + q
                                    if g == 0:
                                        qb_i = 4 * ch + u
                                        hasp = (sb > 0) or (qb_i > 0)
                                        kprev = KT[0][pb:pb + 64, c2, 0, 128 * qb_i:128 * qb_i + 128]
                                        kcur = KT[0][pb:pb + 64, c2, 0, 128 + 128 * qb_i:256 + 128 * qb_i]
                                        qq = QT[0][pb:pb + 64, c2, 0, 128 * qb_i:128 * qb_i + 128]
                                        vprev = VV[0][:, 0, qb_i, 65 * slot:65 * (slot + 1)]
                                        vcur = VV[0][:, 0, qb_i + 1, 65 * slot:65 * (slot + 1)]
                                        ocols = slice(128 * u, 128 * (u + 1))
                                    else:
                                        hasp = (sb > 0) or (ch > 0)
                                        kprev = KT[1][pb:pb + 64, c2, u, 128 * ch:128 * ch + 128]
                                        kcur = KT[1][pb:pb + 64, c2, u, 128 + 128 * ch:256 + 128 * ch]
                                        qq = QT[1][pb:pb + 64, c2, u, 128 * ch:128 * ch + 128]
                                        vprev = VV[1][:, u, ch, 65 * slot:65 * (slot + 1)]
                                        vcur = VV[1][:, u, ch + 1, 65 * slot:65 * (slot + 1)]
                                        ocols = slice(128 * u, 128 * (u + 1))
                                    if hasp:
                                        S.op("pe", lambda e, pss=pss, q=q, kprev=kprev, qq=qq: e.matmul(
                                            pss[:, 256 * q:256 * q + 128], lhsT=kprev, rhs=qq, start=True, stop=True),
                                            reads=["KT%d" % g, "QT%d" % g], writes=[psr])
                                    S.op("pe", lambda e, pss=pss, q=q, kcur=kcur, qq=qq: e.matmul(
                                        pss[:, 256 * q + 128:256 * q + 256], lhsT=kcur, rhs=qq, start=True, stop=True),
                                        reads=["KT%d" % g, "QT%d" % g], writes=[psr])
                                    info.append((hasp, vprev, vcur, ocols))
                                allp = info[0][0] and info[1][0]
                                k0 = 0 if allp else 1
                                if (not allp) and (info[0][0] or info[1][0]):
                                    S.op("act", lambda e, pss=pss, ptb_=ptb_: e.activation(
                                        out=ptb_[:, 1, 0, :], in_=pss[:, 256:384], func=AF.Exp), reads=[psr], writes=[ptr])
                                    S.op("dve", lambda e, ptb_=ptb_: e.tensor_tensor(
                                        out=ptb_[:, 1, 0, :], in0=ptb_[:, 1, 0, :], in1=maskb[:, 0, :], op=ALU.mult),
                                        reads=[ptr, "maskb"], writes=[ptr])
                                S.op("act", lambda e, pss=pss, ptb_=ptb_, k0=k0: e.activation(
                                    out=ptb_[:, :, k0:2, :],
                                    in_=pss[:, :].rearrange("p (q k t) -> p q k t", q=2, k=2)[:, :, k0:2, :], func=AF.Exp),
                                    reads=[psr], writes=[ptr])
                                S.op("dve", lambda e, ptb_=ptb_, k0=k0: e.tensor_tensor(
                                    out=ptb_[:, :, k0:2, :], in0=ptb_[:, :, k0:2, :],
                                    in1=maskb[:, k0:2, :].unsqueeze(1).to_broadcast([128, 2, 2 - k0, 128]), op=ALU.mult),
                                    reads=[ptr, "maskb"], writes=[ptr])
                                if DBG and sb == 0 and slot == 0 and ch == 0:
                                    dpt = dout("dbg_PT%d_%d" % (g, pair), [128, 2, 2, 128], BF16)
                                    dma("sp", dpt, ptb_[:, :, :, :], reads=[ptr], key="dbg")
                                for q in range(2):
                                    hasp, vprev, vcur, ocols = info[q]
                                    pO, pOr = Bk[g]
                                    kbl = (0, 1) if hasp else (1,)
                                    for kb in kbl:
                                        vv = vprev if kb == 0 else vcur
                                        S.op("pe", lambda e, pO=pO, vv=vv, ptb_=ptb_, q=q, kb=kb, ocols=ocols, kbl=kbl: e.matmul(
                                            pO[0:65, ocols], lhsT=vv, rhs=ptb_[:, q, kb, :], start=(kb == kbl[0]),
                                            stop={"0": False, "1": True}.get(_os.environ.get("KSTOPF", ""), kb == 1)),
                                            reads=["VV%d" % g, ptr], writes=[pOr])
                        pO, pOr = Bk[2]
                        for r in range(16):
                            kbs = (0, 1) if sb > 0 else (1,)
                            for kb in kbs:
                                S.op("pe", lambda e, pO=pO, r=r, kb=kb, kbs=kbs: e.matmul(
                                    pO[0:65, 32 * r:32 * (r + 1)], lhsT=VV[2][:, r, kb, 65 * slot:65 * (slot + 1)],
                                    rhs=PT2[:, r, kb, 32 * ch:32 * (ch + 1)], start=(kb == kbs[0]), stop=(kb == 1)),
                                    reads=["VV2", "PT2"], writes=[pOr], sig=(kb == 1))
                        if DBG and sb == 0 and slot == 0 and ch == 0:
                            for gq in range(3):
                                db = dout("dbg_B%d" % gq, [65, 512], F32)
                                S.op("dve", lambda e, gq=gq: e.tensor_copy(out=stg[1][0:65, :], in_=Bk[gq][0][0:65, :]),
                                     reads=[Bk[gq][1]], writes=["stg1"])
                                dma("sp", db, stg[1][0:65, :], reads=["stg1"], key="dbg")
                        tmpb = stg[0]
                        S.op("act", lambda e, tmpb=tmpb, b0=Bk[0][0]: e.activation(out=tmpb[0:65, :], in_=b0[0:65, :], func=AF.Copy),
                             reads=[Bk[0][1]], writes=["stg0"])
                        S.op("dve", lambda e, tmpb=tmpb, b1=Bk[1][0]: e.tensor_tensor(
                            out=tmpb[0:65, :].rearrange("p (j r) -> p j r", r=4),
                            in0=tmpb[0:65, :].rearrange("p (j r) -> p j r", r=4),
                            in1=b1[0:65, :].rearrange("p (r j) -> p j r", r=4), op=ALU.add),
                            reads=[Bk[1][1], "stg0"], writes=["stg0"])
                        S.op("dve", lambda e, tmpb=tmpb, b2=Bk[2][0]: e.tensor_tensor(
                            out=tmpb[0:65, :].rearrange("p (j r) -> p j r", r=16),
                            in0=tmpb[0:65, :].rearrange("p (j r) -> p j r", r=16),
                            in1=b2[0:65, :].rearrange("p (r j) -> p j r", r=16), op=ALU.add),
                            reads=[Bk[2][1], "stg0"], writes=["stg0"])
                        S.op("dve", lambda e, tmpb=tmpb: e.tensor_scalar(out=tmpb[64:65, :], in0=tmpb[64:65, :], scalar1=1e-30,
                                                                        scalar2=None, op0=ALU.add),
                             reads=["stg0"], writes=["stg0"])
                        S.op("dve", lambda e, tmpb=tmpb: e.reciprocal(out=stg[1][64:65, :], in_=tmpb[64:65, :]),
                             reads=["stg0"], writes=["stg1"])
                        pD, pDr = Bk[0]
                        S.op("pe", lambda e, pD=pD: e.matmul(pD[0:64, :], lhsT=onesf[64:65, 0:64], rhs=stg[1][64:65, :],
                                                             start=True, stop=True), reads=["onesf", "stg1"], writes=[pDr])
                        pO, pOr = None, "stg0"
                        rc = rec[0]
                        rcr = "rec0"
                        S.op("dve", lambda e, pD=pD, tmpb=tmpb, slot=slot, ch=ch: e.tensor_tensor(
                            out=oTsb[:, 512 * ch:512 * (ch + 1)], in0=tmpb[0:64, :], in1=pD[0:64, :], op=ALU.mult),
                            reads=[pDr, "stg0"], writes=["oTsb"])
                    dma("sp", oT_d[:, slot, T0:T0 + 2048], oTsb[:, :], reads=["oTsb"], writes=["oT_d"], key="oT_d")
            S.barrier()
            S.emit()
        if KSTOP == 1:
            return nc

        with contextlib.ExitStack() as st2:
            xg = sbuf(st2, "xg", [128, NT // 128, D], F32)
            hT = sbuf(st2, "hT", [128, 8, NT], BF16)
            uext = sbuf(st2, "uext", [128, 8, 30 + NT], BF16)
            dgb = [sbuf(st2, "dgb%d" % i, [128, 31, 128], BF16) for i in range(2)]
            big2 = sbuf(st2, "big2", [128, 12 * NT], F32)
            acc = big2[:, 0:8 * NT].rearrange("p (c t) -> p c t", c=8)
            lnm = big2[:, 8 * NT:12 * NT].rearrange("p (c t) -> p c t", c=4)
            aT = big2[:, 0:11 * NT].bitcast(BF16).rearrange("p (c t) -> p c t", c=22)
            R3 = sbuf(st2, "R3", [128, 11264], F32)
            wdnb = R3[:, :].bitcast(BF16).rearrange("p (c t) -> p c t", c=22)
            sT = R3[:, 0:2048].bitcast(BF16).rearrange("p (c t) -> p c t", c=8)
            mixT = R3[:, 2048:4096].bitcast(BF16).rearrange("p (c t) -> p c t", c=8)
            woutb = R3[:, 4096:8192].bitcast(BF16).rearrange("p (c t) -> p c t", c=8)
            oTg = R3[0:64, 8192:9216].bitcast(BF16).rearrange("p (c t) -> p c t", c=4)
            cb16 = [R3[:, 9216 + 256 * i:9472 + 256 * i].bitcast(BF16) for i in range(2)]
            csq16 = [R3[:, 9728 + 256 * i:9984 + 256 * i].bitcast(BF16) for i in range(2)]
            tt = [R3[:, 10240 + 512 * i:10752 + 512 * i] for i in range(2)]
            S.alias["wdnb"] = ["sT", "mixT", "woutb", "oTg", "cb16_0", "cb16_1", "csq16_0", "csq16_1", "tt0", "tt1"]
            S.alias["aT"] = ["acc%d" % c for c in range(8)] + ["lnm"]
            S.alias["uh"] = ["uext"] + ["uext%d" % c for c in range(8)]
            S.alias["uprod"] = S.alias["uh"]
            wao64 = sbuf(st2, "wao64", [64, 4, D], BF16)
            wao128 = sbuf(st2, "wao128", [128, 2, D], BF16)
            NWB = 3
            wt = [sbuf(st2, "wt%d" % i, [128, 8, 256], BF16) for i in range(NWB)]
            sg_ = [sbuf(st2, "sg%d" % i, [128, NT], F32) for i in range(3)]
            upx = [sbuf(st2, "upx%d" % i, [128, NT + 2], F32) for i in range(2)]
            cgb = [sbuf(st2, "cgb%d" % i, [128, NT], F32) for i in range(2)]
            fh = sbuf(st2, "fh", [128, 44, 2], F32)
            gfin = sbuf(st2, "gfin", [128, D], F32)
            yt = [sbuf(st2, "yt%d" % i, [128, D], F32) for i in range(1)]
            uflat = uext[:, :, :].rearrange("p c t -> p (c t)")[:, 0:4336].bitcast(F32)
            uh = uflat[:, 0:8 * NS * 31].rearrange("p (c s j) -> p c s j", c=8, s=NS)
            uprod = uflat[:, 8 * NS * 31:16 * NS * 31].rearrange("p (c s j) -> p c s j", c=8, s=NS)
            fhs = sbuf(st2, "fhs", [128, 44, 2 * NS], F32)
            upn = sbuf(st2, "upn", [128, 44, NS], F32)
            orow = sbuf(st2, "orow", [30, D], F32)

            dma("sp", gfin[:], gfin_d, writes=["gfin"], key="gfin")
            dma("sp", wao64[:], wb_ao.rearrange("(s d) n -> d s n", d=64), reads=["wb_ao"], writes=["wao"], key="wao64")
            dma("sp", wao128[:], wb_ao.rearrange("(c p) n -> p c n", p=128), reads=["wb_ao"], writes=["wao"], key="wao128")
            S.op("pool", lambda e: e.memset(uext[:, :, 0:30], 0.0), writes=["uext"])
            S.op("pool", lambda e: e.memset(fh[:], 0.0), writes=["fh"])

            wi = [0]

            def wtile(src_ap, rd):
                i = wi[0] % NWB
                wi[0] += 1
                dma("sp", wt[i][:, :, :], src_ap.rearrange("(kc p) n -> p kc n", p=128), reads=[rd],
                    writes=["wt%d" % i], key="wt%d" % i)
                return wt[i], "wt%d" % i

            tgl = [0]

            def alt(a, b):
                tgl[0] += 1
                return a if tgl[0] % 2 else b

            ygi = [0]

            prevN = [NT]

            def group(t0, TT_list, N, sample, first=False, lastg=False, out0=None):
                gi = 1
                for (i, TT) in TT_list:
                    src = xs if sample else xp[t0 + 128 * i:t0 + 128 * i + TT, :]
                    dma("sp", xg[0:TT, i, :], src, writes=["xg%d" % i], key="xg%d" % i)
                    norm_T(xg[0:TT, i, :], "xg%d" % i, TT, hT[:, :, 128 * i:128 * i + TT], "hT", "gmix", ygi[0])
                    ygi[0] += 1
                if not sample:
                    dma("sp", oTg[:, :, 0:N], oT_d[:, :, t0:t0 + N], reads=["oT_d"], writes=["oTg"], key="oTg")
                dma("sp", woutb[:, :, :], wb_out.rearrange("(kc p) n -> p kc n", p=128), reads=["wb_out"],
                    writes=["woutb"], key="woutb")
                if sample:
                    for s in range(NS):
                        dma("sp", orow[:, :], sconv[s, :, :], writes=["orow"], key="scv")
                        for c in range(8):
                            pst, pr = psum()
                            S.op("pe", lambda e, c=c, pst=pst: e.transpose(pst[:, 0:30], orow[0:30, 128 * c:128 * (c + 1)],
                                                                            identf[0:30, 0:30]),
                                 reads=["orow", "identf"], writes=[pr])
                            S.op("act", lambda e, c=c, pst=pst, s=s: e.activation(out=uh[:, c, s, 0:30], in_=pst[:, 0:30],
                                                                                  func=AF.Copy), reads=[pr], writes=["uh"])
                elif not first:
                    pN = prevN[0]
                    S.op("pool", lambda e: e.tensor_copy(out=uext[:, :, 0:30], in_=uext[:, :, pN:pN + 30]),
                         reads=["uext"], writes=["uext"])
                if not sample:
                    prevN[0] = N
                for c in range(8):
                    w, wr = wtile(wb_in[:, 256 * c:256 * (c + 1)], "wb_in1")
                    pl, plr = psum()
                    pg, pgr = psum()
                    for kc in range(8):
                        S.op("pe", lambda e, kc=kc, w=w, pl=pl: e.matmul(pl[:, 0:N], lhsT=w[:, kc, 0:128], rhs=hT[:, kc, 0:N],
                                                                         start=(kc == 0), stop=(kc == 7)),
                             reads=[wr, "hT"], writes=[plr], sig=(kc == 7))
                    for kc in range(8):
                        S.op("pe", lambda e, kc=kc, w=w, pg=pg: e.matmul(pg[:, 0:N], lhsT=w[:, kc, 128:256], rhs=hT[:, kc, 0:N],
                                                                         start=(kc == 0), stop=(kc == 7)),
                             reads=[wr, "hT"], writes=[pgr], sig=(kc == 7))
                    sgb = sg_[c % 2]
                    sgr = "sg%d" % (c % 2)
                    S.op("act", lambda e, pg=pg, sgb=sgb: e.activation(out=sgb[:, 0:N], in_=pg[:, 0:N], func=AF.Sigmoid),
                         reads=[pgr], writes=[sgr])
                    if sample:
                        S.op("dve", lambda e, c=c, pl=pl, sgb=sgb: e.tensor_tensor(out=uh[:, c, :, 30], in0=pl[:, 0:N],
                                                                                   in1=sgb[:, 0:N], op=ALU.mult),
                             reads=[plr, sgr], writes=["uh"])
                    else:
                        S.op("dve", lambda e, c=c, pl=pl, sgb=sgb: e.tensor_tensor(out=uext[:, c, 30:30 + N], in0=pl[:, 0:N],
                                                                                   in1=sgb[:, 0:N], op=ALU.mult),
                             reads=[plr, sgr], writes=["uext%d" % c])
                o_w = PP["wdw"][0]
                if sample:
                    S.op("dve", lambda e: e.tensor_tensor(
                        out=uprod[:, :, :, :], in0=uh[:, :, :, :],
                        in1=pp[:, o_w:o_w + 248].rearrange("p (c j) -> p c j", j=31).unsqueeze(2).to_broadcast([128, 8, NS, 31]),
                        op=ALU.mult), reads=["uh", "pp"], writes=["uprod"])
                    S.op("dve", lambda e: e.tensor_reduce(out=acc[:, :, 0:NS], in_=uprod[:, :, :, :], axis=AX.X, op=ALU.add),
                         reads=["uprod"], writes=["acc%d" % c for c in range(8)])
                    o_b = PP["bdw"][0]
                    S.op("dve", lambda e: e.tensor_tensor(out=acc[:, :, 0:NS], in0=acc[:, :, 0:NS],
                                                          in1=pp[:, o_b:o_b + 8].unsqueeze(2).to_broadcast([128, 8, NS]),
                                                          op=ALU.add), reads=["acc%d" % c for c in range(8)] + ["pp"],
                         writes=["acc%d" % c for c in range(8)])
                else:
                    for c in range(8):
                        dg = dgb[c % 2]
                        dgr = "dgb%d" % (c % 2)
                        S.op("pool", lambda e, c=c, dg=dg: e.tensor_tensor(
                            out=dg[:, :, :], in0=identf[:, :].unsqueeze(1).to_broadcast([128, 31, 128]),
                            in1=pp[:, o_w + 31 * c:o_w + 31 * c + 31].unsqueeze(2).to_broadcast([128, 31, 128]),
                            op=ALU.mult), reads=["identf", "pp"], writes=[dgr])
                        pcv, pcr = psum()
                        for j in range(31):
                            S.op("pe", lambda e, c=c, j=j, dg=dg, pcv=pcv: e.matmul(
                                pcv[:, 0:N], lhsT=dg[:, j, :], rhs=uext[:, c, j:j + N], start=(j == 0), stop=(j == 30)),
                                reads=[dgr, "uext%d" % c, "uext"], writes=[pcr], sig=(j == 30))
                        S.op("act", lambda e, c=c, pcv=pcv: e.activation(out=acc[:, c, 0:N], in_=pcv[:, 0:N], func=AF.Identity,
                                                                         bias=P("bdw", c)),
                             reads=[pcr, "pp"], writes=["acc%d" % c])
                if sample or lastg:
                    for c in range(8):
                        pst, pr = psum()
                        if sample:
                            S.op("pe", lambda e, c=c, pst=pst: e.transpose(pst[0:NS, 0:128], uh[:, c, :, 30], identf[:, :]),
                                 reads=["uh", "identf"], writes=[pr])
                            S.op("act", lambda e, c=c, pst=pst: e.activation(out=orow[0:NS, 128 * c:128 * (c + 1)],
                                                                             in_=pst[0:NS, 0:128], func=AF.Copy),
                                 reads=[pr], writes=["orow"])
                        else:
                            S.op("pe", lambda e, c=c: e.transpose(PTB[0:30, 128 * c:128 * (c + 1)], uext[:, c, NT:NT + 30],
                                                                   identb[:, :]),
                                 reads=["uext%d" % c, "identb"], writes=["ptb"])
                            S.op("act", lambda e, c=c: e.activation(out=orow[0:30, 128 * c:128 * (c + 1)],
                                                                    in_=PTB[0:30, 128 * c:128 * (c + 1)], func=AF.Copy),
                                 reads=["ptb"], writes=["orow"])
                    if sample:
                        dma("sp", conv_s[:, 29, :], orow[0:NS, 0:D], reads=["orow"], key="oorow")
                    else:
                        dma("sp", conv_p[:, :], orow[0:30, 0:D], reads=["orow"], key="oorow")
                if DBG and t0 == HALO and not sample:
                    dma("sp", dbg_acc, acc[:, :, :], reads=["acc%d" % c for c in range(8)], key="dbg")
                p1, p1r = psum()
                p2, p2r = psum()
                for c in range(8):
                    b1 = cb16[c % 2]
                    b2 = csq16[c % 2]
                    S.op("pool", lambda e, c=c, b1=b1: e.tensor_copy(out=b1[:, 0:N], in_=acc[:, c, 0:N]),
                         reads=["acc%d" % c], writes=["cb16_%d" % (c % 2)])
                    S.op("act", lambda e, c=c, b2=b2: e.activation(out=b2[:, 0:N], in_=acc[:, c, 0:N], func=AF.Square),
                         reads=["acc%d" % c], writes=["csq16_%d" % (c % 2)])
                    S.op("pe", lambda e, c=c, b1=b1: e.matmul(p1[:, 0:N], lhsT=onesb[:, :], rhs=b1[:, 0:N], start=(c == 0),
                                                              stop=(c == 7)), reads=["onesb", "cb16_%d" % (c % 2)], writes=[p1r])
                    S.op("pe", lambda e, c=c, b2=b2: e.matmul(p2[:, 0:N], lhsT=onesb[:, :], rhs=b2[:, 0:N], start=(c == 0),
                                                              stop=(c == 7)), reads=["onesb", "csq16_%d" % (c % 2)], writes=[p2r])
                S.op("dve", lambda e: e.tensor_scalar(out=lnm[:, 0, 0:N], in0=p1[:, 0:N], scalar1=1.0 / D, scalar2=None,
                                                      op0=ALU.mult), reads=[p1r], writes=["lnm"])
                S.op("dve", lambda e: e.tensor_tensor(out=lnm[:, 1, 0:N], in0=lnm[:, 0, 0:N], in1=lnm[:, 0, 0:N], op=ALU.mult),
                     reads=["lnm"], writes=["lnm"])
                S.op("dve", lambda e: e.scalar_tensor_tensor(out=lnm[:, 1, 0:N], in0=p2[:, 0:N], scalar=1.0 / D,
                                                             in1=lnm[:, 1, 0:N], op0=ALU.mult, op1=ALU.subtract),
                     reads=[p2r, "lnm"], writes=["lnm"])
                S.op("act", lambda e: e.activation(out=lnm[:, 2, 0:N], in_=lnm[:, 1, 0:N], func=AF.Sqrt, bias=epst[:, 0:1]),
                     reads=["lnm", "epst"], writes=["lnm"])
                S.op("dve", lambda e: e.reciprocal(out=lnm[:, 3, 0:N], in_=lnm[:, 2, 0:N]), reads=["lnm"], writes=["lnm"])
                for c in range(8):
                    tb = tt[c % 2]
                    tr = "tt%d" % (c % 2)
                    S.op("dve", lambda e, c=c, tb=tb: e.tensor_tensor(out=tb[:, 0:N], in0=acc[:, c, 0:N], in1=lnm[:, 0, 0:N],
                                                                      op=ALU.subtract),
                         reads=["acc%d" % c, "lnm"], writes=[tr])
                    S.op("pool", lambda e, tb=tb: e.tensor_tensor(out=tb[:, 0:N], in0=tb[:, 0:N], in1=lnm[:, 3, 0:N],
                                                                  op=ALU.mult), reads=[tr, "lnm"], writes=[tr])
                    S.op("act", lambda e, c=c, tb=tb: e.activation(out=sT[:, c, 0:N], in_=tb[:, 0:N], func=AF.Silu,
                                                                   scale=P("lng", c), bias=P("lnb", c)),
                         reads=[tr, "pp"], writes=["sT"])
                if DBG and t0 == HALO and not sample:
                    dma("sp", dbg_sT, sT[:, :, :], reads=["sT"], key="dbg")
                for c in range(8):
                    if c % 2 == 0:
                        wco_t, wco_r = wtile(wb_co[:, 128 * c:128 * c + 256], "wb_co")
                    wg_t, wg_r = wtile(wb_in[:, 4352 + 256 * c:4352 + 256 * (c + 1)], "wb_in3")
                    pa, par = psum()
                    pbb, pbr = psum()
                    pga, pgar = psum()
                    pgb, pgbr = psum()
                    co = 128 * (c % 2)
                    for kc in range(8):
                        S.op("pe", lambda e, kc=kc, pa=pa, wco_t=wco_t, co=co: e.matmul(
                            pa[:, 0:N], lhsT=wco_t[:, kc, co:co + 128], rhs=sT[:, kc, 0:N], start=(kc == 0), stop=(kc == 7)),
                            reads=[wco_r, "sT"], writes=[par], sig=(kc == 7))
                    if sample:
                        for k2 in range(2):
                            S.op("pe", lambda e, k2=k2, pbb=pbb, c=c: e.matmul(
                                pbb[:, 0:N], lhsT=wao128[:, k2, 128 * c:128 * (c + 1)], rhs=oTs[:, k2, 0:N],
                                start=(k2 == 0), stop=(k2 == 1)), reads=["wao", "oTs"], writes=[pbr], sig=(k2 == 1))
                    else:
                        for sl in range(4):
                            S.op("pe", lambda e, sl=sl, pbb=pbb, c=c: e.matmul(
                                pbb[:, 0:N], lhsT=wao64[:, sl, 128 * c:128 * (c + 1)], rhs=oTg[:, sl, 0:N],
                                start=(sl == 0), stop=(sl == 3)), reads=["wao", "oTg"], writes=[pbr], sig=(sl == 3))
                    for kc in range(8):
                        S.op("pe", lambda e, kc=kc, pga=pga, wg_t=wg_t: e.matmul(
                            pga[:, 0:N], lhsT=wg_t[:, kc, 0:128], rhs=hT[:, kc, 0:N], start=(kc == 0), stop=(kc == 7)),
                            reads=[wg_r, "hT"], writes=[pgar], sig=(kc == 7))
                    for kc in range(8):
                        S.op("pe", lambda e, kc=kc, pgb=pgb, wg_t=wg_t: e.matmul(
                            pgb[:, 0:N], lhsT=wg_t[:, kc, 128:256], rhs=hT[:, kc, 0:N], start=(kc == 0), stop=(kc == 7)),
                            reads=[wg_r, "hT"], writes=[pgbr], sig=(kc == 7))
                    sa, sar = sg_[0], "sg0"
                    sb_, sbr = sg_[1], "sg1"
                    S.op("act", lambda e, pga=pga: e.activation(out=sa[:, 0:N], in_=pga[:, 0:N], func=AF.Sigmoid),
                         reads=[pgar], writes=[sar])
                    S.op("act", lambda e, pgb=pgb: e.activation(out=sb_[:, 0:N], in_=pgb[:, 0:N], func=AF.Sigmoid),
                         reads=[pgbr], writes=[sbr])
                    S.op("dve", lambda e, pa=pa: e.tensor_tensor(out=sa[:, 0:N], in0=pa[:, 0:N], in1=sa[:, 0:N], op=ALU.mult),
                         reads=[par, sar], writes=[sar])
                    S.op("dve", lambda e, pbb=pbb: e.tensor_tensor(out=sb_[:, 0:N], in0=pbb[:, 0:N], in1=sb_[:, 0:N],
                                                                   op=ALU.mult), reads=[pbr, sbr], writes=[sbr])
                    S.op("pool", lambda e, c=c: e.tensor_tensor(out=mixT[:, c, 0:N], in0=sa[:, 0:N], in1=sb_[:, 0:N],
                                                                op=ALU.add), reads=[sar, sbr], writes=["mixT"])
                if DBG and t0 == HALO and not sample:
                    dma("sp", dbg_mix, mixT[:, :, :], reads=["mixT"], key="dbg")
                for (i, TT) in TT_list:
                    WN = 512 if TT == 128 else 256
                    for n in range(D // WN):
                        po, por = psum()
                        for kc in range(8):
                            S.op("pe", lambda e, kc=kc, po=po, i=i, TT=TT, n=n, WN=WN: e.matmul(
                                po[0:TT, 0:WN], lhsT=mixT[:, kc, 128 * i:128 * i + TT], rhs=woutb[:, kc, WN * n:WN * (n + 1)],
                                start=(kc == 0), stop=(kc == 7)), reads=["mixT", "woutb"], writes=[por], sig=(kc == 7))
                        S.op("dve", lambda e, po=po, i=i, TT=TT, n=n, WN=WN: e.tensor_tensor(
                            out=xg[0:TT, i, WN * n:WN * (n + 1)], in0=po[0:TT, 0:WN], in1=xg[0:TT, i, WN * n:WN * (n + 1)],
                            op=ALU.add), reads=[por, "xg%d" % i], writes=["xg%d" % i])
                if DBG and t0 == HALO and not sample:
                    for (i, TT) in TT_list:
                        dma("sp", dbg_xmid[128 * i:128 * (i + 1), :], xg[0:TT, i, :], reads=["xg%d" % i], key="dbg")
                dma("sp", wdnb[:, :, :], wb_dn.rearrange("(kc p) n -> p kc n", p=128), reads=["wb_dn"], writes=["wdnb"],
                    key="wdnb")
                for (i, TT) in TT_list:
                    norm_T(xg[0:TT, i, :], "xg%d" % i, TT, hT[:, :, 128 * i:128 * i + TT], "hT", "gffn", ygi[0])
                    ygi[0] += 1
                if sample:
                    for q in range(44):
                        if q % 8 == 0:
                            wpc = min(1024, 2 * DFF - 128 * q)
                            dma("sp", orow[0:2 * NS, 0:wpc], sffn.rearrange("s j n -> (s j) n")[:, 128 * q:128 * q + wpc],
                                writes=["orow"], key="scv")
                        pst, pr = psum()
                        S.op("pe", lambda e, q=q, pst=pst: e.transpose(
                            pst[:, 0:2 * NS], orow[0:2 * NS, 128 * (q % 8):128 * (q % 8 + 1)], identf[0:2 * NS, 0:2 * NS]),
                             reads=["orow", "identf"], writes=[pr])
                        S.op("act", lambda e, q=q, pst=pst: e.activation(out=fhs[:, q, :], in_=pst[:, 0:2 * NS], func=AF.Copy),
                             reads=[pr], writes=["fhs"])
                o_f = PP["wfdw"][0]
                for j in range(22):
                    w, wr = wtile(wb_up[:, 256 * j:256 * (j + 1)], "wb_up")
                    pgv = []
                    for hv in range(2):
                        pz, pzr = psum()
                        for kc in range(8):
                            S.op("pe", lambda e, kc=kc, w=w, pz=pz, hv=hv: e.matmul(
                                pz[:, 0:N], lhsT=w[:, kc, 128 * hv:128 * (hv + 1)], rhs=hT[:, kc, 0:N], start=(kc == 0),
                                stop=(kc == 7)), reads=[wr, "hT"], writes=[pzr], sig=(kc == 7))
                        pgv.append((pz, pzr))
                    for hv in range(2):
                        q = 2 * j + hv
                        pz, pzr = pgv[hv]
                        ub, ur = upx[hv], "upx%d" % hv
                        cb, cr = cgb[hv], "cgb%d" % hv
                        w0 = pp[:, o_f + 3 * q:o_f + 3 * q + 1]
                        w1 = pp[:, o_f + 3 * q + 1:o_f + 3 * q + 2]
                        w2 = pp[:, o_f + 3 * q + 2:o_f + 3 * q + 3]
                        S.op("act", lambda e, pz=pz, cb=cb, w2=w2, q=q: e.activation(
                            out=cb[:, 0:N], in_=pz[:, 0:N], func=AF.Identity, scale=w2, bias=P("bfdw", q)),
                            reads=[pzr, "pp"], writes=[cr])
                        if sample:
                            S.op("act", lambda e, pz=pz, q=q: e.activation(out=upn[:, q, :], in_=pz[:, 0:NS], func=AF.Copy),
                                 reads=[pzr], writes=["upn"])
                            fv = fhs[:, q, :].rearrange("p (s j) -> p j s", j=2)
                            S.op("dve", lambda e, cb=cb, fv=fv, w1=w1: e.scalar_tensor_tensor(
                                out=cb[:, 0:N], in0=fv[:, 1, :], scalar=w1, in1=cb[:, 0:N], op0=ALU.mult, op1=ALU.add),
                                reads=["fhs", cr, "pp"], writes=[cr])
                            S.op("dve", lambda e, cb=cb, fv=fv, w0=w0: e.scalar_tensor_tensor(
                                out=cb[:, 0:N], in0=fv[:, 0, :], scalar=w0, in1=cb[:, 0:N], op0=ALU.mult, op1=ALU.add),
                                reads=["fhs", cr, "pp"], writes=[cr])
                        else:
                            S.op("pool", lambda e, ub=ub, q=q: e.tensor_copy(out=ub[:, 0:2], in_=fh[:, q, :]),
                                 reads=["fh"], writes=[ur])
                            S.op("act", lambda e, pz=pz, ub=ub: e.activation(out=ub[:, 2:2 + N], in_=pz[:, 0:N], func=AF.Copy),
                                 reads=[pzr], writes=[ur])
                            S.op("pool", lambda e, ub=ub, q=q: e.tensor_copy(out=fh[:, q, :], in_=ub[:, N:N + 2]),
                                 reads=[ur], writes=["fh"])
                            S.op("dve", lambda e, cb=cb, ub=ub, w1=w1: e.scalar_tensor_tensor(
                                out=cb[:, 0:N], in0=ub[:, 1:1 + N], scalar=w1, in1=cb[:, 0:N], op0=ALU.mult, op1=ALU.add),
                                reads=[ur, cr, "pp"], writes=[cr])
                            S.op("dve", lambda e, cb=cb, ub=ub, w0=w0: e.scalar_tensor_tensor(
                                out=cb[:, 0:N], in0=ub[:, 0:N], scalar=w0, in1=cb[:, 0:N], op0=ALU.mult, op1=ALU.add),
                                reads=[ur, cr, "pp"], writes=[cr])
                    sgb, sgr = sg_[2], "sg2"
                    S.op("act", lambda e, sgb=sgb: e.activation(out=sgb[:, 0:N], in_=cgb[0][:, 0:N], func=AF.Silu),
                         reads=["cgb0"], writes=[sgr])
                    S.op(alt("pool", "dve"), lambda e, j=j, sgb=sgb: e.tensor_tensor(out=aT[:, j, 0:N], in0=sgb[:, 0:N],
                                                                                    in1=cgb[1][:, 0:N], op=ALU.mult),
                         reads=[sgr, "cgb1"], writes=["aT"])
                if sample or lastg:
                    nrow = NS if sample else 2
                    for q in range(44):
                        pst, pr = psum()
                        srcT = upn[:, q, :] if sample else fh[:, q, :]
                        S.op("pe", lambda e, pst=pst, srcT=srcT, nrow=nrow: e.transpose(pst[0:nrow, 0:128], srcT, identf[:, :]),
                             reads=["upn" if sample else "fh", "identf"], writes=[pr])
                        S.op("act", lambda e, q=q, pst=pst, nrow=nrow: e.activation(
                            out=orow[0:nrow, 128 * (q % 8):128 * (q % 8 + 1)], in_=pst[0:nrow, 0:128], func=AF.Copy),
                            reads=[pr], writes=["orow"])
                        if q % 8 == 7 or q == 43:
                            q0 = 8 * (q // 8)
                            wpc = 128 * (q - q0 + 1)
                            if sample:
                                dma("sp", ffn_s[:, 1, 128 * q0:128 * q0 + wpc], orow[0:NS, 0:wpc], reads=["orow"], key="oorow")
                            else:
                                dma("sp", ffn_p[:, 128 * q0:128 * q0 + wpc], orow[0:2, 0:wpc], reads=["orow"], key="oorow")
                for (i, TT) in TT_list:
                    WN = 512 if TT == 128 else 256
                    for n in range(D // WN):
                        po, por = psum()
                        for kc in range(22):
                            S.op("pe", lambda e, kc=kc, po=po, i=i, TT=TT, n=n, WN=WN: e.matmul(
                                po[0:TT, 0:WN], lhsT=aT[:, kc, 128 * i:128 * i + TT], rhs=wdnb[:, kc, WN * n:WN * (n + 1)],
                                start=(kc == 0), stop=(kc == 21)), reads=["aT", "wdnb"], writes=[por], sig=(kc == 21))
                        S.op("dve", lambda e, po=po, i=i, TT=TT, n=n, WN=WN: e.tensor_tensor(
                            out=xg[0:TT, i, WN * n:WN * (n + 1)], in0=po[0:TT, 0:WN], in1=xg[0:TT, i, WN * n:WN * (n + 1)],
                            op=ALU.add), reads=[por, "xg%d" % i], writes=["xg%d" % i])
                for (i, TT) in TT_list:
                    col = stat_i[0]
                    stat_i[0] += 1
                    src = xg[0:TT, i, :]
                    S.op("act", lambda e, src=src, TT=TT, col=col: e.activation(
                        out=yt[0][0:TT, :], in_=src, func=AF.Square, accum_out=stat[0:TT, 0, col:col + 1]),
                        reads=["xg%d" % i], writes=["yt0", "stat%d" % col])
                    S.op("act", lambda e, TT=TT, col=col: e.activation(
                        out=stat[0:TT, 1, col:col + 1], in_=stat[0:TT, 0, col:col + 1], func=AF.Sqrt, scale=1.0 / D,
                        bias=epst[0:TT, 0:1]), reads=["stat%d" % col, "epst"], writes=["stat%d" % col])
                    S.op("dve", lambda e, TT=TT, col=col: e.reciprocal(out=stat[0:TT, 2, col:col + 1],
                                                                       in_=stat[0:TT, 1, col:col + 1]),
                         reads=["stat%d" % col], writes=["stat%d" % col])
                    yb = yt[0]
                    yr = "yt0"
                    S.op("dve", lambda e, src=src, TT=TT, col=col, yb=yb: e.scalar_tensor_tensor(
                        out=yb[0:TT, :], in0=src, scalar=stat[0:TT, 2, col:col + 1], in1=gfin[0:TT, :], op0=ALU.mult,
                        op1=ALU.mult), reads=["xg%d" % i, "stat%d" % col, "gfin"], writes=[yr])
                    if sample:
                        dma("sp", y_s[:, :], yb[0:TT, :], reads=[yr], key="o" + yr)
                    elif out0 is not None:
                        dma("sp", y_p[out0 + 128 * i:out0 + 128 * i + TT, :], yb[0:TT, :], reads=[yr], key="o" + yr)

            hvt = sbuf(st2, "hvt", [128, 1], F32)
            dma("sp", hvt[:], hv_d, writes=["hvt"], key="hvt")
            group(HALO - 256, [(0, 128), (1, 128)], 256, False, first=True)
            S.op("dve", lambda e: e.tensor_scalar(out=fh[:, :, :], in0=fh[:, :, :], scalar1=hvt[:, 0:1], scalar2=None,
                                                  op0=ALU.mult), reads=["fh", "hvt"], writes=["fh"])
            for gi in range(NG):
                group(HALO + gi * NT, [(i, 128) for i in range(NT // 128)], NT, False, lastg=(gi == NG - 1),
                      out0=gi * NT)
            group(0, [(0, NS)], NS, True)
            S.barrier()
            S.emit()
    return nc


def _perm_in():
    idx = []
    for c in range(8):
        idx += list(range(128 * c, 128 * (c + 1)))
        idx += list(range(1024 + 128 * c, 1024 + 128 * (c + 1)))
    idx += list(range(2048, 4352))
    for c in range(8):
        idx += list(range(4352 + 128 * c, 4352 + 128 * (c + 1)))
        idx += list(range(5376 + 128 * c, 5376 + 128 * (c + 1)))
    return np.array(idx)


def _perm_up():
    idx = []
    for j in range(22):
        idx += list(range(128 * j, 128 * (j + 1)))
        idx += list(range(DFF + 128 * j, DFF + 128 * (j + 1)))
    return np.array(idx)


def _fm(v, nch):
    return np.ascontiguousarray(v.reshape(nch, 128).T)


def make_shared(inp):
    f = np.float32
    pin = _perm_in()
    pup = _perm_up()
    ppv = np.zeros((128, NPP), f)

    def put(name, arr):
        o, w = PP[name]
        ppv[:, o:o + w] = arr.reshape(128, w)

    put("gmix", _fm(inp["g_mix"][0], 8))
    put("bdw", _fm(inp["b_dw"][0], 8))
    put("lng", _fm(inp["ln_g"][0], 8))
    put("lnb", _fm(inp["ln_b"][0], 8))
    put("gffn", _fm(inp["g_ffn"][0], 8))
    wdw = inp["w_dw"][0]
    put("wdw", np.ascontiguousarray(wdw.T.reshape(8, 128, 31).transpose(1, 0, 2)))
    wf = inp["w_fdw"][0][:, pup]
    put("wfdw", np.ascontiguousarray(wf.T.reshape(44, 128, 3).transpose(1, 0, 2)))
    put("bfdw", _fm(inp["b_fdw"][0][pup], 44))
    inv = (np.float32(500000.0) ** (-np.arange(8, dtype=f) / np.float32(8))).astype(f)
    angs = (np.full((NS, 1), 16384.0, f) * inv[None, :]).astype(f)
    css = np.concatenate([np.cos(angs), np.cos(angs)], 1).astype(f)
    sns = np.concatenate([-np.sin(angs), np.sin(angs)], 1).astype(f)
    j = np.arange(128)[:, None]
    i = np.arange(128)[None, :]
    mask2 = np.stack([(j >= i), (j <= i)], 1).astype(f)
    sel = np.zeros((NS, NS, 128), f)
    for s in range(NS):
        sel[s, s, :] = 1.0
    return {
        "w_in": np.ascontiguousarray(inp["w_in"][0][:, pin]),
        "w_co": np.ascontiguousarray(inp["w_conv_out"][0]),
        "w_ao": np.ascontiguousarray(inp["w_attn_out"][0]),
        "w_out": np.ascontiguousarray(inp["w_out"][0]),
        "w_up": np.ascontiguousarray(inp["w_up"][0][:, pup]),
        "w_dn": np.ascontiguousarray(inp["w_down"][0]),
        "pp": ppv,
        "gfin": np.ascontiguousarray(np.broadcast_to(inp["g_final"][None, :], (128, D))),
        "ident": np.eye(128, dtype=f),
        "mask2": mask2, "css": css, "sns": sns, "sel": sel,
    }


def make_core(inp, b, half, MAIN, s0, pup):
    f = np.float32
    LS = HALO + MAIN
    start = half * MAIN
    absp = start - HALO + np.arange(LS)
    valid = absp >= 0
    x = inp["x_prompt"][b]
    xl = np.zeros((LS, D), f)
    xl[valid] = x[absp[valid]]
    inv = (np.float32(500000.0) ** (-np.arange(8, dtype=f) / np.float32(8))).astype(f)
    pos = np.maximum(absp, 0).astype(f)
    ang = (pos[:, None] * inv[None, :]).astype(f)
    cos, sin = np.cos(ang).astype(f), np.sin(ang).astype(f)
    ntile = LS // 128
    csp = np.ascontiguousarray(np.concatenate([cos, cos], 1).reshape(ntile, 128, 16).transpose(1, 0, 2))
    snp = np.ascontiguousarray(np.concatenate([-sin, sin], 1).reshape(ntile, 128, 16).transpose(1, 0, 2))
    m = {
        "xp": xl, "csp": csp, "snp": snp,
        "hv": np.full((128, 1), 1.0 if start > 0 else 0.0, f),
        "xs": np.ascontiguousarray(inp["x_sample"][s0:s0 + NS, 0]),
        "sconv": np.ascontiguousarray(inp["state_conv"][0, s0:s0 + NS]),
        "sffn": np.ascontiguousarray(inp["state_ffn_conv"][0, s0:s0 + NS][:, :, pup]),
    }
    vf = valid.astype(f)
    nsb = LS // 2048
    for g, (W, d) in enumerate(GROUPS):
        m["vld%d" % g] = np.ascontiguousarray(vf.reshape(nsb, 16 // d, 128, d).transpose(2, 0, 3, 1))
    caches = ((inp["cache_k_w128"], inp["cache_v_w128"]), (inp["cache_k_w512"], inp["cache_v_w512"]),
              (inp["cache_k_w2048"], inp["cache_v_w2048"]))
    for g, W in enumerate((128, 512, 2048)):
        m["ck%d" % g] = np.ascontiguousarray(caches[g][0][0, s0:s0 + NS].reshape(NS, W, 256))
        m["cv%d" % g] = np.ascontiguousarray(caches[g][1][0, s0:s0 + NS].reshape(NS, W, 256))
    return m


_NC_CACHE = {}


def run(inp, n_cores=8):
    inp = {k: np.asarray(v) for k, v in inp.items()}
    B, S_FULL, _ = inp["x_prompt"].shape
    nsamp = inp["x_sample"].shape[0]
    MAIN = S_FULL // 2
    assert n_cores == 2 * B
    if MAIN not in _NC_CACHE:
        _NC_CACHE[MAIN] = build(MAIN)
    nc = _NC_CACHE[MAIN]
    shared = make_shared(inp)
    pup = _perm_up()
    in_maps = []
    for c in range(n_cores):
        m = dict(shared)
        m.update(make_core(inp, c // 2, c % 2, MAIN, (NS * c) % nsamp, pup))
        in_maps.append(m)
    res = run_bass_kernel_spmd(nc, in_maps, core_ids=list(range(n_cores))).results
    global LAST_RES
    LAST_RES = res
    ipup = np.argsort(pup)
    f = np.float32
    y_p = np.stack([np.concatenate([res[2 * b]["y_p"], res[2 * b + 1]["y_p"]], 0) for b in range(B)], 0)
    nsc = nsamp // NS
    y_s = np.concatenate([res[c]["y_s"] for c in range(nsc)], 0)[:, None, :]
    hi = [2 * b + 1 for b in range(B)]
    conv_p = np.stack([res[c]["conv_p"] for c in hi], 0)[None]
    conv_s = np.concatenate([res[c]["conv_s"] for c in range(nsc)], 0)[None]
    outs = [y_p.astype(f), y_s.astype(f), conv_p.astype(f), conv_s.astype(f)]
    for g, W in enumerate((128, 512, 2048)):
        outs.append(np.stack([res[c]["k%d_p" % g] for c in hi], 0).reshape(1, B, W, 4, 64).astype(f))
        outs.append(np.stack([res[c]["v%d_p" % g] for c in hi], 0).reshape(1, B, W, 4, 64).astype(f))
        outs.append(np.concatenate([res[c]["k%d_s" % g] for c in range(nsc)], 0).reshape(1, nsamp, W, 4, 64).astype(f))
        outs.append(np.concatenate([res[c]["v%d_s" % g] for c in range(nsc)], 0).reshape(1, nsamp, W, 4, 64).astype(f))
    ffn_p = np.stack([res[c]["ffn_p"] for c in hi], 0)[:, :, ipup][None]
    ffn_s = np.concatenate([res[c]["ffn_s"] for c in range(nsc)], 0)[:, :, ipup][None]
    outs += [ffn_p.astype(f), ffn_s.astype(f)]
    return tuple(outs)


def kernel(**inputs):
    return run(inputs, 8)
```

```python
import contextlib
import types
import numpy as np
import concourse.bass as bass
import concourse.mybir as mybir
from concourse.bass_utils import run_bass_kernel_spmd

F32 = mybir.dt.float32
BF16 = mybir.dt.bfloat16
ALU = mybir.AluOpType
AF = mybir.ActivationFunctionType
AX = mybir.AxisListType

D = 1024
DFF = 2816
NS = 4
NT = 512
GROUPS = ((128, 1), (512, 4), (2048, 16))
EPS = 1e-6
ENGS = ("pe", "act", "dve", "pool", "sp")


class Sched:
    def __init__(self, nc, st, nsem=100):
        self.nc = nc
        self.ops = {e: [] for e in ENGS}
        self.cnt = {}
        self.res_w = {}
        self.res_r = {}
        self.waited = {e: {} for e in ENGS}
        self.pool = [st.enter_context(nc.semaphore("sm%d" % i)) for i in range(nsem)]
        self.sem = {}
        self.alias = {}

    def _sk(self, k):
        if k not in self.cnt:
            self.cnt[k] = 0
            assert len(self.sem) < len(self.pool), "out of semaphores"
            self.sem[k] = self.pool[len(self.sem)]
        return k

    @staticmethod
    def _freeze(fn):
        if fn.__closure__ is None:
            return fn
        cells = []
        for c in fn.__closure__:
            try:
                cells.append(types.CellType(c.cell_contents))
            except ValueError:
                cells.append(c)
        return types.FunctionType(fn.__code__, fn.__globals__, fn.__name__, fn.__defaults__, tuple(cells))

    def op(self, eng, fn, reads=(), writes=(), dma=None, sig=True):
        fn = self._freeze(fn)
        waits = {}
        reads = [x for r in reads for x in [r] + self.alias.get(r, [])]
        writes = [x for r in writes for x in [r] + self.alias.get(r, [])]
        writes = writes + [r for r in reads if r.startswith("ps") or r.startswith("ptb")]

        def need(w):
            if w[1] > waits.get(w[0], 0):
                waits[w[0]] = w[1]

        for r in reads:
            if r in self.res_w:
                need(self.res_w[r])
        for r in writes:
            if r in self.res_w:
                need(self.res_w[r])
            for sk, v in self.res_r.get(r, {}).items():
                need((sk, v))
        if dma is None:
            sk = self._sk(eng)
            inc = 1 if sig else 0
        else:
            sk = self._sk("d:" + str(dma))
            inc = 16
        self.cnt[sk] += inc
        val = self.cnt[sk] if inc else self.cnt[sk] + 1
        wl = []
        for k, v in waits.items():
            if k == "pe" and eng == "pe" and dma is None:
                continue
            if self.waited[eng].get(k, 0) >= v:
                continue
            self.waited[eng][k] = v
            wl.append((k, v))
        for r in writes:
            self.res_w[r] = (sk, val)
            self.res_r[r] = {}
        for r in reads:
            d = self.res_r.setdefault(r, {})
            if d.get(sk, 0) < val:
                d[sk] = val
        self.ops[eng].append((wl, fn, sk, inc))

    def barrier(self, engs=ENGS):
        for e in engs:
            wl = []
            for k, v in self.cnt.items():
                if v > 0 and self.waited[e].get(k, 0) < v:
                    self.waited[e][k] = v
                    wl.append((k, v))
            if wl:
                self.ops[e].append((wl, None, None, 0))

    def emit(self):
        nc = self.nc
        with nc.Block() as block:
            def run(e, eng):
                for wl, fn, sk, inc in self.ops[eng]:
                    for k, v in wl:
                        e.wait_ge(self.sem[k], v)
                    if fn is not None:
                        ins = fn(e)
                        if inc:
                            ins.then_inc(self.sem[sk], inc)

            @block.tensor
            def _(e):
                run(e, "pe")

            @block.scalar
            def _(e):
                run(e, "act")

            @block.vector
            def _(e):
                run(e, "dve")

            @block.gpsimd
            def _(e):
                run(e, "pool")

            @block.sync
            def _(e):
                run(e, "sp")
        self.ops = {e: [] for e in ENGS}


PP = {}
_o = 0
for _n, _w in (("gmix", 8), ("bdw", 8), ("lng", 8), ("lnb", 8), ("gffn", 8), ("wdw", 8 * 31), ("wfdw", 44 * 3),
               ("bfdw", 44)):
    PP[_n] = (_o, _w)
    _o += _w
NPP = _o


HALO = 4096


def build(MAIN):
    S_TOK = HALO + MAIN
    NSB = S_TOK // 2048
    NTILE = S_TOK // 128
    NG = MAIN // NT
    nc = bass.Bass("TRN2", target_bir_lowering=False)

    def din(name, shape, dt=F32):
        return nc.dram_tensor(name, list(shape), dt, kind="ExternalInput").ap()

    def dout(name, shape, dt=F32):
        return nc.dram_tensor(name, list(shape), dt, kind="ExternalOutput").ap()

    def dscr(name, shape, dt):
        return nc.dram_tensor(name, list(shape), dt, kind="Internal").ap()

    xp = din("xp", [S_TOK, D])
    xs = din("xs", [NS, D])
    sconv = din("sconv", [NS, 30, D])
    cks = [din("ck%d" % g, [NS, GROUPS[g][0], 256]) for g in range(3)]
    cvs = [din("cv%d" % g, [NS, GROUPS[g][0], 256]) for g in range(3)]
    sffn = din("sffn", [NS, 2, 2 * DFF])
    w_in = din("w_in", [D, 6400])
    w_co = din("w_co", [D, D])
    w_ao = din("w_ao", [256, D])
    w_out = din("w_out", [D, D])
    w_up = din("w_up", [D, 2 * DFF])
    w_dn = din("w_dn", [DFF, D])
    pp_d = din("pp", [128, NPP])
    gfin_d = din("gfin", [128, D])
    ident_d = din("ident", [128, 128])
    mask_d = din("mask2", [128, 2, 128])
    csp_d = din("csp", [128, NTILE, 16])
    snp_d = din("snp", [128, NTILE, 16])
    css_d = din("css", [NS, 16])
    sns_d = din("sns", [NS, 16])
    sel_d = din("sel", [NS, NS, 128])
    vld_d = [din("vld%d" % g, [128, NSB, GROUPS[g][1], 16 // GROUPS[g][1]]) for g in range(3)]
    hv_d = din("hv", [128, 1])

    wb_in = dscr("wb_in", [D, 6400], BF16)
    wb_co = dscr("wb_co", [D, D], BF16)
    wb_ao = dscr("wb_ao", [256, D], BF16)
    wb_out = dscr("wb_out", [D, D], BF16)
    wb_up = dscr("wb_up", [D, 2 * DFF], BF16)
    wb_dn = dscr("wb_dn", [DFF, D], BF16)
    import os as _os0
    DBG = bool(_os0.environ.get("KDBG"))
    oT_d = (dout if DBG else dscr)("oT_d", [64, 4, S_TOK], BF16)
    if DBG:
        dbg_sT = dout("dbg_sT", [128, 8, NT], BF16)
        dbg_mix = dout("dbg_mix", [128, 8, NT], BF16)
        dbg_xmid = dout("dbg_xmid", [NT, D], F32)
        dbg_acc = dout("dbg_acc", [128, 8, NT], F32)

    y_p = dout("y_p", [MAIN, D])
    y_s = dout("y_s", [NS, D])
    conv_p = dout("conv_p", [30, D])
    conv_s = dout("conv_s", [NS, 30, D])
    kp = [dout("k%d_p" % g, [min(GROUPS[g][0], S_TOK), 256]) for g in range(3)]
    vp = [dout("v%d_p" % g, [min(GROUPS[g][0], S_TOK), 256]) for g in range(3)]
    ks_o = [dout("k%d_s" % g, [NS, GROUPS[g][0], 256]) for g in range(3)]
    vs_o = [dout("v%d_s" % g, [NS, GROUPS[g][0], 256]) for g in range(3)]
    ffn_p = dout("ffn_p", [2, 2 * DFF])
    ffn_s = dout("ffn_s", [NS, 2, 2 * DFF])

    with contextlib.ExitStack() as gst:
        S = Sched(nc, gst)

        def sbuf(st, name, shape, dt):
            return st.enter_context(nc.sbuf_tensor("sb_" + name, list(shape), dt))

        NPS = 6
        PSB = [gst.enter_context(nc.psum_tensor("psb%d" % i, [128, 512], F32)) for i in range(NPS)]
        PTB = gst.enter_context(nc.psum_tensor("ptb", [128, 1024], BF16))
        PTB2 = gst.enter_context(nc.psum_tensor("ptb2", [128, 1024], BF16))
        ps_i = [0]

        def psum():
            i = ps_i[0] % NPS
            ps_i[0] += 1
            return PSB[i], "ps%d" % i

        pp = sbuf(gst, "pp", [128, NPP], F32)
        identf = sbuf(gst, "identf", [128, 128], F32)
        identb = sbuf(gst, "identb", [128, 128], BF16)
        onesb = sbuf(gst, "onesb", [128, 128], BF16)
        onesf = sbuf(gst, "onesf", [128, 128], F32)
        epst = sbuf(gst, "epst", [128, 1], F32)
        stat = sbuf(gst, "stat", [128, 3, 4 * NTILE + 16], F32)
        xsb = [sbuf(gst, "xsb%d" % i, [128, D], BF16) for i in range(2)]
        oTs = sbuf(gst, "oTs", [128, 2, NS], BF16)
        sts = sbuf(gst, "sts", [NS, 2304], F32)
        stat_i = [0]

        ps45 = [0]

        def psum45():
            i = 4 + ps45[0] % 2
            ps45[0] += 1
            return PSB[i], "ps%d" % i

        def P(name, c=None):
            o, w = PP[name]
            if c is None:
                return pp[:, o:o + w]
            return pp[:, o + c:o + c + 1]

        def dma(eng, out, in_, reads=(), writes=(), key=None):
            S.op(eng, lambda e: e.dma_start(out=out, in_=in_), reads=reads, writes=writes, dma=key)

        dma("sp", pp[:], pp_d, writes=["pp"], key="pp")
        dma("sp", identf[:], ident_d, writes=["identf"], key="identf")
        S.op("dve", lambda e: e.tensor_copy(out=identb[:], in_=identf[:]), reads=["identf"], writes=["identb"])
        S.op("dve", lambda e: e.memset(onesb[:], 1.0), writes=["onesb"])
        S.op("dve", lambda e: e.memset(onesf[:], 1.0), writes=["onesf"])
        S.op("dve", lambda e: e.memset(epst[:], EPS), writes=["epst"])
        S.op("dve", lambda e: e.memset(stat[:], 0.0), writes=["stat"])
        def conv_w(dst, src, rows, c0, c1, key, after=()):
            for r0 in range(0, rows, 256):
                r1 = min(rows, r0 + 256)
                dma("pool", dst[r0:r1, c0:c1], src[r0:r1, c0:c1], reads=list(after), writes=[key], key=key)
        for cg in (2, 4, 0, 1, 3):
            conv_w(wb_in, w_in, D, 2048 + 512 * cg, 2048 + min(512 * (cg + 1), 2304), "wb_qkv%d" % cg)
        QKV_ALL = ["wb_qkv%d" % cg for cg in range(5)]
        LATE_CASTS = [
            [],
            [lambda: conv_w(wb_in, w_in, D, 0, 2048, "wb_in1"), lambda: conv_w(wb_in, w_in, D, 4352, 6400, "wb_in3")],
            [lambda: conv_w(wb_co, w_co, D, 0, D, "wb_co"), lambda: conv_w(wb_ao, w_ao, 256, 0, D, "wb_ao"),
             lambda: conv_w(wb_out, w_out, D, 0, D, "wb_out"), lambda: conv_w(wb_up, w_up, D, 0, 2 * DFF, "wb_up")],
            [lambda: conv_w(wb_dn, w_dn, DFF, 0, D, "wb_dn")],
        ]
        def flat16(ap):
            return ap.rearrange("w c -> (w c)").rearrange("(a b) -> a b", a=16)
        SHIFTS = []
        for g in range(3):
            W = GROUPS[g][0]
            for (src, dst) in ((cks[g], ks_o[g]), (cvs[g], vs_o[g])):
                for s in range(NS):
                    SHIFTS.append((flat16(dst[s, 0:W - 1, :]), flat16(src[s, 1:W, :])))
        for s in range(NS):
            SHIFTS.append((flat16(conv_s[s, 0:29, :]), flat16(sconv[s, 1:30, :])))
            SHIFTS.append((flat16(ffn_s[s, 0:1, :]), flat16(sffn[s, 1:2, :])))

        import os as _os
        KSTOP = int(_os.environ.get("KSTOP", "9"))
        if KSTOP == 0:
            S.barrier()
            S.emit()
            return nc
        def norm_T(src_ap, rd, TT, dst3, dst_res, gain_name, xi):
            col = stat_i[0]
            stat_i[0] += 1
            xb = xsb[xi % 2]
            xr = "xsb%d" % (xi % 2)
            S.op("act", lambda e: e.activation(out=xb[0:TT, :], in_=src_ap, func=AF.Square,
                                               accum_out=stat[0:TT, 0, col:col + 1]),
                 reads=[rd], writes=[xr, "stat%d" % col])
            S.op("act", lambda e: e.activation(out=stat[0:TT, 1, col:col + 1], in_=stat[0:TT, 0, col:col + 1],
                                               func=AF.Sqrt, scale=1.0 / D, bias=epst[0:TT, 0:1]),
                 reads=["stat%d" % col, "epst"], writes=["stat%d" % col])
            S.op("dve", lambda e: e.reciprocal(out=stat[0:TT, 2, col:col + 1], in_=stat[0:TT, 1, col:col + 1]),
                 reads=["stat%d" % col], writes=["stat%d" % col])
            S.op("act", lambda e: e.activation(out=xb[0:TT, :], in_=src_ap, func=AF.Copy,
                                               scale=stat[0:TT, 2, col:col + 1]),
                 reads=[rd, "stat%d" % col], writes=[xr])
            for c in range(8):
                S.op("pe", lambda e, c=c: e.transpose(PTB[:, c * 128:c * 128 + TT], xb[0:TT, c * 128:(c + 1) * 128],
                                                      identb[0:TT, 0:TT]),
                     reads=[xr, "identb"], writes=["ptb"], sig=(c == 7))
            o, w = PP[gain_name]
            S.op("dve", lambda e: e.tensor_tensor(
                out=dst3, in0=PTB[:, :].rearrange("p (c t) -> p c t", c=8)[:, :, 0:TT],
                in1=pp[:, o:o + 8].unsqueeze(2).to_broadcast([128, 8, TT]), op=ALU.mult),
                reads=["ptb", "pp"], writes=[dst_res])
            return col

        with contextlib.ExitStack() as st1:
            hT1 = sbuf(st1, "hT1", [128, 8, 2048], BF16)
            hTs = sbuf(st1, "hTs", [128, 8, NS], BF16)
            QT = [sbuf(st1, "QT%d" % g, [128, 2, GROUPS[g][1], 2048 // GROUPS[g][1]], BF16) for g in range(3)]
            KT = [sbuf(st1, "KT%d" % g, [128, 2, GROUPS[g][1], 128 + 2048 // GROUPS[g][1]], BF16) for g in range(3)]
            NB = [1 + 16 // GROUPS[g][1] for g in range(3)]
            VV = [sbuf(st1, "VV%d" % g, [128, GROUPS[g][1], NB[g], 260], BF16) for g in range(3)]
            Wg = [sbuf(st1, "Wg%d" % i, [128, 8, 512], BF16) for i in range(1)]
            xt = [sbuf(st1, "xt%d" % i, [128, D], F32) for i in range(2)]
            stg = [sbuf(st1, "stg%d" % i, [128, 512], F32) for i in range(2)]
            qkb = [sbuf(st1, "qkb%d" % i, [128, 512], BF16) for i in range(2)]
            rtmp = [sbuf(st1, "rtmp%d" % i, [128, 24, 16], F32) for i in range(2)]
            vst = [sbuf(st1, "vst%d" % i, [128, 256], F32) for i in range(2)]
            PT2 = sbuf(st1, "PT2", [128, 16, 2, 128], BF16)
            PTs = [sbuf(st1, "PTs%d" % i, [128, 2, 2, 128], BF16) for i in range(2)]
            csp = sbuf(st1, "csp", [128, 16, 16], F32)
            snp = sbuf(st1, "snp", [128, 16, 16], F32)
            css = sbuf(st1, "css", [NS, 16], F32)
            sns = sbuf(st1, "sns", [NS, 16], F32)
            maskf = sbuf(st1, "maskf", [128, 2, 128], F32)
            vld = [sbuf(st1, "vld%d" % g, [128, GROUPS[g][1], 16 // GROUPS[g][1]], F32) for g in range(3)]
            maskb = sbuf(st1, "maskb", [128, 2, 128], BF16)
            rec = [sbuf(st1, "rec%d" % i, [64, 512], F32) for i in range(1)]
            oTsb = sbuf(st1, "oTsb", [64, 2048], BF16)
            selb = sbuf(st1, "selb", [NS, NS, 128], F32)
            Kc = [sbuf(st1, "Kc%d" % i, [128, 256], F32) for i in range(3)]
            Vc = [sbuf(st1, "Vc%d" % i, [128, 256], F32) for i in range(3)]
            prod = sbuf(st1, "prod", [128, 256], F32)
            sT4 = sbuf(st1, "sT4", [128, 24], F32)
            sm = sbuf(st1, "sm", [NS, 768], F32)
            pcur = sbuf(st1, "pcur", [NS, 12], F32)
            pcm = sbuf(st1, "pcm", [NS, NS, 12], F32)
            id4 = sbuf(st1, "id4", [NS, NS], F32)
            oTf = sbuf(st1, "oTf", [128, 2, NS], F32)

            for g in range(3):
                S.op("pool", lambda e, g=g: e.memset(VV[g][:, :, :, :], 1.0), writes=["VV%d" % g])
            dma("sp", css[:], css_d, writes=["css"], key="css")
            dma("sp", sns[:], sns_d, writes=["sns"], key="sns")
            dma("sp", maskf[:], mask_d, writes=["maskf"], key="maskf")
            dma("sp", selb[:], sel_d, writes=["selb"], key="selb")
            S.op("dve", lambda e: e.tensor_copy(out=maskb[:], in_=maskf[:]), reads=["maskf"], writes=["maskb"])
            S.op("dve", lambda e: e.tensor_copy(out=id4[:], in_=identf[0:NS, 0:NS]), reads=["identf"], writes=["id4"])

            def rope(stv, rd, TT, nh, cs_ap, sn_ap, tab_res, ri):
                rt = rtmp[ri % 2]
                rr = "rtmp%d" % (ri % 2)
                t1 = rt[0:TT, 0:nh, :]
                csb = cs_ap.unsqueeze(1).to_broadcast([TT, nh, 16])
                S.op("dve", lambda e: e.tensor_tensor(out=t1, in0=stv[:, :, 0:16], in1=csb, op=ALU.mult),
                     reads=[rd] + tab_res, writes=[rr])
                S.op("dve", lambda e: e.tensor_tensor(out=stv[:, :, 0:8], in0=stv[:, :, 0:8],
                                                      in1=sn_ap[:, 8:16].unsqueeze(1).to_broadcast([TT, nh, 8]),
                                                      op=ALU.mult), reads=[rd] + tab_res, writes=[rd])
                S.op("dve", lambda e: e.tensor_tensor(out=stv[:, :, 8:16], in0=stv[:, :, 8:16],
                                                      in1=sn_ap[:, 0:8].unsqueeze(1).to_broadcast([TT, nh, 8]),
                                                      op=ALU.mult), reads=[rd] + tab_res, writes=[rd])
                S.op("dve", lambda e: e.tensor_tensor(out=t1[:, :, 0:8], in0=t1[:, :, 0:8], in1=stv[:, :, 8:16],
                                                      op=ALU.add), reads=[rd, rr], writes=[rr])
                S.op("dve", lambda e: e.tensor_tensor(out=t1[:, :, 8:16], in0=t1[:, :, 8:16], in1=stv[:, :, 0:8],
                                                      op=ALU.add), reads=[rd, rr], writes=[rr])
                S.op("dve", lambda e: e.tensor_copy(out=stv[:, :, 0:16], in_=t1), reads=[rr], writes=[rd])

            xi = 0
            ri = 0
            ev = 0
            bset = [0]
            PTBf = PTB[:, :].bitcast(F32)
            PTB2f = PTB2[:, :].bitcast(F32)
            for _k in range(int(_os.environ.get("KDUMMY", "0"))):
                if _os.environ.get("KDUMMYT") == "memset":
                    S.op("dve", lambda e: e.memset(prod[:, :], 0.0), writes=["prod"])
                else:
                    S.op("dve", lambda e: e.tensor_tensor(out=prod[:, :], in0=prod[:, :], in1=prod[:, :], op=ALU.mult),
                         writes=["prod"])
            for sb in range(NSB):
                T0 = sb * 2048
                last = (sb == NSB - 1)
                if sb < len(LATE_CASTS):
                    for f_ in LATE_CASTS[sb]:
                        f_()
                dma("sp", csp[:], csp_d[:, 16 * sb:16 * (sb + 1), :], writes=["csp"], key="csp")
                for g in range(3):
                    dma("sp", vld[g][:, :, :], vld_d[g][:, sb, :, :], writes=["vld%d" % g], key="vld%d" % g)
                dma("sp", snp[:], snp_d[:, 16 * sb:16 * (sb + 1), :], writes=["snp"], key="snp")
                for i in range(16):
                    b = xi % 2
                    dma("sp", xt[b][:], xp[T0 + 128 * i:T0 + 128 * (i + 1), :], writes=["xt%d" % b], key="xt%d" % b)
                    norm_T(xt[b][:], "xt%d" % b, 128, hT1[:, :, 128 * i:128 * (i + 1)], "hT1_%d" % i, "gmix", xi)
                    xi += 1
                if sb == 1:
                    b = xi % 2
                    dma("sp", xt[b][0:NS, :], xs, writes=["xt%d" % b], key="xt%d" % b)
                    norm_T(xt[b][0:NS, :], "xt%d" % b, NS, hTs[:, :, 0:NS], "hTs", "gmix", xi)
                    xi += 1
                if KSTOP == 10:
                    S.barrier()
                    S.emit()
                    return nc
                hT_all = ["hT1_%d" % i for i in range(16)]
                if sb > 0:
                    for g in range(3):
                        M = 2048 // GROUPS[g][1]
                        S.op("pool", lambda e, g=g, M=M: e.tensor_copy(out=KT[g][:, :, :, 0:128],
                                                                        in_=KT[g][:, :, :, M:M + 128]),
                             reads=["KT%d" % g], writes=["KT%d" % g])
                        S.op("pool", lambda e, g=g: e.tensor_copy(out=VV[g][:, :, 0, :], in_=VV[g][:, :, NB[g] - 1, :]),
                             reads=["VV%d" % g], writes=["VV%d" % g])
                for cg in range(5):
                    if sb == 0 and cg in (0, 1, 3):
                        continue
                    wcols = 512 if cg < 4 else 256
                    wb = Wg[0]
                    wr = "Wg0"
                    dma("sp", wb[:, :, 0:wcols],
                        wb_in[:, 2048 + 512 * cg:2048 + 512 * cg + wcols].rearrange("(kc p) n -> p kc n", p=128),
                        reads=["wb_qkv%d" % cg], writes=[wr], key=wr)
                    if sb == 1:
                        pst, pr = psum()
                        for h0 in range(0, wcols, 256):
                            for kc in range(8):
                                S.op("pe", lambda e, kc=kc, pst=pst, h0=h0: e.matmul(
                                    pst[0:NS, h0:h0 + 256], lhsT=hTs[:, kc, 0:NS], rhs=wb[:, kc, h0:h0 + 256],
                                    start=(kc == 0), stop=(kc == 7)), reads=["hTs", wr], writes=[pr], sig=(kc == 7))
                        for (c0, c1) in ((0, 256), (256, 512)):
                            if c0 >= wcols:
                                continue
                            gcol = 512 * cg + c0
                            sc = 0.125 if gcol < 768 else 1.0
                            S.op("act", lambda e, c0=c0, c1=c1, pst=pst, sc=sc, gcol=gcol: e.activation(
                                out=sts[0:NS, gcol:gcol + 256], in_=pst[0:NS, c0:c1], func=AF.Copy, scale=sc),
                                reads=[pr], writes=["sts"])
                    if KSTOP == 20:
                        S.barrier()
                        S.emit()
                        return nc
                    if cg < 3:
                        for i in range(16):
                            pst, pr = psum()
                            for kc in range(8):
                                S.op("pe", lambda e, kc=kc, pst=pst, i=i: e.matmul(
                                    pst[:, 0:512], lhsT=hT1[:, kc, 128 * i:128 * (i + 1)], rhs=wb[:, kc, 0:512],
                                    start=(kc == 0), stop=(kc == 7)), reads=["hT1_%d" % i, wr], writes=[pr], sig=(kc == 7))
                            sg = stg[ev % 2]
                            sr = "stg%d" % (ev % 2)
                            qb_ = qkb[ev % 2]
                            qr = "qkb%d" % (ev % 2)
                            ev += 1
                            for (c0, c1) in ((0, 256), (256, 512)):
                                sc = 0.125 if (512 * cg + c0) < 768 else 1.0
                                S.op("act", lambda e, c0=c0, c1=c1, pst=pst, sc=sc, sg=sg: e.activation(
                                    out=sg[:, c0:c1], in_=pst[:, c0:c1], func=AF.Copy, scale=sc),
                                    reads=[pr], writes=[sr])
                            ti = sb * 16 + i
                            if not _os.environ.get("KNOROPE"):
                              rope(sg[:, :].rearrange("p (h d) -> p h d", d=64), sr, 128, 8, csp[:, i, :], snp[:, i, :],
                                 ["csp", "snp"], ri)
                            ri += 1
                            S.op("pool", lambda e, sg=sg, qb_=qb_: e.tensor_copy(out=qb_[:, :], in_=sg[:, :]),
                                 reads=[sr], writes=[qr])
                            if last and KSTOP != 24:
                                for g in range(3):
                                    W = min(GROUPS[g][0], S_TOK)
                                    kcol = 768 + 256 * g
                                    if not (512 * cg <= kcol < 512 * cg + 512):
                                        continue
                                    tpos = 2048 - 128 * (16 - i)
                                    if S_TOK - (T0 + 128 * i) > W:
                                        continue
                                    row0 = W - (S_TOK - (T0 + 128 * i))
                                    lc = kcol - 512 * cg
                                    dma("sp", kp[g][row0:row0 + 128, :], sg[:, lc:lc + 256], reads=[sr], key="o" + sr)
                            for j in range(4):
                                S.op("pe", lambda e, j=j, qb_=qb_: e.transpose((PTB if j < 2 else PTB2)[:, j * 128:(j + 1) * 128],
                                                                                qb_[:, j * 128:(j + 1) * 128], identb[:, :]),
                                     reads=[qr, "identb"], writes=["ptb" if j < 2 else "ptb2"], sig=(j % 2 == 1))
                            for j in range(4):
                                if _os.environ.get("KNOEVAC"):
                                    continue
                                cc = cg * 4 + j
                                isq = cc < 6
                                g = (cc % 6) // 2
                                half = cc % 2
                                d = GROUPS[g][1]
                                mlo = 128 * i // d
                                cnt = 128 // d
                                if isq:
                                    dst = QT[g][:, half, :, mlo:mlo + cnt]
                                    dres = "QT%d" % g
                                else:
                                    dst = KT[g][:, half, :, 128 + mlo:128 + mlo + cnt]
                                    dres = "KT%d" % g
                                src = (PTB if j < 2 else PTB2)[:, j * 128:(j + 1) * 128].rearrange("p (m r) -> p r m", r=d)
                                if j < 2:
                                    S.op("act", lambda e, dst=dst, src=src: e.activation(out=dst, in_=src, func=AF.Copy),
                                         reads=["ptb"], writes=[dres])
                                else:
                                    S.op("dve", lambda e, dst=dst, src=src: e.tensor_copy(out=dst, in_=src),
                                         reads=["ptb2"], writes=[dres])
                            if KSTOP == 21:
                                S.barrier()
                                S.emit()
                                return nc
                            if (KSTOP in (22, 24) and cg == 2 and i == 15) or (KSTOP == 25 and i == 1) or (KSTOP == 33 and cg == 1 and i == 14) or (KSTOP == 31 and cg == 1 and i == 1) or (KSTOP == 32 and cg == 1 and i == 7) or (KSTOP == 29 and cg == 1 and i == 15) or (KSTOP == 30 and cg == 2 and i == 0) or (KSTOP == 28 and cg == 1 and i == 0) or (KSTOP == 26 and i == 7) or (KSTOP == 27 and cg == 0 and i == 15):
                                S.barrier()
                                S.emit()
                                return nc
                    else:
                        jobs = []
                        if cg == 3:
                            for blk in range(16):
                                jobs.append((0, 0, blk, 0, hT1[:, :, 128 * blk:128 * (blk + 1)], ["hT1_%d" % blk]))
                            for r in range(4):
                                for blk in range(4):
                                    jobs.append((1, r, blk, 256, hT1[:, :, 512 * blk + r:512 * (blk + 1):4],
                                                 ["hT1_%d" % t for t in range(4 * blk, 4 * blk + 4)]))
                        else:
                            for r in range(16):
                                jobs.append((2, r, 0, 0, hT1[:, :, r:2048:16], hT_all))
                        for (g, r, blk, c0, lh, lres) in jobs:
                            pst, pr = psum()
                            for kc in range(8):
                                S.op("pe", lambda e, kc=kc, pst=pst, lh=lh, c0=c0: e.matmul(
                                    pst[:, 0:256], lhsT=lh[:, kc, :], rhs=wb[:, kc, c0:c0 + 256],
                                    start=(kc == 0), stop=(kc == 7)), reads=lres + [wr], writes=[pr], sig=(kc == 7))
                            S.op("act", lambda e, pst=pst, g=g, r=r, blk=blk: e.activation(
                                out=VV[g][:, r, 1 + blk, :].rearrange("p (s e) -> p s e", e=65)[:, :, 0:64],
                                in_=pst[:, 0:256].rearrange("p (s e) -> p s e", e=64), func=AF.Copy,
                                scale=vld[g][:, r, blk:blk + 1]),
                                reads=[pr, "vld%d" % g], writes=["VV%d" % g])
                            S.op("pool", lambda e, g=g, r=r, blk=blk: e.tensor_copy(
                                out=VV[g][:, r, 1 + blk, :].rearrange("p (s e) -> p s e", e=65)[:, :, 64:65],
                                in_=vld[g][:, r, blk:blk + 1].unsqueeze(1).to_broadcast([128, 4, 1])),
                                reads=["vld%d" % g], writes=["VV%d" % g])
                            if last:
                                W = min(GROUPS[g][0], S_TOK)
                                d = GROUPS[g][1]
                                nblk = 16 // d
                                first_tok = T0 + r + d * 128 * blk
                                if S_TOK - (T0 + d * 128 * blk) <= W:
                                    row0 = W - (S_TOK - first_tok)
                                    vb = vst[ev % 2]
                                    vr = "vst%d" % (ev % 2)
                                    ev += 1
                                    S.op("dve", lambda e, pst=pst, vb=vb: e.tensor_copy(out=vb[:, :], in_=pst[:, 0:256]),
                                         reads=[pr], writes=[vr])
                                    dma("sp", vp[g][row0:W:d, :], vb[:, :], reads=[vr], key="o" + vr)
                if KSTOP == 23 and False:
                    S.barrier()
                    S.emit()
                    return nc
                if KSTOP == 11:
                    S.barrier()
                    S.emit()
                    return nc
                if sb == 1:
                    rope(sts[0:NS, 0:1536].rearrange("p (h d) -> p h d", d=64), "sts", NS, 24, css[:, :], sns[:, :],
                         ["css", "sns"], ri)
                    ri += 1
                    for g in range(3):
                        W = GROUPS[g][0]
                        dma("sp", ks_o[g][:, W - 1, :], sts[0:NS, 768 + 256 * g:768 + 256 * (g + 1)], reads=["sts"],
                            key="osts")
                        dma("sp", vs_o[g][:, W - 1, :], sts[0:NS, 1536 + 256 * g:1536 + 256 * (g + 1)], reads=["sts"],
                            key="osts")
                    S.op("dve", lambda e: e.tensor_tensor(out=sm[0:NS, 0:768], in0=sts[0:NS, 0:768],
                                                          in1=sts[0:NS, 768:1536], op=ALU.mult),
                         reads=["sts"], writes=["sm"])
                    S.op("dve", lambda e: e.tensor_reduce(out=pcur[:, :],
                                                          in_=sm[0:NS, 0:768].rearrange("p (h d) -> p h d", d=64),
                                                          axis=AX.X, op=ALU.add), reads=["sm"], writes=["pcur"])
                    S.op("act", lambda e: e.activation(out=pcur[:, :], in_=pcur[:, :], func=AF.Exp),
                         reads=["pcur"], writes=["pcur"])
                    S.op("dve", lambda e: e.tensor_tensor(out=pcm[:, :, :],
                                                          in0=pcur[:, :].unsqueeze(1).to_broadcast([NS, NS, 12]),
                                                          in1=id4[:, :].unsqueeze(2).to_broadcast([NS, NS, 12]),
                                                          op=ALU.mult), reads=["pcur", "id4"], writes=["pcm"])
                    for s in range(NS):
                        pnum, pnr = psum()
                        for g in range(3):
                            W, d = GROUPS[g]
                            kb_, vb_ = Kc[g], Vc[g]
                            kr, vr = "Kc%d" % g, "Vc%d" % g
                            dma("sp", kb_[:, :], cks[g][s, 0:W:d, :], writes=[kr], key=kr)
                            dma("sp", vb_[:, :], cvs[g][s, 0:W:d, :], writes=[vr], key=vr)
                            pq, pqr = psum()
                            S.op("pe", lambda e, pq=pq, s=s, g=g: e.matmul(pq[:, 0:256], lhsT=selb[0:NS, s, :],
                                                                           rhs=sts[0:NS, 256 * g:256 * (g + 1)],
                                                                           start=True, stop=True),
                                 reads=["selb", "sts"], writes=[pqr])
                            S.op("dve", lambda e, pq=pq, kb_=kb_: e.tensor_tensor(out=prod[:, :], in0=kb_[:, :],
                                                                                    in1=pq[:, 0:256], op=ALU.mult),
                                 reads=[kr, pqr], writes=["prod"])
                            S.op("dve", lambda e, g=g: e.tensor_reduce(out=sT4[:, 4 * g:4 * g + 4],
                                                                  in_=prod[:, :].rearrange("p (h d) -> p h d", d=64),
                                                                  axis=AX.X, op=ALU.add), reads=["prod"], writes=["sT4"])
                        S.op("act", lambda e: e.activation(out=sT4[:, 12:24], in_=sT4[:, 0:12], func=AF.Exp),
                             reads=["sT4"], writes=["sT4"])
                        for c in range(3):
                            for g in range(3):
                                pT_ = sT4[:, 12 + 4 * g:16 + 4 * g]
                                pc_ = pcm[0:NS, s, 4 * g:4 * g + 4]
                                if c < 2:
                                    l1 = Vc[g][:, 128 * c:128 * (c + 1)]
                                    l2 = sts[0:NS, 1536 + 256 * g + 128 * c:1536 + 256 * g + 128 * (c + 1)]
                                    r1 = ["Vc%d" % g, "sT4"]
                                else:
                                    l1 = onesf[:, :]
                                    l2 = onesf[0:NS, :]
                                    r1 = ["onesf", "sT4"]
                                S.op("pe", lambda e, c=c, g=g, pnum=pnum, l1=l1, pT_=pT_: e.matmul(
                                    pnum[:, 4 * c:4 * c + 4], lhsT=l1, rhs=pT_, start=(g == 0), stop=False),
                                    reads=r1, writes=[pnr])
                                S.op("pe", lambda e, c=c, g=g, pnum=pnum, l2=l2, pc_=pc_: e.matmul(
                                    pnum[:, 4 * c:4 * c + 4], lhsT=l2, rhs=pc_, start=False, stop=(g == 2)),
                                    reads=["sts", "pcm", "onesf"], writes=[pnr], sig=(g == 2))
                        S.op("dve", lambda e, pnum=pnum: e.reciprocal(out=sT4[:, 0:4], in_=pnum[:, 8:12]),
                             reads=[pnr], writes=["sT4"])
                        for c in range(2):
                            for hf in range(2):
                                slot = 2 * c + hf
                                S.op("dve", lambda e, c=c, hf=hf, slot=slot, pnum=pnum, s=s: e.tensor_tensor(
                                    out=oTf[64 * hf:64 * (hf + 1), c, s:s + 1],
                                    in0=pnum[64 * hf:64 * (hf + 1), 4 * c + slot:4 * c + slot + 1],
                                    in1=sT4[64 * hf:64 * (hf + 1), slot:slot + 1], op=ALU.mult),
                                    reads=[pnr, "sT4"], writes=["oTf"])
                    S.op("dve", lambda e: e.tensor_copy(out=oTs[:, :, :], in_=oTf[:, :, :]), reads=["oTf"], writes=["oTs"])
                    if DBG:
                        d1 = dout("dbg_oTf", [128, 2, NS], F32)
                        dma("sp", d1, oTf[:, :, :], reads=["oTf"], key="dbg")
                        d2 = dout("dbg_sts", [NS, 2304], F32)
                        dma("sp", d2, sts[:, :], reads=["sts"], key="dbg")
                        d3 = dout("dbg_pcur", [NS, 12], F32)
                        dma("sp", d3, pcur[:, :], reads=["pcur"], key="dbg")

                if KSTOP == 12:
                    S.barrier()
                    S.emit()
                    return nc
                if DBG and sb == 0:
                    for g in range(3):
                        dq = dout("dbg_QT%d" % g, [128, 2, GROUPS[g][1], 2048 // GROUPS[g][1]], BF16)
                        dk = dout("dbg_KT%d" % g, [128, 2, GROUPS[g][1], 128 + 2048 // GROUPS[g][1]], BF16)
                        dv = dout("dbg_VV%d" % g, [128, GROUPS[g][1], NB[g], 260], BF16)
                        dma("sp", dq, QT[g][:, :, :, :], reads=["QT%d" % g], key="dbg")
                        dma("sp", dk, KT[g][:, :, :, :], reads=["KT%d" % g], key="dbg")
                        dma("sp", dv, VV[g][:, :, :, :], reads=["VV%d" % g], key="dbg")
                pti = 0
                if sb == 0:
                    continue
                ch_list = (3,) if sb == 1 else (0, 1, 2, 3)
                for slot in range(4):
                    c2 = slot // 2
                    pb = 64 * (slot % 2)
                    for rp in range(8):
                        pss, psr = psum45()
                        for q in range(2):
                            r = 2 * rp + q
                            if sb > 0:
                                S.op("pe", lambda e, pss=pss, q=q, r=r: e.matmul(
                                    pss[:, 256 * q:256 * q + 128], lhsT=KT[2][pb:pb + 64, c2, r, 0:128],
                                    rhs=QT[2][pb:pb + 64, c2, r, 0:128], start=True, stop=True),
                                    reads=["KT2", "QT2"], writes=[psr])
                            S.op("pe", lambda e, pss=pss, q=q, r=r: e.matmul(
                                pss[:, 256 * q + 128:256 * q + 256], lhsT=KT[2][pb:pb + 64, c2, r, 128:256],
                                rhs=QT[2][pb:pb + 64, c2, r, 0:128], start=True, stop=True),
                                reads=["KT2", "QT2"], writes=[psr])
                        k0 = 0 if sb > 0 else 1
                        S.op("act", lambda e, pss=pss, rp=rp, k0=k0: e.activation(
                            out=PT2[:, 2 * rp:2 * rp + 2, k0:2, :],
                            in_=pss[:, :].rearrange("p (q k t) -> p q k t", q=2, k=2)[:, :, k0:2, :], func=AF.Exp),
                            reads=[psr], writes=["PT2"])
                        S.op("dve", lambda e, rp=rp, k0=k0: e.tensor_tensor(
                            out=PT2[:, 2 * rp:2 * rp + 2, k0:2, :], in0=PT2[:, 2 * rp:2 * rp + 2, k0:2, :],
                            in1=maskb[:, k0:2, :].unsqueeze(1).to_broadcast([128, 2, 2 - k0, 128]), op=ALU.mult),
                            reads=["PT2", "maskb"], writes=["PT2"])
                    if DBG and sb == 0 and slot == 0:
                        dp2 = dout("dbg_PT2", [128, 16, 2, 128], BF16)
                        dma("sp", dp2, PT2[:, :, :, :], reads=["PT2"], key="dbg")
                    for ch in ch_list:
                        bset[0] += 1
                        if bset[0] % 2:
                            Bk = [(PSB[i_], "ps%d" % i_) for i_ in range(3)]
                        else:
                            Bk = [(PSB[3], "ps3"), (PTBf, "ptb"), (PTB2f, "ptb2")]
                        for g in range(2):
                            for pair in range(2):
                                pss, psr = psum45()
                                ptb_ = PTs[pti % 2]
                                ptr = "PTs%d" % (pti % 2)
                                pti += 1
                                info = []
                                for q in range(2):
                                    u = 2 * pair + q
                                    if g == 0:
                                        qb_i = 4 * ch + u
                                        hasp = (sb > 0) or (qb_i > 0)
                                        kprev = KT[0][pb:pb + 64, c2, 0, 128 * qb_i:128 * qb_i + 128]
                                        kcur = KT[0][pb:pb + 64, c2, 0, 128 + 128 * qb_i:256 + 128 * qb_i]
                                        qq = QT[0][pb:pb + 64, c2, 0, 128 * qb_i:128 * qb_i + 128]
                                        vprev = VV[0][:, 0, qb_i, 65 * slot:65 * (slot + 1)]
                                        vcur = VV[0][:, 0, qb_i + 1, 65 * slot:65 * (slot + 1)]
                                        ocols = slice(128 * u, 128 * (u + 1))
                                    else:
                                        hasp = (sb > 0) or (ch > 0)
                                        kprev = KT[1][pb:pb + 64, c2, u, 128 * ch:128 * ch + 128]
                                        kcur = KT[1][pb:pb + 64, c2, u, 128 + 128 * ch:256 + 128 * ch]
                                        qq = QT[1][pb:pb + 64, c2, u, 128 * ch:128 * ch + 128]
                                        vprev = VV[1][:, u, ch, 65 * slot:65 * (slot + 1)]
                                        vcur = VV[1][:, u, ch + 1, 65 * slot:65 * (slot + 1)]
                                        ocols = slice(128 * u, 128 * (u + 1))
                                    if hasp:
                                        S.op("pe", lambda e, pss=pss, q=q, kprev=kprev, qq=qq: e.matmul(
                                            pss[:, 256 * q:256 * q + 128], lhsT=kprev, rhs=qq, start=True, stop=True),
                                            reads=["KT%d" % g, "QT%d" % g], writes=[psr])
                                    S.op("pe", lambda e, pss=pss, q=q, kcur=kcur, qq=qq: e.matmul(
                                        pss[:, 256 * q + 128:256 * q + 256], lhsT=kcur, rhs=qq, start=True, stop=True),
                                        reads=["KT%d" % g, "QT%d" % g], writes=[psr])
                                    info.append((hasp, vprev, vcur, ocols))
                                allp = info[0][0] and info[1][0]
                                k0 = 0 if allp else 1
                                if (not allp) and (info[0][0] or info[1][0]):
                                    S.op("act", lambda e, pss=pss, ptb_=ptb_: e.activation(
                                        out=ptb_[:, 1, 0, :], in_=pss[:, 256:384], func=AF.Exp), reads=[psr], writes=[ptr])
                                    S.op("dve", lambda e, ptb_=ptb_: e.tensor_tensor(
                                        out=ptb_[:, 1, 0, :], in0=ptb_[:, 1, 0, :], in1=maskb[:, 0, :], op=ALU.mult),
                                        reads=[ptr, "maskb"], writes=[ptr])
                                S.op("act", lambda e, pss=pss, ptb_=ptb_, k0=k0: e.activation(
                                    out=ptb_[:, :, k0:2, :],
                                    in_=pss[:, :].rearrange("p (q k t) -> p q k t", q=2, k=2)[:, :, k0:2, :], func=AF.Exp),
                                    reads=[psr], writes=[ptr])
                                S.op("dve", lambda e, ptb_=ptb_, k0=k0: e.tensor_tensor(
                                    out=ptb_[:, :, k0:2, :], in0=ptb_[:, :, k0:2, :],
                                    in1=maskb[:, k0:2, :].unsqueeze(1).to_broadcast([128, 2, 2 - k0, 128]), op=ALU.mult),
                                    reads=[ptr, "maskb"], writes=[ptr])
                                if DBG and sb == 0 and slot == 0 and ch == 0:
                                    dpt = dout("dbg_PT%d_%d" % (g, pair), [128, 2, 2, 128], BF16)
                                    dma("sp", dpt, ptb_[:, :, :, :], reads=[ptr], key="dbg")
                                for q in range(2):
                                    hasp, vprev, vcur, ocols = info[q]
                                    pO, pOr = Bk[g]
                                    kbl = (0, 1) if hasp else (1,)
                                    for kb in kbl:
                                        vv = vprev if kb == 0 else vcur
                                        S.op("pe", lambda e, pO=pO, vv=vv, ptb_=ptb_, q=q, kb=kb, ocols=ocols, kbl=kbl: e.matmul(
                                            pO[0:65, ocols], lhsT=vv, rhs=ptb_[:, q, kb, :], start=(kb == kbl[0]),
                                            stop={"0": False, "1": True}.get(_os.environ.get("KSTOPF", ""), kb == 1)),
                                            reads=["VV%d" % g, ptr], writes=[pOr])
                        pO, pOr = Bk[2]
                        for r in range(16):
                            kbs = (0, 1) if sb > 0 else (1,)
                            for kb in kbs:
                                S.op("pe", lambda e, pO=pO, r=r, kb=kb, kbs=kbs: e.matmul(
                                    pO[0:65, 32 * r:32 * (r + 1)], lhsT=VV[2][:, r, kb, 65 * slot:65 * (slot + 1)],
                                    rhs=PT2[:, r, kb, 32 * ch:32 * (ch + 1)], start=(kb == kbs[0]), stop=(kb == 1)),
                                    reads=["VV2", "PT2"], writes=[pOr], sig=(kb == 1))
                        if DBG and sb == 0 and slot == 0 and ch == 0:
                            for gq in range(3):
                                db = dout("dbg_B%d" % gq, [65, 512], F32)
                                S.op("dve", lambda e, gq=gq: e.tensor_copy(out=stg[1][0:65, :], in_=Bk[gq][0][0:65, :]),
                                     reads=[Bk[gq][1]], writes=["stg1"])
                                dma("sp", db, stg[1][0:65, :], reads=["stg1"], key="dbg")
                        tmpb = stg[0]
                        S.op("act", lambda e, tmpb=tmpb, b0=Bk[0][0]: e.activation(out=tmpb[0:65, :], in_=b0[0:65, :], func=AF.Copy),
                             reads=[Bk[0][1]], writes=["stg0"])
                        S.op("dve", lambda e, tmpb=tmpb, b1=Bk[1][0]: e.tensor_tensor(
                            out=tmpb[0:65, :].rearrange("p (j r) -> p j r", r=4),
                            in0=tmpb[0:65, :].rearrange("p (j r) -> p j r", r=4),
                            in1=b1[0:65, :].rearrange("p (r j) -> p j r", r=4), op=ALU.add),
                            reads=[Bk[1][1], "stg0"], writes=["stg0"])
                        S.op("dve", lambda e, tmpb=tmpb, b2=Bk[2][0]: e.tensor_tensor(
                            out=tmpb[0:65, :].rearrange("p (j r) -> p j r", r=16),
                            in0=tmpb[0:65, :].rearrange("p (j r) -> p j r", r=16),
                            in1=b2[0:65, :].rearrange("p (r j) -> p j r", r=16), op=ALU.add),
                            reads=[Bk[2][1], "stg0"], writes=["stg0"])
                        S.op("dve", lambda e, tmpb=tmpb: e.tensor_scalar(out=tmpb[64:65, :], in0=tmpb[64:65, :], scalar1=1e-30,
                                                                        scalar2=None, op0=ALU.add),
                             reads=["stg0"], writes=["stg0"])
                        S.op("dve", lambda e, tmpb=tmpb: e.reciprocal(out=stg[1][64:65, :], in_=tmpb[64:65, :]),
                             reads=["stg0"], writes=["stg1"])
                        pD, pDr = Bk[0]
                        S.op("pe", lambda e, pD=pD: e.matmul(pD[0:64, :], lhsT=onesf[64:65, 0:64], rhs=stg[1][64:65, :],
                                                             start=True, stop=True), reads=["onesf", "stg1"], writes=[pDr])
                        pO, pOr = None, "stg0"
                        rc = rec[0]
                        rcr = "rec0"
                        S.op("dve", lambda e, pD=pD, tmpb=tmpb, slot=slot, ch=ch: e.tensor_tensor(
                            out=oTsb[:, 512 * ch:512 * (ch + 1)], in0=tmpb[0:64, :], in1=pD[0:64, :], op=ALU.mult),
                            reads=[pDr, "stg0"], writes=["oTsb"])
                    dma("sp", oT_d[:, slot, T0:T0 + 2048], oTsb[:, :], reads=["oTsb"], writes=["oT_d"], key="oT_d")
            for sb_ in range(NSB, len(LATE_CASTS)):
                for f_ in LATE_CASTS[sb_]:
                    f_()
            S.barrier()
            S.emit()
        if KSTOP == 1:
            return nc

        with contextlib.ExitStack() as st2:
            xg = sbuf(st2, "xg", [128, NT // 128, D], F32)
            hT = sbuf(st2, "hT", [128, 8, NT], BF16)
            uext = sbuf(st2, "uext", [128, 8, 30 + NT], BF16)
            dgb = [sbuf(st2, "dgb%d" % i, [128, 31, 128], BF16) for i in range(2)]
            big2 = sbuf(st2, "big2", [128, 12 * NT], F32)
            acc = big2[:, 0:8 * NT].rearrange("p (c t) -> p c t", c=8)
            lnm = big2[:, 8 * NT:12 * NT].rearrange("p (c t) -> p c t", c=4)
            aT = big2[:, 0:11 * NT].bitcast(BF16).rearrange("p (c t) -> p c t", c=22)
            R3 = sbuf(st2, "R3", [128, 11264], F32)
            wdnb = R3[:, :].bitcast(BF16).rearrange("p (c t) -> p c t", c=22)
            sT = R3[:, 0:2048].bitcast(BF16).rearrange("p (c t) -> p c t", c=8)
            mixT = R3[:, 2048:4096].bitcast(BF16).rearrange("p (c t) -> p c t", c=8)
            woutb = R3[:, 4096:8192].bitcast(BF16).rearrange("p (c t) -> p c t", c=8)
            oTg = R3[0:64, 8192:9216].bitcast(BF16).rearrange("p (c t) -> p c t", c=4)
            cb16 = [R3[:, 9216 + 256 * i:9472 + 256 * i].bitcast(BF16) for i in range(2)]
            csq16 = [R3[:, 9728 + 256 * i:9984 + 256 * i].bitcast(BF16) for i in range(2)]
            tt = [R3[:, 10240 + 512 * i:10752 + 512 * i] for i in range(2)]
            S.alias["wdnb"] = ["sT", "mixT", "woutb", "oTg", "cb16_0", "cb16_1", "csq16_0", "csq16_1", "tt0", "tt1"]
            S.alias["aT"] = ["acc%d" % c for c in range(8)] + ["lnm"]
            S.alias["uh"] = ["uext"] + ["uext%d" % c for c in range(8)]
            S.alias["uprod"] = S.alias["uh"]
            wao64 = sbuf(st2, "wao64", [64, 4, D], BF16)
            wao128 = sbuf(st2, "wao128", [128, 2, D], BF16)
            NWB = 3
            wt = [sbuf(st2, "wt%d" % i, [128, 8, 256], BF16) for i in range(NWB)]
            sg_ = [sbuf(st2, "sg%d" % i, [128, NT], F32) for i in range(3)]
            upx = [sbuf(st2, "upx%d" % i, [128, NT + 2], F32) for i in range(2)]
            cgb = [sbuf(st2, "cgb%d" % i, [128, NT], F32) for i in range(2)]
            fh = sbuf(st2, "fh", [128, 44, 2], F32)
            gfin = sbuf(st2, "gfin", [128, D], F32)
            yt = [sbuf(st2, "yt%d" % i, [128, D], F32) for i in range(1)]
            uflat = uext[:, :, :].rearrange("p c t -> p (c t)")[:, 0:4336].bitcast(F32)
            uh = uflat[:, 0:8 * NS * 31].rearrange("p (c s j) -> p c s j", c=8, s=NS)
            uprod = uflat[:, 8 * NS * 31:16 * NS * 31].rearrange("p (c s j) -> p c s j", c=8, s=NS)
            fhs = sbuf(st2, "fhs", [128, 44, 2 * NS], F32)
            upn = sbuf(st2, "upn", [128, 44, NS], F32)
            orow = sbuf(st2, "orow", [30, D], F32)

            dma("sp", gfin[:], gfin_d, writes=["gfin"], key="gfin")
            dma("sp", wao64[:], wb_ao.rearrange("(s d) n -> d s n", d=64), reads=["wb_ao"], writes=["wao"], key="wao64")
            dma("sp", wao128[:], wb_ao.rearrange("(c p) n -> p c n", p=128), reads=["wb_ao"], writes=["wao"], key="wao128")
            S.op("pool", lambda e: e.memset(uext[:, :, 0:30], 0.0), writes=["uext"])
            S.op("pool", lambda e: e.memset(fh[:], 0.0), writes=["fh"])

            wi = [0]

            def wtile(src_ap, rd):
                i = wi[0] % NWB
                wi[0] += 1
                dma("sp", wt[i][:, :, :], src_ap.rearrange("(kc p) n -> p kc n", p=128), reads=[rd],
                    writes=["wt%d" % i], key="wt%d" % i)
                return wt[i], "wt%d" % i

            tgl = [0]

            def alt(a, b):
                tgl[0] += 1
                return a if tgl[0] % 2 else b

            ygi = [0]

            prevN = [NT]

            def group(t0, TT_list, N, sample, first=False, lastg=False, out0=None):
                gi = 1
                for (i, TT) in TT_list:
                    src = xs if sample else xp[t0 + 128 * i:t0 + 128 * i + TT, :]
                    dma("sp", xg[0:TT, i, :], src, writes=["xg%d" % i], key="xg%d" % i)
                    norm_T(xg[0:TT, i, :], "xg%d" % i, TT, hT[:, :, 128 * i:128 * i + TT], "hT", "gmix", ygi[0])
                    ygi[0] += 1
                if not sample:
                    dma("sp", oTg[:, :, 0:N], oT_d[:, :, t0:t0 + N], reads=["oT_d"], writes=["oTg"], key="oTg")
                dma("sp", woutb[:, :, :], wb_out.rearrange("(kc p) n -> p kc n", p=128), reads=["wb_out"],
                    writes=["woutb"], key="woutb")
                if sample:
                    for s in range(NS):
                        dma("sp", orow[:, :], sconv[s, :, :], writes=["orow"], key="scv")
                        for c in range(8):
                            pst, pr = psum()
                            S.op("pe", lambda e, c=c, pst=pst: e.transpose(pst[:, 0:30], orow[0:30, 128 * c:128 * (c + 1)],
                                                                            identf[0:30, 0:30]),
                                 reads=["orow", "identf"], writes=[pr])
                            S.op("act", lambda e, c=c, pst=pst, s=s: e.activation(out=uh[:, c, s, 0:30], in_=pst[:, 0:30],
                                                                                  func=AF.Copy), reads=[pr], writes=["uh"])
                elif not first:
                    pN = prevN[0]
                    S.op("pool", lambda e: e.tensor_copy(out=uext[:, :, 0:30], in_=uext[:, :, pN:pN + 30]),
                         reads=["uext"], writes=["uext"])
                if not sample:
                    prevN[0] = N
                for c in range(8):
                    w, wr = wtile(wb_in[:, 256 * c:256 * (c + 1)], "wb_in1")
                    pl, plr = psum()
                    pg, pgr = psum()
                    for kc in range(8):
                        S.op("pe", lambda e, kc=kc, w=w, pl=pl: e.matmul(pl[:, 0:N], lhsT=w[:, kc, 0:128], rhs=hT[:, kc, 0:N],
                                                                         start=(kc == 0), stop=(kc == 7)),
                             reads=[wr, "hT"], writes=[plr], sig=(kc == 7))
                    for kc in range(8):
                        S.op("pe", lambda e, kc=kc, w=w, pg=pg: e.matmul(pg[:, 0:N], lhsT=w[:, kc, 128:256], rhs=hT[:, kc, 0:N],
                                                                         start=(kc == 0), stop=(kc == 7)),
                             reads=[wr, "hT"], writes=[pgr], sig=(kc == 7))
                    sgb = sg_[c % 2]
                    sgr = "sg%d" % (c % 2)
                    S.op("act", lambda e, pg=pg, sgb=sgb: e.activation(out=sgb[:, 0:N], in_=pg[:, 0:N], func=AF.Sigmoid),
                         reads=[pgr], writes=[sgr])
                    if sample:
                        S.op("dve", lambda e, c=c, pl=pl, sgb=sgb: e.tensor_tensor(out=uh[:, c, :, 30], in0=pl[:, 0:N],
                                                                                   in1=sgb[:, 0:N], op=ALU.mult),
                             reads=[plr, sgr], writes=["uh"])
                    else:
                        S.op("dve", lambda e, c=c, pl=pl, sgb=sgb: e.tensor_tensor(out=uext[:, c, 30:30 + N], in0=pl[:, 0:N],
                                                                                   in1=sgb[:, 0:N], op=ALU.mult),
                             reads=[plr, sgr], writes=["uext%d" % c])
                o_w = PP["wdw"][0]
                if sample:
                    S.op("dve", lambda e: e.tensor_tensor(
                        out=uprod[:, :, :, :], in0=uh[:, :, :, :],
                        in1=pp[:, o_w:o_w + 248].rearrange("p (c j) -> p c j", j=31).unsqueeze(2).to_broadcast([128, 8, NS, 31]),
                        op=ALU.mult), reads=["uh", "pp"], writes=["uprod"])
                    S.op("dve", lambda e: e.tensor_reduce(out=acc[:, :, 0:NS], in_=uprod[:, :, :, :], axis=AX.X, op=ALU.add),
                         reads=["uprod"], writes=["acc%d" % c for c in range(8)])
                    o_b = PP["bdw"][0]
                    S.op("dve", lambda e: e.tensor_tensor(out=acc[:, :, 0:NS], in0=acc[:, :, 0:NS],
                                                          in1=pp[:, o_b:o_b + 8].unsqueeze(2).to_broadcast([128, 8, NS]),
                                                          op=ALU.add), reads=["acc%d" % c for c in range(8)] + ["pp"],
                         writes=["acc%d" % c for c in range(8)])
                else:
                    for c in range(8):
                        dg = dgb[c % 2]
                        dgr = "dgb%d" % (c % 2)
                        S.op("pool", lambda e, c=c, dg=dg: e.tensor_tensor(
                            out=dg[:, :, :], in0=identf[:, :].unsqueeze(1).to_broadcast([128, 31, 128]),
                            in1=pp[:, o_w + 31 * c:o_w + 31 * c + 31].unsqueeze(2).to_broadcast([128, 31, 128]),
                            op=ALU.mult), reads=["identf", "pp"], writes=[dgr])
                        pcv, pcr = psum()
                        for j in range(31):
                            S.op("pe", lambda e, c=c, j=j, dg=dg, pcv=pcv: e.matmul(
                                pcv[:, 0:N], lhsT=dg[:, j, :], rhs=uext[:, c, j:j + N], start=(j == 0), stop=(j == 30)),
                                reads=[dgr, "uext%d" % c, "uext"], writes=[pcr], sig=(j == 30))
                        S.op("act", lambda e, c=c, pcv=pcv: e.activation(out=acc[:, c, 0:N], in_=pcv[:, 0:N], func=AF.Identity,
                                                                         bias=P("bdw", c)),
                             reads=[pcr, "pp"], writes=["acc%d" % c])
                if sample or lastg:
                    for c in range(8):
                        pst, pr = psum()
                        if sample:
                            S.op("pe", lambda e, c=c, pst=pst: e.transpose(pst[0:NS, 0:128], uh[:, c, :, 30], identf[:, :]),
                                 reads=["uh", "identf"], writes=[pr])
                            S.op("act", lambda e, c=c, pst=pst: e.activation(out=orow[0:NS, 128 * c:128 * (c + 1)],
                                                                             in_=pst[0:NS, 0:128], func=AF.Copy),
                                 reads=[pr], writes=["orow"])
                        else:
                            S.op("pe", lambda e, c=c: e.transpose(PTB[0:30, 128 * c:128 * (c + 1)], uext[:, c, NT:NT + 30],
                                                                   identb[:, :]),
                                 reads=["uext%d" % c, "identb"], writes=["ptb"])
                            S.op("act", lambda e, c=c: e.activation(out=orow[0:30, 128 * c:128 * (c + 1)],
                                                                    in_=PTB[0:30, 128 * c:128 * (c + 1)], func=AF.Copy),
                                 reads=["ptb"], writes=["orow"])
                    if sample:
                        dma("sp", conv_s[:, 29, :], orow[0:NS, 0:D], reads=["orow"], key="oorow")
                    else:
                        dma("sp", conv_p[:, :], orow[0:30, 0:D], reads=["orow"], key="oorow")
                if DBG and t0 == HALO and not sample:
                    dma("sp", dbg_acc, acc[:, :, :], reads=["acc%d" % c for c in range(8)], key="dbg")
                p1, p1r = psum()
                p2, p2r = psum()
                for c in range(8):
                    b1 = cb16[c % 2]
                    b2 = csq16[c % 2]
                    S.op("pool", lambda e, c=c, b1=b1: e.tensor_copy(out=b1[:, 0:N], in_=acc[:, c, 0:N]),
                         reads=["acc%d" % c], writes=["cb16_%d" % (c % 2)])
                    S.op("act", lambda e, c=c, b2=b2: e.activation(out=b2[:, 0:N], in_=acc[:, c, 0:N], func=AF.Square),
                         reads=["acc%d" % c], writes=["csq16_%d" % (c % 2)])
                    S.op("pe", lambda e, c=c, b1=b1: e.matmul(p1[:, 0:N], lhsT=onesb[:, :], rhs=b1[:, 0:N], start=(c == 0),
                                                              stop=(c == 7)), reads=["onesb", "cb16_%d" % (c % 2)], writes=[p1r])
                    S.op("pe", lambda e, c=c, b2=b2: e.matmul(p2[:, 0:N], lhsT=onesb[:, :], rhs=b2[:, 0:N], start=(c == 0),
                                                              stop=(c == 7)), reads=["onesb", "csq16_%d" % (c % 2)], writes=[p2r])
                S.op("dve", lambda e: e.tensor_scalar(out=lnm[:, 0, 0:N], in0=p1[:, 0:N], scalar1=1.0 / D, scalar2=None,
                                                      op0=ALU.mult), reads=[p1r], writes=["lnm"])
                S.op("dve", lambda e: e.tensor_tensor(out=lnm[:, 1, 0:N], in0=lnm[:, 0, 0:N], in1=lnm[:, 0, 0:N], op=ALU.mult),
                     reads=["lnm"], writes=["lnm"])
                S.op("dve", lambda e: e.scalar_tensor_tensor(out=lnm[:, 1, 0:N], in0=p2[:, 0:N], scalar=1.0 / D,
                                                             in1=lnm[:, 1, 0:N], op0=ALU.mult, op1=ALU.subtract),
                     reads=[p2r, "lnm"], writes=["lnm"])
                S.op("act", lambda e: e.activation(out=lnm[:, 2, 0:N], in_=lnm[:, 1, 0:N], func=AF.Sqrt, bias=epst[:, 0:1]),
                     reads=["lnm", "epst"], writes=["lnm"])
                S.op("dve", lambda e: e.reciprocal(out=lnm[:, 3, 0:N], in_=lnm[:, 2, 0:N]), reads=["lnm"], writes=["lnm"])
                for c in range(8):
                    tb = tt[c % 2]
                    tr = "tt%d" % (c % 2)
                    S.op("dve", lambda e, c=c, tb=tb: e.tensor_tensor(out=tb[:, 0:N], in0=acc[:, c, 0:N], in1=lnm[:, 0, 0:N],
                                                                      op=ALU.subtract),
                         reads=["acc%d" % c, "lnm"], writes=[tr])
                    S.op("pool", lambda e, tb=tb: e.tensor_tensor(out=tb[:, 0:N], in0=tb[:, 0:N], in1=lnm[:, 3, 0:N],
                                                                  op=ALU.mult), reads=[tr, "lnm"], writes=[tr])
                    S.op("act", lambda e, c=c, tb=tb: e.activation(out=sT[:, c, 0:N], in_=tb[:, 0:N], func=AF.Silu,
                                                                   scale=P("lng", c), bias=P("lnb", c)),
                         reads=[tr, "pp"], writes=["sT"])
                if DBG and t0 == HALO and not sample:
                    dma("sp", dbg_sT, sT[:, :, :], reads=["sT"], key="dbg")
                for c in range(8):
                    if c % 2 == 0:
                        wco_t, wco_r = wtile(wb_co[:, 128 * c:128 * c + 256], "wb_co")
                    wg_t, wg_r = wtile(wb_in[:, 4352 + 256 * c:4352 + 256 * (c + 1)], "wb_in3")
                    pa, par = psum()
                    pbb, pbr = psum()
                    pga, pgar = psum()
                    pgb, pgbr = psum()
                    co = 128 * (c % 2)
                    for kc in range(8):
                        S.op("pe", lambda e, kc=kc, pa=pa, wco_t=wco_t, co=co: e.matmul(
                            pa[:, 0:N], lhsT=wco_t[:, kc, co:co + 128], rhs=sT[:, kc, 0:N], start=(kc == 0), stop=(kc == 7)),
                            reads=[wco_r, "sT"], writes=[par], sig=(kc == 7))
                    if sample:
                        for k2 in range(2):
                            S.op("pe", lambda e, k2=k2, pbb=pbb, c=c: e.matmul(
                                pbb[:, 0:N], lhsT=wao128[:, k2, 128 * c:128 * (c + 1)], rhs=oTs[:, k2, 0:N],
                                start=(k2 == 0), stop=(k2 == 1)), reads=["wao", "oTs"], writes=[pbr], sig=(k2 == 1))
                    else:
                        for sl in range(4):
                            S.op("pe", lambda e, sl=sl, pbb=pbb, c=c: e.matmul(
                                pbb[:, 0:N], lhsT=wao64[:, sl, 128 * c:128 * (c + 1)], rhs=oTg[:, sl, 0:N],
                                start=(sl == 0), stop=(sl == 3)), reads=["wao", "oTg"], writes=[pbr], sig=(sl == 3))
                    for kc in range(8):
                        S.op("pe", lambda e, kc=kc, pga=pga, wg_t=wg_t: e.matmul(
                            pga[:, 0:N], lhsT=wg_t[:, kc, 0:128], rhs=hT[:, kc, 0:N], start=(kc == 0), stop=(kc == 7)),
                            reads=[wg_r, "hT"], writes=[pgar], sig=(kc == 7))
                    for kc in range(8):
                        S.op("pe", lambda e, kc=kc, pgb=pgb, wg_t=wg_t: e.matmul(
                            pgb[:, 0:N], lhsT=wg_t[:, kc, 128:256], rhs=hT[:, kc, 0:N], start=(kc == 0), stop=(kc == 7)),
                            reads=[wg_r, "hT"], writes=[pgbr], sig=(kc == 7))
                    sa, sar = sg_[0], "sg0"
                    sb_, sbr = sg_[1], "sg1"
                    S.op("act", lambda e, pga=pga: e.activation(out=sa[:, 0:N], in_=pga[:, 0:N], func=AF.Sigmoid),
                         reads=[pgar], writes=[sar])
                    S.op("act", lambda e, pgb=pgb: e.activation(out=sb_[:, 0:N], in_=pgb[:, 0:N], func=AF.Sigmoid),
                         reads=[pgbr], writes=[sbr])
                    S.op("dve", lambda e, pa=pa: e.tensor_tensor(out=sa[:, 0:N], in0=pa[:, 0:N], in1=sa[:, 0:N], op=ALU.mult),
                         reads=[par, sar], writes=[sar])
                    S.op("dve", lambda e, pbb=pbb: e.tensor_tensor(out=sb_[:, 0:N], in0=pbb[:, 0:N], in1=sb_[:, 0:N],
                                                                   op=ALU.mult), reads=[pbr, sbr], writes=[sbr])
                    S.op("pool", lambda e, c=c: e.tensor_tensor(out=mixT[:, c, 0:N], in0=sa[:, 0:N], in1=sb_[:, 0:N],
                                                                op=ALU.add), reads=[sar, sbr], writes=["mixT"])
                if DBG and t0 == HALO and not sample:
                    dma("sp", dbg_mix, mixT[:, :, :], reads=["mixT"], key="dbg")
                for (i, TT) in TT_list:
                    WN = 512 if TT == 128 else 256
                    for n in range(D // WN):
                        po, por = psum()
                        for kc in range(8):
                            S.op("pe", lambda e, kc=kc, po=po, i=i, TT=TT, n=n, WN=WN: e.matmul(
                                po[0:TT, 0:WN], lhsT=mixT[:, kc, 128 * i:128 * i + TT], rhs=woutb[:, kc, WN * n:WN * (n + 1)],
                                start=(kc == 0), stop=(kc == 7)), reads=["mixT", "woutb"], writes=[por], sig=(kc == 7))
                        S.op("dve", lambda e, po=po, i=i, TT=TT, n=n, WN=WN: e.tensor_tensor(
                            out=xg[0:TT, i, WN * n:WN * (n + 1)], in0=po[0:TT, 0:WN], in1=xg[0:TT, i, WN * n:WN * (n + 1)],
                            op=ALU.add), reads=[por, "xg%d" % i], writes=["xg%d" % i])
                if DBG and t0 == HALO and not sample:
                    for (i, TT) in TT_list:
                        dma("sp", dbg_xmid[128 * i:128 * (i + 1), :], xg[0:TT, i, :], reads=["xg%d" % i], key="dbg")
                dma("sp", wdnb[:, :, :], wb_dn.rearrange("(kc p) n -> p kc n", p=128), reads=["wb_dn"], writes=["wdnb"],
                    key="wdnb")
                for (i, TT) in TT_list:
                    norm_T(xg[0:TT, i, :], "xg%d" % i, TT, hT[:, :, 128 * i:128 * i + TT], "hT", "gffn", ygi[0])
                    ygi[0] += 1
                if sample:
                    for q in range(44):
                        if q % 8 == 0:
                            wpc = min(1024, 2 * DFF - 128 * q)
                            dma("sp", orow[0:2 * NS, 0:wpc], sffn.rearrange("s j n -> (s j) n")[:, 128 * q:128 * q + wpc],
                                writes=["orow"], key="scv")
                        pst, pr = psum()
                        S.op("pe", lambda e, q=q, pst=pst: e.transpose(
                            pst[:, 0:2 * NS], orow[0:2 * NS, 128 * (q % 8):128 * (q % 8 + 1)], identf[0:2 * NS, 0:2 * NS]),
                             reads=["orow", "identf"], writes=[pr])
                        S.op("act", lambda e, q=q, pst=pst: e.activation(out=fhs[:, q, :], in_=pst[:, 0:2 * NS], func=AF.Copy),
                             reads=[pr], writes=["fhs"])
                o_f = PP["wfdw"][0]
                for j in range(22):
                    w, wr = wtile(wb_up[:, 256 * j:256 * (j + 1)], "wb_up")
                    pgv = []
                    for hv in range(2):
                        pz, pzr = psum()
                        for kc in range(8):
                            S.op("pe", lambda e, kc=kc, w=w, pz=pz, hv=hv: e.matmul(
                                pz[:, 0:N], lhsT=w[:, kc, 128 * hv:128 * (hv + 1)], rhs=hT[:, kc, 0:N], start=(kc == 0),
                                stop=(kc == 7)), reads=[wr, "hT"], writes=[pzr], sig=(kc == 7))
                        pgv.append((pz, pzr))
                    for hv in range(2):
                        q = 2 * j + hv
                        pz, pzr = pgv[hv]
                        ub, ur = upx[hv], "upx%d" % hv
                        cb, cr = cgb[hv], "cgb%d" % hv
                        w0 = pp[:, o_f + 3 * q:o_f + 3 * q + 1]
                        w1 = pp[:, o_f + 3 * q + 1:o_f + 3 * q + 2]
                        w2 = pp[:, o_f + 3 * q + 2:o_f + 3 * q + 3]
                        S.op("act", lambda e, pz=pz, cb=cb, w2=w2, q=q: e.activation(
                            out=cb[:, 0:N], in_=pz[:, 0:N], func=AF.Identity, scale=w2, bias=P("bfdw", q)),
                            reads=[pzr, "pp"], writes=[cr])
                        if sample:
                            S.op("act", lambda e, pz=pz, q=q: e.activation(out=upn[:, q, :], in_=pz[:, 0:NS], func=AF.Copy),
                                 reads=[pzr], writes=["upn"])
                            fv = fhs[:, q, :].rearrange("p (s j) -> p j s", j=2)
                            S.op("dve", lambda e, cb=cb, fv=fv, w1=w1: e.scalar_tensor_tensor(
                                out=cb[:, 0:N], in0=fv[:, 1, :], scalar=w1, in1=cb[:, 0:N], op0=ALU.mult, op1=ALU.add),
                                reads=["fhs", cr, "pp"], writes=[cr])
                            S.op("dve", lambda e, cb=cb, fv=fv, w0=w0: e.scalar_tensor_tensor(
                                out=cb[:, 0:N], in0=fv[:, 0, :], scalar=w0, in1=cb[:, 0:N], op0=ALU.mult, op1=ALU.add),
                                reads=["fhs", cr, "pp"], writes=[cr])
                        else:
                            S.op("pool", lambda e, ub=ub, q=q: e.tensor_copy(out=ub[:, 0:2], in_=fh[:, q, :]),
                                 reads=["fh"], writes=[ur])
                            S.op("act", lambda e, pz=pz, ub=ub: e.activation(out=ub[:, 2:2 + N], in_=pz[:, 0:N], func=AF.Copy),
                                 reads=[pzr], writes=[ur])
                            S.op("pool", lambda e, ub=ub, q=q: e.tensor_copy(out=fh[:, q, :], in_=ub[:, N:N + 2]),
                                 reads=[ur], writes=["fh"])
                            S.op("dve", lambda e, cb=cb, ub=ub, w1=w1: e.scalar_tensor_tensor(
                                out=cb[:, 0:N], in0=ub[:, 1:1 + N], scalar=w1, in1=cb[:, 0:N], op0=ALU.mult, op1=ALU.add),
                                reads=[ur, cr, "pp"], writes=[cr])
                            S.op("dve", lambda e, cb=cb, ub=ub, w0=w0: e.scalar_tensor_tensor(
                                out=cb[:, 0:N], in0=ub[:, 0:N], scalar=w0, in1=cb[:, 0:N], op0=ALU.mult, op1=ALU.add),
                                reads=[ur, cr, "pp"], writes=[cr])
                    sgb, sgr = sg_[2], "sg2"
                    S.op("act", lambda e, sgb=sgb: e.activation(out=sgb[:, 0:N], in_=cgb[0][:, 0:N], func=AF.Silu),
                         reads=["cgb0"], writes=[sgr])
                    S.op(alt("pool", "dve"), lambda e, j=j, sgb=sgb: e.tensor_tensor(out=aT[:, j, 0:N], in0=sgb[:, 0:N],
                                                                                    in1=cgb[1][:, 0:N], op=ALU.mult),
                         reads=[sgr, "cgb1"], writes=["aT"])
                if sample or lastg:
                    nrow = NS if sample else 2
                    for q in range(44):
                        pst, pr = psum()
                        srcT = upn[:, q, :] if sample else fh[:, q, :]
                        S.op("pe", lambda e, pst=pst, srcT=srcT, nrow=nrow: e.transpose(pst[0:nrow, 0:128], srcT, identf[:, :]),
                             reads=["upn" if sample else "fh", "identf"], writes=[pr])
                        S.op("act", lambda e, q=q, pst=pst, nrow=nrow: e.activation(
                            out=orow[0:nrow, 128 * (q % 8):128 * (q % 8 + 1)], in_=pst[0:nrow, 0:128], func=AF.Copy),
                            reads=[pr], writes=["orow"])
                        if q % 8 == 7 or q == 43:
                            q0 = 8 * (q // 8)
                            wpc = 128 * (q - q0 + 1)
                            if sample:
                                dma("sp", ffn_s[:, 1, 128 * q0:128 * q0 + wpc], orow[0:NS, 0:wpc], reads=["orow"], key="oorow")
                            else:
                                dma("sp", ffn_p[:, 128 * q0:128 * q0 + wpc], orow[0:2, 0:wpc], reads=["orow"], key="oorow")
                for (i, TT) in TT_list:
                    WN = 512 if TT == 128 else 256
                    for n in range(D // WN):
                        po, por = psum()
                        for kc in range(22):
                            S.op("pe", lambda e, kc=kc, po=po, i=i, TT=TT, n=n, WN=WN: e.matmul(
                                po[0:TT, 0:WN], lhsT=aT[:, kc, 128 * i:128 * i + TT], rhs=wdnb[:, kc, WN * n:WN * (n + 1)],
                                start=(kc == 0), stop=(kc == 21)), reads=["aT", "wdnb"], writes=[por], sig=(kc == 21))
                        S.op("dve", lambda e, po=po, i=i, TT=TT, n=n, WN=WN: e.tensor_tensor(
                            out=xg[0:TT, i, WN * n:WN * (n + 1)], in0=po[0:TT, 0:WN], in1=xg[0:TT, i, WN * n:WN * (n + 1)],
                            op=ALU.add), reads=[por, "xg%d" % i], writes=["xg%d" % i])
                for (i, TT) in TT_list:
                    col = stat_i[0]
                    stat_i[0] += 1
                    src = xg[0:TT, i, :]
                    S.op("act", lambda e, src=src, TT=TT, col=col: e.activation(
                        out=yt[0][0:TT, :], in_=src, func=AF.Square, accum_out=stat[0:TT, 0, col:col + 1]),
                        reads=["xg%d" % i], writes=["yt0", "stat%d" % col])
                    S.op("act", lambda e, TT=TT, col=col: e.activation(
                        out=stat[0:TT, 1, col:col + 1], in_=stat[0:TT, 0, col:col + 1], func=AF.Sqrt, scale=1.0 / D,
                        bias=epst[0:TT, 0:1]), reads=["stat%d" % col, "epst"], writes=["stat%d" % col])
                    S.op("dve", lambda e, TT=TT, col=col: e.reciprocal(out=stat[0:TT, 2, col:col + 1],
                                                                       in_=stat[0:TT, 1, col:col + 1]),
                         reads=["stat%d" % col], writes=["stat%d" % col])
                    yb = yt[0]
                    yr = "yt0"
                    S.op("dve", lambda e, src=src, TT=TT, col=col, yb=yb: e.scalar_tensor_tensor(
                        out=yb[0:TT, :], in0=src, scalar=stat[0:TT, 2, col:col + 1], in1=gfin[0:TT, :], op0=ALU.mult,
                        op1=ALU.mult), reads=["xg%d" % i, "stat%d" % col, "gfin"], writes=[yr])
                    if sample:
                        dma("sp", y_s[:, :], yb[0:TT, :], reads=[yr], key="o" + yr)
                    elif out0 is not None:
                        dma("sp", y_p[out0 + 128 * i:out0 + 128 * i + TT, :], yb[0:TT, :], reads=[yr], key="o" + yr)

            hvt = sbuf(st2, "hvt", [128, 1], F32)
            dma("sp", hvt[:], hv_d, writes=["hvt"], key="hvt")
            group(HALO - 256, [(0, 128), (1, 128)], 256, False, first=True)
            S.op("dve", lambda e: e.tensor_scalar(out=fh[:, :, :], in0=fh[:, :, :], scalar1=hvt[:, 0:1], scalar2=None,
                                                  op0=ALU.mult), reads=["fh", "hvt"], writes=["fh"])
            for gi in range(NG):
                for k_, (dst_, src_) in enumerate(SHIFTS):
                    if k_ % NG == gi:
                        dma("pool", dst_, src_, key="cshift")
                group(HALO + gi * NT, [(i, 128) for i in range(NT // 128)], NT, False, lastg=(gi == NG - 1),
                      out0=gi * NT)
            group(0, [(0, NS)], NS, True)
            S.barrier()
            S.emit()
    return nc


def _perm_in():
    idx = []
    for c in range(8):
        idx += list(range(128 * c, 128 * (c + 1)))
        idx += list(range(1024 + 128 * c, 1024 + 128 * (c + 1)))
    idx += list(range(2048, 4352))
    for c in range(8):
        idx += list(range(4352 + 128 * c, 4352 + 128 * (c + 1)))
        idx += list(range(5376 + 128 * c, 5376 + 128 * (c + 1)))
    return np.array(idx)


def _perm_up():
    idx = []
    for j in range(22):
        idx += list(range(128 * j, 128 * (j + 1)))
        idx += list(range(DFF + 128 * j, DFF + 128 * (j + 1)))
    return np.array(idx)


def _fm(v, nch):
    return np.ascontiguousarray(v.reshape(nch, 128).T)


def make_shared(inp):
    f = np.float32
    pin = _perm_in()
    pup = _perm_up()
    ppv = np.zeros((128, NPP), f)

    def put(name, arr):
        o, w = PP[name]
        ppv[:, o:o + w] = arr.reshape(128, w)

    put("gmix", _fm(inp["g_mix"][0], 8))
    put("bdw", _fm(inp["b_dw"][0], 8))
    put("lng", _fm(inp["ln_g"][0], 8))
    put("lnb", _fm(inp["ln_b"][0], 8))
    put("gffn", _fm(inp["g_ffn"][0], 8))
    wdw = inp["w_dw"][0]
    put("wdw", np.ascontiguousarray(wdw.T.reshape(8, 128, 31).transpose(1, 0, 2)))
    wf = inp["w_fdw"][0][:, pup]
    put("wfdw", np.ascontiguousarray(wf.T.reshape(44, 128, 3).transpose(1, 0, 2)))
    put("bfdw", _fm(inp["b_fdw"][0][pup], 44))
    inv = (np.float32(500000.0) ** (-np.arange(8, dtype=f) / np.float32(8))).astype(f)
    angs = (np.full((NS, 1), 16384.0, f) * inv[None, :]).astype(f)
    css = np.concatenate([np.cos(angs), np.cos(angs)], 1).astype(f)
    sns = np.concatenate([-np.sin(angs), np.sin(angs)], 1).astype(f)
    j = np.arange(128)[:, None]
    i = np.arange(128)[None, :]
    mask2 = np.stack([(j >= i), (j <= i)], 1).astype(f)
    sel = np.zeros((NS, NS, 128), f)
    for s in range(NS):
        sel[s, s, :] = 1.0
    return {
        "w_in": np.ascontiguousarray(inp["w_in"][0][:, pin]),
        "w_co": np.ascontiguousarray(inp["w_conv_out"][0]),
        "w_ao": np.ascontiguousarray(inp["w_attn_out"][0]),
        "w_out": np.ascontiguousarray(inp["w_out"][0]),
        "w_up": np.ascontiguousarray(inp["w_up"][0][:, pup]),
        "w_dn": np.ascontiguousarray(inp["w_down"][0]),
        "pp": ppv,
        "gfin": np.ascontiguousarray(np.broadcast_to(inp["g_final"][None, :], (128, D))),
        "ident": np.eye(128, dtype=f),
        "mask2": mask2, "css": css, "sns": sns, "sel": sel,
    }


def make_core(inp, b, half, MAIN, s0, pup):
    f = np.float32
    LS = HALO + MAIN
    start = half * MAIN
    absp = start - HALO + np.arange(LS)
    valid = absp >= 0
    x = inp["x_prompt"][b]
    xl = np.zeros((LS, D), f)
    xl[valid] = x[absp[valid]]
    inv = (np.float32(500000.0) ** (-np.arange(8, dtype=f) / np.float32(8))).astype(f)
    pos = np.maximum(absp, 0).astype(f)
    ang = (pos[:, None] * inv[None, :]).astype(f)
    cos, sin = np.cos(ang).astype(f), np.sin(ang).astype(f)
    ntile = LS // 128
    csp = np.ascontiguousarray(np.concatenate([cos, cos], 1).reshape(ntile, 128, 16).transpose(1, 0, 2))
    snp = np.ascontiguousarray(np.concatenate([-sin, sin], 1).reshape(ntile, 128, 16).transpose(1, 0, 2))
    m = {
        "xp": xl, "csp": csp, "snp": snp,
        "hv": np.full((128, 1), 1.0 if start > 0 else 0.0, f),
        "xs": np.ascontiguousarray(inp["x_sample"][s0:s0 + NS, 0]),
        "sconv": np.ascontiguousarray(inp["state_conv"][0, s0:s0 + NS]),
        "sffn": np.ascontiguousarray(inp["state_ffn_conv"][0, s0:s0 + NS][:, :, pup]),
    }
    vf = valid.astype(f)
    nsb = LS // 2048
    for g, (W, d) in enumerate(GROUPS):
        m["vld%d" % g] = np.ascontiguousarray(vf.reshape(nsb, 16 // d, 128, d).transpose(2, 0, 3, 1))
    caches = ((inp["cache_k_w128"], inp["cache_v_w128"]), (inp["cache_k_w512"], inp["cache_v_w512"]),
              (inp["cache_k_w2048"], inp["cache_v_w2048"]))
    for g, W in enumerate((128, 512, 2048)):
        m["ck%d" % g] = np.ascontiguousarray(caches[g][0][0, s0:s0 + NS].reshape(NS, W, 256))
        m["cv%d" % g] = np.ascontiguousarray(caches[g][1][0, s0:s0 + NS].reshape(NS, W, 256))
    return m


_NC_CACHE = {}


def run(inp, n_cores=8):
    inp = {k: np.asarray(v) for k, v in inp.items()}
    B, S_FULL, _ = inp["x_prompt"].shape
    nsamp = inp["x_sample"].shape[0]
    MAIN = S_FULL // 2
    assert n_cores == 2 * B
    if MAIN not in _NC_CACHE:
        _NC_CACHE[MAIN] = build(MAIN)
    nc = _NC_CACHE[MAIN]
    shared = make_shared(inp)
    pup = _perm_up()
    in_maps = []
    for c in range(n_cores):
        m = dict(shared)
        m.update(make_core(inp, c // 2, c % 2, MAIN, (NS * c) % nsamp, pup))
        in_maps.append(m)
    res = run_bass_kernel_spmd(nc, in_maps, core_ids=list(range(n_cores))).results
    global LAST_RES
    LAST_RES = res
    ipup = np.argsort(pup)
    f = np.float32
    y_p = np.stack([np.concatenate([res[2 * b]["y_p"], res[2 * b + 1]["y_p"]], 0) for b in range(B)], 0)
    nsc = nsamp // NS
    y_s = np.concatenate([res[c]["y_s"] for c in range(nsc)], 0)[:, None, :]
    hi = [2 * b + 1 for b in range(B)]
    conv_p = np.stack([res[c]["conv_p"] for c in hi], 0)[None]
    conv_s = np.concatenate([res[c]["conv_s"] for c in range(nsc)], 0)[None]
    outs = [y_p.astype(f), y_s.astype(f), conv_p.astype(f), conv_s.astype(f)]
    for g, W in enumerate((128, 512, 2048)):
        outs.append(np.stack([res[c]["k%d_p" % g] for c in hi], 0).reshape(1, B, W, 4, 64).astype(f))
        outs.append(np.stack([res[c]["v%d_p" % g] for c in hi], 0).reshape(1, B, W, 4, 64).astype(f))
        outs.append(np.concatenate([res[c]["k%d_s" % g] for c in range(nsc)], 0).reshape(1, nsamp, W, 4, 64).astype(f))
        outs.append(np.concatenate([res[c]["v%d_s" % g] for c in range(nsc)], 0).reshape(1, nsamp, W, 4, 64).astype(f))
    ffn_p = np.stack([res[c]["ffn_p"] for c in hi], 0)[:, :, ipup][None]
    ffn_s = np.concatenate([res[c]["ffn_s"] for c in range(nsc)], 0)[:, :, ipup][None]
    outs += [ffn_p.astype(f), ffn_s.astype(f)]
    return tuple(outs)


def kernel(**inputs):
    return run(inputs, 8)
```

```python
import contextlib
import types
import numpy as np
import concourse.bass as bass
import concourse.mybir as mybir
from concourse.bass_utils import run_bass_kernel_spmd

F32 = mybir.dt.float32
BF16 = mybir.dt.bfloat16
ALU = mybir.AluOpType
AF = mybir.ActivationFunctionType
AX = mybir.AxisListType

D = 1024
DFF = 2816
NS = 4
NT = 512
GROUPS = ((128, 1), (512, 4), (2048, 16))
EPS = 1e-6
ENGS = ("pe", "act", "dve", "pool", "sp")


class Sched:
    def __init__(self, nc, st, nsem=100):
        self.nc = nc
        self.ops = {e: [] for e in ENGS}
        self.cnt = {}
        self.res_w = {}
        self.res_r = {}
        self.waited = {e: {} for e in ENGS}
        self.pool = [st.enter_context(nc.semaphore("sm%d" % i)) for i in range(nsem)]
        self.sem = {}
        self.alias = {}

    def _sk(self, k):
        if k not in self.cnt:
            self.cnt[k] = 0
            assert len(self.sem) < len(self.pool), "out of semaphores"
            self.sem[k] = self.pool[len(self.sem)]
        return k

    @staticmethod
    def _freeze(fn):
        if fn.__closure__ is None:
            return fn
        cells = []
        for c in fn.__closure__:
            try:
                cells.append(types.CellType(c.cell_contents))
            except ValueError:
                cells.append(c)
        return types.FunctionType(fn.__code__, fn.__globals__, fn.__name__, fn.__defaults__, tuple(cells))

    def op(self, eng, fn, reads=(), writes=(), dma=None, sig=True):
        fn = self._freeze(fn)
        waits = {}
        reads = [x for r in reads for x in [r] + self.alias.get(r, [])]
        writes = [x for r in writes for x in [r] + self.alias.get(r, [])]
        writes = writes + [r for r in reads if r.startswith("ps") or r.startswith("ptb")]

        def need(w):
            if w[1] > waits.get(w[0], 0):
                waits[w[0]] = w[1]

        for r in reads:
            if r in self.res_w:
                need(self.res_w[r])
        for r in writes:
            if r in self.res_w:
                need(self.res_w[r])
            for sk, v in self.res_r.get(r, {}).items():
                need((sk, v))
        if dma is None:
            sk = self._sk(eng)
            inc = 1 if sig else 0
        else:
            sk = self._sk("d:" + str(dma))
            inc = 16
        self.cnt[sk] += inc
        val = self.cnt[sk] if inc else self.cnt[sk] + 1
        wl = []
        for k, v in waits.items():
            if k == "pe" and eng == "pe" and dma is None:
                continue
            if self.waited[eng].get(k, 0) >= v:
                continue
            self.waited[eng][k] = v
            wl.append((k, v))
        for r in writes:
            self.res_w[r] = (sk, val)
            self.res_r[r] = {}
        for r in reads:
            d = self.res_r.setdefault(r, {})
            if d.get(sk, 0) < val:
                d[sk] = val
        self.ops[eng].append((wl, fn, sk, inc))

    def barrier(self, engs=ENGS):
        for e in engs:
            wl = []
            for k, v in self.cnt.items():
                if v > 0 and self.waited[e].get(k, 0) < v:
                    self.waited[e][k] = v
                    wl.append((k, v))
            if wl:
                self.ops[e].append((wl, None, None, 0))

    def emit(self):
        nc = self.nc
        with nc.Block() as block:
            def run(e, eng):
                for wl, fn, sk, inc in self.ops[eng]:
                    for k, v in wl:
                        e.wait_ge(self.sem[k], v)
                    if fn is not None:
                        ins = fn(e)
                        if inc:
                            ins.then_inc(self.sem[sk], inc)

            @block.tensor
            def _(e):
                run(e, "pe")

            @block.scalar
            def _(e):
                run(e, "act")

            @block.vector
            def _(e):
                run(e, "dve")

            @block.gpsimd
            def _(e):
                run(e, "pool")

            @block.sync
            def _(e):
                run(e, "sp")
        self.ops = {e: [] for e in ENGS}


PP = {}
_o = 0
for _n, _w in (("gmix", 8), ("bdw", 8), ("lng", 8), ("lnb", 8), ("gffn", 8), ("wdw", 8 * 31), ("wfdw", 44 * 3),
               ("bfdw", 44)):
    PP[_n] = (_o, _w)
    _o += _w
NPP = _o


HALO = 4096


def build(MAIN):
    S_TOK = HALO + MAIN
    NSB = S_TOK // 2048
    NTILE = S_TOK // 128
    NG = MAIN // NT
    nc = bass.Bass("TRN2", target_bir_lowering=False)

    def din(name, shape, dt=F32):
        return nc.dram_tensor(name, list(shape), dt, kind="ExternalInput").ap()

    def dout(name, shape, dt=F32):
        return nc.dram_tensor(name, list(shape), dt, kind="ExternalOutput").ap()

    def dscr(name, shape, dt):
        return nc.dram_tensor(name, list(shape), dt, kind="Internal").ap()

    xp = din("xp", [S_TOK, D])
    xs = din("xs", [NS, D])
    sconv = din("sconv", [NS, 30, D])
    cks = [din("ck%d" % g, [NS, GROUPS[g][0], 256]) for g in range(3)]
    cvs = [din("cv%d" % g, [NS, GROUPS[g][0], 256]) for g in range(3)]
    sffn = din("sffn", [NS, 2, 2 * DFF])
    w_in = din("w_in", [D, 6400])
    w_co = din("w_co", [D, D])
    w_ao = din("w_ao", [256, D])
    w_out = din("w_out", [D, D])
    w_up = din("w_up", [D, 2 * DFF])
    w_dn = din("w_dn", [DFF, D])
    pp_d = din("pp", [128, NPP])
    gfin_d = din("gfin", [128, D])
    ident_d = din("ident", [128, 128])
    mask_d = din("mask2", [128, 2, 128])
    csp_d = din("csp", [128, NTILE, 16])
    snp_d = din("snp", [128, NTILE, 16])
    css_d = din("css", [NS, 16])
    sns_d = din("sns", [NS, 16])
    sel_d = din("sel", [NS, NS, 128])
    vld_d = [din("vld%d" % g, [128, NSB, GROUPS[g][1], 16 // GROUPS[g][1]]) for g in range(3)]
    hv_d = din("hv", [128, 1])

    wb_in = dscr("wb_in", [D, 6400], BF16)
    wb_co = dscr("wb_co", [D, D], BF16)
    wb_ao = dscr("wb_ao", [256, D], BF16)
    wb_out = dscr("wb_out", [D, D], BF16)
    wb_up = dscr("wb_up", [D, 2 * DFF], BF16)
    wb_dn = dscr("wb_dn", [DFF, D], BF16)
    import os as _os0
    DBG = bool(_os0.environ.get("KDBG"))
    oT_d = (dout if DBG else dscr)("oT_d", [64, 4, S_TOK], BF16)
    if DBG:
        dbg_sT = dout("dbg_sT", [128, 8, NT], BF16)
        dbg_mix = dout("dbg_mix", [128, 8, NT], BF16)
        dbg_xmid = dout("dbg_xmid", [NT, D], F32)
        dbg_acc = dout("dbg_acc", [128, 8, NT], F32)

    y_p = dout("y_p", [MAIN, D])
    y_s = dout("y_s", [NS, D])
    conv_p = dout("conv_p", [30, D])
    conv_s = dout("conv_s", [NS, 30, D])
    kp = [dout("k%d_p" % g, [min(GROUPS[g][0], S_TOK), 256]) for g in range(3)]
    vp = [dout("v%d_p" % g, [min(GROUPS[g][0], S_TOK), 256]) for g in range(3)]
    ks_o = [dout("k%d_s" % g, [NS, GROUPS[g][0], 256]) for g in range(3)]
    vs_o = [dout("v%d_s" % g, [NS, GROUPS[g][0], 256]) for g in range(3)]
    ffn_p = dout("ffn_p", [2, 2 * DFF])
    ffn_s = dout("ffn_s", [NS, 2, 2 * DFF])

    with contextlib.ExitStack() as gst:
        S = Sched(nc, gst)

        def sbuf(st, name, shape, dt):
            return st.enter_context(nc.sbuf_tensor("sb_" + name, list(shape), dt))

        NPS = 6
        PSB = [gst.enter_context(nc.psum_tensor("psb%d" % i, [128, 512], F32)) for i in range(NPS)]
        PTB = gst.enter_context(nc.psum_tensor("ptb", [128, 1024], BF16))
        PTB2 = gst.enter_context(nc.psum_tensor("ptb2", [128, 1024], BF16))
        ps_i = [0]

        def psum():
            i = ps_i[0] % NPS
            ps_i[0] += 1
            return PSB[i], "ps%d" % i

        pp = sbuf(gst, "pp", [128, NPP], F32)
        identf = sbuf(gst, "identf", [128, 128], F32)
        identb = sbuf(gst, "identb", [128, 128], BF16)
        onesb = sbuf(gst, "onesb", [128, 128], BF16)
        onesf = sbuf(gst, "onesf", [128, 128], F32)
        epst = sbuf(gst, "epst", [128, 1], F32)
        stat = sbuf(gst, "stat", [128, 3, 4 * NTILE + 16], F32)
        xsb = [sbuf(gst, "xsb%d" % i, [128, D], BF16) for i in range(2)]
        oTs = sbuf(gst, "oTs", [128, 2, NS], BF16)
        sts = sbuf(gst, "sts", [NS, 2304], F32)
        stat_i = [0]

        ps45 = [0]

        def psum45():
            i = 4 + ps45[0] % 2
            ps45[0] += 1
            return PSB[i], "ps%d" % i

        def P(name, c=None):
            o, w = PP[name]
            if c is None:
                return pp[:, o:o + w]
            return pp[:, o + c:o + c + 1]

        def dma(eng, out, in_, reads=(), writes=(), key=None):
            S.op(eng, lambda e: e.dma_start(out=out, in_=in_), reads=reads, writes=writes, dma=key)

        dma("sp", pp[:], pp_d, writes=["pp"], key="pp")
        dma("sp", identf[:], ident_d, writes=["identf"], key="identf")
        S.op("dve", lambda e: e.tensor_copy(out=identb[:], in_=identf[:]), reads=["identf"], writes=["identb"])
        S.op("dve", lambda e: e.memset(onesb[:], 1.0), writes=["onesb"])
        S.op("dve", lambda e: e.memset(onesf[:], 1.0), writes=["onesf"])
        S.op("dve", lambda e: e.memset(epst[:], EPS), writes=["epst"])
        S.op("dve", lambda e: e.memset(stat[:], 0.0), writes=["stat"])
        def conv_w(dst, src, rows, c0, c1, key, after=()):
            for r0 in range(0, rows, 256):
                r1 = min(rows, r0 + 256)
                dma("pool", dst[r0:r1, c0:c1], src[r0:r1, c0:c1], reads=list(after), writes=[key], key=key)
        for cg in (2, 4, 0, 1, 3):
            conv_w(wb_in, w_in, D, 2048 + 512 * cg, 2048 + min(512 * (cg + 1), 2304), "wb_qkv%d" % cg)
        QKV_ALL = ["wb_qkv%d" % cg for cg in range(5)]
        LATE_CASTS = [
            [],
            [lambda: conv_w(wb_in, w_in, D, 0, 2048, "wb_in1"), lambda: conv_w(wb_in, w_in, D, 4352, 6400, "wb_in3")],
            [lambda: conv_w(wb_co, w_co, D, 0, D, "wb_co"), lambda: conv_w(wb_ao, w_ao, 256, 0, D, "wb_ao"),
             lambda: conv_w(wb_out, w_out, D, 0, D, "wb_out"), lambda: conv_w(wb_up, w_up, D, 0, 2 * DFF, "wb_up")],
            [lambda: conv_w(wb_dn, w_dn, DFF, 0, D, "wb_dn")],
        ]
        def flat16(ap):
            return ap.rearrange("w c -> (w c)").rearrange("(a b) -> a b", a=16)
        SHIFTS = []
        for g in range(3):
            W = GROUPS[g][0]
            for (src, dst) in ((cks[g], ks_o[g]), (cvs[g], vs_o[g])):
                for s in range(NS):
                    SHIFTS.append((flat16(dst[s, 0:W - 1, :]), flat16(src[s, 1:W, :])))
        for s in range(NS):
            SHIFTS.append((flat16(conv_s[s, 0:29, :]), flat16(sconv[s, 1:30, :])))
            SHIFTS.append((flat16(ffn_s[s, 0:1, :]), flat16(sffn[s, 1:2, :])))

        import os as _os
        KSTOP = int(_os.environ.get("KSTOP", "9"))
        if KSTOP == 0:
            S.barrier()
            S.emit()
            return nc
        def norm_T(src_ap, rd, TT, dst3, dst_res, gain_name, xi):
            col = stat_i[0]
            stat_i[0] += 1
            xb = xsb[xi % 2]
            xr = "xsb%d" % (xi % 2)
            S.op("act", lambda e: e.activation(out=xb[0:TT, :], in_=src_ap, func=AF.Square,
                                               accum_out=stat[0:TT, 0, col:col + 1]),
                 reads=[rd], writes=[xr, "stat%d" % col])
            S.op("act", lambda e: e.activation(out=stat[0:TT, 1, col:col + 1], in_=stat[0:TT, 0, col:col + 1],
                                               func=AF.Sqrt, scale=1.0 / D, bias=epst[0:TT, 0:1]),
                 reads=["stat%d" % col, "epst"], writes=["stat%d" % col])
            S.op("dve", lambda e: e.reciprocal(out=stat[0:TT, 2, col:col + 1], in_=stat[0:TT, 1, col:col + 1]),
                 reads=["stat%d" % col], writes=["stat%d" % col])
            S.op("act", lambda e: e.activation(out=xb[0:TT, :], in_=src_ap, func=AF.Copy,
                                               scale=stat[0:TT, 2, col:col + 1]),
                 reads=[rd, "stat%d" % col], writes=[xr])
            for c in range(8):
                S.op("pe", lambda e, c=c: e.transpose(PTB[:, c * 128:c * 128 + TT], xb[0:TT, c * 128:(c + 1) * 128],
                                                      identb[0:TT, 0:TT]),
                     reads=[xr, "identb"], writes=["ptb"], sig=(c == 7))
            o, w = PP[gain_name]
            S.op("dve", lambda e: e.tensor_tensor(
                out=dst3, in0=PTB[:, :].rearrange("p (c t) -> p c t", c=8)[:, :, 0:TT],
                in1=pp[:, o:o + 8].unsqueeze(2).to_broadcast([128, 8, TT]), op=ALU.mult),
                reads=["ptb", "pp"], writes=[dst_res])
            return col

        with contextlib.ExitStack() as st1:
            hT1 = sbuf(st1, "hT1", [128, 8, 2048], BF16)
            hTs = sbuf(st1, "hTs", [128, 8, NS], BF16)
            QT = [sbuf(st1, "QT%d" % g, [128, 2, GROUPS[g][1], 2048 // GROUPS[g][1]], BF16) for g in range(3)]
            KT = [sbuf(st1, "KT%d" % g, [128, 2, GROUPS[g][1], 128 + 2048 // GROUPS[g][1]], BF16) for g in range(3)]
            NB = [1 + 16 // GROUPS[g][1] for g in range(3)]
            VV = [sbuf(st1, "VV%d" % g, [128, GROUPS[g][1], NB[g], 260], BF16) for g in range(3)]
            Wg = [sbuf(st1, "Wg%d" % i, [128, 8, 512], BF16) for i in range(1)]
            xt = [sbuf(st1, "xt%d" % i, [128, D], F32) for i in range(2)]
            stg = [sbuf(st1, "stg%d" % i, [128, 512], F32) for i in range(2)]
            qkb = [sbuf(st1, "qkb%d" % i, [128, 512], BF16) for i in range(2)]
            rtmp = [sbuf(st1, "rtmp%d" % i, [128, 24, 16], F32) for i in range(2)]
            vst = [sbuf(st1, "vst%d" % i, [128, 256], F32) for i in range(2)]
            PT2 = sbuf(st1, "PT2", [128, 16, 2, 128], BF16)
            PTs = [sbuf(st1, "PTs%d" % i, [128, 2, 2, 128], BF16) for i in range(2)]
            csp = sbuf(st1, "csp", [128, 16, 16], F32)
            snp = sbuf(st1, "snp", [128, 16, 16], F32)
            css = sbuf(st1, "css", [NS, 16], F32)
            sns = sbuf(st1, "sns", [NS, 16], F32)
            maskf = sbuf(st1, "maskf", [128, 2, 128], F32)
            vld = [sbuf(st1, "vld%d" % g, [128, GROUPS[g][1], 16 // GROUPS[g][1]], F32) for g in range(3)]
            maskb = sbuf(st1, "maskb", [128, 2, 128], BF16)
            rec = [sbuf(st1, "rec%d" % i, [64, 512], F32) for i in range(1)]
            oTsb = sbuf(st1, "oTsb", [64, 2048], BF16)
            selb = sbuf(st1, "selb", [NS, NS, 128], F32)
            Kc = [sbuf(st1, "Kc%d" % i, [128, 256], F32) for i in range(3)]
            Vc = [sbuf(st1, "Vc%d" % i, [128, 256], F32) for i in range(3)]
            prod = sbuf(st1, "prod", [128, 256], F32)
            sT4 = sbuf(st1, "sT4", [128, 24], F32)
            sm = sbuf(st1, "sm", [NS, 768], F32)
            pcur = sbuf(st1, "pcur", [NS, 12], F32)
            pcm = sbuf(st1, "pcm", [NS, NS, 12], F32)
            id4 = sbuf(st1, "id4", [NS, NS], F32)
            oTf = sbuf(st1, "oTf", [128, 2, NS], F32)

            for g in range(3):
                S.op("pool", lambda e, g=g: e.memset(VV[g][:, :, :, :], 1.0), writes=["VV%d" % g])
            dma("sp", css[:], css_d, writes=["css"], key="css")
            dma("sp", sns[:], sns_d, writes=["sns"], key="sns")
            dma("sp", maskf[:], mask_d, writes=["maskf"], key="maskf")
            dma("sp", selb[:], sel_d, writes=["selb"], key="selb")
            S.op("dve", lambda e: e.tensor_copy(out=maskb[:], in_=maskf[:]), reads=["maskf"], writes=["maskb"])
            S.op("dve", lambda e: e.tensor_copy(out=id4[:], in_=identf[0:NS, 0:NS]), reads=["identf"], writes=["id4"])

            def rope(stv, rd, TT, nh, cs_ap, sn_ap, tab_res, ri):
                rt = rtmp[ri % 2]
                rr = "rtmp%d" % (ri % 2)
                t1 = rt[0:TT, 0:nh, :]
                csb = cs_ap.unsqueeze(1).to_broadcast([TT, nh, 16])
                S.op("dve", lambda e: e.tensor_tensor(out=t1, in0=stv[:, :, 0:16], in1=csb, op=ALU.mult),
                     reads=[rd] + tab_res, writes=[rr])
                S.op("dve", lambda e: e.tensor_tensor(out=stv[:, :, 0:8], in0=stv[:, :, 0:8],
                                                      in1=sn_ap[:, 8:16].unsqueeze(1).to_broadcast([TT, nh, 8]),
                                                      op=ALU.mult), reads=[rd] + tab_res, writes=[rd])
                S.op("dve", lambda e: e.tensor_tensor(out=stv[:, :, 8:16], in0=stv[:, :, 8:16],
                                                      in1=sn_ap[:, 0:8].unsqueeze(1).to_broadcast([TT, nh, 8]),
                                                      op=ALU.mult), reads=[rd] + tab_res, writes=[rd])
                S.op("dve", lambda e: e.tensor_tensor(out=t1[:, :, 0:8], in0=t1[:, :, 0:8], in1=stv[:, :, 8:16],
                                                      op=ALU.add), reads=[rd, rr], writes=[rr])
                S.op("dve", lambda e: e.tensor_tensor(out=t1[:, :, 8:16], in0=t1[:, :, 8:16], in1=stv[:, :, 0:8],
                                                      op=ALU.add), reads=[rd, rr], writes=[rr])
                S.op("dve", lambda e: e.tensor_copy(out=stv[:, :, 0:16], in_=t1), reads=[rr], writes=[rd])

            xi = 0
            ri = 0
            ev = 0
            bset = [0]
            PTBf = PTB[:, :].bitcast(F32)
            PTB2f = PTB2[:, :].bitcast(F32)
            for _k in range(int(_os.environ.get("KDUMMY", "0"))):
                if _os.environ.get("KDUMMYT") == "memset":
                    S.op("dve", lambda e: e.memset(prod[:, :], 0.0), writes=["prod"])
                else:
                    S.op("dve", lambda e: e.tensor_tensor(out=prod[:, :], in0=prod[:, :], in1=prod[:, :], op=ALU.mult),
                         writes=["prod"])
            for sb in range(NSB):
                T0 = sb * 2048
                last = (sb == NSB - 1)
                if sb < len(LATE_CASTS):
                    for f_ in LATE_CASTS[sb]:
                        f_()
                dma("sp", csp[:], csp_d[:, 16 * sb:16 * (sb + 1), :], writes=["csp"], key="csp")
                for g in range(3):
                    dma("sp", vld[g][:, :, :], vld_d[g][:, sb, :, :], writes=["vld%d" % g], key="vld%d" % g)
                dma("sp", snp[:], snp_d[:, 16 * sb:16 * (sb + 1), :], writes=["snp"], key="snp")
                for i in range(16):
                    b = xi % 2
                    dma("sp", xt[b][:], xp[T0 + 128 * i:T0 + 128 * (i + 1), :], writes=["xt%d" % b], key="xt%d" % b)
                    norm_T(xt[b][:], "xt%d" % b, 128, hT1[:, :, 128 * i:128 * (i + 1)], "hT1_%d" % i, "gmix", xi)
                    xi += 1
                if sb == 1:
                    b = xi % 2
                    dma("sp", xt[b][0:NS, :], xs, writes=["xt%d" % b], key="xt%d" % b)
                    norm_T(xt[b][0:NS, :], "xt%d" % b, NS, hTs[:, :, 0:NS], "hTs", "gmix", xi)
                    xi += 1
                if KSTOP == 10:
                    S.barrier()
                    S.emit()
                    return nc
                hT_all = ["hT1_%d" % i for i in range(16)]
                if sb > 0:
                    for g in range(3):
                        M = 2048 // GROUPS[g][1]
                        S.op("pool", lambda e, g=g, M=M: e.tensor_copy(out=KT[g][:, :, :, 0:128],
                                                                        in_=KT[g][:, :, :, M:M + 128]),
                             reads=["KT%d" % g], writes=["KT%d" % g])
                        S.op("pool", lambda e, g=g: e.tensor_copy(out=VV[g][:, :, 0, :], in_=VV[g][:, :, NB[g] - 1, :]),
                             reads=["VV%d" % g], writes=["VV%d" % g])
                for cg in range(5):
                    if sb == 0 and cg in (0, 1, 3):
                        continue
                    wcols = 512 if cg < 4 else 256
                    wb = Wg[0]
                    wr = "Wg0"
                    dma("sp", wb[:, :, 0:wcols],
                        wb_in[:, 2048 + 512 * cg:2048 + 512 * cg + wcols].rearrange("(kc p) n -> p kc n", p=128),
                        reads=["wb_qkv%d" % cg], writes=[wr], key=wr)
                    if sb == 1:
                        pst, pr = psum()
                        for h0 in range(0, wcols, 256):
                            for kc in range(8):
                                S.op("pe", lambda e, kc=kc, pst=pst, h0=h0: e.matmul(
                                    pst[0:NS, h0:h0 + 256], lhsT=hTs[:, kc, 0:NS], rhs=wb[:, kc, h0:h0 + 256],
                                    start=(kc == 0), stop=(kc == 7)), reads=["hTs", wr], writes=[pr], sig=(kc == 7))
                        for (c0, c1) in ((0, 256), (256, 512)):
                            if c0 >= wcols:
                                continue
                            gcol = 512 * cg + c0
                            sc = 0.125 if gcol < 768 else 1.0
                            S.op("act", lambda e, c0=c0, c1=c1, pst=pst, sc=sc, gcol=gcol: e.activation(
                                out=sts[0:NS, gcol:gcol + 256], in_=pst[0:NS, c0:c1], func=AF.Copy, scale=sc),
                                reads=[pr], writes=["sts"])
                    if KSTOP == 20:
                        S.barrier()
                        S.emit()
                        return nc
                    if cg < 3:
                        for i in range(16):
                            pst, pr = psum()
                            for kc in range(8):
                                S.op("pe", lambda e, kc=kc, pst=pst, i=i: e.matmul(
                                    pst[:, 0:512], lhsT=hT1[:, kc, 128 * i:128 * (i + 1)], rhs=wb[:, kc, 0:512],
                                    start=(kc == 0), stop=(kc == 7)), reads=["hT1_%d" % i, wr], writes=[pr], sig=(kc == 7))
                            sg = stg[ev % 2]
                            sr = "stg%d" % (ev % 2)
                            qb_ = qkb[ev % 2]
                            qr = "qkb%d" % (ev % 2)
                            ev += 1
                            for (c0, c1) in ((0, 256), (256, 512)):
                                sc = 0.125 if (512 * cg + c0) < 768 else 1.0
                                S.op("act", lambda e, c0=c0, c1=c1, pst=pst, sc=sc, sg=sg: e.activation(
                                    out=sg[:, c0:c1], in_=pst[:, c0:c1], func=AF.Copy, scale=sc),
                                    reads=[pr], writes=[sr])
                            ti = sb * 16 + i
                            if not _os.environ.get("KNOROPE"):
                              rope(sg[:, :].rearrange("p (h d) -> p h d", d=64), sr, 128, 8, csp[:, i, :], snp[:, i, :],
                                 ["csp", "snp"], ri)
                            ri += 1
                            S.op("pool", lambda e, sg=sg, qb_=qb_: e.tensor_copy(out=qb_[:, :], in_=sg[:, :]),
                                 reads=[sr], writes=[qr])
                            if last and KSTOP != 24:
                                for g in range(3):
                                    W = min(GROUPS[g][0], S_TOK)
                                    kcol = 768 + 256 * g
                                    if not (512 * cg <= kcol < 512 * cg + 512):
                                        continue
                                    tpos = 2048 - 128 * (16 - i)
                                    if S_TOK - (T0 + 128 * i) > W:
                                        continue
                                    row0 = W - (S_TOK - (T0 + 128 * i))
                                    lc = kcol - 512 * cg
                                    dma("sp", kp[g][row0:row0 + 128, :], sg[:, lc:lc + 256], reads=[sr], key="o" + sr)
                            for j in range(4):
                                S.op("pe", lambda e, j=j, qb_=qb_: e.transpose((PTB if j < 2 else PTB2)[:, j * 128:(j + 1) * 128],
                                                                                qb_[:, j * 128:(j + 1) * 128], identb[:, :]),
                                     reads=[qr, "identb"], writes=["ptb" if j < 2 else "ptb2"], sig=(j % 2 == 1))
                            for j in range(4):
                                if _os.environ.get("KNOEVAC"):
                                    continue
                                cc = cg * 4 + j
                                isq = cc < 6
                                g = (cc % 6) // 2
                                half = cc % 2
                                d = GROUPS[g][1]
                                mlo = 128 * i // d
                                cnt = 128 // d
                                if isq:
                                    dst = QT[g][:, half, :, mlo:mlo + cnt]
                                    dres = "QT%d" % g
                                else:
                                    dst = KT[g][:, half, :, 128 + mlo:128 + mlo + cnt]
                                    dres = "KT%d" % g
                                src = (PTB if j < 2 else PTB2)[:, j * 128:(j + 1) * 128].rearrange("p (m r) -> p r m", r=d)
                                if j < 2:
                                    S.op("act", lambda e, dst=dst, src=src: e.activation(out=dst, in_=src, func=AF.Copy),
                                         reads=["ptb"], writes=[dres])
                                else:
                                    S.op("dve", lambda e, dst=dst, src=src: e.tensor_copy(out=dst, in_=src),
                                         reads=["ptb2"], writes=[dres])
                            if KSTOP == 21:
                                S.barrier()
                                S.emit()
                                return nc
                            if (KSTOP in (22, 24) and cg == 2 and i == 15) or (KSTOP == 25 and i == 1) or (KSTOP == 33 and cg == 1 and i == 14) or (KSTOP == 31 and cg == 1 and i == 1) or (KSTOP == 32 and cg == 1 and i == 7) or (KSTOP == 29 and cg == 1 and i == 15) or (KSTOP == 30 and cg == 2 and i == 0) or (KSTOP == 28 and cg == 1 and i == 0) or (KSTOP == 26 and i == 7) or (KSTOP == 27 and cg == 0 and i == 15):
                                S.barrier()
                                S.emit()
                                return nc
                    else:
                        jobs = []
                        if cg == 3:
                            for blk in range(16):
                                jobs.append((0, 0, blk, 0, hT1[:, :, 128 * blk:128 * (blk + 1)], ["hT1_%d" % blk]))
                            for r in range(4):
                                for blk in range(4):
                                    jobs.append((1, r, blk, 256, hT1[:, :, 512 * blk + r:512 * (blk + 1):4],
                                                 ["hT1_%d" % t for t in range(4 * blk, 4 * blk + 4)]))
                        else:
                            for r in range(16):
                                jobs.append((2, r, 0, 0, hT1[:, :, r:2048:16], hT_all))
                        for (g, r, blk, c0, lh, lres) in jobs:
                            pst, pr = psum()
                            for kc in range(8):
                                S.op("pe", lambda e, kc=kc, pst=pst, lh=lh, c0=c0: e.matmul(
                                    pst[:, 0:256], lhsT=lh[:, kc, :], rhs=wb[:, kc, c0:c0 + 256],
                                    start=(kc == 0), stop=(kc == 7)), reads=lres + [wr], writes=[pr], sig=(kc == 7))
                            S.op("act", lambda e, pst=pst, g=g, r=r, blk=blk: e.activation(
                                out=VV[g][:, r, 1 + blk, :].rearrange("p (s e) -> p s e", e=65)[:, :, 0:64],
                                in_=pst[:, 0:256].rearrange("p (s e) -> p s e", e=64), func=AF.Copy,
                                scale=vld[g][:, r, blk:blk + 1]),
                                reads=[pr, "vld%d" % g], writes=["VV%d" % g])
                            S.op("pool", lambda e, g=g, r=r, blk=blk: e.tensor_copy(
                                out=VV[g][:, r, 1 + blk, :].rearrange("p (s e) -> p s e", e=65)[:, :, 64:65],
                                in_=vld[g][:, r, blk:blk + 1].unsqueeze(1).to_broadcast([128, 4, 1])),
                                reads=["vld%d" % g], writes=["VV%d" % g])
                            if last:
                                W = min(GROUPS[g][0], S_TOK)
                                d = GROUPS[g][1]
                                nblk = 16 // d
                                first_tok = T0 + r + d * 128 * blk
                                if S_TOK - (T0 + d * 128 * blk) <= W:
                                    row0 = W - (S_TOK - first_tok)
                                    vb = vst[ev % 2]
                                    vr = "vst%d" % (ev % 2)
                                    ev += 1
                                    S.op("dve", lambda e, pst=pst, vb=vb: e.tensor_copy(out=vb[:, :], in_=pst[:, 0:256]),
                                         reads=[pr], writes=[vr])
                                    dma("sp", vp[g][row0:W:d, :], vb[:, :], reads=[vr], key="o" + vr)
                if KSTOP == 23 and False:
                    S.barrier()
                    S.emit()
                    return nc
                if KSTOP == 11:
                    S.barrier()
                    S.emit()
                    return nc
                if sb == 1:
                    rope(sts[0:NS, 0:1536].rearrange("p (h d) -> p h d", d=64), "sts", NS, 24, css[:, :], sns[:, :],
                         ["css", "sns"], ri)
                    ri += 1
                    for g in range(3):
                        W = GROUPS[g][0]
                        dma("sp", ks_o[g][:, W - 1, :], sts[0:NS, 768 + 256 * g:768 + 256 * (g + 1)], reads=["sts"],
                            key="osts")
                        dma("sp", vs_o[g][:, W - 1, :], sts[0:NS, 1536 + 256 * g:1536 + 256 * (g + 1)], reads=["sts"],
                            key="osts")
                    S.op("dve", lambda e: e.tensor_tensor(out=sm[0:NS, 0:768], in0=sts[0:NS, 0:768],
                                                          in1=sts[0:NS, 768:1536], op=ALU.mult),
                         reads=["sts"], writes=["sm"])
                    S.op("dve", lambda e: e.tensor_reduce(out=pcur[:, :],
                                                          in_=sm[0:NS, 0:768].rearrange("p (h d) -> p h d", d=64),
                                                          axis=AX.X, op=ALU.add), reads=["sm"], writes=["pcur"])
                    S.op("act", lambda e: e.activation(out=pcur[:, :], in_=pcur[:, :], func=AF.Exp),
                         reads=["pcur"], writes=["pcur"])
                    S.op("dve", lambda e: e.tensor_tensor(out=pcm[:, :, :],
                                                          in0=pcur[:, :].unsqueeze(1).to_broadcast([NS, NS, 12]),
                                                          in1=id4[:, :].unsqueeze(2).to_broadcast([NS, NS, 12]),
                                                          op=ALU.mult), reads=["pcur", "id4"], writes=["pcm"])
                    for s in range(NS):
                        pnum, pnr = psum()
                        for g in range(3):
                            W, d = GROUPS[g]
                            kb_, vb_ = Kc[g], Vc[g]
                            kr, vr = "Kc%d" % g, "Vc%d" % g
                            dma("sp", kb_[:, :], cks[g][s, 0:W:d, :], writes=[kr], key=kr)
                            dma("sp", vb_[:, :], cvs[g][s, 0:W:d, :], writes=[vr], key=vr)
                            pq, pqr = psum()
                            S.op("pe", lambda e, pq=pq, s=s, g=g: e.matmul(pq[:, 0:256], lhsT=selb[0:NS, s, :],
                                                                           rhs=sts[0:NS, 256 * g:256 * (g + 1)],
                                                                           start=True, stop=True),
                                 reads=["selb", "sts"], writes=[pqr])
                            S.op("dve", lambda e, pq=pq, kb_=kb_: e.tensor_tensor(out=prod[:, :], in0=kb_[:, :],
                                                                                    in1=pq[:, 0:256], op=ALU.mult),
                                 reads=[kr, pqr], writes=["prod"])
                            S.op("dve", lambda e, g=g: e.tensor_reduce(out=sT4[:, 4 * g:4 * g + 4],
                                                                  in_=prod[:, :].rearrange("p (h d) -> p h d", d=64),
                                                                  axis=AX.X, op=ALU.add), reads=["prod"], writes=["sT4"])
                        S.op("act", lambda e: e.activation(out=sT4[:, 12:24], in_=sT4[:, 0:12], func=AF.Exp),
                             reads=["sT4"], writes=["sT4"])
                        for c in range(3):
                            for g in range(3):
                                pT_ = sT4[:, 12 + 4 * g:16 + 4 * g]
                                pc_ = pcm[0:NS, s, 4 * g:4 * g + 4]
                                if c < 2:
                                    l1 = Vc[g][:, 128 * c:128 * (c + 1)]
                                    l2 = sts[0:NS, 1536 + 256 * g + 128 * c:1536 + 256 * g + 128 * (c + 1)]
                                    r1 = ["Vc%d" % g, "sT4"]
                                else:
                                    l1 = onesf[:, :]
                                    l2 = onesf[0:NS, :]
                                    r1 = ["onesf", "sT4"]
                                S.op("pe", lambda e, c=c, g=g, pnum=pnum, l1=l1, pT_=pT_: e.matmul(
                                    pnum[:, 4 * c:4 * c + 4], lhsT=l1, rhs=pT_, start=(g == 0), stop=False),
                                    reads=r1, writes=[pnr])
                                S.op("pe", lambda e, c=c, g=g, pnum=pnum, l2=l2, pc_=pc_: e.matmul(
                                    pnum[:, 4 * c:4 * c + 4], lhsT=l2, rhs=pc_, start=False, stop=(g == 2)),
                                    reads=["sts", "pcm", "onesf"], writes=[pnr], sig=(g == 2))
                        S.op("dve", lambda e, pnum=pnum: e.reciprocal(out=sT4[:, 0:4], in_=pnum[:, 8:12]),
                             reads=[pnr], writes=["sT4"])
                        for c in range(2):
                            for hf in range(2):
                                slot = 2 * c + hf
                                S.op("dve", lambda e, c=c, hf=hf, slot=slot, pnum=pnum, s=s: e.tensor_tensor(
                                    out=oTf[64 * hf:64 * (hf + 1), c, s:s + 1],
                                    in0=pnum[64 * hf:64 * (hf + 1), 4 * c + slot:4 * c + slot + 1],
                                    in1=sT4[64 * hf:64 * (hf + 1), slot:slot + 1], op=ALU.mult),
                                    reads=[pnr, "sT4"], writes=["oTf"])
                    S.op("dve", lambda e: e.tensor_copy(out=oTs[:, :, :], in_=oTf[:, :, :]), reads=["oTf"], writes=["oTs"])
                    if DBG:
                        d1 = dout("dbg_oTf", [128, 2, NS], F32)
                        dma("sp", d1, oTf[:, :, :], reads=["oTf"], key="dbg")
                        d2 = dout("dbg_sts", [NS, 2304], F32)
                        dma("sp", d2, sts[:, :], reads=["sts"], key="dbg")
                        d3 = dout("dbg_pcur", [NS, 12], F32)
                        dma("sp", d3, pcur[:, :], reads=["pcur"], key="dbg")

                if KSTOP == 12:
                    S.barrier()
                    S.emit()
                    return nc
                if DBG and sb == 0:
                    for g in range(3):
                        dq = dout("dbg_QT%d" % g, [128, 2, GROUPS[g][1], 2048 // GROUPS[g][1]], BF16)
                        dk = dout("dbg_KT%d" % g, [128, 2, GROUPS[g][1], 128 + 2048 // GROUPS[g][1]], BF16)
                        dv = dout("dbg_VV%d" % g, [128, GROUPS[g][1], NB[g], 260], BF16)
                        dma("sp", dq, QT[g][:, :, :, :], reads=["QT%d" % g], key="dbg")
                        dma("sp", dk, KT[g][:, :, :, :], reads=["KT%d" % g], key="dbg")
                        dma("sp", dv, VV[g][:, :, :, :], reads=["VV%d" % g], key="dbg")
                pti = 0
                if sb == 0:
                    continue
                ch_list = (3,) if sb == 1 else (0, 1, 2, 3)
                for slot in range(4):
                    c2 = slot // 2
                    pb = 64 * (slot % 2)
                    for rp in range(8):
                        pss, psr = psum45()
                        for q in range(2):
                            r = 2 * rp + q
                            if sb > 0:
                                S.op("pe", lambda e, pss=pss, q=q, r=r: e.matmul(
                                    pss[:, 256 * q:256 * q + 128], lhsT=KT[2][pb:pb + 64, c2, r, 0:128],
                                    rhs=QT[2][pb:pb + 64, c2, r, 0:128], start=True, stop=True),
                                    reads=["KT2", "QT2"], writes=[psr])
                            S.op("pe", lambda e, pss=pss, q=q, r=r: e.matmul(
                                pss[:, 256 * q + 128:256 * q + 256], lhsT=KT[2][pb:pb + 64, c2, r, 128:256],
                                rhs=QT[2][pb:pb + 64, c2, r, 0:128], start=True, stop=True),
                                reads=["KT2", "QT2"], writes=[psr])
                        k0 = 0 if sb > 0 else 1
                        S.op("act", lambda e, pss=pss, rp=rp, k0=k0: e.activation(
                            out=PT2[:, 2 * rp:2 * rp + 2, k0:2, :],
                            in_=pss[:, :].rearrange("p (q k t) -> p q k t", q=2, k=2)[:, :, k0:2, :], func=AF.Exp),
                            reads=[psr], writes=["PT2"])
                        S.op("dve", lambda e, rp=rp, k0=k0: e.tensor_tensor(
                            out=PT2[:, 2 * rp:2 * rp + 2, k0:2, :], in0=PT2[:, 2 * rp:2 * rp + 2, k0:2, :],
                            in1=maskb[:, k0:2, :].unsqueeze(1).to_broadcast([128, 2, 2 - k0, 128]), op=ALU.mult),
                            reads=["PT2", "maskb"], writes=["PT2"])
                    if DBG and sb == 0 and slot == 0:
                        dp2 = dout("dbg_PT2", [128, 16, 2, 128], BF16)
                        dma("sp", dp2, PT2[:, :, :, :], reads=["PT2"], key="dbg")
                    pendB = [None]
                    for ch in ch_list:
                        bset[0] += 1
                        if bset[0] % 2:
                            Bk = [(PSB[i_], "ps%d" % i_) for i_ in range(3)]
                        else:
                            Bk = [(PSB[3], "ps3"), (PTBf, "ptb"), (PTB2f, "ptb2")]
                        for g in range(2):
                            for pair in range(2):
                                pss, psr = psum45()
                                ptb_ = PTs[pti % 2]
                                ptr = "PTs%d" % (pti % 2)
                                pti += 1
                                info = []
                                for q in range(2):
                                    u = 2 * pair + q
                                    if g == 0:
                                        qb_i = 4 * ch + u
                                        hasp = (sb > 0) or (qb_i > 0)
                                        kprev = KT[0][pb:pb + 64, c2, 0, 128 * qb_i:128 * qb_i + 128]
                                        kcur = KT[0][pb:pb + 64, c2, 0, 128 + 128 * qb_i:256 + 128 * qb_i]
                                        qq = QT[0][pb:pb + 64, c2, 0, 128 * qb_i:128 * qb_i + 128]
                                        vprev = VV[0][:, 0, qb_i, 65 * slot:65 * (slot + 1)]
                                        vcur = VV[0][:, 0, qb_i + 1, 65 * slot:65 * (slot + 1)]
                                        ocols = slice(128 * u, 128 * (u + 1))
                                    else:
                                        hasp = (sb > 0) or (ch > 0)
                                        kprev = KT[1][pb:pb + 64, c2, u, 128 * ch:128 * ch + 128]
                                        kcur = KT[1][pb:pb + 64, c2, u, 128 + 128 * ch:256 + 128 * ch]
                                        qq = QT[1][pb:pb + 64, c2, u, 128 * ch:128 * ch + 128]
                                        vprev = VV[1][:, u, ch, 65 * slot:65 * (slot + 1)]
                                        vcur = VV[1][:, u, ch + 1, 65 * slot:65 * (slot + 1)]
                                        ocols = slice(128 * u, 128 * (u + 1))
                                    if hasp:
                                        S.op("pe", lambda e, pss=pss, q=q, kprev=kprev, qq=qq: e.matmul(
                                            pss[:, 256 * q:256 * q + 128], lhsT=kprev, rhs=qq, start=True, stop=True),
                                            reads=["KT%d" % g, "QT%d" % g], writes=[psr])
                                    S.op("pe", lambda e, pss=pss, q=q, kcur=kcur, qq=qq: e.matmul(
                                        pss[:, 256 * q + 128:256 * q + 256], lhsT=kcur, rhs=qq, start=True, stop=True),
                                        reads=["KT%d" % g, "QT%d" % g], writes=[psr])
                                    info.append((hasp, vprev, vcur, ocols))
                                allp = info[0][0] and info[1][0]
                                k0 = 0 if allp else 1
                                if (not allp) and (info[0][0] or info[1][0]):
                                    S.op("act", lambda e, pss=pss, ptb_=ptb_: e.activation(
                                        out=ptb_[:, 1, 0, :], in_=pss[:, 256:384], func=AF.Exp), reads=[psr], writes=[ptr])
                                    S.op("dve", lambda e, ptb_=ptb_: e.tensor_tensor(
                                        out=ptb_[:, 1, 0, :], in0=ptb_[:, 1, 0, :], in1=maskb[:, 0, :], op=ALU.mult),
                                        reads=[ptr, "maskb"], writes=[ptr])
                                S.op("act", lambda e, pss=pss, ptb_=ptb_, k0=k0: e.activation(
                                    out=ptb_[:, :, k0:2, :],
                                    in_=pss[:, :].rearrange("p (q k t) -> p q k t", q=2, k=2)[:, :, k0:2, :], func=AF.Exp),
                                    reads=[psr], writes=[ptr])
                                S.op("dve", lambda e, ptb_=ptb_, k0=k0: e.tensor_tensor(
                                    out=ptb_[:, :, k0:2, :], in0=ptb_[:, :, k0:2, :],
                                    in1=maskb[:, k0:2, :].unsqueeze(1).to_broadcast([128, 2, 2 - k0, 128]), op=ALU.mult),
                                    reads=[ptr, "maskb"], writes=[ptr])
                                if DBG and sb == 0 and slot == 0 and ch == 0:
                                    dpt = dout("dbg_PT%d_%d" % (g, pair), [128, 2, 2, 128], BF16)
                                    dma("sp", dpt, ptb_[:, :, :, :], reads=[ptr], key="dbg")

                                def emitB(info=info, ptb_=ptb_, ptr=ptr, g=g, Bk=Bk):
                                    for q in range(2):
                                        hasp, vprev, vcur, ocols = info[q]
                                        pO, pOr = Bk[g]
                                        kbl = (0, 1) if hasp else (1,)
                                        for kb in kbl:
                                            vv = vprev if kb == 0 else vcur
                                            S.op("pe", lambda e, pO=pO, vv=vv, ptb_=ptb_, q=q, kb=kb, ocols=ocols, kbl=kbl: e.matmul(
                                                pO[0:65, ocols], lhsT=vv, rhs=ptb_[:, q, kb, :], start=(kb == kbl[0]),
                                                stop=(kb == 1)), reads=["VV%d" % g, ptr], writes=[pOr])
                                if pendB[0] is not None:
                                    pendB[0]()
                                pendB[0] = emitB
                        pendB[0]()
                        pendB[0] = None
                        pO, pOr = Bk[2]
                        for r in range(16):
                            kbs = (0, 1) if sb > 0 else (1,)
                            for kb in kbs:
                                S.op("pe", lambda e, pO=pO, r=r, kb=kb, kbs=kbs: e.matmul(
                                    pO[0:65, 32 * r:32 * (r + 1)], lhsT=VV[2][:, r, kb, 65 * slot:65 * (slot + 1)],
                                    rhs=PT2[:, r, kb, 32 * ch:32 * (ch + 1)], start=(kb == kbs[0]), stop=(kb == 1)),
                                    reads=["VV2", "PT2"], writes=[pOr], sig=(kb == 1))
                        if DBG and sb == 0 and slot == 0 and ch == 0:
                            for gq in range(3):
                                db = dout("dbg_B%d" % gq, [65, 512], F32)
                                S.op("dve", lambda e, gq=gq: e.tensor_copy(out=stg[1][0:65, :], in_=Bk[gq][0][0:65, :]),
                                     reads=[Bk[gq][1]], writes=["stg1"])
                                dma("sp", db, stg[1][0:65, :], reads=["stg1"], key="dbg")
                        tmpb = stg[0]
                        S.op("act", lambda e, tmpb=tmpb, b0=Bk[0][0]: e.activation(out=tmpb[0:65, :], in_=b0[0:65, :], func=AF.Copy),
                             reads=[Bk[0][1]], writes=["stg0"])
                        S.op("dve", lambda e, tmpb=tmpb, b1=Bk[1][0]: e.tensor_tensor(
                            out=tmpb[0:65, :].rearrange("p (j r) -> p j r", r=4),
                            in0=tmpb[0:65, :].rearrange("p (j r) -> p j r", r=4),
                            in1=b1[0:65, :].rearrange("p (r j) -> p j r", r=4), op=ALU.add),
                            reads=[Bk[1][1], "stg0"], writes=["stg0"])
                        S.op("dve", lambda e, tmpb=tmpb, b2=Bk[2][0]: e.tensor_tensor(
                            out=tmpb[0:65, :].rearrange("p (j r) -> p j r", r=16),
                            in0=tmpb[0:65, :].rearrange("p (j r) -> p j r", r=16),
                            in1=b2[0:65, :].rearrange("p (r j) -> p j r", r=16), op=ALU.add),
                            reads=[Bk[2][1], "stg0"], writes=["stg0"])
                        S.op("dve", lambda e, tmpb=tmpb: e.tensor_scalar(out=tmpb[64:65, :], in0=tmpb[64:65, :], scalar1=1e-30,
                                                                        scalar2=None, op0=ALU.add),
                             reads=["stg0"], writes=["stg0"])
                        S.op("dve", lambda e, tmpb=tmpb: e.reciprocal(out=stg[1][64:65, :], in_=tmpb[64:65, :]),
                             reads=["stg0"], writes=["stg1"])
                        pD, pDr = Bk[0]
                        S.op("pe", lambda e, pD=pD: e.matmul(pD[0:64, :], lhsT=onesf[64:65, 0:64], rhs=stg[1][64:65, :],
                                                             start=True, stop=True), reads=["onesf", "stg1"], writes=[pDr])
                        pO, pOr = None, "stg0"
                        rc = rec[0]
                        rcr = "rec0"
                        S.op("dve", lambda e, pD=pD, tmpb=tmpb, slot=slot, ch=ch: e.tensor_tensor(
                            out=oTsb[:, 512 * ch:512 * (ch + 1)], in0=tmpb[0:64, :], in1=pD[0:64, :], op=ALU.mult),
                            reads=[pDr, "stg0"], writes=["oTsb"])
                    dma("sp", oT_d[:, slot, T0:T0 + 2048], oTsb[:, :], reads=["oTsb"], writes=["oT_d"], key="oT_d")
            for sb_ in range(NSB, len(LATE_CASTS)):
                for f_ in LATE_CASTS[sb_]:
                    f_()
            S.barrier()
            S.emit()
        if KSTOP == 1:
            return nc

        with contextlib.ExitStack() as st2:
            xg = sbuf(st2, "xg", [128, NT // 128, D], F32)
            hT = sbuf(st2, "hT", [128, 8, NT], BF16)
            uext = sbuf(st2, "uext", [128, 8, 30 + NT], BF16)
            dgb = [sbuf(st2, "dgb%d" % i, [128, 31, 128], BF16) for i in range(2)]
            big2 = sbuf(st2, "big2", [128, 12 * NT], F32)
            acc = big2[:, 0:8 * NT].rearrange("p (c t) -> p c t", c=8)
            lnm = big2[:, 8 * NT:12 * NT].rearrange("p (c t) -> p c t", c=4)
            aT = big2[:, 0:11 * NT].bitcast(BF16).rearrange("p (c t) -> p c t", c=22)
            R3 = sbuf(st2, "R3", [128, 11264], F32)
            wdnb = R3[:, :].bitcast(BF16).rearrange("p (c t) -> p c t", c=22)
            sT = R3[:, 0:2048].bitcast(BF16).rearrange("p (c t) -> p c t", c=8)
            mixT = R3[:, 2048:4096].bitcast(BF16).rearrange("p (c t) -> p c t", c=8)
            woutb = R3[:, 4096:8192].bitcast(BF16).rearrange("p (c t) -> p c t", c=8)
            oTg = R3[0:64, 8192:9216].bitcast(BF16).rearrange("p (c t) -> p c t", c=4)
            cb16 = [R3[:, 9216 + 256 * i:9472 + 256 * i].bitcast(BF16) for i in range(2)]
            csq16 = [R3[:, 9728 + 256 * i:9984 + 256 * i].bitcast(BF16) for i in range(2)]
            tt = [R3[:, 10240 + 512 * i:10752 + 512 * i] for i in range(2)]
            S.alias["wdnb"] = ["sT", "mixT", "woutb", "oTg", "cb16_0", "cb16_1", "csq16_0", "csq16_1", "tt0", "tt1"]
            S.alias["aT"] = ["acc%d" % c for c in range(8)] + ["lnm"]
            S.alias["uh"] = ["uext"] + ["uext%d" % c for c in range(8)]
            S.alias["uprod"] = S.alias["uh"]
            wao64 = sbuf(st2, "wao64", [64, 4, D], BF16)
            wao128 = sbuf(st2, "wao128", [128, 2, D], BF16)
            NWB = 3
            wt = [sbuf(st2, "wt%d" % i, [128, 8, 256], BF16) for i in range(NWB)]
            sg_ = [sbuf(st2, "sg%d" % i, [128, NT], F32) for i in range(3)]
            upx = [sbuf(st2, "upx%d" % i, [128, NT + 2], F32) for i in range(2)]
            cgb = [sbuf(st2, "cgb%d" % i, [128, NT], F32) for i in range(2)]
            fh = sbuf(st2, "fh", [128, 44, 2], F32)
            gfin = sbuf(st2, "gfin", [128, D], F32)
            yt = [sbuf(st2, "yt%d" % i, [128, D], F32) for i in range(1)]
            uflat = uext[:, :, :].rearrange("p c t -> p (c t)")[:, 0:4336].bitcast(F32)
            uh = uflat[:, 0:8 * NS * 31].rearrange("p (c s j) -> p c s j", c=8, s=NS)
            uprod = uflat[:, 8 * NS * 31:16 * NS * 31].rearrange("p (c s j) -> p c s j", c=8, s=NS)
            fhs = sbuf(st2, "fhs", [128, 44, 2 * NS], F32)
            upn = sbuf(st2, "upn", [128, 44, NS], F32)
            orow = sbuf(st2, "orow", [30, D], F32)

            dma("sp", gfin[:], gfin_d, writes=["gfin"], key="gfin")
            dma("sp", wao64[:], wb_ao.rearrange("(s d) n -> d s n", d=64), reads=["wb_ao"], writes=["wao"], key="wao64")
            dma("sp", wao128[:], wb_ao.rearrange("(c p) n -> p c n", p=128), reads=["wb_ao"], writes=["wao"], key="wao128")
            S.op("pool", lambda e: e.memset(uext[:, :, 0:30], 0.0), writes=["uext"])
            S.op("pool", lambda e: e.memset(fh[:], 0.0), writes=["fh"])

            wi = [0]

            def wtile(src_ap, rd):
                i = wi[0] % NWB
                wi[0] += 1
                dma("sp", wt[i][:, :, :], src_ap.rearrange("(kc p) n -> p kc n", p=128), reads=[rd],
                    writes=["wt%d" % i], key="wt%d" % i)
                return wt[i], "wt%d" % i

            tgl = [0]

            def alt(a, b):
                tgl[0] += 1
                return a if tgl[0] % 2 else b

            ygi = [0]

            prevN = [NT]

            def group(t0, TT_list, N, sample, first=False, lastg=False, out0=None):
                gi = 1
                for (i, TT) in TT_list:
                    src = xs if sample else xp[t0 + 128 * i:t0 + 128 * i + TT, :]
                    dma("sp", xg[0:TT, i, :], src, writes=["xg%d" % i], key="xg%d" % i)
                    norm_T(xg[0:TT, i, :], "xg%d" % i, TT, hT[:, :, 128 * i:128 * i + TT], "hT", "gmix", ygi[0])
                    ygi[0] += 1
                if not sample:
                    dma("sp", oTg[:, :, 0:N], oT_d[:, :, t0:t0 + N], reads=["oT_d"], writes=["oTg"], key="oTg")
                dma("sp", woutb[:, :, :], wb_out.rearrange("(kc p) n -> p kc n", p=128), reads=["wb_out"],
                    writes=["woutb"], key="woutb")
                if sample:
                    for s in range(NS):
                        dma("sp", orow[:, :], sconv[s, :, :], writes=["orow"], key="scv")
                        for c in range(8):
                            pst, pr = psum()
                            S.op("pe", lambda e, c=c, pst=pst: e.transpose(pst[:, 0:30], orow[0:30, 128 * c:128 * (c + 1)],
                                                                            identf[0:30, 0:30]),
                                 reads=["orow", "identf"], writes=[pr])
                            S.op("act", lambda e, c=c, pst=pst, s=s: e.activation(out=uh[:, c, s, 0:30], in_=pst[:, 0:30],
                                                                                  func=AF.Copy), reads=[pr], writes=["uh"])
                elif not first:
                    pN = prevN[0]
                    S.op("pool", lambda e: e.tensor_copy(out=uext[:, :, 0:30], in_=uext[:, :, pN:pN + 30]),
                         reads=["uext"], writes=["uext"])
                if not sample:
                    prevN[0] = N
                for c in range(8):
                    w, wr = wtile(wb_in[:, 256 * c:256 * (c + 1)], "wb_in1")
                    pl, plr = psum()
                    pg, pgr = psum()
                    for kc in range(8):
                        S.op("pe", lambda e, kc=kc, w=w, pl=pl: e.matmul(pl[:, 0:N], lhsT=w[:, kc, 0:128], rhs=hT[:, kc, 0:N],
                                                                         start=(kc == 0), stop=(kc == 7)),
                             reads=[wr, "hT"], writes=[plr], sig=(kc == 7))
                    for kc in range(8):
                        S.op("pe", lambda e, kc=kc, w=w, pg=pg: e.matmul(pg[:, 0:N], lhsT=w[:, kc, 128:256], rhs=hT[:, kc, 0:N],
                                                                         start=(kc == 0), stop=(kc == 7)),
                             reads=[wr, "hT"], writes=[pgr], sig=(kc == 7))
                    sgb = sg_[c % 2]
                    sgr = "sg%d" % (c % 2)
                    S.op("act", lambda e, pg=pg, sgb=sgb: e.activation(out=sgb[:, 0:N], in_=pg[:, 0:N], func=AF.Sigmoid),
                         reads=[pgr], writes=[sgr])
                    if sample:
                        S.op("dve", lambda e, c=c, pl=pl, sgb=sgb: e.tensor_tensor(out=uh[:, c, :, 30], in0=pl[:, 0:N],
                                                                                   in1=sgb[:, 0:N], op=ALU.mult),
                             reads=[plr, sgr], writes=["uh"])
                    else:
                        S.op("dve", lambda e, c=c, pl=pl, sgb=sgb: e.tensor_tensor(out=uext[:, c, 30:30 + N], in0=pl[:, 0:N],
                                                                                   in1=sgb[:, 0:N], op=ALU.mult),
                             reads=[plr, sgr], writes=["uext%d" % c])
                o_w = PP["wdw"][0]
                if sample:
                    S.op("dve", lambda e: e.tensor_tensor(
                        out=uprod[:, :, :, :], in0=uh[:, :, :, :],
                        in1=pp[:, o_w:o_w + 248].rearrange("p (c j) -> p c j", j=31).unsqueeze(2).to_broadcast([128, 8, NS, 31]),
                        op=ALU.mult), reads=["uh", "pp"], writes=["uprod"])
                    S.op("dve", lambda e: e.tensor_reduce(out=acc[:, :, 0:NS], in_=uprod[:, :, :, :], axis=AX.X, op=ALU.add),
                         reads=["uprod"], writes=["acc%d" % c for c in range(8)])
                    o_b = PP["bdw"][0]
                    S.op("dve", lambda e: e.tensor_tensor(out=acc[:, :, 0:NS], in0=acc[:, :, 0:NS],
                                                          in1=pp[:, o_b:o_b + 8].unsqueeze(2).to_broadcast([128, 8, NS]),
                                                          op=ALU.add), reads=["acc%d" % c for c in range(8)] + ["pp"],
                         writes=["acc%d" % c for c in range(8)])
                else:
                    for c in range(8):
                        dg = dgb[c % 2]
                        dgr = "dgb%d" % (c % 2)
                        S.op("pool", lambda e, c=c, dg=dg: e.tensor_tensor(
                            out=dg[:, :, :], in0=identf[:, :].unsqueeze(1).to_broadcast([128, 31, 128]),
                            in1=pp[:, o_w + 31 * c:o_w + 31 * c + 31].unsqueeze(2).to_broadcast([128, 31, 128]),
                            op=ALU.mult), reads=["identf", "pp"], writes=[dgr])
                        pcv, pcr = psum()
                        for j in range(31):
                            S.op("pe", lambda e, c=c, j=j, dg=dg, pcv=pcv: e.matmul(
                                pcv[:, 0:N], lhsT=dg[:, j, :], rhs=uext[:, c, j:j + N], start=(j == 0), stop=(j == 30)),
                                reads=[dgr, "uext%d" % c, "uext"], writes=[pcr], sig=(j == 30))
                        S.op("act", lambda e, c=c, pcv=pcv: e.activation(out=acc[:, c, 0:N], in_=pcv[:, 0:N], func=AF.Identity,
                                                                         bias=P("bdw", c)),
                             reads=[pcr, "pp"], writes=["acc%d" % c])
                if sample or lastg:
                    for c in range(8):
                        pst, pr = psum()
                        if sample:
                            S.op("pe", lambda e, c=c, pst=pst: e.transpose(pst[0:NS, 0:128], uh[:, c, :, 30], identf[:, :]),
                                 reads=["uh", "identf"], writes=[pr])
                            S.op("act", lambda e, c=c, pst=pst: e.activation(out=orow[0:NS, 128 * c:128 * (c + 1)],
                                                                             in_=pst[0:NS, 0:128], func=AF.Copy),
                                 reads=[pr], writes=["orow"])
                        else:
                            S.op("pe", lambda e, c=c: e.transpose(PTB[0:30, 128 * c:128 * (c + 1)], uext[:, c, NT:NT + 30],
                                                                   identb[:, :]),
                                 reads=["uext%d" % c, "identb"], writes=["ptb"])
                            S.op("act", lambda e, c=c: e.activation(out=orow[0:30, 128 * c:128 * (c + 1)],
                                                                    in_=PTB[0:30, 128 * c:128 * (c + 1)], func=AF.Copy),
                                 reads=["ptb"], writes=["orow"])
                    if sample:
                        dma("sp", conv_s[:, 29, :], orow[0:NS, 0:D], reads=["orow"], key="oorow")
                    else:
                        dma("sp", conv_p[:, :], orow[0:30, 0:D], reads=["orow"], key="oorow")
                if DBG and t0 == HALO and not sample:
                    dma("sp", dbg_acc, acc[:, :, :], reads=["acc%d" % c for c in range(8)], key="dbg")
                p1, p1r = psum()
                p2, p2r = psum()
                for c in range(8):
                    b1 = cb16[c % 2]
                    b2 = csq16[c % 2]
                    S.op("pool", lambda e, c=c, b1=b1: e.tensor_copy(out=b1[:, 0:N], in_=acc[:, c, 0:N]),
                         reads=["acc%d" % c], writes=["cb16_%d" % (c % 2)])
                    S.op("act", lambda e, c=c, b2=b2: e.activation(out=b2[:, 0:N], in_=acc[:, c, 0:N], func=AF.Square),
                         reads=["acc%d" % c], writes=["csq16_%d" % (c % 2)])
                    S.op("pe", lambda e, c=c, b1=b1: e.matmul(p1[:, 0:N], lhsT=onesb[:, :], rhs=b1[:, 0:N], start=(c == 0),
                                                              stop=(c == 7)), reads=["onesb", "cb16_%d" % (c % 2)], writes=[p1r])
                    S.op("pe", lambda e, c=c, b2=b2: e.matmul(p2[:, 0:N], lhsT=onesb[:, :], rhs=b2[:, 0:N], start=(c == 0),
                                                              stop=(c == 7)), reads=["onesb", "csq16_%d" % (c % 2)], writes=[p2r])
                S.op("dve", lambda e: e.tensor_scalar(out=lnm[:, 0, 0:N], in0=p1[:, 0:N], scalar1=1.0 / D, scalar2=None,
                                                      op0=ALU.mult), reads=[p1r], writes=["lnm"])
                S.op("dve", lambda e: e.tensor_tensor(out=lnm[:, 1, 0:N], in0=lnm[:, 0, 0:N], in1=lnm[:, 0, 0:N], op=ALU.mult),
                     reads=["lnm"], writes=["lnm"])
                S.op("dve", lambda e: e.scalar_tensor_tensor(out=lnm[:, 1, 0:N], in0=p2[:, 0:N], scalar=1.0 / D,
                                                             in1=lnm[:, 1, 0:N], op0=ALU.mult, op1=ALU.subtract),
                     reads=[p2r, "lnm"], writes=["lnm"])
                S.op("act", lambda e: e.activation(out=lnm[:, 2, 0:N], in_=lnm[:, 1, 0:N], func=AF.Sqrt, bias=epst[:, 0:1]),
                     reads=["lnm", "epst"], writes=["lnm"])
                S.op("dve", lambda e: e.reciprocal(out=lnm[:, 3, 0:N], in_=lnm[:, 2, 0:N]), reads=["lnm"], writes=["lnm"])
                for c in range(8):
                    tb = tt[c % 2]
                    tr = "tt%d" % (c % 2)
                    S.op("dve", lambda e, c=c, tb=tb: e.tensor_tensor(out=tb[:, 0:N], in0=acc[:, c, 0:N], in1=lnm[:, 0, 0:N],
                                                                      op=ALU.subtract),
                         reads=["acc%d" % c, "lnm"], writes=[tr])
                    S.op("pool", lambda e, tb=tb: e.tensor_tensor(out=tb[:, 0:N], in0=tb[:, 0:N], in1=lnm[:, 3, 0:N],
                                                                  op=ALU.mult), reads=[tr, "lnm"], writes=[tr])
                    S.op("act", lambda e, c=c, tb=tb: e.activation(out=sT[:, c, 0:N], in_=tb[:, 0:N], func=AF.Silu,
                                                                   scale=P("lng", c), bias=P("lnb", c)),
                         reads=[tr, "pp"], writes=["sT"])
                if DBG and t0 == HALO and not sample:
                    dma("sp", dbg_sT, sT[:, :, :], reads=["sT"], key="dbg")
                for c in range(8):
                    if c % 2 == 0:
                        wco_t, wco_r = wtile(wb_co[:, 128 * c:128 * c + 256], "wb_co")
                    wg_t, wg_r = wtile(wb_in[:, 4352 + 256 * c:4352 + 256 * (c + 1)], "wb_in3")
                    pa, par = psum()
                    pbb, pbr = psum()
                    pga, pgar = psum()
                    pgb, pgbr = psum()
                    co = 128 * (c % 2)
                    for kc in range(8):
                        S.op("pe", lambda e, kc=kc, pa=pa, wco_t=wco_t, co=co: e.matmul(
                            pa[:, 0:N], lhsT=wco_t[:, kc, co:co + 128], rhs=sT[:, kc, 0:N], start=(kc == 0), stop=(kc == 7)),
                            reads=[wco_r, "sT"], writes=[par], sig=(kc == 7))
                    if sample:
                        for k2 in range(2):
                            S.op("pe", lambda e, k2=k2, pbb=pbb, c=c: e.matmul(
                                pbb[:, 0:N], lhsT=wao128[:, k2, 128 * c:128 * (c + 1)], rhs=oTs[:, k2, 0:N],
                                start=(k2 == 0), stop=(k2 == 1)), reads=["wao", "oTs"], writes=[pbr], sig=(k2 == 1))
                    else:
                        for sl in range(4):
                            S.op("pe", lambda e, sl=sl, pbb=pbb, c=c: e.matmul(
                                pbb[:, 0:N], lhsT=wao64[:, sl, 128 * c:128 * (c + 1)], rhs=oTg[:, sl, 0:N],
                                start=(sl == 0), stop=(sl == 3)), reads=["wao", "oTg"], writes=[pbr], sig=(sl == 3))
                    for kc in range(8):
                        S.op("pe", lambda e, kc=kc, pga=pga, wg_t=wg_t: e.matmul(
                            pga[:, 0:N], lhsT=wg_t[:, kc, 0:128], rhs=hT[:, kc, 0:N], start=(kc == 0), stop=(kc == 7)),
                            reads=[wg_r, "hT"], writes=[pgar], sig=(kc == 7))
                    for kc in range(8):
                        S.op("pe", lambda e, kc=kc, pgb=pgb, wg_t=wg_t: e.matmul(
                            pgb[:, 0:N], lhsT=wg_t[:, kc, 128:256], rhs=hT[:, kc, 0:N], start=(kc == 0), stop=(kc == 7)),
                            reads=[wg_r, "hT"], writes=[pgbr], sig=(kc == 7))
                    sa, sar = sg_[0], "sg0"
                    sb_, sbr = sg_[1], "sg1"
                    S.op("act", lambda e, pga=pga: e.activation(out=sa[:, 0:N], in_=pga[:, 0:N], func=AF.Sigmoid),
                         reads=[pgar], writes=[sar])
                    S.op("act", lambda e, pgb=pgb: e.activation(out=sb_[:, 0:N], in_=pgb[:, 0:N], func=AF.Sigmoid),
                         reads=[pgbr], writes=[sbr])
                    S.op("dve", lambda e, pa=pa: e.tensor_tensor(out=sa[:, 0:N], in0=pa[:, 0:N], in1=sa[:, 0:N], op=ALU.mult),
                         reads=[par, sar], writes=[sar])
                    S.op("dve", lambda e, pbb=pbb: e.tensor_tensor(out=sb_[:, 0:N], in0=pbb[:, 0:N], in1=sb_[:, 0:N],
                                                                   op=ALU.mult), reads=[pbr, sbr], writes=[sbr])
                    S.op("pool", lambda e, c=c: e.tensor_tensor(out=mixT[:, c, 0:N], in0=sa[:, 0:N], in1=sb_[:, 0:N],
                                                                op=ALU.add), reads=[sar, sbr], writes=["mixT"])
                if DBG and t0 == HALO and not sample:
                    dma("sp", dbg_mix, mixT[:, :, :], reads=["mixT"], key="dbg")
                for (i, TT) in TT_list:
                    WN = 512 if TT == 128 else 256
                    for n in range(D // WN):
                        po, por = psum()
                        for kc in range(8):
                            S.op("pe", lambda e, kc=kc, po=po, i=i, TT=TT, n=n, WN=WN: e.matmul(
                                po[0:TT, 0:WN], lhsT=mixT[:, kc, 128 * i:128 * i + TT], rhs=woutb[:, kc, WN * n:WN * (n + 1)],
                                start=(kc == 0), stop=(kc == 7)), reads=["mixT", "woutb"], writes=[por], sig=(kc == 7))
                        S.op("dve", lambda e, po=po, i=i, TT=TT, n=n, WN=WN: e.tensor_tensor(
                            out=xg[0:TT, i, WN * n:WN * (n + 1)], in0=po[0:TT, 0:WN], in1=xg[0:TT, i, WN * n:WN * (n + 1)],
                            op=ALU.add), reads=[por, "xg%d" % i], writes=["xg%d" % i])
                if DBG and t0 == HALO and not sample:
                    for (i, TT) in TT_list:
                        dma("sp", dbg_xmid[128 * i:128 * (i + 1), :], xg[0:TT, i, :], reads=["xg%d" % i], key="dbg")
                dma("sp", wdnb[:, :, :], wb_dn.rearrange("(kc p) n -> p kc n", p=128), reads=["wb_dn"], writes=["wdnb"],
                    key="wdnb")
                for (i, TT) in TT_list:
                    norm_T(xg[0:TT, i, :], "xg%d" % i, TT, hT[:, :, 128 * i:128 * i + TT], "hT", "gffn", ygi[0])
                    ygi[0] += 1
                if sample:
                    for q in range(44):
                        if q % 8 == 0:
                            wpc = min(1024, 2 * DFF - 128 * q)
                            dma("sp", orow[0:2 * NS, 0:wpc], sffn.rearrange("s j n -> (s j) n")[:, 128 * q:128 * q + wpc],
                                writes=["orow"], key="scv")
                        pst, pr = psum()
                        S.op("pe", lambda e, q=q, pst=pst: e.transpose(
                            pst[:, 0:2 * NS], orow[0:2 * NS, 128 * (q % 8):128 * (q % 8 + 1)], identf[0:2 * NS, 0:2 * NS]),
                             reads=["orow", "identf"], writes=[pr])
                        S.op("act", lambda e, q=q, pst=pst: e.activation(out=fhs[:, q, :], in_=pst[:, 0:2 * NS], func=AF.Copy),
                             reads=[pr], writes=["fhs"])
                o_f = PP["wfdw"][0]
                for j in range(22):
                    w, wr = wtile(wb_up[:, 256 * j:256 * (j + 1)], "wb_up")
                    pgv = []
                    for hv in range(2):
                        pz, pzr = psum()
                        for kc in range(8):
                            S.op("pe", lambda e, kc=kc, w=w, pz=pz, hv=hv: e.matmul(
                                pz[:, 0:N], lhsT=w[:, kc, 128 * hv:128 * (hv + 1)], rhs=hT[:, kc, 0:N], start=(kc == 0),
                                stop=(kc == 7)), reads=[wr, "hT"], writes=[pzr], sig=(kc == 7))
                        pgv.append((pz, pzr))
                    for hv in range(2):
                        q = 2 * j + hv
                        pz, pzr = pgv[hv]
                        ub, ur = upx[hv], "upx%d" % hv
                        cb, cr = cgb[hv], "cgb%d" % hv
                        w0 = pp[:, o_f + 3 * q:o_f + 3 * q + 1]
                        w1 = pp[:, o_f + 3 * q + 1:o_f + 3 * q + 2]
                        w2 = pp[:, o_f + 3 * q + 2:o_f + 3 * q + 3]
                        S.op("act", lambda e, pz=pz, cb=cb, w2=w2, q=q: e.activation(
                            out=cb[:, 0:N], in_=pz[:, 0:N], func=AF.Identity, scale=w2, bias=P("bfdw", q)),
                            reads=[pzr, "pp"], writes=[cr])
                        if sample:
                            S.op("act", lambda e, pz=pz, q=q: e.activation(out=upn[:, q, :], in_=pz[:, 0:NS], func=AF.Copy),
                                 reads=[pzr], writes=["upn"])
                            fv = fhs[:, q, :].rearrange("p (s j) -> p j s", j=2)
                            S.op("dve", lambda e, cb=cb, fv=fv, w1=w1: e.scalar_tensor_tensor(
                                out=cb[:, 0:N], in0=fv[:, 1, :], scalar=w1, in1=cb[:, 0:N], op0=ALU.mult, op1=ALU.add),
                                reads=["fhs", cr, "pp"], writes=[cr])
                            S.op("dve", lambda e, cb=cb, fv=fv, w0=w0: e.scalar_tensor_tensor(
                                out=cb[:, 0:N], in0=fv[:, 0, :], scalar=w0, in1=cb[:, 0:N], op0=ALU.mult, op1=ALU.add),
                                reads=["fhs", cr, "pp"], writes=[cr])
                        else:
                            S.op("pool", lambda e, ub=ub, q=q: e.tensor_copy(out=ub[:, 0:2], in_=fh[:, q, :]),
                                 reads=["fh"], writes=[ur])
                            S.op("act", lambda e, pz=pz, ub=ub: e.activation(out=ub[:, 2:2 + N], in_=pz[:, 0:N], func=AF.Copy),
                                 reads=[pzr], writes=[ur])
                            S.op("pool", lambda e, ub=ub, q=q: e.tensor_copy(out=fh[:, q, :], in_=ub[:, N:N + 2]),
                                 reads=[ur], writes=["fh"])
                            S.op("dve", lambda e, cb=cb, ub=ub, w1=w1: e.scalar_tensor_tensor(
                                out=cb[:, 0:N], in0=ub[:, 1:1 + N], scalar=w1, in1=cb[:, 0:N], op0=ALU.mult, op1=ALU.add),
                                reads=[ur, cr, "pp"], writes=[cr])
                            S.op("dve", lambda e, cb=cb, ub=ub, w0=w0: e.scalar_tensor_tensor(
                                out=cb[:, 0:N], in0=ub[:, 0:N], scalar=w0, in1=cb[:, 0:N], op0=ALU.mult, op1=ALU.add),
                                reads=[ur, cr, "pp"], writes=[cr])
                    sgb, sgr = sg_[2], "sg2"
                    S.op("act", lambda e, sgb=sgb: e.activation(out=sgb[:, 0:N], in_=cgb[0][:, 0:N], func=AF.Silu),
                         reads=["cgb0"], writes=[sgr])
                    S.op(alt("pool", "dve"), lambda e, j=j, sgb=sgb: e.tensor_tensor(out=aT[:, j, 0:N], in0=sgb[:, 0:N],
                                                                                    in1=cgb[1][:, 0:N], op=ALU.mult),
                         reads=[sgr, "cgb1"], writes=["aT"])
                if sample or lastg:
                    nrow = NS if sample else 2
                    for q in range(44):
                        pst, pr = psum()
                        srcT = upn[:, q, :] if sample else fh[:, q, :]
                        S.op("pe", lambda e, pst=pst, srcT=srcT, nrow=nrow: e.transpose(pst[0:nrow, 0:128], srcT, identf[:, :]),
                             reads=["upn" if sample else "fh", "identf"], writes=[pr])
                        S.op("act", lambda e, q=q, pst=pst, nrow=nrow: e.activation(
                            out=orow[0:nrow, 128 * (q % 8):128 * (q % 8 + 1)], in_=pst[0:nrow, 0:128], func=AF.Copy),
                            reads=[pr], writes=["orow"])
                        if q % 8 == 7 or q == 43:
                            q0 = 8 * (q // 8)
                            wpc = 128 * (q - q0 + 1)
                            if sample:
                                dma("sp", ffn_s[:, 1, 128 * q0:128 * q0 + wpc], orow[0:NS, 0:wpc], reads=["orow"], key="oorow")
                            else:
                                dma("sp", ffn_p[:, 128 * q0:128 * q0 + wpc], orow[0:2, 0:wpc], reads=["orow"], key="oorow")
                for (i, TT) in TT_list:
                    WN = 512 if TT == 128 else 256
                    for n in range(D // WN):
                        po, por = psum()
                        for kc in range(22):
                            S.op("pe", lambda e, kc=kc, po=po, i=i, TT=TT, n=n, WN=WN: e.matmul(
                                po[0:TT, 0:WN], lhsT=aT[:, kc, 128 * i:128 * i + TT], rhs=wdnb[:, kc, WN * n:WN * (n + 1)],
                                start=(kc == 0), stop=(kc == 21)), reads=["aT", "wdnb"], writes=[por], sig=(kc == 21))
                        S.op("dve", lambda e, po=po, i=i, TT=TT, n=n, WN=WN: e.tensor_tensor(
                            out=xg[0:TT, i, WN * n:WN * (n + 1)], in0=po[0:TT, 0:WN], in1=xg[0:TT, i, WN * n:WN * (n + 1)],
                            op=ALU.add), reads=[por, "xg%d" % i], writes=["xg%d" % i])
                for (i, TT) in TT_list:
                    col = stat_i[0]
                    stat_i[0] += 1
                    src = xg[0:TT, i, :]
                    S.op("act", lambda e, src=src, TT=TT, col=col: e.activation(
                        out=yt[0][0:TT, :], in_=src, func=AF.Square, accum_out=stat[0:TT, 0, col:col + 1]),
                        reads=["xg%d" % i], writes=["yt0", "stat%d" % col])
                    S.op("act", lambda e, TT=TT, col=col: e.activation(
                        out=stat[0:TT, 1, col:col + 1], in_=stat[0:TT, 0, col:col + 1], func=AF.Sqrt, scale=1.0 / D,
                        bias=epst[0:TT, 0:1]), reads=["stat%d" % col, "epst"], writes=["stat%d" % col])
                    S.op("dve", lambda e, TT=TT, col=col: e.reciprocal(out=stat[0:TT, 2, col:col + 1],
                                                                       in_=stat[0:TT, 1, col:col + 1]),
                         reads=["stat%d" % col], writes=["stat%d" % col])
                    yb = yt[0]
                    yr = "yt0"
                    S.op("dve", lambda e, src=src, TT=TT, col=col, yb=yb: e.scalar_tensor_tensor(
                        out=yb[0:TT, :], in0=src, scalar=stat[0:TT, 2, col:col + 1], in1=gfin[0:TT, :], op0=ALU.mult,
                        op1=ALU.mult), reads=["xg%d" % i, "stat%d" % col, "gfin"], writes=[yr])
                    if sample:
                        dma("sp", y_s[:, :], yb[0:TT, :], reads=[yr], key="o" + yr)
                    elif out0 is not None:
                        dma("sp", y_p[out0 + 128 * i:out0 + 128 * i + TT, :], yb[0:TT, :], reads=[yr], key="o" + yr)

            hvt = sbuf(st2, "hvt", [128, 1], F32)
            dma("sp", hvt[:], hv_d, writes=["hvt"], key="hvt")
            group(HALO - 256, [(0, 128), (1, 128)], 256, False, first=True)
            S.op("dve", lambda e: e.tensor_scalar(out=fh[:, :, :], in0=fh[:, :, :], scalar1=hvt[:, 0:1], scalar2=None,
                                                  op0=ALU.mult), reads=["fh", "hvt"], writes=["fh"])
            for gi in range(NG):
                for k_, (dst_, src_) in enumerate(SHIFTS):
                    if k_ % NG == gi:
                        dma("pool", dst_, src_, key="cshift")
                group(HALO + gi * NT, [(i, 128) for i in range(NT // 128)], NT, False, lastg=(gi == NG - 1),
                      out0=gi * NT)
            group(0, [(0, NS)], NS, True)
            S.barrier()
            S.emit()
    return nc


def _perm_in():
    idx = []
    for c in range(8):
        idx += list(range(128 * c, 128 * (c + 1)))
        idx += list(range(1024 + 128 * c, 1024 + 128 * (c + 1)))
    idx += list(range(2048, 4352))
    for c in range(8):
        idx += list(range(4352 + 128 * c, 4352 + 128 * (c + 1)))
        idx += list(range(5376 + 128 * c, 5376 + 128 * (c + 1)))
    return np.array(idx)


def _perm_up():
    idx = []
    for j in range(22):
        idx += list(range(128 * j, 128 * (j + 1)))
        idx += list(range(DFF + 128 * j, DFF + 128 * (j + 1)))
    return np.array(idx)


def _fm(v, nch):
    return np.ascontiguousarray(v.reshape(nch, 128).T)


def make_shared(inp):
    f = np.float32
    pin = _perm_in()
    pup = _perm_up()
    ppv = np.zeros((128, NPP), f)

    def put(name, arr):
        o, w = PP[name]
        ppv[:, o:o + w] = arr.reshape(128, w)

    put("gmix", _fm(inp["g_mix"][0], 8))
    put("bdw", _fm(inp["b_dw"][0], 8))
    put("lng", _fm(inp["ln_g"][0], 8))
    put("lnb", _fm(inp["ln_b"][0], 8))
    put("gffn", _fm(inp["g_ffn"][0], 8))
    wdw = inp["w_dw"][0]
    put("wdw", np.ascontiguousarray(wdw.T.reshape(8, 128, 31).transpose(1, 0, 2)))
    wf = inp["w_fdw"][0][:, pup]
    put("wfdw", np.ascontiguousarray(wf.T.reshape(44, 128, 3).transpose(1, 0, 2)))
    put("bfdw", _fm(inp["b_fdw"][0][pup], 44))
    inv = (np.float32(500000.0) ** (-np.arange(8, dtype=f) / np.float32(8))).astype(f)
    angs = (np.full((NS, 1), 16384.0, f) * inv[None, :]).astype(f)
    css = np.concatenate([np.cos(angs), np.cos(angs)], 1).astype(f)
    sns = np.concatenate([-np.sin(angs), np.sin(angs)], 1).astype(f)
    j = np.arange(128)[:, None]
    i = np.arange(128)[None, :]
    mask2 = np.stack([(j >= i), (j <= i)], 1).astype(f)
    sel = np.zeros((NS, NS, 128), f)
    for s in range(NS):
        sel[s, s, :] = 1.0
    return {
        "w_in": np.ascontiguousarray(inp["w_in"][0][:, pin]),
        "w_co": np.ascontiguousarray(inp["w_conv_out"][0]),
        "w_ao": np.ascontiguousarray(inp["w_attn_out"][0]),
        "w_out": np.ascontiguousarray(inp["w_out"][0]),
        "w_up": np.ascontiguousarray(inp["w_up"][0][:, pup]),
        "w_dn": np.ascontiguousarray(inp["w_down"][0]),
        "pp": ppv,
        "gfin": np.ascontiguousarray(np.broadcast_to(inp["g_final"][None, :], (128, D))),
        "ident": np.eye(128, dtype=f),
        "mask2": mask2, "css": css, "sns": sns, "sel": sel,
    }


def make_core(inp, b, half, MAIN, s0, pup):
    f = np.float32
    LS = HALO + MAIN
    start = half * MAIN
    absp = start - HALO + np.arange(LS)
    valid = absp >= 0
    x = inp["x_prompt"][b]
    xl = np.zeros((LS, D), f)
    xl[valid] = x[absp[valid]]
    inv = (np.float32(500000.0) ** (-np.arange(8, dtype=f) / np.float32(8))).astype(f)
    pos = np.maximum(absp, 0).astype(f)
    ang = (pos[:, None] * inv[None, :]).astype(f)
    cos, sin = np.cos(ang).astype(f), np.sin(ang).astype(f)
    ntile = LS // 128
    csp = np.ascontiguousarray(np.concatenate([cos, cos], 1).reshape(ntile, 128, 16).transpose(1, 0, 2))
    snp = np.ascontiguousarray(np.concatenate([-sin, sin], 1).reshape(ntile, 128, 16).transpose(1, 0, 2))
    m = {
        "xp": xl, "csp": csp, "snp": snp,
        "hv": np.full((128, 1), 1.0 if start > 0 else 0.0, f),
        "xs": np.ascontiguousarray(inp["x_sample"][s0:s0 + NS, 0]),
        "sconv": np.ascontiguousarray(inp["state_conv"][0, s0:s0 + NS]),
        "sffn": np.ascontiguousarray(inp["state_ffn_conv"][0, s0:s0 + NS][:, :, pup]),
    }
    vf = valid.astype(f)
    nsb = LS // 2048
    for g, (W, d) in enumerate(GROUPS):
        m["vld%d" % g] = np.ascontiguousarray(vf.reshape(nsb, 16 // d, 128, d).transpose(2, 0, 3, 1))
    caches = ((inp["cache_k_w128"], inp["cache_v_w128"]), (inp["cache_k_w512"], inp["cache_v_w512"]),
              (inp["cache_k_w2048"], inp["cache_v_w2048"]))
    for g, W in enumerate((128, 512, 2048)):
        m["ck%d" % g] = np.ascontiguousarray(caches[g][0][0, s0:s0 + NS].reshape(NS, W, 256))
        m["cv%d" % g] = np.ascontiguousarray(caches[g][1][0, s0:s0 + NS].reshape(NS, W, 256))
    return m


_NC_CACHE = {}


def run(inp, n_cores=8):
    inp = {k: np.asarray(v) for k, v in inp.items()}
    B, S_FULL, _ = inp["x_prompt"].shape
    nsamp = inp["x_sample"].shape[0]
    MAIN = S_FULL // 2
    assert n_cores == 2 * B
    if MAIN not in _NC_CACHE:
        _NC_CACHE[MAIN] = build(MAIN)
    nc = _NC_CACHE[MAIN]
    shared = make_shared(inp)
    pup = _perm_up()
    in_maps = []
    for c in range(n_cores):
        m = dict(shared)
        m.update(make_core(inp, c // 2, c % 2, MAIN, (NS * c) % nsamp, pup))
        in_maps.append(m)
    res = run_bass_kernel_spmd(nc, in_maps, core_ids=list(range(n_cores))).results
    global LAST_RES
    LAST_RES = res
    ipup = np.argsort(pup)
    f = np.float32
    y_p = np.stack([np.concatenate([res[2 * b]["y_p"], res[2 * b + 1]["y_p"]], 0) for b in range(B)], 0)
    nsc = nsamp // NS
    y_s = np.concatenate([res[c]["y_s"] for c in range(nsc)], 0)[:, None, :]
    hi = [2 * b + 1 for b in range(B)]
    conv_p = np.stack([res[c]["conv_p"] for c in hi], 0)[None]
    conv_s = np.concatenate([res[c]["conv_s"] for c in range(nsc)], 0)[None]
    outs = [y_p.astype(f), y_s.astype(f), conv_p.astype(f), conv_s.astype(f)]
    for g, W in enumerate((128, 512, 2048)):
        outs.append(np.stack([res[c]["k%d_p" % g] for c in hi], 0).reshape(1, B, W, 4, 64).astype(f))
        outs.append(np.stack([res[c]["v%d_p" % g] for c in hi], 0).reshape(1, B, W, 4, 64).astype(f))
        outs.append(np.concatenate([res[c]["k%d_s" % g] for c in range(nsc)], 0).reshape(1, nsamp, W, 4, 64).astype(f))
        outs.append(np.concatenate([res[c]["v%d_s" % g] for c in range(nsc)], 0).reshape(1, nsamp, W, 4, 64).astype(f))
    ffn_p = np.stack([res[c]["ffn_p"] for c in hi], 0)[:, :, ipup][None]
    ffn_s = np.concatenate([res[c]["ffn_s"] for c in range(nsc)], 0)[:, :, ipup][None]
    outs += [ffn_p.astype(f), ffn_s.astype(f)]
    return tuple(outs)


def kernel(**inputs):
    return run(inputs, 8)
```

```python
import contextlib
import types
import numpy as np
import concourse.bass as bass
import concourse.mybir as mybir
from concourse.bass_utils import run_bass_kernel_spmd

F32 = mybir.dt.float32
BF16 = mybir.dt.bfloat16
ALU = mybir.AluOpType
AF = mybir.ActivationFunctionType
AX = mybir.AxisListType

D = 1024
DFF = 2816
NS = 4
NT = 512
GROUPS = ((128, 1), (512, 4), (2048, 16))
EPS = 1e-6
ENGS = ("pe", "act", "dve", "pool", "sp")


class Sched:
    def __init__(self, nc, st, nsem=100):
        self.nc = nc
        self.ops = {e: [] for e in ENGS}
        self.cnt = {}
        self.res_w = {}
        self.res_r = {}
        self.waited = {e: {} for e in ENGS}
        self.pool = [st.enter_context(nc.semaphore("sm%d" % i)) for i in range(nsem)]
        self.sem = {}
        self.alias = {}

    def _sk(self, k):
        if k not in self.cnt:
            self.cnt[k] = 0
            assert len(self.sem) < len(self.pool), "out of semaphores"
            self.sem[k] = self.pool[len(self.sem)]
        return k

    @staticmethod
    def _freeze(fn):
        if fn.__closure__ is None:
            return fn
        cells = []
        for c in fn.__closure__:
            try:
                cells.append(types.CellType(c.cell_contents))
            except ValueError:
                cells.append(c)
        return types.FunctionType(fn.__code__, fn.__globals__, fn.__name__, fn.__defaults__, tuple(cells))

    def op(self, eng, fn, reads=(), writes=(), dma=None, sig=True):
        fn = self._freeze(fn)
        waits = {}
        reads = [x for r in reads for x in [r] + self.alias.get(r, [])]
        writes = [x for r in writes for x in [r] + self.alias.get(r, [])]
        writes = writes + [r for r in reads if r.startswith("ps") or r.startswith("ptb")]

        def need(w):
            if w[1] > waits.get(w[0], 0):
                waits[w[0]] = w[1]

        for r in reads:
            if r in self.res_w:
                need(self.res_w[r])
        for r in writes:
            if r in self.res_w:
                need(self.res_w[r])
            for sk, v in self.res_r.get(r, {}).items():
                need((sk, v))
        if dma is None:
            sk = self._sk(eng)
            inc = 1 if sig else 0
        else:
            sk = self._sk("d:" + str(dma))
            inc = 16
        self.cnt[sk] += inc
        val = self.cnt[sk] if inc else self.cnt[sk] + 1
        wl = []
        for k, v in waits.items():
            if k == "pe" and eng == "pe" and dma is None:
                continue
            if self.waited[eng].get(k, 0) >= v:
                continue
            self.waited[eng][k] = v
            wl.append((k, v))
        for r in writes:
            self.res_w[r] = (sk, val)
            self.res_r[r] = {}
        for r in reads:
            d = self.res_r.setdefault(r, {})
            if d.get(sk, 0) < val:
                d[sk] = val
        self.ops[eng].append((wl, fn, sk, inc))

    def barrier(self, engs=ENGS):
        for e in engs:
            wl = []
            for k, v in self.cnt.items():
                if v > 0 and self.waited[e].get(k, 0) < v:
                    self.waited[e][k] = v
                    wl.append((k, v))
            if wl:
                self.ops[e].append((wl, None, None, 0))

    def emit(self):
        nc = self.nc
        with nc.Block() as block:
            def run(e, eng):
                for wl, fn, sk, inc in self.ops[eng]:
                    for k, v in wl:
                        e.wait_ge(self.sem[k], v)
                    if fn is not None:
                        ins = fn(e)
                        if inc:
                            ins.then_inc(self.sem[sk], inc)

            @block.tensor
            def _(e):
                run(e, "pe")

            @block.scalar
            def _(e):
                run(e, "act")

            @block.vector
            def _(e):
                run(e, "dve")

            @block.gpsimd
            def _(e):
                run(e, "pool")

            @block.sync
            def _(e):
                run(e, "sp")
        self.ops = {e: [] for e in ENGS}


PP = {}
_o = 0
for _n, _w in (("gmix", 8), ("bdw", 8), ("lng", 8), ("lnb", 8), ("gffn", 8), ("wdw", 8 * 31), ("wfdw", 44 * 3),
               ("bfdw", 44)):
    PP[_n] = (_o, _w)
    _o += _w
NPP = _o


HALO = 4096


def build(MAIN):
    S_TOK = HALO + MAIN
    NSB = S_TOK // 2048
    NTILE = S_TOK // 128
    NG = MAIN // NT
    nc = bass.Bass("TRN2", target_bir_lowering=False)

    def din(name, shape, dt=F32):
        return nc.dram_tensor(name, list(shape), dt, kind="ExternalInput").ap()

    def dout(name, shape, dt=F32):
        return nc.dram_tensor(name, list(shape), dt, kind="ExternalOutput").ap()

    def dscr(name, shape, dt):
        return nc.dram_tensor(name, list(shape), dt, kind="Internal").ap()

    xp = din("xp", [S_TOK, D])
    xs = din("xs", [NS, D])
    sconv = din("sconv", [NS, 30, D])
    cks = [din("ck%d" % g, [NS, GROUPS[g][0], 256]) for g in range(3)]
    cvs = [din("cv%d" % g, [NS, GROUPS[g][0], 256]) for g in range(3)]
    sffn = din("sffn", [NS, 2, 2 * DFF])
    w_in = din("w_in", [D, 6400])
    w_co = din("w_co", [D, D])
    w_ao = din("w_ao", [256, D])
    w_out = din("w_out", [D, D])
    w_up = din("w_up", [D, 2 * DFF])
    w_dn = din("w_dn", [DFF, D])
    pp_d = din("pp", [128, NPP])
    gfin_d = din("gfin", [128, D])
    ident_d = din("ident", [128, 128])
    mask_d = din("mask2", [128, 2, 128])
    csp_d = din("csp", [128, NTILE, 16])
    snp_d = din("snp", [128, NTILE, 16])
    css_d = din("css", [NS, 16])
    sns_d = din("sns", [NS, 16])
    sel_d = din("sel", [NS, NS, 128])
    vld_d = [din("vld%d" % g, [128, NSB, GROUPS[g][1], 16 // GROUPS[g][1]]) for g in range(3)]
    hv_d = din("hv", [128, 1])

    wb_in = dscr("wb_in", [D, 6400], BF16)
    wb_co = dscr("wb_co", [D, D], BF16)
    wb_ao = dscr("wb_ao", [256, D], BF16)
    wb_out = dscr("wb_out", [D, D], BF16)
    wb_up = dscr("wb_up", [D, 2 * DFF], BF16)
    wb_dn = dscr("wb_dn", [DFF, D], BF16)
    import os as _os0
    DBG = bool(_os0.environ.get("KDBG"))
    oT_d = (dout if DBG else dscr)("oT_d", [64, 4, S_TOK], BF16)
    if DBG:
        dbg_sT = dout("dbg_sT", [128, 8, NT], BF16)
        dbg_mix = dout("dbg_mix", [128, 8, NT], BF16)
        dbg_xmid = dout("dbg_xmid", [NT, D], F32)
        dbg_acc = dout("dbg_acc", [128, 8, NT], F32)

    y_p = dout("y_p", [MAIN, D])
    y_s = dout("y_s", [NS, D])
    conv_p = dout("conv_p", [30, D])
    conv_s = dout("conv_s", [NS, 30, D])
    kp = [dout("k%d_p" % g, [min(GROUPS[g][0], S_TOK), 256]) for g in range(3)]
    vp = [dout("v%d_p" % g, [min(GROUPS[g][0], S_TOK), 256]) for g in range(3)]
    ks_o = [dout("k%d_s" % g, [NS, GROUPS[g][0], 256]) for g in range(3)]
    vs_o = [dout("v%d_s" % g, [NS, GROUPS[g][0], 256]) for g in range(3)]
    ffn_p = dout("ffn_p", [2, 2 * DFF])
    ffn_s = dout("ffn_s", [NS, 2, 2 * DFF])

    with contextlib.ExitStack() as gst:
        S = Sched(nc, gst)

        def sbuf(st, name, shape, dt):
            return st.enter_context(nc.sbuf_tensor("sb_" + name, list(shape), dt))

        NPS = 6
        PSB = [gst.enter_context(nc.psum_tensor("psb%d" % i, [128, 512], F32)) for i in range(NPS)]
        PTB = gst.enter_context(nc.psum_tensor("ptb", [128, 1024], BF16))
        PTB2 = gst.enter_context(nc.psum_tensor("ptb2", [128, 1024], BF16))
        ps_i = [0]

        def psum():
            i = ps_i[0] % NPS
            ps_i[0] += 1
            return PSB[i], "ps%d" % i

        pp = sbuf(gst, "pp", [128, NPP], F32)
        identf = sbuf(gst, "identf", [128, 128], F32)
        identb = sbuf(gst, "identb", [128, 128], BF16)
        onesb = sbuf(gst, "onesb", [128, 128], BF16)
        onesf = sbuf(gst, "onesf", [128, 128], F32)
        epst = sbuf(gst, "epst", [128, 1], F32)
        stat = sbuf(gst, "stat", [128, 3, 4 * NTILE + 16], F32)
        xsb = [sbuf(gst, "xsb%d" % i, [128, D], BF16) for i in range(2)]
        oTs = sbuf(gst, "oTs", [128, 2, NS], BF16)
        sts = sbuf(gst, "sts", [NS, 2304], F32)
        stat_i = [0]

        ps45 = [0]

        def psum45():
            i = 4 + ps45[0] % 2
            ps45[0] += 1
            return PSB[i], "ps%d" % i

        def P(name, c=None):
            o, w = PP[name]
            if c is None:
                return pp[:, o:o + w]
            return pp[:, o + c:o + c + 1]

        def dma(eng, out, in_, reads=(), writes=(), key=None):
            S.op(eng, lambda e: e.dma_start(out=out, in_=in_), reads=reads, writes=writes, dma=key)

        dma("sp", pp[:], pp_d, writes=["pp"], key="pp")
        dma("sp", identf[:], ident_d, writes=["identf"], key="identf")
        S.op("dve", lambda e: e.tensor_copy(out=identb[:], in_=identf[:]), reads=["identf"], writes=["identb"])
        S.op("dve", lambda e: e.memset(onesb[:], 1.0), writes=["onesb"])
        S.op("dve", lambda e: e.memset(onesf[:], 1.0), writes=["onesf"])
        S.op("dve", lambda e: e.memset(epst[:], EPS), writes=["epst"])
        S.op("dve", lambda e: e.memset(stat[:], 0.0), writes=["stat"])
        def conv_w(dst, src, rows, c0, c1, key, after=()):
            for r0 in range(0, rows, 256):
                r1 = min(rows, r0 + 256)
                dma("pool", dst[r0:r1, c0:c1], src[r0:r1, c0:c1], reads=list(after), writes=[key], key=key)
        for cg in (2, 4, 0, 1, 3):
            conv_w(wb_in, w_in, D, 2048 + 512 * cg, 2048 + min(512 * (cg + 1), 2304), "wb_qkv%d" % cg)
        QKV_ALL = ["wb_qkv%d" % cg for cg in range(5)]
        LATE_CASTS = [
            [],
            [lambda: conv_w(wb_in, w_in, D, 0, 2048, "wb_in1"), lambda: conv_w(wb_in, w_in, D, 4352, 6400, "wb_in3")],
            [lambda: conv_w(wb_co, w_co, D, 0, D, "wb_co"), lambda: conv_w(wb_ao, w_ao, 256, 0, D, "wb_ao"),
             lambda: conv_w(wb_out, w_out, D, 0, D, "wb_out"), lambda: conv_w(wb_up, w_up, D, 0, 2 * DFF, "wb_up")],
            [lambda: conv_w(wb_dn, w_dn, DFF, 0, D, "wb_dn")],
        ]
        def flat16(ap):
            return ap.rearrange("w c -> (w c)").rearrange("(a b) -> a b", a=16)
        SHIFTS = []
        for g in range(3):
            W = GROUPS[g][0]
            for (src, dst) in ((cks[g], ks_o[g]), (cvs[g], vs_o[g])):
                for s in range(NS):
                    SHIFTS.append((flat16(dst[s, 0:W - 1, :]), flat16(src[s, 1:W, :])))
        for s in range(NS):
            SHIFTS.append((flat16(conv_s[s, 0:29, :]), flat16(sconv[s, 1:30, :])))
            SHIFTS.append((flat16(ffn_s[s, 0:1, :]), flat16(sffn[s, 1:2, :])))

        import os as _os
        KSTOP = int(_os.environ.get("KSTOP", "9"))
        if KSTOP == 0:
            S.barrier()
            S.emit()
            return nc
        def norm_a(src_ap, rd, TT, xb, xr):
            col = stat_i[0]
            stat_i[0] += 1
            S.op("act", lambda e: e.activation(out=xb[0:TT, :], in_=src_ap, func=AF.Square,
                                               accum_out=stat[0:TT, 0, col:col + 1]),
                 reads=[rd], writes=[xr, "stat%d" % col])
            S.op("act", lambda e: e.activation(out=stat[0:TT, 1, col:col + 1], in_=stat[0:TT, 0, col:col + 1],
                                               func=AF.Sqrt, scale=1.0 / D, bias=epst[0:TT, 0:1]),
                 reads=["stat%d" % col, "epst"], writes=["stat%d" % col])
            S.op("dve", lambda e: e.reciprocal(out=stat[0:TT, 2, col:col + 1], in_=stat[0:TT, 1, col:col + 1]),
                 reads=["stat%d" % col], writes=["stat%d" % col])
            S.op("act", lambda e: e.activation(out=xb[0:TT, :], in_=src_ap, func=AF.Copy,
                                               scale=stat[0:TT, 2, col:col + 1]),
                 reads=[rd, "stat%d" % col], writes=[xr])

        def norm_b(TT, xb, xr, dst3, dst_res, gain_name):
            for c in range(8):
                S.op("pe", lambda e, c=c: e.transpose(PTB[:, c * 128:c * 128 + TT], xb[0:TT, c * 128:(c + 1) * 128],
                                                      identb[0:TT, 0:TT]),
                     reads=[xr, "identb"], writes=["ptb"], sig=(c == 7))
            o, w = PP[gain_name]
            S.op("dve", lambda e: e.tensor_tensor(
                out=dst3, in0=PTB[:, :].rearrange("p (c t) -> p c t", c=8)[:, :, 0:TT],
                in1=pp[:, o:o + 8].unsqueeze(2).to_broadcast([128, 8, TT]), op=ALU.mult),
                reads=["ptb", "pp"], writes=[dst_res])

        def norm_T(src_ap, rd, TT, dst3, dst_res, gain_name, xi):
            xb = xsb[xi % 2]
            xr = "xsb%d" % (xi % 2)
            norm_a(src_ap, rd, TT, xb, xr)
            norm_b(TT, xb, xr, dst3, dst_res, gain_name)

        with contextlib.ExitStack() as st1:
            hT1 = sbuf(st1, "hT1", [128, 8, 2048], BF16)
            hTs = sbuf(st1, "hTs", [128, 8, NS], BF16)
            QT = [sbuf(st1, "QT%d" % g, [128, 2, GROUPS[g][1], 2048 // GROUPS[g][1]], BF16) for g in range(3)]
            KT = [sbuf(st1, "KT%d" % g, [128, 2, GROUPS[g][1], 128 + 2048 // GROUPS[g][1]], BF16) for g in range(3)]
            NB = [1 + 16 // GROUPS[g][1] for g in range(3)]
            VV = [sbuf(st1, "VV%d" % g, [128, GROUPS[g][1], NB[g], 260], BF16) for g in range(3)]
            Wg = [sbuf(st1, "Wg%d" % i, [128, 8, 512], BF16) for i in range(1)]
            xt = [sbuf(st1, "xt%d" % i, [128, D], F32) for i in range(2)]
            stg = [sbuf(st1, "stg%d" % i, [128, 512], F32) for i in range(2)]
            qkb = [sbuf(st1, "qkb%d" % i, [128, 512], BF16) for i in range(2)]
            rtmp = [sbuf(st1, "rtmp%d" % i, [128, 24, 16], F32) for i in range(2)]
            vst = [sbuf(st1, "vst%d" % i, [128, 256], F32) for i in range(2)]
            PT2 = sbuf(st1, "PT2", [128, 16, 2, 128], BF16)
            PTs = [sbuf(st1, "PTs%d" % i, [128, 2, 2, 128], BF16) for i in range(2)]
            csp = sbuf(st1, "csp", [128, 16, 16], F32)
            snp = sbuf(st1, "snp", [128, 16, 16], F32)
            css = sbuf(st1, "css", [NS, 16], F32)
            sns = sbuf(st1, "sns", [NS, 16], F32)
            maskf = sbuf(st1, "maskf", [128, 2, 128], F32)
            vld = [sbuf(st1, "vld%d" % g, [128, GROUPS[g][1], 16 // GROUPS[g][1]], F32) for g in range(3)]
            maskb = sbuf(st1, "maskb", [128, 2, 128], BF16)
            rec = [sbuf(st1, "rec%d" % i, [64, 512], F32) for i in range(1)]
            oTsb = sbuf(st1, "oTsb", [64, 2048], BF16)
            selb = sbuf(st1, "selb", [NS, NS, 128], F32)
            Kc = [sbuf(st1, "Kc%d" % i, [128, 256], F32) for i in range(3)]
            Vc = [sbuf(st1, "Vc%d" % i, [128, 256], F32) for i in range(3)]
            prod = sbuf(st1, "prod", [128, 256], F32)
            sT4 = sbuf(st1, "sT4", [128, 24], F32)
            sm = sbuf(st1, "sm", [NS, 768], F32)
            pcur = sbuf(st1, "pcur", [NS, 12], F32)
            pcm = sbuf(st1, "pcm", [NS, NS, 12], F32)
            id4 = sbuf(st1, "id4", [NS, NS], F32)
            oTf = sbuf(st1, "oTf", [128, 2, NS], F32)

            for g in range(3):
                S.op("pool", lambda e, g=g: e.memset(VV[g][:, :, :, :], 1.0), writes=["VV%d" % g])
            dma("sp", css[:], css_d, writes=["css"], key="css")
            dma("sp", sns[:], sns_d, writes=["sns"], key="sns")
            dma("sp", maskf[:], mask_d, writes=["maskf"], key="maskf")
            dma("sp", selb[:], sel_d, writes=["selb"], key="selb")
            S.op("dve", lambda e: e.tensor_copy(out=maskb[:], in_=maskf[:]), reads=["maskf"], writes=["maskb"])
            S.op("dve", lambda e: e.tensor_copy(out=id4[:], in_=identf[0:NS, 0:NS]), reads=["identf"], writes=["id4"])

            def rope(stv, rd, TT, nh, cs_ap, sn_ap, tab_res, ri):
                rt = rtmp[ri % 2]
                rr = "rtmp%d" % (ri % 2)
                t1 = rt[0:TT, 0:nh, :]
                csb = cs_ap.unsqueeze(1).to_broadcast([TT, nh, 16])
                S.op("dve", lambda e: e.tensor_tensor(out=t1, in0=stv[:, :, 0:16], in1=csb, op=ALU.mult),
                     reads=[rd] + tab_res, writes=[rr])
                S.op("dve", lambda e: e.tensor_tensor(out=stv[:, :, 0:8], in0=stv[:, :, 0:8],
                                                      in1=sn_ap[:, 8:16].unsqueeze(1).to_broadcast([TT, nh, 8]),
                                                      op=ALU.mult), reads=[rd] + tab_res, writes=[rd])
                S.op("dve", lambda e: e.tensor_tensor(out=stv[:, :, 8:16], in0=stv[:, :, 8:16],
                                                      in1=sn_ap[:, 0:8].unsqueeze(1).to_broadcast([TT, nh, 8]),
                                                      op=ALU.mult), reads=[rd] + tab_res, writes=[rd])
                S.op("dve", lambda e: e.tensor_tensor(out=t1[:, :, 0:8], in0=t1[:, :, 0:8], in1=stv[:, :, 8:16],
                                                      op=ALU.add), reads=[rd, rr], writes=[rr])
                S.op("dve", lambda e: e.tensor_tensor(out=t1[:, :, 8:16], in0=t1[:, :, 8:16], in1=stv[:, :, 0:8],
                                                      op=ALU.add), reads=[rd, rr], writes=[rr])
                S.op("dve", lambda e: e.tensor_copy(out=stv[:, :, 0:16], in_=t1), reads=[rr], writes=[rd])

            xi = 0
            ri = 0
            ev = 0
            bset = [0]
            PTBf = PTB[:, :].bitcast(F32)
            PTB2f = PTB2[:, :].bitcast(F32)
            for _k in range(int(_os.environ.get("KDUMMY", "0"))):
                if _os.environ.get("KDUMMYT") == "memset":
                    S.op("dve", lambda e: e.memset(prod[:, :], 0.0), writes=["prod"])
                else:
                    S.op("dve", lambda e: e.tensor_tensor(out=prod[:, :], in0=prod[:, :], in1=prod[:, :], op=ALU.mult),
                         writes=["prod"])
            for sb in range(NSB):
                T0 = sb * 2048
                last = (sb == NSB - 1)
                if sb < len(LATE_CASTS):
                    for f_ in LATE_CASTS[sb]:
                        f_()
                dma("sp", csp[:], csp_d[:, 16 * sb:16 * (sb + 1), :], writes=["csp"], key="csp")
                for g in range(3):
                    dma("sp", vld[g][:, :, :], vld_d[g][:, sb, :, :], writes=["vld%d" % g], key="vld%d" % g)
                dma("sp", snp[:], snp_d[:, 16 * sb:16 * (sb + 1), :], writes=["snp"], key="snp")
                for i in range(16):
                    b = xi % 2
                    dma("sp", xt[b][:], xp[T0 + 128 * i:T0 + 128 * (i + 1), :], writes=["xt%d" % b], key="xt%d" % b)
                    norm_T(xt[b][:], "xt%d" % b, 128, hT1[:, :, 128 * i:128 * (i + 1)], "hT1_%d" % i, "gmix", xi)
                    xi += 1
                if sb == 1:
                    b = xi % 2
                    dma("sp", xt[b][0:NS, :], xs, writes=["xt%d" % b], key="xt%d" % b)
                    norm_T(xt[b][0:NS, :], "xt%d" % b, NS, hTs[:, :, 0:NS], "hTs", "gmix", xi)
                    xi += 1
                if KSTOP == 10:
                    S.barrier()
                    S.emit()
                    return nc
                hT_all = ["hT1_%d" % i for i in range(16)]
                if sb > 0:
                    for g in range(3):
                        M = 2048 // GROUPS[g][1]
                        S.op("pool", lambda e, g=g, M=M: e.tensor_copy(out=KT[g][:, :, :, 0:128],
                                                                        in_=KT[g][:, :, :, M:M + 128]),
                             reads=["KT%d" % g], writes=["KT%d" % g])
                        S.op("pool", lambda e, g=g: e.tensor_copy(out=VV[g][:, :, 0, :], in_=VV[g][:, :, NB[g] - 1, :]),
                             reads=["VV%d" % g], writes=["VV%d" % g])
                for cg in range(5):
                    if sb == 0 and cg in (0, 1, 3):
                        continue
                    wcols = 512 if cg < 4 else 256
                    wb = Wg[0]
                    wr = "Wg0"
                    dma("sp", wb[:, :, 0:wcols],
                        wb_in[:, 2048 + 512 * cg:2048 + 512 * cg + wcols].rearrange("(kc p) n -> p kc n", p=128),
                        reads=["wb_qkv%d" % cg], writes=[wr], key=wr)
                    if sb == 1:
                        pst, pr = psum()
                        for h0 in range(0, wcols, 256):
                            for kc in range(8):
                                S.op("pe", lambda e, kc=kc, pst=pst, h0=h0: e.matmul(
                                    pst[0:NS, h0:h0 + 256], lhsT=hTs[:, kc, 0:NS], rhs=wb[:, kc, h0:h0 + 256],
                                    start=(kc == 0), stop=(kc == 7)), reads=["hTs", wr], writes=[pr], sig=(kc == 7))
                        for (c0, c1) in ((0, 256), (256, 512)):
                            if c0 >= wcols:
                                continue
                            gcol = 512 * cg + c0
                            sc = 0.125 if gcol < 768 else 1.0
                            S.op("act", lambda e, c0=c0, c1=c1, pst=pst, sc=sc, gcol=gcol: e.activation(
                                out=sts[0:NS, gcol:gcol + 256], in_=pst[0:NS, c0:c1], func=AF.Copy, scale=sc),
                                reads=[pr], writes=["sts"])
                    if KSTOP == 20:
                        S.barrier()
                        S.emit()
                        return nc
                    if cg < 3:
                        for i in range(16):
                            pst, pr = psum()
                            for kc in range(8):
                                S.op("pe", lambda e, kc=kc, pst=pst, i=i: e.matmul(
                                    pst[:, 0:512], lhsT=hT1[:, kc, 128 * i:128 * (i + 1)], rhs=wb[:, kc, 0:512],
                                    start=(kc == 0), stop=(kc == 7)), reads=["hT1_%d" % i, wr], writes=[pr], sig=(kc == 7))
                            sg = stg[ev % 2]
                            sr = "stg%d" % (ev % 2)
                            qb_ = qkb[ev % 2]
                            qr = "qkb%d" % (ev % 2)
                            ev += 1
                            for (c0, c1) in ((0, 256), (256, 512)):
                                sc = 0.125 if (512 * cg + c0) < 768 else 1.0
                                S.op("act", lambda e, c0=c0, c1=c1, pst=pst, sc=sc, sg=sg: e.activation(
                                    out=sg[:, c0:c1], in_=pst[:, c0:c1], func=AF.Copy, scale=sc),
                                    reads=[pr], writes=[sr])
                            ti = sb * 16 + i
                            if not _os.environ.get("KNOROPE"):
                              rope(sg[:, :].rearrange("p (h d) -> p h d", d=64), sr, 128, 8, csp[:, i, :], snp[:, i, :],
                                 ["csp", "snp"], ri)
                            ri += 1
                            S.op("dve", lambda e, sg=sg, qb_=qb_: e.tensor_copy(out=qb_[:, :], in_=sg[:, :]),
                                 reads=[sr], writes=[qr])
                            if last and KSTOP != 24:
                                for g in range(3):
                                    W = min(GROUPS[g][0], S_TOK)
                                    kcol = 768 + 256 * g
                                    if not (512 * cg <= kcol < 512 * cg + 512):
                                        continue
                                    tpos = 2048 - 128 * (16 - i)
                                    if S_TOK - (T0 + 128 * i) > W:
                                        continue
                                    row0 = W - (S_TOK - (T0 + 128 * i))
                                    lc = kcol - 512 * cg
                                    dma("sp", kp[g][row0:row0 + 128, :], sg[:, lc:lc + 256], reads=[sr], key="o" + sr)
                            for j in range(4):
                                S.op("pe", lambda e, j=j, qb_=qb_: e.transpose((PTB if j < 2 else PTB2)[:, j * 128:(j + 1) * 128],
                                                                                qb_[:, j * 128:(j + 1) * 128], identb[:, :]),
                                     reads=[qr, "identb"], writes=["ptb" if j < 2 else "ptb2"], sig=(j % 2 == 1))
                            for j in range(4):
                                if _os.environ.get("KNOEVAC"):
                                    continue
                                cc = cg * 4 + j
                                isq = cc < 6
                                g = (cc % 6) // 2
                                half = cc % 2
                                d = GROUPS[g][1]
                                mlo = 128 * i // d
                                cnt = 128 // d
                                if isq:
                                    dst = QT[g][:, half, :, mlo:mlo + cnt]
                                    dres = "QT%d" % g
                                else:
                                    dst = KT[g][:, half, :, 128 + mlo:128 + mlo + cnt]
                                    dres = "KT%d" % g
                                src = (PTB if j < 2 else PTB2)[:, j * 128:(j + 1) * 128].rearrange("p (m r) -> p r m", r=d)
                                if j < 2:
                                    S.op("act", lambda e, dst=dst, src=src: e.activation(out=dst, in_=src, func=AF.Copy),
                                         reads=["ptb"], writes=[dres])
                                else:
                                    S.op("dve", lambda e, dst=dst, src=src: e.tensor_copy(out=dst, in_=src),
                                         reads=["ptb2"], writes=[dres])
                            if KSTOP == 21:
                                S.barrier()
                                S.emit()
                                return nc
                            if (KSTOP in (22, 24) and cg == 2 and i == 15) or (KSTOP == 25 and i == 1) or (KSTOP == 33 and cg == 1 and i == 14) or (KSTOP == 31 and cg == 1 and i == 1) or (KSTOP == 32 and cg == 1 and i == 7) or (KSTOP == 29 and cg == 1 and i == 15) or (KSTOP == 30 and cg == 2 and i == 0) or (KSTOP == 28 and cg == 1 and i == 0) or (KSTOP == 26 and i == 7) or (KSTOP == 27 and cg == 0 and i == 15):
                                S.barrier()
                                S.emit()
                                return nc
                    else:
                        jobs = []
                        if cg == 3:
                            for blk in range(16):
                                jobs.append((0, 0, blk, 0, hT1[:, :, 128 * blk:128 * (blk + 1)], ["hT1_%d" % blk]))
                            for r in range(4):
                                for blk in range(4):
                                    jobs.append((1, r, blk, 256, hT1[:, :, 512 * blk + r:512 * (blk + 1):4],
                                                 ["hT1_%d" % t for t in range(4 * blk, 4 * blk + 4)]))
                        else:
                            for r in range(16):
                                jobs.append((2, r, 0, 0, hT1[:, :, r:2048:16], hT_all))
                        for (g, r, blk, c0, lh, lres) in jobs:
                            pst, pr = psum()
                            for kc in range(8):
                                S.op("pe", lambda e, kc=kc, pst=pst, lh=lh, c0=c0: e.matmul(
                                    pst[:, 0:256], lhsT=lh[:, kc, :], rhs=wb[:, kc, c0:c0 + 256],
                                    start=(kc == 0), stop=(kc == 7)), reads=lres + [wr], writes=[pr], sig=(kc == 7))
                            S.op("act", lambda e, pst=pst, g=g, r=r, blk=blk: e.activation(
                                out=VV[g][:, r, 1 + blk, :].rearrange("p (s e) -> p s e", e=65)[:, :, 0:64],
                                in_=pst[:, 0:256].rearrange("p (s e) -> p s e", e=64), func=AF.Copy,
                                scale=vld[g][:, r, blk:blk + 1]),
                                reads=[pr, "vld%d" % g], writes=["VV%d" % g])
                            S.op("pool", lambda e, g=g, r=r, blk=blk: e.tensor_copy(
                                out=VV[g][:, r, 1 + blk, :].rearrange("p (s e) -> p s e", e=65)[:, :, 64:65],
                                in_=vld[g][:, r, blk:blk + 1].unsqueeze(1).to_broadcast([128, 4, 1])),
                                reads=["vld%d" % g], writes=["VV%d" % g])
                            if last:
                                W = min(GROUPS[g][0], S_TOK)
                                d = GROUPS[g][1]
                                nblk = 16 // d
                                first_tok = T0 + r + d * 128 * blk
                                if S_TOK - (T0 + d * 128 * blk) <= W:
                                    row0 = W - (S_TOK - first_tok)
                                    vb = vst[ev % 2]
                                    vr = "vst%d" % (ev % 2)
                                    ev += 1
                                    S.op("dve", lambda e, pst=pst, vb=vb: e.tensor_copy(out=vb[:, :], in_=pst[:, 0:256]),
                                         reads=[pr], writes=[vr])
                                    dma("sp", vp[g][row0:W:d, :], vb[:, :], reads=[vr], key="o" + vr)
                if KSTOP == 23 and False:
                    S.barrier()
                    S.emit()
                    return nc
                if KSTOP == 11:
                    S.barrier()
                    S.emit()
                    return nc
                if sb == 1:
                    rope(sts[0:NS, 0:1536].rearrange("p (h d) -> p h d", d=64), "sts", NS, 24, css[:, :], sns[:, :],
                         ["css", "sns"], ri)
                    ri += 1
                    for g in range(3):
                        W = GROUPS[g][0]
                        dma("sp", ks_o[g][:, W - 1, :], sts[0:NS, 768 + 256 * g:768 + 256 * (g + 1)], reads=["sts"],
                            key="osts")
                        dma("sp", vs_o[g][:, W - 1, :], sts[0:NS, 1536 + 256 * g:1536 + 256 * (g + 1)], reads=["sts"],
                            key="osts")
                    S.op("dve", lambda e: e.tensor_tensor(out=sm[0:NS, 0:768], in0=sts[0:NS, 0:768],
                                                          in1=sts[0:NS, 768:1536], op=ALU.mult),
                         reads=["sts"], writes=["sm"])
                    S.op("dve", lambda e: e.tensor_reduce(out=pcur[:, :],
                                                          in_=sm[0:NS, 0:768].rearrange("p (h d) -> p h d", d=64),
                                                          axis=AX.X, op=ALU.add), reads=["sm"], writes=["pcur"])
                    S.op("act", lambda e: e.activation(out=pcur[:, :], in_=pcur[:, :], func=AF.Exp),
                         reads=["pcur"], writes=["pcur"])
                    S.op("dve", lambda e: e.tensor_tensor(out=pcm[:, :, :],
                                                          in0=pcur[:, :].unsqueeze(1).to_broadcast([NS, NS, 12]),
                                                          in1=id4[:, :].unsqueeze(2).to_broadcast([NS, NS, 12]),
                                                          op=ALU.mult), reads=["pcur", "id4"], writes=["pcm"])
                    for s in range(NS):
                        pnum, pnr = psum()
                        for g in range(3):
                            W, d = GROUPS[g]
                            kb_, vb_ = Kc[g], Vc[g]
                            kr, vr = "Kc%d" % g, "Vc%d" % g
                            dma("sp", kb_[:, :], cks[g][s, 0:W:d, :], writes=[kr], key=kr)
                            dma("sp", vb_[:, :], cvs[g][s, 0:W:d, :], writes=[vr], key=vr)
                            pq, pqr = psum()
                            S.op("pe", lambda e, pq=pq, s=s, g=g: e.matmul(pq[:, 0:256], lhsT=selb[0:NS, s, :],
                                                                           rhs=sts[0:NS, 256 * g:256 * (g + 1)],
                                                                           start=True, stop=True),
                                 reads=["selb", "sts"], writes=[pqr])
                            S.op("dve", lambda e, pq=pq, kb_=kb_: e.tensor_tensor(out=prod[:, :], in0=kb_[:, :],
                                                                                    in1=pq[:, 0:256], op=ALU.mult),
                                 reads=[kr, pqr], writes=["prod"])
                            S.op("dve", lambda e, g=g: e.tensor_reduce(out=sT4[:, 4 * g:4 * g + 4],
                                                                  in_=prod[:, :].rearrange("p (h d) -> p h d", d=64),
                                                                  axis=AX.X, op=ALU.add), reads=["prod"], writes=["sT4"])
                        S.op("act", lambda e: e.activation(out=sT4[:, 12:24], in_=sT4[:, 0:12], func=AF.Exp),
                             reads=["sT4"], writes=["sT4"])
                        for c in range(3):
                            for g in range(3):
                                pT_ = sT4[:, 12 + 4 * g:16 + 4 * g]
                                pc_ = pcm[0:NS, s, 4 * g:4 * g + 4]
                                if c < 2:
                                    l1 = Vc[g][:, 128 * c:128 * (c + 1)]
                                    l2 = sts[0:NS, 1536 + 256 * g + 128 * c:1536 + 256 * g + 128 * (c + 1)]
                                    r1 = ["Vc%d" % g, "sT4"]
                                else:
                                    l1 = onesf[:, :]
                                    l2 = onesf[0:NS, :]
                                    r1 = ["onesf", "sT4"]
                                S.op("pe", lambda e, c=c, g=g, pnum=pnum, l1=l1, pT_=pT_: e.matmul(
                                    pnum[:, 4 * c:4 * c + 4], lhsT=l1, rhs=pT_, start=(g == 0), stop=False),
                                    reads=r1, writes=[pnr])
                                S.op("pe", lambda e, c=c, g=g, pnum=pnum, l2=l2, pc_=pc_: e.matmul(
                                    pnum[:, 4 * c:4 * c + 4], lhsT=l2, rhs=pc_, start=False, stop=(g == 2)),
                                    reads=["sts", "pcm", "onesf"], writes=[pnr], sig=(g == 2))
                        S.op("dve", lambda e, pnum=pnum: e.reciprocal(out=sT4[:, 0:4], in_=pnum[:, 8:12]),
                             reads=[pnr], writes=["sT4"])
                        for c in range(2):
                            for hf in range(2):
                                slot = 2 * c + hf
                                S.op("dve", lambda e, c=c, hf=hf, slot=slot, pnum=pnum, s=s: e.tensor_tensor(
                                    out=oTf[64 * hf:64 * (hf + 1), c, s:s + 1],
                                    in0=pnum[64 * hf:64 * (hf + 1), 4 * c + slot:4 * c + slot + 1],
                                    in1=sT4[64 * hf:64 * (hf + 1), slot:slot + 1], op=ALU.mult),
                                    reads=[pnr, "sT4"], writes=["oTf"])
                    S.op("dve", lambda e: e.tensor_copy(out=oTs[:, :, :], in_=oTf[:, :, :]), reads=["oTf"], writes=["oTs"])
                    if DBG:
                        d1 = dout("dbg_oTf", [128, 2, NS], F32)
                        dma("sp", d1, oTf[:, :, :], reads=["oTf"], key="dbg")
                        d2 = dout("dbg_sts", [NS, 2304], F32)
                        dma("sp", d2, sts[:, :], reads=["sts"], key="dbg")
                        d3 = dout("dbg_pcur", [NS, 12], F32)
                        dma("sp", d3, pcur[:, :], reads=["pcur"], key="dbg")

                if KSTOP == 12:
                    S.barrier()
                    S.emit()
                    return nc
                if DBG and sb == 0:
                    for g in range(3):
                        dq = dout("dbg_QT%d" % g, [128, 2, GROUPS[g][1], 2048 // GROUPS[g][1]], BF16)
                        dk = dout("dbg_KT%d" % g, [128, 2, GROUPS[g][1], 128 + 2048 // GROUPS[g][1]], BF16)
                        dv = dout("dbg_VV%d" % g, [128, GROUPS[g][1], NB[g], 260], BF16)
                        dma("sp", dq, QT[g][:, :, :, :], reads=["QT%d" % g], key="dbg")
                        dma("sp", dk, KT[g][:, :, :, :], reads=["KT%d" % g], key="dbg")
                        dma("sp", dv, VV[g][:, :, :, :], reads=["VV%d" % g], key="dbg")
                pti = 0
                if sb == 0:
                    continue
                ch_list = (3,) if sb == 1 else (0, 1, 2, 3)
                for slot in range(4):
                    c2 = slot // 2
                    pb = 64 * (slot % 2)
                    for rp in range(8):
                        pss, psr = psum45()
                        for q in range(2):
                            r = 2 * rp + q
                            if sb > 0:
                                S.op("pe", lambda e, pss=pss, q=q, r=r: e.matmul(
                                    pss[:, 256 * q:256 * q + 128], lhsT=KT[2][pb:pb + 64, c2, r, 0:128],
                                    rhs=QT[2][pb:pb + 64, c2, r, 0:128], start=True, stop=True),
                                    reads=["KT2", "QT2"], writes=[psr])
                            S.op("pe", lambda e, pss=pss, q=q, r=r: e.matmul(
                                pss[:, 256 * q + 128:256 * q + 256], lhsT=KT[2][pb:pb + 64, c2, r, 128:256],
                                rhs=QT[2][pb:pb + 64, c2, r, 0:128], start=True, stop=True),
                                reads=["KT2", "QT2"], writes=[psr])
                        k0 = 0 if sb > 0 else 1
                        S.op("act", lambda e, pss=pss, rp=rp, k0=k0: e.activation(
                            out=PT2[:, 2 * rp:2 * rp + 2, k0:2, :],
                            in_=pss[:, :].rearrange("p (q k t) -> p q k t", q=2, k=2)[:, :, k0:2, :], func=AF.Exp),
                            reads=[psr], writes=["PT2"])
                        S.op("dve", lambda e, rp=rp, k0=k0: e.tensor_tensor(
                            out=PT2[:, 2 * rp:2 * rp + 2, k0:2, :], in0=PT2[:, 2 * rp:2 * rp + 2, k0:2, :],
                            in1=maskb[:, k0:2, :].unsqueeze(1).to_broadcast([128, 2, 2 - k0, 128]), op=ALU.mult),
                            reads=["PT2", "maskb"], writes=["PT2"])
                    if DBG and sb == 0 and slot == 0:
                        dp2 = dout("dbg_PT2", [128, 16, 2, 128], BF16)
                        dma("sp", dp2, PT2[:, :, :, :], reads=["PT2"], key="dbg")
                    pendB = [None]
                    for ch in ch_list:
                        bset[0] += 1
                        if bset[0] % 2:
                            Bk = [(PSB[i_], "ps%d" % i_) for i_ in range(3)]
                        else:
                            Bk = [(PSB[3], "ps3"), (PTBf, "ptb"), (PTB2f, "ptb2")]
                        for g in range(2):
                            for pair in range(2):
                                pss, psr = psum45()
                                ptb_ = PTs[pti % 2]
                                ptr = "PTs%d" % (pti % 2)
                                pti += 1
                                info = []
                                for q in range(2):
                                    u = 2 * pair + q
                                    if g == 0:
                                        qb_i = 4 * ch + u
                                        hasp = (sb > 0) or (qb_i > 0)
                                        kprev = KT[0][pb:pb + 64, c2, 0, 128 * qb_i:128 * qb_i + 128]
                                        kcur = KT[0][pb:pb + 64, c2, 0, 128 + 128 * qb_i:256 + 128 * qb_i]
                                        qq = QT[0][pb:pb + 64, c2, 0, 128 * qb_i:128 * qb_i + 128]
                                        vprev = VV[0][:, 0, qb_i, 65 * slot:65 * (slot + 1)]
                                        vcur = VV[0][:, 0, qb_i + 1, 65 * slot:65 * (slot + 1)]
                                        ocols = slice(128 * u, 128 * (u + 1))
                                    else:
                                        hasp = (sb > 0) or (ch > 0)
                                        kprev = KT[1][pb:pb + 64, c2, u, 128 * ch:128 * ch + 128]
                                        kcur = KT[1][pb:pb + 64, c2, u, 128 + 128 * ch:256 + 128 * ch]
                                        qq = QT[1][pb:pb + 64, c2, u, 128 * ch:128 * ch + 128]
                                        vprev = VV[1][:, u, ch, 65 * slot:65 * (slot + 1)]
                                        vcur = VV[1][:, u, ch + 1, 65 * slot:65 * (slot + 1)]
                                        ocols = slice(128 * u, 128 * (u + 1))
                                    if hasp:
                                        S.op("pe", lambda e, pss=pss, q=q, kprev=kprev, qq=qq: e.matmul(
                                            pss[:, 256 * q:256 * q + 128], lhsT=kprev, rhs=qq, start=True, stop=True),
                                            reads=["KT%d" % g, "QT%d" % g], writes=[psr])
                                    S.op("pe", lambda e, pss=pss, q=q, kcur=kcur, qq=qq: e.matmul(
                                        pss[:, 256 * q + 128:256 * q + 256], lhsT=kcur, rhs=qq, start=True, stop=True),
                                        reads=["KT%d" % g, "QT%d" % g], writes=[psr])
                                    info.append((hasp, vprev, vcur, ocols))
                                allp = info[0][0] and info[1][0]
                                k0 = 0 if allp else 1
                                if (not allp) and (info[0][0] or info[1][0]):
                                    S.op("act", lambda e, pss=pss, ptb_=ptb_: e.activation(
                                        out=ptb_[:, 1, 0, :], in_=pss[:, 256:384], func=AF.Exp), reads=[psr], writes=[ptr])
                                    S.op("dve", lambda e, ptb_=ptb_: e.tensor_tensor(
                                        out=ptb_[:, 1, 0, :], in0=ptb_[:, 1, 0, :], in1=maskb[:, 0, :], op=ALU.mult),
                                        reads=[ptr, "maskb"], writes=[ptr])
                                S.op("act", lambda e, pss=pss, ptb_=ptb_, k0=k0: e.activation(
                                    out=ptb_[:, :, k0:2, :],
                                    in_=pss[:, :].rearrange("p (q k t) -> p q k t", q=2, k=2)[:, :, k0:2, :], func=AF.Exp),
                                    reads=[psr], writes=[ptr])
                                S.op("dve", lambda e, ptb_=ptb_, k0=k0: e.tensor_tensor(
                                    out=ptb_[:, :, k0:2, :], in0=ptb_[:, :, k0:2, :],
                                    in1=maskb[:, k0:2, :].unsqueeze(1).to_broadcast([128, 2, 2 - k0, 128]), op=ALU.mult),
                                    reads=[ptr, "maskb"], writes=[ptr])
                                if DBG and sb == 0 and slot == 0 and ch == 0:
                                    dpt = dout("dbg_PT%d_%d" % (g, pair), [128, 2, 2, 128], BF16)
                                    dma("sp", dpt, ptb_[:, :, :, :], reads=[ptr], key="dbg")

                                def emitB(info=info, ptb_=ptb_, ptr=ptr, g=g, Bk=Bk):
                                    for q in range(2):
                                        hasp, vprev, vcur, ocols = info[q]
                                        pO, pOr = Bk[g]
                                        kbl = (0, 1) if hasp else (1,)
                                        for kb in kbl:
                                            vv = vprev if kb == 0 else vcur
                                            S.op("pe", lambda e, pO=pO, vv=vv, ptb_=ptb_, q=q, kb=kb, ocols=ocols, kbl=kbl: e.matmul(
                                                pO[0:65, ocols], lhsT=vv, rhs=ptb_[:, q, kb, :], start=(kb == kbl[0]),
                                                stop=(kb == 1)), reads=["VV%d" % g, ptr], writes=[pOr])
                                if pendB[0] is not None:
                                    pendB[0]()
                                pendB[0] = emitB
                        pendB[0]()
                        pendB[0] = None
                        pO, pOr = Bk[2]
                        for r in range(16):
                            kbs = (0, 1) if sb > 0 else (1,)
                            for kb in kbs:
                                S.op("pe", lambda e, pO=pO, r=r, kb=kb, kbs=kbs: e.matmul(
                                    pO[0:65, 32 * r:32 * (r + 1)], lhsT=VV[2][:, r, kb, 65 * slot:65 * (slot + 1)],
                                    rhs=PT2[:, r, kb, 32 * ch:32 * (ch + 1)], start=(kb == kbs[0]), stop=(kb == 1)),
                                    reads=["VV2", "PT2"], writes=[pOr], sig=(kb == 1))
                        if DBG and sb == 0 and slot == 0 and ch == 0:
                            for gq in range(3):
                                db = dout("dbg_B%d" % gq, [65, 512], F32)
                                S.op("dve", lambda e, gq=gq: e.tensor_copy(out=stg[1][0:65, :], in_=Bk[gq][0][0:65, :]),
                                     reads=[Bk[gq][1]], writes=["stg1"])
                                dma("sp", db, stg[1][0:65, :], reads=["stg1"], key="dbg")
                        tmpb = stg[0]
                        S.op("act", lambda e, tmpb=tmpb, b0=Bk[0][0]: e.activation(out=tmpb[0:65, :], in_=b0[0:65, :], func=AF.Copy),
                             reads=[Bk[0][1]], writes=["stg0"])
                        S.op("dve", lambda e, tmpb=tmpb, b1=Bk[1][0]: e.tensor_tensor(
                            out=tmpb[0:65, :].rearrange("p (j r) -> p j r", r=4),
                            in0=tmpb[0:65, :].rearrange("p (j r) -> p j r", r=4),
                            in1=b1[0:65, :].rearrange("p (r j) -> p j r", r=4), op=ALU.add),
                            reads=[Bk[1][1], "stg0"], writes=["stg0"])
                        S.op("dve", lambda e, tmpb=tmpb, b2=Bk[2][0]: e.tensor_tensor(
                            out=tmpb[0:65, :].rearrange("p (j r) -> p j r", r=16),
                            in0=tmpb[0:65, :].rearrange("p (j r) -> p j r", r=16),
                            in1=b2[0:65, :].rearrange("p (r j) -> p j r", r=16), op=ALU.add),
                            reads=[Bk[2][1], "stg0"], writes=["stg0"])
                        S.op("dve", lambda e, tmpb=tmpb: e.tensor_scalar(out=tmpb[64:65, :], in0=tmpb[64:65, :], scalar1=1e-30,
                                                                        scalar2=None, op0=ALU.add),
                             reads=["stg0"], writes=["stg0"])
                        S.op("dve", lambda e, tmpb=tmpb: e.reciprocal(out=stg[1][64:65, :], in_=tmpb[64:65, :]),
                             reads=["stg0"], writes=["stg1"])
                        pD, pDr = Bk[0]
                        S.op("pe", lambda e, pD=pD: e.matmul(pD[0:64, :], lhsT=onesf[64:65, 0:64], rhs=stg[1][64:65, :],
                                                             start=True, stop=True), reads=["onesf", "stg1"], writes=[pDr])
                        pO, pOr = None, "stg0"
                        rc = rec[0]
                        rcr = "rec0"
                        S.op("dve", lambda e, pD=pD, tmpb=tmpb, slot=slot, ch=ch: e.tensor_tensor(
                            out=oTsb[:, 512 * ch:512 * (ch + 1)], in0=tmpb[0:64, :], in1=pD[0:64, :], op=ALU.mult),
                            reads=[pDr, "stg0"], writes=["oTsb"])
                    dma("sp", oT_d[:, slot, T0:T0 + 2048], oTsb[:, :], reads=["oTsb"], writes=["oT_d"], key="oT_d")
            for sb_ in range(NSB, len(LATE_CASTS)):
                for f_ in LATE_CASTS[sb_]:
                    f_()
            S.barrier()
            S.emit()
        if KSTOP == 1:
            return nc

        with contextlib.ExitStack() as st2:
            xg2 = [sbuf(st2, "xg_%d" % i, [128, NT // 128, D], F32) for i in range(2)]
            xs4 = xsb + [sbuf(st2, "xsb%d" % i, [128, D], BF16) for i in (2, 3)]
            hT = sbuf(st2, "hT", [128, 8, NT], BF16)
            uext = sbuf(st2, "uext", [128, 8, 30 + NT], BF16)
            dgb = [sbuf(st2, "dgb%d" % i, [128, 31, 128], BF16) for i in range(2)]
            big2 = sbuf(st2, "big2", [128, 12 * NT], F32)
            acc = big2[:, 0:8 * NT].rearrange("p (c t) -> p c t", c=8)
            lnm = big2[:, 8 * NT:12 * NT].rearrange("p (c t) -> p c t", c=4)
            aT = big2[:, 0:11 * NT].bitcast(BF16).rearrange("p (c t) -> p c t", c=22)
            R3 = sbuf(st2, "R3", [128, 11264], F32)
            wdnb = R3[:, :].bitcast(BF16).rearrange("p (c t) -> p c t", c=22)
            sT = R3[:, 0:2048].bitcast(BF16).rearrange("p (c t) -> p c t", c=8)
            mixT = R3[:, 2048:4096].bitcast(BF16).rearrange("p (c t) -> p c t", c=8)
            woutb = R3[:, 4096:8192].bitcast(BF16).rearrange("p (c t) -> p c t", c=8)
            oTg = R3[0:64, 8192:9216].bitcast(BF16).rearrange("p (c t) -> p c t", c=4)
            cb16 = [R3[:, 9216 + 256 * i:9472 + 256 * i].bitcast(BF16) for i in range(2)]
            csq16 = [R3[:, 9728 + 256 * i:9984 + 256 * i].bitcast(BF16) for i in range(2)]
            tt = [R3[:, 10240 + 512 * i:10752 + 512 * i] for i in range(2)]
            S.alias["wdnb"] = ["sT", "mixT", "woutb", "oTg", "cb16_0", "cb16_1", "csq16_0", "csq16_1", "tt0", "tt1"]
            S.alias["aT"] = ["acc%d" % c for c in range(8)] + ["lnm"]
            S.alias["uh"] = ["uext"] + ["uext%d" % c for c in range(8)]
            S.alias["uprod"] = S.alias["uh"]
            wao64 = sbuf(st2, "wao64", [64, 4, D], BF16)
            wao128 = sbuf(st2, "wao128", [128, 2, D], BF16)
            NWB = 3
            wt = [sbuf(st2, "wt%d" % i, [128, 8, 256], BF16) for i in range(NWB)]
            sg_ = [sbuf(st2, "sg%d" % i, [128, NT], F32) for i in range(3)]
            upx = [sbuf(st2, "upx%d" % i, [128, NT + 2], F32) for i in range(2)]
            cgb = [sbuf(st2, "cgb%d" % i, [128, NT], F32) for i in range(2)]
            fh = sbuf(st2, "fh", [128, 44, 2], F32)
            gfin = sbuf(st2, "gfin", [128, D], F32)
            yt = [sbuf(st2, "yt%d" % i, [128, D], F32) for i in range(1)]
            uflat = uext[:, :, :].rearrange("p c t -> p (c t)")[:, 0:4336].bitcast(F32)
            uh = uflat[:, 0:8 * NS * 31].rearrange("p (c s j) -> p c s j", c=8, s=NS)
            uprod = uflat[:, 8 * NS * 31:16 * NS * 31].rearrange("p (c s j) -> p c s j", c=8, s=NS)
            fhs = sbuf(st2, "fhs", [128, 44, 2 * NS], F32)
            upn = sbuf(st2, "upn", [128, 44, NS], F32)
            orow = yt[0][0:30, :]
            S.alias["orow"] = ["yt0"]

            dma("sp", gfin[:], gfin_d, writes=["gfin"], key="gfin")
            dma("sp", wao64[:], wb_ao.rearrange("(s d) n -> d s n", d=64), reads=["wb_ao"], writes=["wao"], key="wao64")
            dma("sp", wao128[:], wb_ao.rearrange("(c p) n -> p c n", p=128), reads=["wb_ao"], writes=["wao"], key="wao128")
            S.op("pool", lambda e: e.memset(uext[:, :, 0:30], 0.0), writes=["uext"])
            S.op("pool", lambda e: e.memset(fh[:], 0.0), writes=["fh"])

            wi = [0]

            def wtile(src_ap, rd):
                i = wi[0] % NWB
                wi[0] += 1
                dma("sp", wt[i][:, :, :], src_ap.rearrange("(kc p) n -> p kc n", p=128), reads=[rd],
                    writes=["wt%d" % i], key="wt%d" % i)
                return wt[i], "wt%d" % i

            tgl = [0]

            def alt(a, b):
                tgl[0] += 1
                return a if tgl[0] % 2 else b

            ygi = [0]

            prevN = [NT]

            def head(sp_, k):
                xgk = xg2[k % 2]
                for (i, TT) in sp_["tiles"]:
                    src = xs if sp_["sample"] else xp[sp_["t0"] + 128 * i:sp_["t0"] + 128 * i + TT, :]
                    dma("sp", xgk[0:TT, i, :], src, writes=["xg%d_%d" % (k % 2, i)], key="xg%d_%d" % (k % 2, i))
                    norm_a(xgk[0:TT, i, :], "xg%d_%d" % (k % 2, i), TT, xs4[i], "xsb%d" % i)

            def head_b(sp_, k):
                for (i, TT) in sp_["tiles"]:
                    norm_b(TT, xs4[i], "xsb%d" % i, hT[:, :, 128 * i:128 * i + TT], "hT", "gmix")

            def body(t0, TT_list, N, sample, first=False, lastg=False, out0=None, kidx=0, hook_a=None, hook_b=None):
                gi = 1
                xg = xg2[kidx % 2]
                XG = "xg%d_" % (kidx % 2)
                if not sample:
                    dma("sp", oTg[:, :, 0:N], oT_d[:, :, t0:t0 + N], reads=["oT_d"], writes=["oTg"], key="oTg")
                dma("sp", woutb[:, :, :], wb_out.rearrange("(kc p) n -> p kc n", p=128), reads=["wb_out"],
                    writes=["woutb"], key="woutb")
                if sample:
                    for s in range(NS):
                        dma("sp", orow[:, :], sconv[s, :, :], writes=["orow"], key="scv")
                        for c in range(8):
                            pst, pr = psum()
                            S.op("pe", lambda e, c=c, pst=pst: e.transpose(pst[:, 0:30], orow[0:30, 128 * c:128 * (c + 1)],
                                                                            identf[0:30, 0:30]),
                                 reads=["orow", "identf"], writes=[pr])
                            S.op("act", lambda e, c=c, pst=pst, s=s: e.activation(out=uh[:, c, s, 0:30], in_=pst[:, 0:30],
                                                                                  func=AF.Copy), reads=[pr], writes=["uh"])
                elif not first:
                    pN = prevN[0]
                    S.op("pool", lambda e: e.tensor_copy(out=uext[:, :, 0:30], in_=uext[:, :, pN:pN + 30]),
                         reads=["uext"], writes=["uext"])
                if not sample:
                    prevN[0] = N
                for c in range(8):
                    w, wr = wtile(wb_in[:, 256 * c:256 * (c + 1)], "wb_in1")
                    pl, plr = psum()
                    pg, pgr = psum()
                    for kc in range(8):
                        S.op("pe", lambda e, kc=kc, w=w, pl=pl: e.matmul(pl[:, 0:N], lhsT=w[:, kc, 0:128], rhs=hT[:, kc, 0:N],
                                                                         start=(kc == 0), stop=(kc == 7)),
                             reads=[wr, "hT"], writes=[plr], sig=(kc == 7))
                    for kc in range(8):
                        S.op("pe", lambda e, kc=kc, w=w, pg=pg: e.matmul(pg[:, 0:N], lhsT=w[:, kc, 128:256], rhs=hT[:, kc, 0:N],
                                                                         start=(kc == 0), stop=(kc == 7)),
                             reads=[wr, "hT"], writes=[pgr], sig=(kc == 7))
                    sgb = sg_[c % 2]
                    sgr = "sg%d" % (c % 2)
                    S.op("act", lambda e, pg=pg, sgb=sgb: e.activation(out=sgb[:, 0:N], in_=pg[:, 0:N], func=AF.Sigmoid),
                         reads=[pgr], writes=[sgr])
                    if sample:
                        S.op("dve", lambda e, c=c, pl=pl, sgb=sgb: e.tensor_tensor(out=uh[:, c, :, 30], in0=pl[:, 0:N],
                                                                                   in1=sgb[:, 0:N], op=ALU.mult),
                             reads=[plr, sgr], writes=["uh"])
                    else:
                        S.op("dve", lambda e, c=c, pl=pl, sgb=sgb: e.tensor_tensor(out=uext[:, c, 30:30 + N], in0=pl[:, 0:N],
                                                                                   in1=sgb[:, 0:N], op=ALU.mult),
                             reads=[plr, sgr], writes=["uext%d" % c])
                o_w = PP["wdw"][0]
                if sample:
                    S.op("dve", lambda e: e.tensor_tensor(
                        out=uprod[:, :, :, :], in0=uh[:, :, :, :],
                        in1=pp[:, o_w:o_w + 248].rearrange("p (c j) -> p c j", j=31).unsqueeze(2).to_broadcast([128, 8, NS, 31]),
                        op=ALU.mult), reads=["uh", "pp"], writes=["uprod"])
                    S.op("dve", lambda e: e.tensor_reduce(out=acc[:, :, 0:NS], in_=uprod[:, :, :, :], axis=AX.X, op=ALU.add),
                         reads=["uprod"], writes=["acc%d" % c for c in range(8)])
                    o_b = PP["bdw"][0]
                    S.op("dve", lambda e: e.tensor_tensor(out=acc[:, :, 0:NS], in0=acc[:, :, 0:NS],
                                                          in1=pp[:, o_b:o_b + 8].unsqueeze(2).to_broadcast([128, 8, NS]),
                                                          op=ALU.add), reads=["acc%d" % c for c in range(8)] + ["pp"],
                         writes=["acc%d" % c for c in range(8)])
                else:
                    for c in range(8):
                        dg = dgb[c % 2]
                        dgr = "dgb%d" % (c % 2)
                        S.op("pool", lambda e, c=c, dg=dg: e.tensor_tensor(
                            out=dg[:, :, :], in0=identf[:, :].unsqueeze(1).to_broadcast([128, 31, 128]),
                            in1=pp[:, o_w + 31 * c:o_w + 31 * c + 31].unsqueeze(2).to_broadcast([128, 31, 128]),
                            op=ALU.mult), reads=["identf", "pp"], writes=[dgr])
                        pcv, pcr = psum()
                        for j in range(31):
                            S.op("pe", lambda e, c=c, j=j, dg=dg, pcv=pcv: e.matmul(
                                pcv[:, 0:N], lhsT=dg[:, j, :], rhs=uext[:, c, j:j + N], start=(j == 0), stop=(j == 30)),
                                reads=[dgr, "uext%d" % c, "uext"], writes=[pcr], sig=(j == 30))
                        S.op("act", lambda e, c=c, pcv=pcv: e.activation(out=acc[:, c, 0:N], in_=pcv[:, 0:N], func=AF.Identity,
                                                                         bias=P("bdw", c)),
                             reads=[pcr, "pp"], writes=["acc%d" % c])
                if sample or lastg:
                    for c in range(8):
                        pst, pr = psum()
                        if sample:
                            S.op("pe", lambda e, c=c, pst=pst: e.transpose(pst[0:NS, 0:128], uh[:, c, :, 30], identf[:, :]),
                                 reads=["uh", "identf"], writes=[pr])
                            S.op("act", lambda e, c=c, pst=pst: e.activation(out=orow[0:NS, 128 * c:128 * (c + 1)],
                                                                             in_=pst[0:NS, 0:128], func=AF.Copy),
                                 reads=[pr], writes=["orow"])
                        else:
                            S.op("pe", lambda e, c=c: e.transpose(PTB[0:30, 128 * c:128 * (c + 1)], uext[:, c, NT:NT + 30],
                                                                   identb[:, :]),
                                 reads=["uext%d" % c, "identb"], writes=["ptb"])
                            S.op("act", lambda e, c=c: e.activation(out=orow[0:30, 128 * c:128 * (c + 1)],
                                                                    in_=PTB[0:30, 128 * c:128 * (c + 1)], func=AF.Copy),
                                 reads=["ptb"], writes=["orow"])
                    if sample:
                        dma("sp", conv_s[:, 29, :], orow[0:NS, 0:D], reads=["orow"], key="oorow")
                    else:
                        dma("sp", conv_p[:, :], orow[0:30, 0:D], reads=["orow"], key="oorow")
                if DBG and t0 == HALO and not sample:
                    dma("sp", dbg_acc, acc[:, :, :], reads=["acc%d" % c for c in range(8)], key="dbg")
                p1, p1r = psum()
                p2, p2r = psum()
                for c in range(8):
                    b1 = cb16[c % 2]
                    b2 = csq16[c % 2]
                    S.op("dve", lambda e, c=c, b1=b1: e.tensor_copy(out=b1[:, 0:N], in_=acc[:, c, 0:N]),
                         reads=["acc%d" % c], writes=["cb16_%d" % (c % 2)])
                    S.op("act", lambda e, c=c, b2=b2: e.activation(out=b2[:, 0:N], in_=acc[:, c, 0:N], func=AF.Square),
                         reads=["acc%d" % c], writes=["csq16_%d" % (c % 2)])
                    S.op("pe", lambda e, c=c, b1=b1: e.matmul(p1[:, 0:N], lhsT=onesb[:, :], rhs=b1[:, 0:N], start=(c == 0),
                                                              stop=(c == 7)), reads=["onesb", "cb16_%d" % (c % 2)], writes=[p1r])
                    S.op("pe", lambda e, c=c, b2=b2: e.matmul(p2[:, 0:N], lhsT=onesb[:, :], rhs=b2[:, 0:N], start=(c == 0),
                                                              stop=(c == 7)), reads=["onesb", "csq16_%d" % (c % 2)], writes=[p2r])
                S.op("dve", lambda e: e.tensor_scalar(out=lnm[:, 0, 0:N], in0=p1[:, 0:N], scalar1=1.0 / D, scalar2=None,
                                                      op0=ALU.mult), reads=[p1r], writes=["lnm"])
                S.op("dve", lambda e: e.tensor_tensor(out=lnm[:, 1, 0:N], in0=lnm[:, 0, 0:N], in1=lnm[:, 0, 0:N], op=ALU.mult),
                     reads=["lnm"], writes=["lnm"])
                S.op("dve", lambda e: e.scalar_tensor_tensor(out=lnm[:, 1, 0:N], in0=p2[:, 0:N], scalar=1.0 / D,
                                                             in1=lnm[:, 1, 0:N], op0=ALU.mult, op1=ALU.subtract),
                     reads=[p2r, "lnm"], writes=["lnm"])
                S.op("act", lambda e: e.activation(out=lnm[:, 2, 0:N], in_=lnm[:, 1, 0:N], func=AF.Sqrt, bias=epst[:, 0:1]),
                     reads=["lnm", "epst"], writes=["lnm"])
                S.op("dve", lambda e: e.reciprocal(out=lnm[:, 3, 0:N], in_=lnm[:, 2, 0:N]), reads=["lnm"], writes=["lnm"])
                for c in range(8):
                    tb = tt[c % 2]
                    tr = "tt%d" % (c % 2)
                    S.op("dve", lambda e, c=c, tb=tb: e.tensor_tensor(out=tb[:, 0:N], in0=acc[:, c, 0:N], in1=lnm[:, 0, 0:N],
                                                                      op=ALU.subtract),
                         reads=["acc%d" % c, "lnm"], writes=[tr])
                    S.op("dve", lambda e, tb=tb: e.tensor_tensor(out=tb[:, 0:N], in0=tb[:, 0:N], in1=lnm[:, 3, 0:N],
                                                                  op=ALU.mult), reads=[tr, "lnm"], writes=[tr])
                    S.op("act", lambda e, c=c, tb=tb: e.activation(out=sT[:, c, 0:N], in_=tb[:, 0:N], func=AF.Silu,
                                                                   scale=P("lng", c), bias=P("lnb", c)),
                         reads=[tr, "pp"], writes=["sT"])
                if DBG and t0 == HALO and not sample:
                    dma("sp", dbg_sT, sT[:, :, :], reads=["sT"], key="dbg")
                for c in range(8):
                    if c % 2 == 0:
                        wco_t, wco_r = wtile(wb_co[:, 128 * c:128 * c + 256], "wb_co")
                    wg_t, wg_r = wtile(wb_in[:, 4352 + 256 * c:4352 + 256 * (c + 1)], "wb_in3")
                    pa, par = psum()
                    pbb, pbr = psum()
                    pga, pgar = psum()
                    pgb, pgbr = psum()
                    co = 128 * (c % 2)
                    for kc in range(8):
                        S.op("pe", lambda e, kc=kc, pa=pa, wco_t=wco_t, co=co: e.matmul(
                            pa[:, 0:N], lhsT=wco_t[:, kc, co:co + 128], rhs=sT[:, kc, 0:N], start=(kc == 0), stop=(kc == 7)),
                            reads=[wco_r, "sT"], writes=[par], sig=(kc == 7))
                    if sample:
                        for k2 in range(2):
                            S.op("pe", lambda e, k2=k2, pbb=pbb, c=c: e.matmul(
                                pbb[:, 0:N], lhsT=wao128[:, k2, 128 * c:128 * (c + 1)], rhs=oTs[:, k2, 0:N],
                                start=(k2 == 0), stop=(k2 == 1)), reads=["wao", "oTs"], writes=[pbr], sig=(k2 == 1))
                    else:
                        for sl in range(4):
                            S.op("pe", lambda e, sl=sl, pbb=pbb, c=c: e.matmul(
                                pbb[:, 0:N], lhsT=wao64[:, sl, 128 * c:128 * (c + 1)], rhs=oTg[:, sl, 0:N],
                                start=(sl == 0), stop=(sl == 3)), reads=["wao", "oTg"], writes=[pbr], sig=(sl == 3))
                    for kc in range(8):
                        S.op("pe", lambda e, kc=kc, pga=pga, wg_t=wg_t: e.matmul(
                            pga[:, 0:N], lhsT=wg_t[:, kc, 0:128], rhs=hT[:, kc, 0:N], start=(kc == 0), stop=(kc == 7)),
                            reads=[wg_r, "hT"], writes=[pgar], sig=(kc == 7))
                    for kc in range(8):
                        S.op("pe", lambda e, kc=kc, pgb=pgb, wg_t=wg_t: e.matmul(
                            pgb[:, 0:N], lhsT=wg_t[:, kc, 128:256], rhs=hT[:, kc, 0:N], start=(kc == 0), stop=(kc == 7)),
                            reads=[wg_r, "hT"], writes=[pgbr], sig=(kc == 7))
                    sa, sar = sg_[0], "sg0"
                    sb_, sbr = sg_[1], "sg1"
                    S.op("act", lambda e, pga=pga: e.activation(out=sa[:, 0:N], in_=pga[:, 0:N], func=AF.Sigmoid),
                         reads=[pgar], writes=[sar])
                    S.op("act", lambda e, pgb=pgb: e.activation(out=sb_[:, 0:N], in_=pgb[:, 0:N], func=AF.Sigmoid),
                         reads=[pgbr], writes=[sbr])
                    S.op("dve", lambda e, pa=pa: e.tensor_tensor(out=sa[:, 0:N], in0=pa[:, 0:N], in1=sa[:, 0:N], op=ALU.mult),
                         reads=[par, sar], writes=[sar])
                    S.op("dve", lambda e, pbb=pbb: e.tensor_tensor(out=sb_[:, 0:N], in0=pbb[:, 0:N], in1=sb_[:, 0:N],
                                                                   op=ALU.mult), reads=[pbr, sbr], writes=[sbr])
                    S.op("dve", lambda e, c=c: e.tensor_tensor(out=mixT[:, c, 0:N], in0=sa[:, 0:N], in1=sb_[:, 0:N],
                                                                op=ALU.add), reads=[sar, sbr], writes=["mixT"])
                if DBG and t0 == HALO and not sample:
                    dma("sp", dbg_mix, mixT[:, :, :], reads=["mixT"], key="dbg")
                for (i, TT) in TT_list:
                    WN = 512 if TT == 128 else 256
                    for n in range(D // WN):
                        po, por = psum()
                        for kc in range(8):
                            S.op("pe", lambda e, kc=kc, po=po, i=i, TT=TT, n=n, WN=WN: e.matmul(
                                po[0:TT, 0:WN], lhsT=mixT[:, kc, 128 * i:128 * i + TT], rhs=woutb[:, kc, WN * n:WN * (n + 1)],
                                start=(kc == 0), stop=(kc == 7)), reads=["mixT", "woutb"], writes=[por], sig=(kc == 7))
                        S.op("dve", lambda e, po=po, i=i, TT=TT, n=n, WN=WN: e.tensor_tensor(
                            out=xg[0:TT, i, WN * n:WN * (n + 1)], in0=po[0:TT, 0:WN], in1=xg[0:TT, i, WN * n:WN * (n + 1)],
                            op=ALU.add), reads=[por, XG + str(i)], writes=[XG + str(i)])
                if DBG and t0 == HALO and not sample:
                    for (i, TT) in TT_list:
                        dma("sp", dbg_xmid[128 * i:128 * (i + 1), :], xg[0:TT, i, :], reads=[XG + str(i)], key="dbg")
                dma("sp", wdnb[:, :, :], wb_dn.rearrange("(kc p) n -> p kc n", p=128), reads=["wb_dn"], writes=["wdnb"],
                    key="wdnb")
                for (i, TT) in TT_list:
                    norm_T(xg[0:TT, i, :], XG + str(i), TT, hT[:, :, 128 * i:128 * i + TT], "hT", "gffn", ygi[0])
                    ygi[0] += 1
                if sample:
                    for q in range(44):
                        if q % 8 == 0:
                            wpc = min(1024, 2 * DFF - 128 * q)
                            dma("sp", orow[0:2 * NS, 0:wpc], sffn.rearrange("s j n -> (s j) n")[:, 128 * q:128 * q + wpc],
                                writes=["orow"], key="scv")
                        pst, pr = psum()
                        S.op("pe", lambda e, q=q, pst=pst: e.transpose(
                            pst[:, 0:2 * NS], orow[0:2 * NS, 128 * (q % 8):128 * (q % 8 + 1)], identf[0:2 * NS, 0:2 * NS]),
                             reads=["orow", "identf"], writes=[pr])
                        S.op("act", lambda e, q=q, pst=pst: e.activation(out=fhs[:, q, :], in_=pst[:, 0:2 * NS], func=AF.Copy),
                             reads=[pr], writes=["fhs"])
                o_f = PP["wfdw"][0]
                for j in range(22):
                    w, wr = wtile(wb_up[:, 256 * j:256 * (j + 1)], "wb_up")
                    pgv = []
                    for hv in range(2):
                        pz, pzr = psum()
                        for kc in range(8):
                            S.op("pe", lambda e, kc=kc, w=w, pz=pz, hv=hv: e.matmul(
                                pz[:, 0:N], lhsT=w[:, kc, 128 * hv:128 * (hv + 1)], rhs=hT[:, kc, 0:N], start=(kc == 0),
                                stop=(kc == 7)), reads=[wr, "hT"], writes=[pzr], sig=(kc == 7))
                        pgv.append((pz, pzr))
                    for hv in range(2):
                        q = 2 * j + hv
                        pz, pzr = pgv[hv]
                        ub, ur = upx[hv], "upx%d" % hv
                        cb, cr = cgb[hv], "cgb%d" % hv
                        w0 = pp[:, o_f + 3 * q:o_f + 3 * q + 1]
                        w1 = pp[:, o_f + 3 * q + 1:o_f + 3 * q + 2]
                        w2 = pp[:, o_f + 3 * q + 2:o_f + 3 * q + 3]
                        if sample:
                            S.op("act", lambda e, pz=pz, cb=cb, w2=w2, q=q: e.activation(
                                out=cb[:, 0:N], in_=pz[:, 0:N], func=AF.Identity, scale=w2, bias=P("bfdw", q)),
                                reads=[pzr, "pp"], writes=[cr])
                            S.op("act", lambda e, pz=pz, q=q: e.activation(out=upn[:, q, :], in_=pz[:, 0:NS], func=AF.Copy),
                                 reads=[pzr], writes=["upn"])
                            fv = fhs[:, q, :].rearrange("p (s j) -> p j s", j=2)
                            S.op("dve", lambda e, cb=cb, fv=fv, w1=w1: e.scalar_tensor_tensor(
                                out=cb[:, 0:N], in0=fv[:, 1, :], scalar=w1, in1=cb[:, 0:N], op0=ALU.mult, op1=ALU.add),
                                reads=["fhs", cr, "pp"], writes=[cr])
                            S.op("dve", lambda e, cb=cb, fv=fv, w0=w0: e.scalar_tensor_tensor(
                                out=cb[:, 0:N], in0=fv[:, 0, :], scalar=w0, in1=cb[:, 0:N], op0=ALU.mult, op1=ALU.add),
                                reads=["fhs", cr, "pp"], writes=[cr])
                        else:
                            S.op("dve", lambda e, ub=ub, q=q: e.tensor_copy(out=ub[:, 0:2], in_=fh[:, q, :]),
                                 reads=["fh"], writes=[ur])
                            S.op("act", lambda e, pz=pz, ub=ub: e.activation(out=ub[:, 2:2 + N], in_=pz[:, 0:N], func=AF.Copy),
                                 reads=[pzr], writes=[ur])
                            S.op("dve", lambda e, ub=ub, q=q: e.tensor_copy(out=fh[:, q, :], in_=ub[:, N:N + 2]),
                                 reads=[ur], writes=["fh"])
                            S.op("dve", lambda e, cb=cb, ub=ub, w2=w2, q=q: e.tensor_scalar(
                                out=cb[:, 0:N], in0=ub[:, 2:2 + N], scalar1=w2, scalar2=P("bfdw", q), op0=ALU.mult,
                                op1=ALU.add), reads=[ur, "pp"], writes=[cr])
                            S.op("dve", lambda e, cb=cb, ub=ub, w1=w1: e.scalar_tensor_tensor(
                                out=cb[:, 0:N], in0=ub[:, 1:1 + N], scalar=w1, in1=cb[:, 0:N], op0=ALU.mult, op1=ALU.add),
                                reads=[ur, cr, "pp"], writes=[cr])
                            S.op("dve", lambda e, cb=cb, ub=ub, w0=w0: e.scalar_tensor_tensor(
                                out=cb[:, 0:N], in0=ub[:, 0:N], scalar=w0, in1=cb[:, 0:N], op0=ALU.mult, op1=ALU.add),
                                reads=[ur, cr, "pp"], writes=[cr])
                    sgb, sgr = sg_[2], "sg2"
                    S.op("act", lambda e, sgb=sgb: e.activation(out=sgb[:, 0:N], in_=cgb[0][:, 0:N], func=AF.Silu),
                         reads=["cgb0"], writes=[sgr])
                    S.op("dve" if sample else "pool", lambda e, j=j, sgb=sgb: e.tensor_tensor(out=aT[:, j, 0:N], in0=sgb[:, 0:N],
                                                                                    in1=cgb[1][:, 0:N], op=ALU.mult),
                         reads=[sgr, "cgb1"], writes=["aT"])
                if sample or lastg:
                    nrow = NS if sample else 2
                    for q in range(44):
                        pst, pr = psum()
                        srcT = upn[:, q, :] if sample else fh[:, q, :]
                        S.op("pe", lambda e, pst=pst, srcT=srcT, nrow=nrow: e.transpose(pst[0:nrow, 0:128], srcT, identf[:, :]),
                             reads=["upn" if sample else "fh", "identf"], writes=[pr])
                        S.op("act", lambda e, q=q, pst=pst, nrow=nrow: e.activation(
                            out=orow[0:nrow, 128 * (q % 8):128 * (q % 8 + 1)], in_=pst[0:nrow, 0:128], func=AF.Copy),
                            reads=[pr], writes=["orow"])
                        if q % 8 == 7 or q == 43:
                            q0 = 8 * (q // 8)
                            wpc = 128 * (q - q0 + 1)
                            if sample:
                                dma("sp", ffn_s[:, 1, 128 * q0:128 * q0 + wpc], orow[0:NS, 0:wpc], reads=["orow"], key="oorow")
                            else:
                                dma("sp", ffn_p[:, 128 * q0:128 * q0 + wpc], orow[0:2, 0:wpc], reads=["orow"], key="oorow")
                if hook_a is not None:
                    hook_a()
                for (i, TT) in TT_list:
                    WN = 512 if TT == 128 else 256
                    for n in range(D // WN):
                        po, por = psum()
                        for kc in range(22):
                            S.op("pe", lambda e, kc=kc, po=po, i=i, TT=TT, n=n, WN=WN: e.matmul(
                                po[0:TT, 0:WN], lhsT=aT[:, kc, 128 * i:128 * i + TT], rhs=wdnb[:, kc, WN * n:WN * (n + 1)],
                                start=(kc == 0), stop=(kc == 21)), reads=["aT", "wdnb"], writes=[por], sig=(kc == 21))
                        S.op("dve", lambda e, po=po, i=i, TT=TT, n=n, WN=WN: e.tensor_tensor(
                            out=xg[0:TT, i, WN * n:WN * (n + 1)], in0=po[0:TT, 0:WN], in1=xg[0:TT, i, WN * n:WN * (n + 1)],
                            op=ALU.add), reads=[por, XG + str(i)], writes=[XG + str(i)])
                if hook_b is not None:
                    hook_b()
                for (i, TT) in TT_list:
                    col = stat_i[0]
                    stat_i[0] += 1
                    src = xg[0:TT, i, :]
                    S.op("act", lambda e, src=src, TT=TT, col=col: e.activation(
                        out=yt[0][0:TT, :], in_=src, func=AF.Square, accum_out=stat[0:TT, 0, col:col + 1]),
                        reads=[XG + str(i)], writes=["yt0", "stat%d" % col])
                    S.op("act", lambda e, TT=TT, col=col: e.activation(
                        out=stat[0:TT, 1, col:col + 1], in_=stat[0:TT, 0, col:col + 1], func=AF.Sqrt, scale=1.0 / D,
                        bias=epst[0:TT, 0:1]), reads=["stat%d" % col, "epst"], writes=["stat%d" % col])
                    S.op("dve", lambda e, TT=TT, col=col: e.reciprocal(out=stat[0:TT, 2, col:col + 1],
                                                                       in_=stat[0:TT, 1, col:col + 1]),
                         reads=["stat%d" % col], writes=["stat%d" % col])
                    yb = yt[0]
                    yr = "yt0"
                    S.op("dve", lambda e, src=src, TT=TT, col=col, yb=yb: e.scalar_tensor_tensor(
                        out=yb[0:TT, :], in0=src, scalar=stat[0:TT, 2, col:col + 1], in1=gfin[0:TT, :], op0=ALU.mult,
                        op1=ALU.mult), reads=[XG + str(i), "stat%d" % col, "gfin"], writes=[yr])
                    if sample:
                        dma("sp", y_s[:, :], yb[0:TT, :], reads=[yr], key="o" + yr)
                    elif out0 is not None:
                        dma("sp", y_p[out0 + 128 * i:out0 + 128 * i + TT, :], yb[0:TT, :], reads=[yr], key="o" + yr)

            hvt = sbuf(st2, "hvt", [128, 1], F32)
            dma("sp", hvt[:], hv_d, writes=["hvt"], key="hvt")
            specs = [dict(t0=HALO - 256, tiles=[(0, 128), (1, 128)], N=256, sample=False, first=True, lastg=False, out0=None)]
            for gi in range(NG):
                specs.append(dict(t0=HALO + gi * NT, tiles=[(i, 128) for i in range(NT // 128)], N=NT, sample=False,
                                  first=False, lastg=(gi == NG - 1), out0=gi * NT))
            specs.append(dict(t0=0, tiles=[(0, NS)], N=NS, sample=True, first=False, lastg=False, out0=None))
            head(specs[0], 0)
            head_b(specs[0], 0)
            for k, sp_ in enumerate(specs):
                if 1 <= k <= NG:
                    for k_, (dst_, src_) in enumerate(SHIFTS):
                        if k_ % NG == k - 1:
                            dma("pool", dst_, src_, key="cshift")
                nxt = specs[k + 1] if k + 1 < len(specs) else None
                body(sp_["t0"], sp_["tiles"], sp_["N"], sp_["sample"], first=sp_["first"], lastg=sp_["lastg"],
                     out0=sp_["out0"], kidx=k,
                     hook_a=(lambda nxt=nxt, k=k: head(nxt, k + 1)) if nxt else None,
                     hook_b=(lambda nxt=nxt, k=k: head_b(nxt, k + 1)) if nxt else None)
                if k == 0:
                    S.op("dve", lambda e: e.tensor_scalar(out=fh[:, :, :], in0=fh[:, :, :], scalar1=hvt[:, 0:1], scalar2=None,
                                                          op0=ALU.mult), reads=["fh", "hvt"], writes=["fh"])
            S.barrier()
            S.emit()
    return nc


def _perm_in():
    idx = []
    for c in range(8):
        idx += list(range(128 * c, 128 * (c + 1)))
        idx += list(range(1024 + 128 * c, 1024 + 128 * (c + 1)))
    idx += list(range(2048, 4352))
    for c in range(8):
        idx += list(range(4352 + 128 * c, 4352 + 128 * (c + 1)))
        idx += list(range(5376 + 128 * c, 5376 + 128 * (c + 1)))
    return np.array(idx)


def _perm_up():
    idx = []
    for j in range(22):
        idx += list(range(128 * j, 128 * (j + 1)))
        idx += list(range(DFF + 128 * j, DFF + 128 * (j + 1)))
    return np.array(idx)


def _fm(v, nch):
    return np.ascontiguousarray(v.reshape(nch, 128).T)


def make_shared(inp):
    f = np.float32
    pin = _perm_in()
    pup = _perm_up()
    ppv = np.zeros((128, NPP), f)

    def put(name, arr):
        o, w = PP[name]
        ppv[:, o:o + w] = arr.reshape(128, w)

    put("gmix", _fm(inp["g_mix"][0], 8))
    put("bdw", _fm(inp["b_dw"][0], 8))
    put("lng", _fm(inp["ln_g"][0], 8))
    put("lnb", _fm(inp["ln_b"][0], 8))
    put("gffn", _fm(inp["g_ffn"][0], 8))
    wdw = inp["w_dw"][0]
    put("wdw", np.ascontiguousarray(wdw.T.reshape(8, 128, 31).transpose(1, 0, 2)))
    wf = inp["w_fdw"][0][:, pup]
    put("wfdw", np.ascontiguousarray(wf.T.reshape(44, 128, 3).transpose(1, 0, 2)))
    put("bfdw", _fm(inp["b_fdw"][0][pup], 44))
    inv = (np.float32(500000.0) ** (-np.arange(8, dtype=f) / np.float32(8))).astype(f)
    angs = (np.full((NS, 1), 16384.0, f) * inv[None, :]).astype(f)
    css = np.concatenate([np.cos(angs), np.cos(angs)], 1).astype(f)
    sns = np.concatenate([-np.sin(angs), np.sin(angs)], 1).astype(f)
    j = np.arange(128)[:, None]
    i = np.arange(128)[None, :]
    mask2 = np.stack([(j >= i), (j <= i)], 1).astype(f)
    sel = np.zeros((NS, NS, 128), f)
    for s in range(NS):
        sel[s, s, :] = 1.0
    return {
        "w_in": np.ascontiguousarray(inp["w_in"][0][:, pin]),
        "w_co": np.ascontiguousarray(inp["w_conv_out"][0]),
        "w_ao": np.ascontiguousarray(inp["w_attn_out"][0]),
        "w_out": np.ascontiguousarray(inp["w_out"][0]),
        "w_up": np.ascontiguousarray(inp["w_up"][0][:, pup]),
        "w_dn": np.ascontiguousarray(inp["w_down"][0]),
        "pp": ppv,
        "gfin": np.ascontiguousarray(np.broadcast_to(inp["g_final"][None, :], (128, D))),
        "ident": np.eye(128, dtype=f),
        "mask2": mask2, "css": css, "sns": sns, "sel": sel,
    }


def make_core(inp, b, half, MAIN, s0, pup):
    f = np.float32
    LS = HALO + MAIN
    start = half * MAIN
    absp = start - HALO + np.arange(LS)
    valid = absp >= 0
    x = inp["x_prompt"][b]
    xl = np.zeros((LS, D), f)
    xl[valid] = x[absp[valid]]
    inv = (np.float32(500000.0) ** (-np.arange(8, dtype=f) / np.float32(8))).astype(f)
    pos = np.maximum(absp, 0).astype(f)
    ang = (pos[:, None] * inv[None, :]).astype(f)
    cos, sin = np.cos(ang).astype(f), np.sin(ang).astype(f)
    ntile = LS // 128
    csp = np.ascontiguousarray(np.concatenate([cos, cos], 1).reshape(ntile, 128, 16).transpose(1, 0, 2))
    snp = np.ascontiguousarray(np.concatenate([-sin, sin], 1).reshape(ntile, 128, 16).transpose(1, 0, 2))
    m = {
        "xp": xl, "csp": csp, "snp": snp,
        "hv": np.full((128, 1), 1.0 if start > 0 else 0.0, f),
        "xs": np.ascontiguousarray(inp["x_sample"][s0:s0 + NS, 0]),
        "sconv": np.ascontiguousarray(inp["state_conv"][0, s0:s0 + NS]),
        "sffn": np.ascontiguousarray(inp["state_ffn_conv"][0, s0:s0 + NS][:, :, pup]),
    }
    vf = valid.astype(f)
    nsb = LS // 2048
    for g, (W, d) in enumerate(GROUPS):
        m["vld%d" % g] = np.ascontiguousarray(vf.reshape(nsb, 16 // d, 128, d).transpose(2, 0, 3, 1))
    caches = ((inp["cache_k_w128"], inp["cache_v_w128"]), (inp["cache_k_w512"], inp["cache_v_w512"]),
              (inp["cache_k_w2048"], inp["cache_v_w2048"]))
    for g, W in enumerate((128, 512, 2048)):
        m["ck%d" % g] = np.ascontiguousarray(caches[g][0][0, s0:s0 + NS].reshape(NS, W, 256))
        m["cv%d" % g] = np.ascontiguousarray(caches[g][1][0, s0:s0 + NS].reshape(NS, W, 256))
    return m


_NC_CACHE = {}


def run(inp, n_cores=8):
    inp = {k: np.asarray(v) for k, v in inp.items()}
    B, S_FULL, _ = inp["x_prompt"].shape
    nsamp = inp["x_sample"].shape[0]
    MAIN = S_FULL // 2
    assert n_cores == 2 * B
    if MAIN not in _NC_CACHE:
        _NC_CACHE[MAIN] = build(MAIN)
    nc = _NC_CACHE[MAIN]
    shared = make_shared(inp)
    pup = _perm_up()
    in_maps = []
    for c in range(n_cores):
        m = dict(shared)
        m.update(make_core(inp, c // 2, c % 2, MAIN, (NS * c) % nsamp, pup))
        in_maps.append(m)
    res = run_bass_kernel_spmd(nc, in_maps, core_ids=list(range(n_cores))).results
    global LAST_RES
    LAST_RES = res
    ipup = np.argsort(pup)
    f = np.float32
    y_p = np.stack([np.concatenate([res[2 * b]["y_p"], res[2 * b + 1]["y_p"]], 0) for b in range(B)], 0)
    nsc = nsamp // NS
    y_s = np.concatenate([res[c]["y_s"] for c in range(nsc)], 0)[:, None, :]
    hi = [2 * b + 1 for b in range(B)]
    conv_p = np.stack([res[c]["conv_p"] for c in hi], 0)[None]
    conv_s = np.concatenate([res[c]["conv_s"] for c in range(nsc)], 0)[None]
    outs = [y_p.astype(f), y_s.astype(f), conv_p.astype(f), conv_s.astype(f)]
    for g, W in enumerate((128, 512, 2048)):
        outs.append(np.stack([res[c]["k%d_p" % g] for c in hi], 0).reshape(1, B, W, 4, 64).astype(f))
        outs.append(np.stack([res[c]["v%d_p" % g] for c in hi], 0).reshape(1, B, W, 4, 64).astype(f))
        outs.append(np.concatenate([res[c]["k%d_s" % g] for c in range(nsc)], 0).reshape(1, nsamp, W, 4, 64).astype(f))
        outs.append(np.concatenate([res[c]["v%d_s" % g] for c in range(nsc)], 0).reshape(1, nsamp, W, 4, 64).astype(f))
    ffn_p = np.stack([res[c]["ffn_p"] for c in hi], 0)[:, :, ipup][None]
    ffn_s = np.concatenate([res[c]["ffn_s"] for c in range(nsc)], 0)[:, :, ipup][None]
    outs += [ffn_p.astype(f), ffn_s.astype(f)]
    return tuple(outs)


def kernel(**inputs):
    return run(inputs, 8)
```

```python
import contextlib
import types
import numpy as np
import concourse.bass as bass
import concourse.mybir as mybir
from concourse.bass_utils import run_bass_kernel_spmd

F32 = mybir.dt.float32
BF16 = mybir.dt.bfloat16
ALU = mybir.AluOpType
AF = mybir.ActivationFunctionType
AX = mybir.AxisListType

D = 1024
DFF = 2816
NS = 4
NT = 512
GROUPS = ((128, 1), (512, 4), (2048, 16))
EPS = 1e-6
ENGS = ("pe", "act", "dve", "pool", "sp")


class Sched:
    def __init__(self, nc, st, nsem=100):
        self.nc = nc
        self.ops = {e: [] for e in ENGS}
        self.cnt = {}
        self.res_w = {}
        self.res_r = {}
        self.waited = {e: {} for e in ENGS}
        self.pool = [st.enter_context(nc.semaphore("sm%d" % i)) for i in range(nsem)]
        self.sem = {}
        self.alias = {}

    def _sk(self, k):
        if k not in self.cnt:
            self.cnt[k] = 0
            assert len(self.sem) < len(self.pool), "out of semaphores"
            self.sem[k] = self.pool[len(self.sem)]
        return k

    @staticmethod
    def _freeze(fn):
        if fn.__closure__ is None:
            return fn
        cells = []
        for c in fn.__closure__:
            try:
                cells.append(types.CellType(c.cell_contents))
            except ValueError:
                cells.append(c)
        return types.FunctionType(fn.__code__, fn.__globals__, fn.__name__, fn.__defaults__, tuple(cells))

    def op(self, eng, fn, reads=(), writes=(), dma=None, sig=True):
        fn = self._freeze(fn)
        waits = {}
        reads = [x for r in reads for x in [r] + self.alias.get(r, [])]
        writes = [x for r in writes for x in [r] + self.alias.get(r, [])]
        writes = writes + [r for r in reads if r.startswith("ps") or r.startswith("ptb")]

        def need(w):
            if w[1] > waits.get(w[0], 0):
                waits[w[0]] = w[1]

        for r in reads:
            if r in self.res_w:
                need(self.res_w[r])
        for r in writes:
            if r in self.res_w:
                need(self.res_w[r])
            for sk, v in self.res_r.get(r, {}).items():
                need((sk, v))
        if dma is None:
            sk = self._sk(eng)
            inc = 1 if sig else 0
        else:
            sk = self._sk("d:" + str(dma))
            inc = 16
        self.cnt[sk] += inc
        val = self.cnt[sk] if inc else self.cnt[sk] + 1
        wl = []
        for k, v in waits.items():
            if k == "pe" and eng == "pe" and dma is None:
                continue
            if self.waited[eng].get(k, 0) >= v:
                continue
            self.waited[eng][k] = v
            wl.append((k, v))
        for r in writes:
            self.res_w[r] = (sk, val)
            self.res_r[r] = {}
        for r in reads:
            d = self.res_r.setdefault(r, {})
            if d.get(sk, 0) < val:
                d[sk] = val
        self.ops[eng].append((wl, fn, sk, inc))

    def barrier(self, engs=ENGS):
        for e in engs:
            wl = []
            for k, v in self.cnt.items():
                if v > 0 and self.waited[e].get(k, 0) < v:
                    self.waited[e][k] = v
                    wl.append((k, v))
            if wl:
                self.ops[e].append((wl, None, None, 0))

    def emit(self):
        nc = self.nc
        with nc.Block() as block:
            def run(e, eng):
                for wl, fn, sk, inc in self.ops[eng]:
                    for k, v in wl:
                        e.wait_ge(self.sem[k], v)
                    if fn is not None:
                        ins = fn(e)
                        if inc:
                            ins.then_inc(self.sem[sk], inc)

            @block.tensor
            def _(e):
                run(e, "pe")

            @block.scalar
            def _(e):
                run(e, "act")

            @block.vector
            def _(e):
                run(e, "dve")

            @block.gpsimd
            def _(e):
                run(e, "pool")

            @block.sync
            def _(e):
                run(e, "sp")
        self.ops = {e: [] for e in ENGS}


PP = {}
_o = 0
for _n, _w in (("gmix", 8), ("bdw", 8), ("lng", 8), ("lnb", 8), ("gffn", 8), ("wdw", 8 * 31), ("wfdw", 44 * 3),
               ("bfdw", 44)):
    PP[_n] = (_o, _w)
    _o += _w
NPP = _o


HALO = 4096


def build(MAIN):
    S_TOK = HALO + MAIN
    NSB = S_TOK // 2048
    NTILE = S_TOK // 128
    NG = MAIN // NT
    nc = bass.Bass("TRN2", target_bir_lowering=False)

    def din(name, shape, dt=F32):
        return nc.dram_tensor(name, list(shape), dt, kind="ExternalInput").ap()

    def dout(name, shape, dt=F32):
        return nc.dram_tensor(name, list(shape), dt, kind="ExternalOutput").ap()

    def dscr(name, shape, dt):
        return nc.dram_tensor(name, list(shape), dt, kind="Internal").ap()

    xp = din("xp", [S_TOK, D])
    xs = din("xs", [NS, D])
    sconv = din("sconv", [NS, 30, D])
    cks = [din("ck%d" % g, [NS, GROUPS[g][0], 256]) for g in range(3)]
    cvs = [din("cv%d" % g, [NS, GROUPS[g][0], 256]) for g in range(3)]
    sffn = din("sffn", [NS, 2, 2 * DFF])
    w_in = din("w_in", [D, 6400])
    w_co = din("w_co", [D, D])
    w_ao = din("w_ao", [256, D])
    w_out = din("w_out", [D, D])
    w_up = din("w_up", [D, 2 * DFF])
    w_dn = din("w_dn", [DFF, D])
    pp_d = din("pp", [128, NPP])
    gfin_d = din("gfin", [128, D])
    ident_d = din("ident", [128, 128])
    mask_d = din("mask2", [128, 2, 128])
    csp_d = din("csp", [128, NTILE, 16])
    snp_d = din("snp", [128, NTILE, 16])
    css_d = din("css", [NS, 16])
    sns_d = din("sns", [NS, 16])
    sel_d = din("sel", [NS, NS, 128])
    vld_d = [din("vld%d" % g, [128, NSB, GROUPS[g][1], 16 // GROUPS[g][1]]) for g in range(3)]
    hv_d = din("hv", [128, 1])

    wb_in = dscr("wb_in", [D, 6400], BF16)
    wb_co = dscr("wb_co", [D, D], BF16)
    wb_ao = dscr("wb_ao", [256, D], BF16)
    wb_out = dscr("wb_out", [D, D], BF16)
    wb_up = dscr("wb_up", [D, 2 * DFF], BF16)
    wb_dn = dscr("wb_dn", [DFF, D], BF16)
    import os as _os0
    DBG = bool(_os0.environ.get("KDBG"))
    oT_d = (dout if DBG else dscr)("oT_d", [64, 4, S_TOK], BF16)
    if DBG:
        dbg_sT = dout("dbg_sT", [128, 8, NT], BF16)
        dbg_mix = dout("dbg_mix", [128, 8, NT], BF16)
        dbg_xmid = dout("dbg_xmid", [NT, D], F32)
        dbg_acc = dout("dbg_acc", [128, 8, NT], F32)

    y_p = dout("y_p", [MAIN, D])
    y_s = dout("y_s", [NS, D])
    conv_p = dout("conv_p", [30, D])
    conv_s = dout("conv_s", [NS, 30, D])
    kp = [dout("k%d_p" % g, [min(GROUPS[g][0], S_TOK), 256]) for g in range(3)]
    vp = [dout("v%d_p" % g, [min(GROUPS[g][0], S_TOK), 256]) for g in range(3)]
    ks_o = [dout("k%d_s" % g, [NS, GROUPS[g][0], 256]) for g in range(3)]
    vs_o = [dout("v%d_s" % g, [NS, GROUPS[g][0], 256]) for g in range(3)]
    ffn_p = dout("ffn_p", [2, 2 * DFF])
    ffn_s = dout("ffn_s", [NS, 2, 2 * DFF])

    with contextlib.ExitStack() as gst:
        S = Sched(nc, gst)

        def sbuf(st, name, shape, dt):
            return st.enter_context(nc.sbuf_tensor("sb_" + name, list(shape), dt))

        NPS = 6
        PSB = [gst.enter_context(nc.psum_tensor("psb%d" % i, [128, 512], F32)) for i in range(NPS)]
        PTB = gst.enter_context(nc.psum_tensor("ptb", [128, 1024], BF16))
        PTB2 = gst.enter_context(nc.psum_tensor("ptb2", [128, 1024], BF16))
        ps_i = [0]

        def psum():
            i = ps_i[0] % NPS
            ps_i[0] += 1
            return PSB[i], "ps%d" % i

        pp = sbuf(gst, "pp", [128, NPP], F32)
        identf = sbuf(gst, "identf", [128, 128], F32)
        identb = sbuf(gst, "identb", [128, 128], BF16)
        onesb = sbuf(gst, "onesb", [128, 128], BF16)
        onesf = sbuf(gst, "onesf", [128, 128], F32)
        epst = sbuf(gst, "epst", [128, 1], F32)
        stat = sbuf(gst, "stat", [128, 3, 4 * NTILE + 16], F32)
        xsb = [sbuf(gst, "xsb%d" % i, [128, D], BF16) for i in range(2)]
        oTs = sbuf(gst, "oTs", [128, 2, NS], BF16)
        sts = sbuf(gst, "sts", [NS, 2304], F32)
        stat_i = [0]

        ps45 = [0]

        def psum45():
            i = 4 + ps45[0] % 2
            ps45[0] += 1
            return PSB[i], "ps%d" % i

        def P(name, c=None):
            o, w = PP[name]
            if c is None:
                return pp[:, o:o + w]
            return pp[:, o + c:o + c + 1]

        def dma(eng, out, in_, reads=(), writes=(), key=None):
            S.op(eng, lambda e: e.dma_start(out=out, in_=in_), reads=reads, writes=writes, dma=key)

        dma("sp", pp[:], pp_d, writes=["pp"], key="pp")
        dma("sp", identf[:], ident_d, writes=["identf"], key="identf")
        S.op("dve", lambda e: e.tensor_copy(out=identb[:], in_=identf[:]), reads=["identf"], writes=["identb"])
        S.op("dve", lambda e: e.memset(onesb[:], 1.0), writes=["onesb"])
        S.op("dve", lambda e: e.memset(onesf[:], 1.0), writes=["onesf"])
        S.op("dve", lambda e: e.memset(epst[:], EPS), writes=["epst"])
        S.op("dve", lambda e: e.memset(stat[:], 0.0), writes=["stat"])
        def conv_w(dst, src, rows, c0, c1, key, after=()):
            for r0 in range(0, rows, 256):
                r1 = min(rows, r0 + 256)
                dma("pool", dst[r0:r1, c0:c1], src[r0:r1, c0:c1], reads=list(after), writes=[key], key=key)
        for cg in (2, 4, 0, 1, 3):
            conv_w(wb_in, w_in, D, 2048 + 512 * cg, 2048 + min(512 * (cg + 1), 2304), "wb_qkv%d" % cg)
        QKV_ALL = ["wb_qkv%d" % cg for cg in range(5)]
        LATE_CASTS = [
            [],
            [lambda: conv_w(wb_in, w_in, D, 0, 2048, "wb_in1"), lambda: conv_w(wb_in, w_in, D, 4352, 6400, "wb_in3")],
            [lambda: conv_w(wb_co, w_co, D, 0, D, "wb_co"), lambda: conv_w(wb_ao, w_ao, 256, 0, D, "wb_ao"),
             lambda: conv_w(wb_out, w_out, D, 0, D, "wb_out"), lambda: conv_w(wb_up, w_up, D, 0, 2 * DFF, "wb_up")],
            [lambda: conv_w(wb_dn, w_dn, DFF, 0, D, "wb_dn")],
        ]
        def flat16(ap):
            return ap.rearrange("w c -> (w c)").rearrange("(a b) -> a b", a=16)
        SHIFTS = []
        for g in range(3):
            W = GROUPS[g][0]
            for (src, dst) in ((cks[g], ks_o[g]), (cvs[g], vs_o[g])):
                for s in range(NS):
                    SHIFTS.append((flat16(dst[s, 0:W - 1, :]), flat16(src[s, 1:W, :])))
        for s in range(NS):
            SHIFTS.append((flat16(conv_s[s, 0:29, :]), flat16(sconv[s, 1:30, :])))
            SHIFTS.append((flat16(ffn_s[s, 0:1, :]), flat16(sffn[s, 1:2, :])))

        import os as _os
        KSTOP = int(_os.environ.get("KSTOP", "9"))
        if KSTOP == 0:
            S.barrier()
            S.emit()
            return nc
        def norm_a(src_ap, rd, TT, xb, xr):
            col = stat_i[0]
            stat_i[0] += 1
            S.op("act", lambda e: e.activation(out=xb[0:TT, :], in_=src_ap, func=AF.Square,
                                               accum_out=stat[0:TT, 0, col:col + 1]),
                 reads=[rd], writes=[xr, "stat%d" % col])
            S.op("act", lambda e: e.activation(out=stat[0:TT, 1, col:col + 1], in_=stat[0:TT, 0, col:col + 1],
                                               func=AF.Sqrt, scale=1.0 / D, bias=epst[0:TT, 0:1]),
                 reads=["stat%d" % col, "epst"], writes=["stat%d" % col])
            S.op("dve", lambda e: e.reciprocal(out=stat[0:TT, 2, col:col + 1], in_=stat[0:TT, 1, col:col + 1]),
                 reads=["stat%d" % col], writes=["stat%d" % col])
            S.op("act", lambda e: e.activation(out=xb[0:TT, :], in_=src_ap, func=AF.Copy,
                                               scale=stat[0:TT, 2, col:col + 1]),
                 reads=[rd, "stat%d" % col], writes=[xr])

        def norm_b(TT, xb, xr, dst3, dst_res, gain_name):
            for c in range(8):
                S.op("pe", lambda e, c=c: e.transpose(PTB[:, c * 128:c * 128 + TT], xb[0:TT, c * 128:(c + 1) * 128],
                                                      identb[0:TT, 0:TT]),
                     reads=[xr, "identb"], writes=["ptb"], sig=(c == 7))
            o, w = PP[gain_name]
            S.op("dve", lambda e: e.tensor_tensor(
                out=dst3, in0=PTB[:, :].rearrange("p (c t) -> p c t", c=8)[:, :, 0:TT],
                in1=pp[:, o:o + 8].unsqueeze(2).to_broadcast([128, 8, TT]), op=ALU.mult),
                reads=["ptb", "pp"], writes=[dst_res])

        def norm_T(src_ap, rd, TT, dst3, dst_res, gain_name, xi):
            xb = xsb[xi % 2]
            xr = "xsb%d" % (xi % 2)
            norm_a(src_ap, rd, TT, xb, xr)
            norm_b(TT, xb, xr, dst3, dst_res, gain_name)

        with contextlib.ExitStack() as st1:
            hT1 = sbuf(st1, "hT1", [128, 8, 2048], BF16)
            hTs = sbuf(st1, "hTs", [128, 8, NS], BF16)
            QT = [sbuf(st1, "QT%d" % g, [128, 2, GROUPS[g][1], 2048 // GROUPS[g][1]], BF16) for g in range(3)]
            KT = [sbuf(st1, "KT%d" % g, [128, 2, GROUPS[g][1], 128 + 2048 // GROUPS[g][1]], BF16) for g in range(3)]
            NB = [1 + 16 // GROUPS[g][1] for g in range(3)]
            VV = [sbuf(st1, "VV%d" % g, [128, GROUPS[g][1], NB[g], 260], BF16) for g in range(3)]
            Wg = [sbuf(st1, "Wg%d" % i, [128, 8, 512], BF16) for i in range(1)]
            xt = [sbuf(st1, "xt%d" % i, [128, D], F32) for i in range(2)]
            stg = [sbuf(st1, "stg%d" % i, [128, 512], F32) for i in range(2)]
            qkb = [sbuf(st1, "qkb%d" % i, [128, 512], BF16) for i in range(2)]
            rtmp = [sbuf(st1, "rtmp%d" % i, [128, 24, 16], F32) for i in range(2)]
            vst = [sbuf(st1, "vst%d" % i, [128, 256], F32) for i in range(2)]
            PT2 = sbuf(st1, "PT2", [128, 16, 2, 128], BF16)
            PTs = [sbuf(st1, "PTs%d" % i, [128, 2, 2, 128], BF16) for i in range(2)]
            csp = sbuf(st1, "csp", [128, 16, 16], F32)
            snp = sbuf(st1, "snp", [128, 16, 16], F32)
            css = sbuf(st1, "css", [NS, 16], F32)
            sns = sbuf(st1, "sns", [NS, 16], F32)
            maskf = sbuf(st1, "maskf", [128, 2, 128], F32)
            vld = [sbuf(st1, "vld%d" % g, [128, GROUPS[g][1], 16 // GROUPS[g][1]], F32) for g in range(3)]
            maskb = sbuf(st1, "maskb", [128, 2, 128], BF16)
            rec = [sbuf(st1, "rec%d" % i, [64, 512], F32) for i in range(1)]
            oTsb = sbuf(st1, "oTsb", [64, 2048], BF16)
            selb = sbuf(st1, "selb", [NS, NS, 128], F32)
            Kc = [sbuf(st1, "Kc%d" % i, [128, 256], F32) for i in range(3)]
            Vc = [sbuf(st1, "Vc%d" % i, [128, 256], F32) for i in range(3)]
            prod = sbuf(st1, "prod", [128, 256], F32)
            sT4 = sbuf(st1, "sT4", [128, 24], F32)
            sm = sbuf(st1, "sm", [NS, 768], F32)
            pcur = sbuf(st1, "pcur", [NS, 12], F32)
            pcm = sbuf(st1, "pcm", [NS, NS, 12], F32)
            id4 = sbuf(st1, "id4", [NS, NS], F32)
            oTf = sbuf(st1, "oTf", [128, 2, NS], F32)

            for g in range(3):
                S.op("pool", lambda e, g=g: e.memset(VV[g][:, :, :, :], 1.0), writes=["VV%d" % g])
            dma("sp", css[:], css_d, writes=["css"], key="css")
            dma("sp", sns[:], sns_d, writes=["sns"], key="sns")
            dma("sp", maskf[:], mask_d, writes=["maskf"], key="maskf")
            dma("sp", selb[:], sel_d, writes=["selb"], key="selb")
            S.op("dve", lambda e: e.tensor_copy(out=maskb[:], in_=maskf[:]), reads=["maskf"], writes=["maskb"])
            S.op("dve", lambda e: e.tensor_copy(out=id4[:], in_=identf[0:NS, 0:NS]), reads=["identf"], writes=["id4"])

            def rope(stv, rd, TT, nh, cs_ap, sn_ap, tab_res, ri):
                rt = rtmp[ri % 2]
                rr = "rtmp%d" % (ri % 2)
                t1 = rt[0:TT, 0:nh, :]
                csb = cs_ap.unsqueeze(1).to_broadcast([TT, nh, 16])
                S.op("dve", lambda e: e.tensor_tensor(out=t1, in0=stv[:, :, 0:16], in1=csb, op=ALU.mult),
                     reads=[rd] + tab_res, writes=[rr])
                S.op("dve", lambda e: e.tensor_tensor(out=stv[:, :, 0:8], in0=stv[:, :, 0:8],
                                                      in1=sn_ap[:, 8:16].unsqueeze(1).to_broadcast([TT, nh, 8]),
                                                      op=ALU.mult), reads=[rd] + tab_res, writes=[rd])
                S.op("dve", lambda e: e.tensor_tensor(out=stv[:, :, 8:16], in0=stv[:, :, 8:16],
                                                      in1=sn_ap[:, 0:8].unsqueeze(1).to_broadcast([TT, nh, 8]),
                                                      op=ALU.mult), reads=[rd] + tab_res, writes=[rd])
                S.op("dve", lambda e: e.tensor_tensor(out=t1[:, :, 0:8], in0=t1[:, :, 0:8], in1=stv[:, :, 8:16],
                                                      op=ALU.add), reads=[rd, rr], writes=[rr])
                S.op("dve", lambda e: e.tensor_tensor(out=t1[:, :, 8:16], in0=t1[:, :, 8:16], in1=stv[:, :, 0:8],
                                                      op=ALU.add), reads=[rd, rr], writes=[rr])
                S.op("dve", lambda e: e.tensor_copy(out=stv[:, :, 0:16], in_=t1), reads=[rr], writes=[rd])

            xi = 0
            ri = 0
            ev = 0
            bset = [0]
            PTBf = PTB[:, :].bitcast(F32)
            PTB2f = PTB2[:, :].bitcast(F32)
            for _k in range(int(_os.environ.get("KDUMMY", "0"))):
                if _os.environ.get("KDUMMYT") == "memset":
                    S.op("dve", lambda e: e.memset(prod[:, :], 0.0), writes=["prod"])
                else:
                    S.op("dve", lambda e: e.tensor_tensor(out=prod[:, :], in0=prod[:, :], in1=prod[:, :], op=ALU.mult),
                         writes=["prod"])
            for sb in range(NSB):
                T0 = sb * 2048
                last = (sb == NSB - 1)
                if sb < len(LATE_CASTS):
                    for f_ in LATE_CASTS[sb]:
                        f_()
                dma("sp", csp[:], csp_d[:, 16 * sb:16 * (sb + 1), :], writes=["csp"], key="csp")
                for g in range(3):
                    dma("sp", vld[g][:, :, :], vld_d[g][:, sb, :, :], writes=["vld%d" % g], key="vld%d" % g)
                dma("sp", snp[:], snp_d[:, 16 * sb:16 * (sb + 1), :], writes=["snp"], key="snp")
                for i in range(16):
                    b = xi % 2
                    dma("sp", xt[b][:], xp[T0 + 128 * i:T0 + 128 * (i + 1), :], writes=["xt%d" % b], key="xt%d" % b)
                    norm_T(xt[b][:], "xt%d" % b, 128, hT1[:, :, 128 * i:128 * (i + 1)], "hT1_%d" % i, "gmix", xi)
                    xi += 1
                if sb == 1:
                    b = xi % 2
                    dma("sp", xt[b][0:NS, :], xs, writes=["xt%d" % b], key="xt%d" % b)
                    norm_T(xt[b][0:NS, :], "xt%d" % b, NS, hTs[:, :, 0:NS], "hTs", "gmix", xi)
                    xi += 1
                if KSTOP == 10:
                    S.barrier()
                    S.emit()
                    return nc
                hT_all = ["hT1_%d" % i for i in range(16)]
                if sb > 0:
                    for g in range(3):
                        M = 2048 // GROUPS[g][1]
                        S.op("pool", lambda e, g=g, M=M: e.tensor_copy(out=KT[g][:, :, :, 0:128],
                                                                        in_=KT[g][:, :, :, M:M + 128]),
                             reads=["KT%d" % g], writes=["KT%d" % g])
                        S.op("pool", lambda e, g=g: e.tensor_copy(out=VV[g][:, :, 0, :], in_=VV[g][:, :, NB[g] - 1, :]),
                             reads=["VV%d" % g], writes=["VV%d" % g])
                for cg in range(5):
                    if sb == 0 and cg in (0, 1, 3):
                        continue
                    wcols = 512 if cg < 4 else 256
                    wb = Wg[0]
                    wr = "Wg0"
                    dma("sp", wb[:, :, 0:wcols],
                        wb_in[:, 2048 + 512 * cg:2048 + 512 * cg + wcols].rearrange("(kc p) n -> p kc n", p=128),
                        reads=["wb_qkv%d" % cg], writes=[wr], key=wr)
                    if sb == 1:
                        pst, pr = psum()
                        for h0 in range(0, wcols, 256):
                            for kc in range(8):
                                S.op("pe", lambda e, kc=kc, pst=pst, h0=h0: e.matmul(
                                    pst[0:NS, h0:h0 + 256], lhsT=hTs[:, kc, 0:NS], rhs=wb[:, kc, h0:h0 + 256],
                                    start=(kc == 0), stop=(kc == 7)), reads=["hTs", wr], writes=[pr], sig=(kc == 7))
                        for (c0, c1) in ((0, 256), (256, 512)):
                            if c0 >= wcols:
                                continue
                            gcol = 512 * cg + c0
                            sc = 0.125 if gcol < 768 else 1.0
                            S.op("act", lambda e, c0=c0, c1=c1, pst=pst, sc=sc, gcol=gcol: e.activation(
                                out=sts[0:NS, gcol:gcol + 256], in_=pst[0:NS, c0:c1], func=AF.Copy, scale=sc),
                                reads=[pr], writes=["sts"])
                    if KSTOP == 20:
                        S.barrier()
                        S.emit()
                        return nc
                    if cg < 3:
                        for i in range(16):
                            pst, pr = psum()
                            for kc in range(8):
                                S.op("pe", lambda e, kc=kc, pst=pst, i=i: e.matmul(
                                    pst[:, 0:512], lhsT=hT1[:, kc, 128 * i:128 * (i + 1)], rhs=wb[:, kc, 0:512],
                                    start=(kc == 0), stop=(kc == 7)), reads=["hT1_%d" % i, wr], writes=[pr], sig=(kc == 7))
                            sg = stg[ev % 2]
                            sr = "stg%d" % (ev % 2)
                            qb_ = qkb[ev % 2]
                            qr = "qkb%d" % (ev % 2)
                            ev += 1
                            for (c0, c1) in ((0, 256), (256, 512)):
                                sc = 0.125 if (512 * cg + c0) < 768 else 1.0
                                S.op("act", lambda e, c0=c0, c1=c1, pst=pst, sc=sc, sg=sg: e.activation(
                                    out=sg[:, c0:c1], in_=pst[:, c0:c1], func=AF.Copy, scale=sc),
                                    reads=[pr], writes=[sr])
                            ti = sb * 16 + i
                            if not _os.environ.get("KNOROPE"):
                              rope(sg[:, :].rearrange("p (h d) -> p h d", d=64), sr, 128, 8, csp[:, i, :], snp[:, i, :],
                                 ["csp", "snp"], ri)
                            ri += 1
                            S.op("dve", lambda e, sg=sg, qb_=qb_: e.tensor_copy(out=qb_[:, :], in_=sg[:, :]),
                                 reads=[sr], writes=[qr])
                            if last and KSTOP != 24:
                                for g in range(3):
                                    W = min(GROUPS[g][0], S_TOK)
                                    kcol = 768 + 256 * g
                                    if not (512 * cg <= kcol < 512 * cg + 512):
                                        continue
                                    tpos = 2048 - 128 * (16 - i)
                                    if S_TOK - (T0 + 128 * i) > W:
                                        continue
                                    row0 = W - (S_TOK - (T0 + 128 * i))
                                    lc = kcol - 512 * cg
                                    dma("sp", kp[g][row0:row0 + 128, :], sg[:, lc:lc + 256], reads=[sr], key="o" + sr)
                            for j in range(4):
                                S.op("pe", lambda e, j=j, qb_=qb_: e.transpose((PTB if j < 2 else PTB2)[:, j * 128:(j + 1) * 128],
                                                                                qb_[:, j * 128:(j + 1) * 128], identb[:, :]),
                                     reads=[qr, "identb"], writes=["ptb" if j < 2 else "ptb2"], sig=(j % 2 == 1))
                            for j in range(4):
                                if _os.environ.get("KNOEVAC"):
                                    continue
                                cc = cg * 4 + j
                                isq = cc < 6
                                g = (cc % 6) // 2
                                half = cc % 2
                                d = GROUPS[g][1]
                                mlo = 128 * i // d
                                cnt = 128 // d
                                if isq:
                                    dst = QT[g][:, half, :, mlo:mlo + cnt]
                                    dres = "QT%d" % g
                                else:
                                    dst = KT[g][:, half, :, 128 + mlo:128 + mlo + cnt]
                                    dres = "KT%d" % g
                                src = (PTB if j < 2 else PTB2)[:, j * 128:(j + 1) * 128].rearrange("p (m r) -> p r m", r=d)
                                if j < 2:
                                    S.op("act", lambda e, dst=dst, src=src: e.activation(out=dst, in_=src, func=AF.Copy),
                                         reads=["ptb"], writes=[dres])
                                else:
                                    S.op("dve", lambda e, dst=dst, src=src: e.tensor_copy(out=dst, in_=src),
                                         reads=["ptb2"], writes=[dres])
                            if KSTOP == 21:
                                S.barrier()
                                S.emit()
                                return nc
                            if (KSTOP in (22, 24) and cg == 2 and i == 15) or (KSTOP == 25 and i == 1) or (KSTOP == 33 and cg == 1 and i == 14) or (KSTOP == 31 and cg == 1 and i == 1) or (KSTOP == 32 and cg == 1 and i == 7) or (KSTOP == 29 and cg == 1 and i == 15) or (KSTOP == 30 and cg == 2 and i == 0) or (KSTOP == 28 and cg == 1 and i == 0) or (KSTOP == 26 and i == 7) or (KSTOP == 27 and cg == 0 and i == 15):
                                S.barrier()
                                S.emit()
                                return nc
                    else:
                        jobs = []
                        if cg == 3:
                            for blk in range(16):
                                jobs.append((0, 0, blk, 0, hT1[:, :, 128 * blk:128 * (blk + 1)], ["hT1_%d" % blk]))
                            for r in range(4):
                                for blk in range(4):
                                    jobs.append((1, r, blk, 256, hT1[:, :, 512 * blk + r:512 * (blk + 1):4],
                                                 ["hT1_%d" % t for t in range(4 * blk, 4 * blk + 4)]))
                        else:
                            for r in range(16):
                                jobs.append((2, r, 0, 0, hT1[:, :, r:2048:16], hT_all))
                        for (g, r, blk, c0, lh, lres) in jobs:
                            pst, pr = psum()
                            for kc in range(8):
                                S.op("pe", lambda e, kc=kc, pst=pst, lh=lh, c0=c0: e.matmul(
                                    pst[:, 0:256], lhsT=lh[:, kc, :], rhs=wb[:, kc, c0:c0 + 256],
                                    start=(kc == 0), stop=(kc == 7)), reads=lres + [wr], writes=[pr], sig=(kc == 7))
                            S.op("act", lambda e, pst=pst, g=g, r=r, blk=blk: e.activation(
                                out=VV[g][:, r, 1 + blk, :].rearrange("p (s e) -> p s e", e=65)[:, :, 0:64],
                                in_=pst[:, 0:256].rearrange("p (s e) -> p s e", e=64), func=AF.Copy,
                                scale=vld[g][:, r, blk:blk + 1]),
                                reads=[pr, "vld%d" % g], writes=["VV%d" % g])
                            S.op("pool", lambda e, g=g, r=r, blk=blk: e.tensor_copy(
                                out=VV[g][:, r, 1 + blk, :].rearrange("p (s e) -> p s e", e=65)[:, :, 64:65],
                                in_=vld[g][:, r, blk:blk + 1].unsqueeze(1).to_broadcast([128, 4, 1])),
                                reads=["vld%d" % g], writes=["VV%d" % g])
                            if last:
                                W = min(GROUPS[g][0], S_TOK)
                                d = GROUPS[g][1]
                                nblk = 16 // d
                                first_tok = T0 + r + d * 128 * blk
                                if S_TOK - (T0 + d * 128 * blk) <= W:
                                    row0 = W - (S_TOK - first_tok)
                                    vb = vst[ev % 2]
                                    vr = "vst%d" % (ev % 2)
                                    ev += 1
                                    S.op("dve", lambda e, pst=pst, vb=vb: e.tensor_copy(out=vb[:, :], in_=pst[:, 0:256]),
                                         reads=[pr], writes=[vr])
                                    dma("sp", vp[g][row0:W:d, :], vb[:, :], reads=[vr], key="o" + vr)
                if KSTOP == 23 and False:
                    S.barrier()
                    S.emit()
                    return nc
                if KSTOP == 11:
                    S.barrier()
                    S.emit()
                    return nc
                if sb == 1:
                    rope(sts[0:NS, 0:1536].rearrange("p (h d) -> p h d", d=64), "sts", NS, 24, css[:, :], sns[:, :],
                         ["css", "sns"], ri)
                    ri += 1
                    for g in range(3):
                        W = GROUPS[g][0]
                        dma("sp", ks_o[g][:, W - 1, :], sts[0:NS, 768 + 256 * g:768 + 256 * (g + 1)], reads=["sts"],
                            key="osts")
                        dma("sp", vs_o[g][:, W - 1, :], sts[0:NS, 1536 + 256 * g:1536 + 256 * (g + 1)], reads=["sts"],
                            key="osts")
                    S.op("dve", lambda e: e.tensor_tensor(out=sm[0:NS, 0:768], in0=sts[0:NS, 0:768],
                                                          in1=sts[0:NS, 768:1536], op=ALU.mult),
                         reads=["sts"], writes=["sm"])
                    S.op("dve", lambda e: e.tensor_reduce(out=pcur[:, :],
                                                          in_=sm[0:NS, 0:768].rearrange("p (h d) -> p h d", d=64),
                                                          axis=AX.X, op=ALU.add), reads=["sm"], writes=["pcur"])
                    S.op("act", lambda e: e.activation(out=pcur[:, :], in_=pcur[:, :], func=AF.Exp),
                         reads=["pcur"], writes=["pcur"])
                    S.op("dve", lambda e: e.tensor_tensor(out=pcm[:, :, :],
                                                          in0=pcur[:, :].unsqueeze(1).to_broadcast([NS, NS, 12]),
                                                          in1=id4[:, :].unsqueeze(2).to_broadcast([NS, NS, 12]),
                                                          op=ALU.mult), reads=["pcur", "id4"], writes=["pcm"])
                    for s in range(NS):
                        pnum, pnr = psum()
                        for g in range(3):
                            W, d = GROUPS[g]
                            kb_, vb_ = Kc[g], Vc[g]
                            kr, vr = "Kc%d" % g, "Vc%d" % g
                            dma("sp", kb_[:, :], cks[g][s, 0:W:d, :], writes=[kr], key=kr)
                            dma("sp", vb_[:, :], cvs[g][s, 0:W:d, :], writes=[vr], key=vr)
                            pq, pqr = psum()
                            S.op("pe", lambda e, pq=pq, s=s, g=g: e.matmul(pq[:, 0:256], lhsT=selb[0:NS, s, :],
                                                                           rhs=sts[0:NS, 256 * g:256 * (g + 1)],
                                                                           start=True, stop=True),
                                 reads=["selb", "sts"], writes=[pqr])
                            S.op("dve", lambda e, pq=pq, kb_=kb_: e.tensor_tensor(out=prod[:, :], in0=kb_[:, :],
                                                                                    in1=pq[:, 0:256], op=ALU.mult),
                                 reads=[kr, pqr], writes=["prod"])
                            S.op("dve", lambda e, g=g: e.tensor_reduce(out=sT4[:, 4 * g:4 * g + 4],
                                                                  in_=prod[:, :].rearrange("p (h d) -> p h d", d=64),
                                                                  axis=AX.X, op=ALU.add), reads=["prod"], writes=["sT4"])
                        S.op("act", lambda e: e.activation(out=sT4[:, 12:24], in_=sT4[:, 0:12], func=AF.Exp),
                             reads=["sT4"], writes=["sT4"])
                        for c in range(3):
                            for g in range(3):
                                pT_ = sT4[:, 12 + 4 * g:16 + 4 * g]
                                pc_ = pcm[0:NS, s, 4 * g:4 * g + 4]
                                if c < 2:
                                    l1 = Vc[g][:, 128 * c:128 * (c + 1)]
                                    l2 = sts[0:NS, 1536 + 256 * g + 128 * c:1536 + 256 * g + 128 * (c + 1)]
                                    r1 = ["Vc%d" % g, "sT4"]
                                else:
                                    l1 = onesf[:, :]
                                    l2 = onesf[0:NS, :]
                                    r1 = ["onesf", "sT4"]
                                S.op("pe", lambda e, c=c, g=g, pnum=pnum, l1=l1, pT_=pT_: e.matmul(
                                    pnum[:, 4 * c:4 * c + 4], lhsT=l1, rhs=pT_, start=(g == 0), stop=False),
                                    reads=r1, writes=[pnr])
                                S.op("pe", lambda e, c=c, g=g, pnum=pnum, l2=l2, pc_=pc_: e.matmul(
                                    pnum[:, 4 * c:4 * c + 4], lhsT=l2, rhs=pc_, start=False, stop=(g == 2)),
                                    reads=["sts", "pcm", "onesf"], writes=[pnr], sig=(g == 2))
                        S.op("dve", lambda e, pnum=pnum: e.reciprocal(out=sT4[:, 0:4], in_=pnum[:, 8:12]),
                             reads=[pnr], writes=["sT4"])
                        for c in range(2):
                            for hf in range(2):
                                slot = 2 * c + hf
                                S.op("dve", lambda e, c=c, hf=hf, slot=slot, pnum=pnum, s=s: e.tensor_tensor(
                                    out=oTf[64 * hf:64 * (hf + 1), c, s:s + 1],
                                    in0=pnum[64 * hf:64 * (hf + 1), 4 * c + slot:4 * c + slot + 1],
                                    in1=sT4[64 * hf:64 * (hf + 1), slot:slot + 1], op=ALU.mult),
                                    reads=[pnr, "sT4"], writes=["oTf"])
                    S.op("dve", lambda e: e.tensor_copy(out=oTs[:, :, :], in_=oTf[:, :, :]), reads=["oTf"], writes=["oTs"])
                    if DBG:
                        d1 = dout("dbg_oTf", [128, 2, NS], F32)
                        dma("sp", d1, oTf[:, :, :], reads=["oTf"], key="dbg")
                        d2 = dout("dbg_sts", [NS, 2304], F32)
                        dma("sp", d2, sts[:, :], reads=["sts"], key="dbg")
                        d3 = dout("dbg_pcur", [NS, 12], F32)
                        dma("sp", d3, pcur[:, :], reads=["pcur"], key="dbg")

                if KSTOP == 12:
                    S.barrier()
                    S.emit()
                    return nc
                if DBG and sb == 0:
                    for g in range(3):
                        dq = dout("dbg_QT%d" % g, [128, 2, GROUPS[g][1], 2048 // GROUPS[g][1]], BF16)
                        dk = dout("dbg_KT%d" % g, [128, 2, GROUPS[g][1], 128 + 2048 // GROUPS[g][1]], BF16)
                        dv = dout("dbg_VV%d" % g, [128, GROUPS[g][1], NB[g], 260], BF16)
                        dma("sp", dq, QT[g][:, :, :, :], reads=["QT%d" % g], key="dbg")
                        dma("sp", dk, KT[g][:, :, :, :], reads=["KT%d" % g], key="dbg")
                        dma("sp", dv, VV[g][:, :, :, :], reads=["VV%d" % g], key="dbg")
                pti = 0
                if sb == 0:
                    continue
                ch_list = (3,) if sb == 1 else (0, 1, 2, 3)
                for slot in range(4):
                    c2 = slot // 2
                    pb = 64 * (slot % 2)
                    for rp in range(8):
                        pss, psr = psum45()
                        for q in range(2):
                            r = 2 * rp + q
                            if sb > 0:
                                S.op("pe", lambda e, pss=pss, q=q, r=r: e.matmul(
                                    pss[:, 256 * q:256 * q + 128], lhsT=KT[2][pb:pb + 64, c2, r, 0:128],
                                    rhs=QT[2][pb:pb + 64, c2, r, 0:128], start=True, stop=True),
                                    reads=["KT2", "QT2"], writes=[psr])
                            S.op("pe", lambda e, pss=pss, q=q, r=r: e.matmul(
                                pss[:, 256 * q + 128:256 * q + 256], lhsT=KT[2][pb:pb + 64, c2, r, 128:256],
                                rhs=QT[2][pb:pb + 64, c2, r, 0:128], start=True, stop=True),
                                reads=["KT2", "QT2"], writes=[psr])
                        k0 = 0 if sb > 0 else 1
                        S.op("act", lambda e, pss=pss, rp=rp, k0=k0: e.activation(
                            out=PT2[:, 2 * rp:2 * rp + 2, k0:2, :],
                            in_=pss[:, :].rearrange("p (q k t) -> p q k t", q=2, k=2)[:, :, k0:2, :], func=AF.Exp),
                            reads=[psr], writes=["PT2"])
                        S.op("dve", lambda e, rp=rp, k0=k0: e.tensor_tensor(
                            out=PT2[:, 2 * rp:2 * rp + 2, k0:2, :], in0=PT2[:, 2 * rp:2 * rp + 2, k0:2, :],
                            in1=maskb[:, k0:2, :].unsqueeze(1).to_broadcast([128, 2, 2 - k0, 128]), op=ALU.mult),
                            reads=["PT2", "maskb"], writes=["PT2"])
                    if DBG and sb == 0 and slot == 0:
                        dp2 = dout("dbg_PT2", [128, 16, 2, 128], BF16)
                        dma("sp", dp2, PT2[:, :, :, :], reads=["PT2"], key="dbg")
                    pendB = [None]
                    for ch in ch_list:
                        bset[0] += 1
                        if bset[0] % 2:
                            Bk = [(PSB[i_], "ps%d" % i_) for i_ in range(3)]
                        else:
                            Bk = [(PSB[3], "ps3"), (PTBf, "ptb"), (PTB2f, "ptb2")]
                        for g in range(2):
                            for pair in range(2):
                                pss, psr = psum45()
                                ptb_ = PTs[pti % 2]
                                ptr = "PTs%d" % (pti % 2)
                                pti += 1
                                info = []
                                for q in range(2):
                                    u = 2 * pair + q
                                    if g == 0:
                                        qb_i = 4 * ch + u
                                        hasp = (sb > 0) or (qb_i > 0)
                                        kprev = KT[0][pb:pb + 64, c2, 0, 128 * qb_i:128 * qb_i + 128]
                                        kcur = KT[0][pb:pb + 64, c2, 0, 128 + 128 * qb_i:256 + 128 * qb_i]
                                        qq = QT[0][pb:pb + 64, c2, 0, 128 * qb_i:128 * qb_i + 128]
                                        vprev = VV[0][:, 0, qb_i, 65 * slot:65 * (slot + 1)]
                                        vcur = VV[0][:, 0, qb_i + 1, 65 * slot:65 * (slot + 1)]
                                        ocols = slice(128 * u, 128 * (u + 1))
                                    else:
                                        hasp = (sb > 0) or (ch > 0)
                                        kprev = KT[1][pb:pb + 64, c2, u, 128 * ch:128 * ch + 128]
                                        kcur = KT[1][pb:pb + 64, c2, u, 128 + 128 * ch:256 + 128 * ch]
                                        qq = QT[1][pb:pb + 64, c2, u, 128 * ch:128 * ch + 128]
                                        vprev = VV[1][:, u, ch, 65 * slot:65 * (slot + 1)]
                                        vcur = VV[1][:, u, ch + 1, 65 * slot:65 * (slot + 1)]
                                        ocols = slice(128 * u, 128 * (u + 1))
                                    if hasp:
                                        S.op("pe", lambda e, pss=pss, q=q, kprev=kprev, qq=qq: e.matmul(
                                            pss[:, 256 * q:256 * q + 128], lhsT=kprev, rhs=qq, start=True, stop=True),
                                            reads=["KT%d" % g, "QT%d" % g], writes=[psr])
                                    S.op("pe", lambda e, pss=pss, q=q, kcur=kcur, qq=qq: e.matmul(
                                        pss[:, 256 * q + 128:256 * q + 256], lhsT=kcur, rhs=qq, start=True, stop=True),
                                        reads=["KT%d" % g, "QT%d" % g], writes=[psr])
                                    info.append((hasp, vprev, vcur, ocols))
                                allp = info[0][0] and info[1][0]
                                k0 = 0 if allp else 1
                                if (not allp) and (info[0][0] or info[1][0]):
                                    S.op("act", lambda e, pss=pss, ptb_=ptb_: e.activation(
                                        out=ptb_[:, 1, 0, :], in_=pss[:, 256:384], func=AF.Exp), reads=[psr], writes=[ptr])
                                    S.op("dve", lambda e, ptb_=ptb_: e.tensor_tensor(
                                        out=ptb_[:, 1, 0, :], in0=ptb_[:, 1, 0, :], in1=maskb[:, 0, :], op=ALU.mult),
                                        reads=[ptr, "maskb"], writes=[ptr])
                                S.op("act", lambda e, pss=pss, ptb_=ptb_, k0=k0: e.activation(
                                    out=ptb_[:, :, k0:2, :],
                                    in_=pss[:, :].rearrange("p (q k t) -> p q k t", q=2, k=2)[:, :, k0:2, :], func=AF.Exp),
                                    reads=[psr], writes=[ptr])
                                S.op("dve", lambda e, ptb_=ptb_, k0=k0: e.tensor_tensor(
                                    out=ptb_[:, :, k0:2, :], in0=ptb_[:, :, k0:2, :],
                                    in1=maskb[:, k0:2, :].unsqueeze(1).to_broadcast([128, 2, 2 - k0, 128]), op=ALU.mult),
                                    reads=[ptr, "maskb"], writes=[ptr])
                                if DBG and sb == 0 and slot == 0 and ch == 0:
                                    dpt = dout("dbg_PT%d_%d" % (g, pair), [128, 2, 2, 128], BF16)
                                    dma("sp", dpt, ptb_[:, :, :, :], reads=[ptr], key="dbg")

                                def emitB(info=info, ptb_=ptb_, ptr=ptr, g=g, Bk=Bk):
                                    for q in range(2):
                                        hasp, vprev, vcur, ocols = info[q]
                                        pO, pOr = Bk[g]
                                        kbl = (0, 1) if hasp else (1,)
                                        for kb in kbl:
                                            vv = vprev if kb == 0 else vcur
                                            S.op("pe", lambda e, pO=pO, vv=vv, ptb_=ptb_, q=q, kb=kb, ocols=ocols, kbl=kbl: e.matmul(
                                                pO[0:65, ocols], lhsT=vv, rhs=ptb_[:, q, kb, :], start=(kb == kbl[0]),
                                                stop=(kb == 1)), reads=["VV%d" % g, ptr], writes=[pOr])
                                if pendB[0] is not None:
                                    pendB[0]()
                                pendB[0] = emitB
                        pendB[0]()
                        pendB[0] = None
                        pO, pOr = Bk[2]
                        for r in range(16):
                            kbs = (0, 1) if sb > 0 else (1,)
                            for kb in kbs:
                                S.op("pe", lambda e, pO=pO, r=r, kb=kb, kbs=kbs: e.matmul(
                                    pO[0:65, 32 * r:32 * (r + 1)], lhsT=VV[2][:, r, kb, 65 * slot:65 * (slot + 1)],
                                    rhs=PT2[:, r, kb, 32 * ch:32 * (ch + 1)], start=(kb == kbs[0]), stop=(kb == 1)),
                                    reads=["VV2", "PT2"], writes=[pOr], sig=(kb == 1))
                        if DBG and sb == 0 and slot == 0 and ch == 0:
                            for gq in range(3):
                                db = dout("dbg_B%d" % gq, [65, 512], F32)
                                S.op("dve", lambda e, gq=gq: e.tensor_copy(out=stg[1][0:65, :], in_=Bk[gq][0][0:65, :]),
                                     reads=[Bk[gq][1]], writes=["stg1"])
                                dma("sp", db, stg[1][0:65, :], reads=["stg1"], key="dbg")
                        tmpb = stg[0]
                        S.op("act", lambda e, tmpb=tmpb, b0=Bk[0][0]: e.activation(out=tmpb[0:65, :], in_=b0[0:65, :], func=AF.Copy),
                             reads=[Bk[0][1]], writes=["stg0"])
                        S.op("dve", lambda e, tmpb=tmpb, b1=Bk[1][0]: e.tensor_tensor(
                            out=tmpb[0:65, :].rearrange("p (j r) -> p j r", r=4),
                            in0=tmpb[0:65, :].rearrange("p (j r) -> p j r", r=4),
                            in1=b1[0:65, :].rearrange("p (r j) -> p j r", r=4), op=ALU.add),
                            reads=[Bk[1][1], "stg0"], writes=["stg0"])
                        S.op("dve", lambda e, tmpb=tmpb, b2=Bk[2][0]: e.tensor_tensor(
                            out=tmpb[0:65, :].rearrange("p (j r) -> p j r", r=16),
                            in0=tmpb[0:65, :].rearrange("p (j r) -> p j r", r=16),
                            in1=b2[0:65, :].rearrange("p (r j) -> p j r", r=16), op=ALU.add),
                            reads=[Bk[2][1], "stg0"], writes=["stg0"])
                        S.op("dve", lambda e, tmpb=tmpb: e.tensor_scalar(out=tmpb[64:65, :], in0=tmpb[64:65, :], scalar1=1e-30,
                                                                        scalar2=None, op0=ALU.add),
                             reads=["stg0"], writes=["stg0"])
                        S.op("dve", lambda e, tmpb=tmpb: e.reciprocal(out=stg[1][64:65, :], in_=tmpb[64:65, :]),
                             reads=["stg0"], writes=["stg1"])
                        pD, pDr = Bk[0]
                        S.op("pe", lambda e, pD=pD: e.matmul(pD[0:64, :], lhsT=onesf[64:65, 0:64], rhs=stg[1][64:65, :],
                                                             start=True, stop=True), reads=["onesf", "stg1"], writes=[pDr])
                        pO, pOr = None, "stg0"
                        rc = rec[0]
                        rcr = "rec0"
                        S.op("dve", lambda e, pD=pD, tmpb=tmpb, slot=slot, ch=ch: e.tensor_tensor(
                            out=oTsb[:, 512 * ch:512 * (ch + 1)], in0=tmpb[0:64, :], in1=pD[0:64, :], op=ALU.mult),
                            reads=[pDr, "stg0"], writes=["oTsb"])
                    dma("sp", oT_d[:, slot, T0:T0 + 2048], oTsb[:, :], reads=["oTsb"], writes=["oT_d"], key="oT_d")
            for sb_ in range(NSB, len(LATE_CASTS)):
                for f_ in LATE_CASTS[sb_]:
                    f_()
            S.barrier()
            S.emit()
        if KSTOP == 1:
            return nc

        with contextlib.ExitStack() as st2:
            xg2 = [sbuf(st2, "xg_%d" % i, [128, NT // 128, D], F32) for i in range(2)]
            xs4 = xsb + [sbuf(st2, "xsb%d" % i, [128, D], BF16) for i in (2, 3)]
            hT = sbuf(st2, "hT", [128, 8, NT], BF16)
            uext = sbuf(st2, "uext", [128, 8, 30 + NT], BF16)
            dgb = [sbuf(st2, "dgb%d" % i, [128, 31, 128], BF16) for i in range(2)]
            big2 = sbuf(st2, "big2", [128, 12 * NT], F32)
            acc = big2[:, 0:8 * NT].rearrange("p (c t) -> p c t", c=8)
            lnm = big2[:, 8 * NT:12 * NT].rearrange("p (c t) -> p c t", c=4)
            aT = big2[:, 0:11 * NT].bitcast(BF16).rearrange("p (c t) -> p c t", c=22)
            R3 = sbuf(st2, "R3", [128, 11264], F32)
            wdnb = R3[:, :].bitcast(BF16).rearrange("p (c t) -> p c t", c=22)
            sT = R3[:, 0:2048].bitcast(BF16).rearrange("p (c t) -> p c t", c=8)
            mixT = R3[:, 2048:4096].bitcast(BF16).rearrange("p (c t) -> p c t", c=8)
            woutb = R3[:, 4096:8192].bitcast(BF16).rearrange("p (c t) -> p c t", c=8)
            oTg = R3[0:64, 8192:9216].bitcast(BF16).rearrange("p (c t) -> p c t", c=4)
            cb16 = [R3[:, 9216 + 256 * i:9472 + 256 * i].bitcast(BF16) for i in range(2)]
            csq16 = [R3[:, 9728 + 256 * i:9984 + 256 * i].bitcast(BF16) for i in range(2)]
            tt = [R3[:, 10240 + 512 * i:10752 + 512 * i] for i in range(2)]
            S.alias["wdnb"] = ["sT", "mixT", "woutb", "oTg", "cb16_0", "cb16_1", "csq16_0", "csq16_1", "tt0", "tt1"]
            S.alias["aT"] = ["acc%d" % c for c in range(8)] + ["lnm"]
            S.alias["uh"] = ["uext"] + ["uext%d" % c for c in range(8)]
            S.alias["uprod"] = S.alias["uh"]
            wao64 = sbuf(st2, "wao64", [64, 4, D], BF16)
            wao128 = sbuf(st2, "wao128", [128, 2, D], BF16)
            NWB = 3
            wt = [sbuf(st2, "wt%d" % i, [128, 8, 256], BF16) for i in range(NWB)]
            sg_ = [sbuf(st2, "sg%d" % i, [128, NT], F32) for i in range(3)]
            upx = [sbuf(st2, "upx%d" % i, [128, NT + 2], F32) for i in range(2)]
            cgb = [sbuf(st2, "cgb%d" % i, [128, NT], F32) for i in range(2)]
            fh = sbuf(st2, "fh", [128, 44, 2], F32)
            gfin = sbuf(st2, "gfin", [128, D], F32)
            yt = [sbuf(st2, "yt%d" % i, [128, D], F32) for i in range(1)]
            uflat = uext[:, :, :].rearrange("p c t -> p (c t)")[:, 0:4336].bitcast(F32)
            uh = uflat[:, 0:8 * NS * 31].rearrange("p (c s j) -> p c s j", c=8, s=NS)
            uprod = uflat[:, 8 * NS * 31:16 * NS * 31].rearrange("p (c s j) -> p c s j", c=8, s=NS)
            fhs = sbuf(st2, "fhs", [128, 44, 2 * NS], F32)
            upn = sbuf(st2, "upn", [128, 44, NS], F32)
            orow = yt[0][0:30, :]
            S.alias["orow"] = ["yt0"]

            dma("sp", gfin[:], gfin_d, writes=["gfin"], key="gfin")
            dma("sp", wao64[:], wb_ao.rearrange("(s d) n -> d s n", d=64), reads=["wb_ao"], writes=["wao"], key="wao64")
            dma("sp", wao128[:], wb_ao.rearrange("(c p) n -> p c n", p=128), reads=["wb_ao"], writes=["wao"], key="wao128")
            S.op("pool", lambda e: e.memset(uext[:, :, 0:30], 0.0), writes=["uext"])
            S.op("pool", lambda e: e.memset(fh[:], 0.0), writes=["fh"])

            wi = [0]

            def wtile(src_ap, rd):
                i = wi[0] % NWB
                wi[0] += 1
                dma("sp", wt[i][:, :, :], src_ap.rearrange("(kc p) n -> p kc n", p=128), reads=[rd],
                    writes=["wt%d" % i], key="wt%d" % i)
                return wt[i], "wt%d" % i

            tgl = [0]

            def alt(a, b):
                tgl[0] += 1
                return a if tgl[0] % 2 else b

            ygi = [0]

            prevN = [NT]

            def head(sp_, k):
                xgk = xg2[k % 2]
                for (i, TT) in sp_["tiles"]:
                    src = xs if sp_["sample"] else xp[sp_["t0"] + 128 * i:sp_["t0"] + 128 * i + TT, :]
                    dma("sp", xgk[0:TT, i, :], src, writes=["xg%d_%d" % (k % 2, i)], key="xg%d_%d" % (k % 2, i))
                    norm_a(xgk[0:TT, i, :], "xg%d_%d" % (k % 2, i), TT, xs4[i], "xsb%d" % i)

            def head_b(sp_, k):
                for (i, TT) in sp_["tiles"]:
                    norm_b(TT, xs4[i], "xsb%d" % i, hT[:, :, 128 * i:128 * i + TT], "hT", "gmix")

            def body(t0, TT_list, N, sample, first=False, lastg=False, out0=None, kidx=0, hook_a=None, hook_b=None):
                gi = 1
                xg = xg2[kidx % 2]
                XG = "xg%d_" % (kidx % 2)
                if sample:
                    for s in range(NS):
                        dma("sp", orow[:, :], sconv[s, :, :], writes=["orow"], key="scv")
                        for c in range(8):
                            pst, pr = psum()
                            S.op("pe", lambda e, c=c, pst=pst: e.transpose(pst[:, 0:30], orow[0:30, 128 * c:128 * (c + 1)],
                                                                            identf[0:30, 0:30]),
                                 reads=["orow", "identf"], writes=[pr])
                            S.op("act", lambda e, c=c, pst=pst, s=s: e.activation(out=uh[:, c, s, 0:30], in_=pst[:, 0:30],
                                                                                  func=AF.Copy), reads=[pr], writes=["uh"])
                elif not first:
                    pN = prevN[0]
                    S.op("pool", lambda e: e.tensor_copy(out=uext[:, :, 0:30], in_=uext[:, :, pN:pN + 30]),
                         reads=["uext"], writes=["uext"])
                if not sample:
                    prevN[0] = N
                for c in range(8):
                    w, wr = wtile(wb_in[:, 256 * c:256 * (c + 1)], "wb_in1")
                    pl, plr = psum()
                    pg, pgr = psum()
                    for kc in range(8):
                        S.op("pe", lambda e, kc=kc, w=w, pl=pl: e.matmul(pl[:, 0:N], lhsT=w[:, kc, 0:128], rhs=hT[:, kc, 0:N],
                                                                         start=(kc == 0), stop=(kc == 7)),
                             reads=[wr, "hT"], writes=[plr], sig=(kc == 7))
                    for kc in range(8):
                        S.op("pe", lambda e, kc=kc, w=w, pg=pg: e.matmul(pg[:, 0:N], lhsT=w[:, kc, 128:256], rhs=hT[:, kc, 0:N],
                                                                         start=(kc == 0), stop=(kc == 7)),
                             reads=[wr, "hT"], writes=[pgr], sig=(kc == 7))
                    sgb = sg_[c % 2]
                    sgr = "sg%d" % (c % 2)
                    S.op("act", lambda e, pg=pg, sgb=sgb: e.activation(out=sgb[:, 0:N], in_=pg[:, 0:N], func=AF.Sigmoid),
                         reads=[pgr], writes=[sgr])
                    if sample:
                        S.op("dve", lambda e, c=c, pl=pl, sgb=sgb: e.tensor_tensor(out=uh[:, c, :, 30], in0=pl[:, 0:N],
                                                                                   in1=sgb[:, 0:N], op=ALU.mult),
                             reads=[plr, sgr], writes=["uh"])
                    else:
                        S.op("dve", lambda e, c=c, pl=pl, sgb=sgb: e.tensor_tensor(out=uext[:, c, 30:30 + N], in0=pl[:, 0:N],
                                                                                   in1=sgb[:, 0:N], op=ALU.mult),
                             reads=[plr, sgr], writes=["uext%d" % c])
                if not sample:
                    dma("sp", oTg[:, :, 0:N], oT_d[:, :, t0:t0 + N], reads=["oT_d"], writes=["oTg"], key="oTg")
                dma("sp", woutb[:, :, :], wb_out.rearrange("(kc p) n -> p kc n", p=128), reads=["wb_out"],
                    writes=["woutb"], key="woutb")
                o_w = PP["wdw"][0]
                if sample:
                    S.op("dve", lambda e: e.tensor_tensor(
                        out=uprod[:, :, :, :], in0=uh[:, :, :, :],
                        in1=pp[:, o_w:o_w + 248].rearrange("p (c j) -> p c j", j=31).unsqueeze(2).to_broadcast([128, 8, NS, 31]),
                        op=ALU.mult), reads=["uh", "pp"], writes=["uprod"])
                    S.op("dve", lambda e: e.tensor_reduce(out=acc[:, :, 0:NS], in_=uprod[:, :, :, :], axis=AX.X, op=ALU.add),
                         reads=["uprod"], writes=["acc%d" % c for c in range(8)])
                    o_b = PP["bdw"][0]
                    S.op("dve", lambda e: e.tensor_tensor(out=acc[:, :, 0:NS], in0=acc[:, :, 0:NS],
                                                          in1=pp[:, o_b:o_b + 8].unsqueeze(2).to_broadcast([128, 8, NS]),
                                                          op=ALU.add), reads=["acc%d" % c for c in range(8)] + ["pp"],
                         writes=["acc%d" % c for c in range(8)])
                else:
                    for c in range(8):
                        dg = dgb[c % 2]
                        dgr = "dgb%d" % (c % 2)
                        S.op("pool", lambda e, c=c, dg=dg: e.tensor_tensor(
                            out=dg[:, :, :], in0=identf[:, :].unsqueeze(1).to_broadcast([128, 31, 128]),
                            in1=pp[:, o_w + 31 * c:o_w + 31 * c + 31].unsqueeze(2).to_broadcast([128, 31, 128]),
                            op=ALU.mult), reads=["identf", "pp"], writes=[dgr])
                        pcv, pcr = psum()
                        for j in range(31):
                            S.op("pe", lambda e, c=c, j=j, dg=dg, pcv=pcv: e.matmul(
                                pcv[:, 0:N], lhsT=dg[:, j, :], rhs=uext[:, c, j:j + N], start=(j == 0), stop=(j == 30)),
                                reads=[dgr, "uext%d" % c, "uext"], writes=[pcr], sig=(j == 30))
                        S.op("act", lambda e, c=c, pcv=pcv: e.activation(out=acc[:, c, 0:N], in_=pcv[:, 0:N], func=AF.Identity,
                                                                         bias=P("bdw", c)),
                             reads=[pcr, "pp"], writes=["acc%d" % c])
                if sample or lastg:
                    for c in range(8):
                        pst, pr = psum()
                        if sample:
                            S.op("pe", lambda e, c=c, pst=pst: e.transpose(pst[0:NS, 0:128], uh[:, c, :, 30], identf[:, :]),
                                 reads=["uh", "identf"], writes=[pr])
                            S.op("act", lambda e, c=c, pst=pst: e.activation(out=orow[0:NS, 128 * c:128 * (c + 1)],
                                                                             in_=pst[0:NS, 0:128], func=AF.Copy),
                                 reads=[pr], writes=["orow"])
                        else:
                            S.op("pe", lambda e, c=c: e.transpose(PTB[0:30, 128 * c:128 * (c + 1)], uext[:, c, NT:NT + 30],
                                                                   identb[:, :]),
                                 reads=["uext%d" % c, "identb"], writes=["ptb"])
                            S.op("act", lambda e, c=c: e.activation(out=orow[0:30, 128 * c:128 * (c + 1)],
                                                                    in_=PTB[0:30, 128 * c:128 * (c + 1)], func=AF.Copy),
                                 reads=["ptb"], writes=["orow"])
                    if sample:
                        dma("sp", conv_s[:, 29, :], orow[0:NS, 0:D], reads=["orow"], key="oorow")
                    else:
                        dma("sp", conv_p[:, :], orow[0:30, 0:D], reads=["orow"], key="oorow")
                if DBG and t0 == HALO and not sample:
                    dma("sp", dbg_acc, acc[:, :, :], reads=["acc%d" % c for c in range(8)], key="dbg")
                p1, p1r = psum()
                p2, p2r = psum()
                for c in range(8):
                    b1 = cb16[c % 2]
                    b2 = csq16[c % 2]
                    S.op("dve", lambda e, c=c, b1=b1: e.tensor_copy(out=b1[:, 0:N], in_=acc[:, c, 0:N]),
                         reads=["acc%d" % c], writes=["cb16_%d" % (c % 2)])
                    S.op("act", lambda e, c=c, b2=b2: e.activation(out=b2[:, 0:N], in_=acc[:, c, 0:N], func=AF.Square),
                         reads=["acc%d" % c], writes=["csq16_%d" % (c % 2)])
                    S.op("pe", lambda e, c=c, b1=b1: e.matmul(p1[:, 0:N], lhsT=onesb[:, :], rhs=b1[:, 0:N], start=(c == 0),
                                                              stop=(c == 7)), reads=["onesb", "cb16_%d" % (c % 2)], writes=[p1r])
                    S.op("pe", lambda e, c=c, b2=b2: e.matmul(p2[:, 0:N], lhsT=onesb[:, :], rhs=b2[:, 0:N], start=(c == 0),
                                                              stop=(c == 7)), reads=["onesb", "csq16_%d" % (c % 2)], writes=[p2r])
                S.op("dve", lambda e: e.tensor_scalar(out=lnm[:, 0, 0:N], in0=p1[:, 0:N], scalar1=1.0 / D, scalar2=None,
                                                      op0=ALU.mult), reads=[p1r], writes=["lnm"])
                S.op("dve", lambda e: e.tensor_tensor(out=lnm[:, 1, 0:N], in0=lnm[:, 0, 0:N], in1=lnm[:, 0, 0:N], op=ALU.mult),
                     reads=["lnm"], writes=["lnm"])
                S.op("dve", lambda e: e.scalar_tensor_tensor(out=lnm[:, 1, 0:N], in0=p2[:, 0:N], scalar=1.0 / D,
                                                             in1=lnm[:, 1, 0:N], op0=ALU.mult, op1=ALU.subtract),
                     reads=[p2r, "lnm"], writes=["lnm"])
                S.op("act", lambda e: e.activation(out=lnm[:, 2, 0:N], in_=lnm[:, 1, 0:N], func=AF.Sqrt, bias=epst[:, 0:1]),
                     reads=["lnm", "epst"], writes=["lnm"])
                S.op("dve", lambda e: e.reciprocal(out=lnm[:, 3, 0:N], in_=lnm[:, 2, 0:N]), reads=["lnm"], writes=["lnm"])
                for c in range(8):
                    tb = tt[c % 2]
                    tr = "tt%d" % (c % 2)
                    S.op("dve", lambda e, c=c, tb=tb: e.tensor_tensor(out=tb[:, 0:N], in0=acc[:, c, 0:N], in1=lnm[:, 0, 0:N],
                                                                      op=ALU.subtract),
                         reads=["acc%d" % c, "lnm"], writes=[tr])
                    S.op("dve", lambda e, tb=tb: e.tensor_tensor(out=tb[:, 0:N], in0=tb[:, 0:N], in1=lnm[:, 3, 0:N],
                                                                  op=ALU.mult), reads=[tr, "lnm"], writes=[tr])
                    S.op("act", lambda e, c=c, tb=tb: e.activation(out=sT[:, c, 0:N], in_=tb[:, 0:N], func=AF.Silu,
                                                                   scale=P("lng", c), bias=P("lnb", c)),
                         reads=[tr, "pp"], writes=["sT"])
                if DBG and t0 == HALO and not sample:
                    dma("sp", dbg_sT, sT[:, :, :], reads=["sT"], key="dbg")
                for c in range(8):
                    if c % 2 == 0:
                        wco_t, wco_r = wtile(wb_co[:, 128 * c:128 * c + 256], "wb_co")
                    wg_t, wg_r = wtile(wb_in[:, 4352 + 256 * c:4352 + 256 * (c + 1)], "wb_in3")
                    pa, par = psum()
                    pbb, pbr = psum()
                    pga, pgar = psum()
                    pgb, pgbr = psum()
                    co = 128 * (c % 2)
                    for kc in range(8):
                        S.op("pe", lambda e, kc=kc, pa=pa, wco_t=wco_t, co=co: e.matmul(
                            pa[:, 0:N], lhsT=wco_t[:, kc, co:co + 128], rhs=sT[:, kc, 0:N], start=(kc == 0), stop=(kc == 7)),
                            reads=[wco_r, "sT"], writes=[par], sig=(kc == 7))
                    if sample:
                        for k2 in range(2):
                            S.op("pe", lambda e, k2=k2, pbb=pbb, c=c: e.matmul(
                                pbb[:, 0:N], lhsT=wao128[:, k2, 128 * c:128 * (c + 1)], rhs=oTs[:, k2, 0:N],
                                start=(k2 == 0), stop=(k2 == 1)), reads=["wao", "oTs"], writes=[pbr], sig=(k2 == 1))
                    else:
                        for sl in range(4):
                            S.op("pe", lambda e, sl=sl, pbb=pbb, c=c: e.matmul(
                                pbb[:, 0:N], lhsT=wao64[:, sl, 128 * c:128 * (c + 1)], rhs=oTg[:, sl, 0:N],
                                start=(sl == 0), stop=(sl == 3)), reads=["wao", "oTg"], writes=[pbr], sig=(sl == 3))
                    for kc in range(8):
                        S.op("pe", lambda e, kc=kc, pga=pga, wg_t=wg_t: e.matmul(
                            pga[:, 0:N], lhsT=wg_t[:, kc, 0:128], rhs=hT[:, kc, 0:N], start=(kc == 0), stop=(kc == 7)),
                            reads=[wg_r, "hT"], writes=[pgar], sig=(kc == 7))
                    for kc in range(8):
                        S.op("pe", lambda e, kc=kc, pgb=pgb, wg_t=wg_t: e.matmul(
                            pgb[:, 0:N], lhsT=wg_t[:, kc, 128:256], rhs=hT[:, kc, 0:N], start=(kc == 0), stop=(kc == 7)),
                            reads=[wg_r, "hT"], writes=[pgbr], sig=(kc == 7))
                    sa, sar = sg_[0], "sg0"
                    sb_, sbr = sg_[1], "sg1"
                    S.op("act", lambda e, pga=pga: e.activation(out=sa[:, 0:N], in_=pga[:, 0:N], func=AF.Sigmoid),
                         reads=[pgar], writes=[sar])
                    S.op("act", lambda e, pgb=pgb: e.activation(out=sb_[:, 0:N], in_=pgb[:, 0:N], func=AF.Sigmoid),
                         reads=[pgbr], writes=[sbr])
                    S.op("dve", lambda e, pa=pa: e.tensor_tensor(out=sa[:, 0:N], in0=pa[:, 0:N], in1=sa[:, 0:N], op=ALU.mult),
                         reads=[par, sar], writes=[sar])
                    S.op("dve", lambda e, pbb=pbb: e.tensor_tensor(out=sb_[:, 0:N], in0=pbb[:, 0:N], in1=sb_[:, 0:N],
                                                                   op=ALU.mult), reads=[pbr, sbr], writes=[sbr])
                    S.op("dve", lambda e, c=c: e.tensor_tensor(out=mixT[:, c, 0:N], in0=sa[:, 0:N], in1=sb_[:, 0:N],
                                                                op=ALU.add), reads=[sar, sbr], writes=["mixT"])
                if DBG and t0 == HALO and not sample:
                    dma("sp", dbg_mix, mixT[:, :, :], reads=["mixT"], key="dbg")
                for (i, TT) in TT_list:
                    WN = 512 if TT == 128 else 256
                    for n in range(D // WN):
                        po, por = psum()
                        for kc in range(8):
                            S.op("pe", lambda e, kc=kc, po=po, i=i, TT=TT, n=n, WN=WN: e.matmul(
                                po[0:TT, 0:WN], lhsT=mixT[:, kc, 128 * i:128 * i + TT], rhs=woutb[:, kc, WN * n:WN * (n + 1)],
                                start=(kc == 0), stop=(kc == 7)), reads=["mixT", "woutb"], writes=[por], sig=(kc == 7))
                        S.op("dve", lambda e, po=po, i=i, TT=TT, n=n, WN=WN: e.tensor_tensor(
                            out=xg[0:TT, i, WN * n:WN * (n + 1)], in0=po[0:TT, 0:WN], in1=xg[0:TT, i, WN * n:WN * (n + 1)],
                            op=ALU.add), reads=[por, XG + str(i)], writes=[XG + str(i)])
                if DBG and t0 == HALO and not sample:
                    for (i, TT) in TT_list:
                        dma("sp", dbg_xmid[128 * i:128 * (i + 1), :], xg[0:TT, i, :], reads=[XG + str(i)], key="dbg")
                dma("sp", wdnb[:, :, :], wb_dn.rearrange("(kc p) n -> p kc n", p=128), reads=["wb_dn"], writes=["wdnb"],
                    key="wdnb")
                for (i, TT) in TT_list:
                    norm_T(xg[0:TT, i, :], XG + str(i), TT, hT[:, :, 128 * i:128 * i + TT], "hT", "gffn", ygi[0])
                    ygi[0] += 1
                if sample:
                    for q in range(44):
                        if q % 8 == 0:
                            wpc = min(1024, 2 * DFF - 128 * q)
                            dma("sp", orow[0:2 * NS, 0:wpc], sffn.rearrange("s j n -> (s j) n")[:, 128 * q:128 * q + wpc],
                                writes=["orow"], key="scv")
                        pst, pr = psum()
                        S.op("pe", lambda e, q=q, pst=pst: e.transpose(
                            pst[:, 0:2 * NS], orow[0:2 * NS, 128 * (q % 8):128 * (q % 8 + 1)], identf[0:2 * NS, 0:2 * NS]),
                             reads=["orow", "identf"], writes=[pr])
                        S.op("act", lambda e, q=q, pst=pst: e.activation(out=fhs[:, q, :], in_=pst[:, 0:2 * NS], func=AF.Copy),
                             reads=[pr], writes=["fhs"])
                o_f = PP["wfdw"][0]
                for j in range(22):
                    w, wr = wtile(wb_up[:, 256 * j:256 * (j + 1)], "wb_up")
                    pgv = []
                    for hv in range(2):
                        pz, pzr = psum()
                        for kc in range(8):
                            S.op("pe", lambda e, kc=kc, w=w, pz=pz, hv=hv: e.matmul(
                                pz[:, 0:N], lhsT=w[:, kc, 128 * hv:128 * (hv + 1)], rhs=hT[:, kc, 0:N], start=(kc == 0),
                                stop=(kc == 7)), reads=[wr, "hT"], writes=[pzr], sig=(kc == 7))
                        pgv.append((pz, pzr))
                    for hv in range(2):
                        q = 2 * j + hv
                        pz, pzr = pgv[hv]
                        ub, ur = upx[hv], "upx%d" % hv
                        cb, cr = cgb[hv], "cgb%d" % hv
                        w0 = pp[:, o_f + 3 * q:o_f + 3 * q + 1]
                        w1 = pp[:, o_f + 3 * q + 1:o_f + 3 * q + 2]
                        w2 = pp[:, o_f + 3 * q + 2:o_f + 3 * q + 3]
                        if sample:
                            S.op("act", lambda e, pz=pz, cb=cb, w2=w2, q=q: e.activation(
                                out=cb[:, 0:N], in_=pz[:, 0:N], func=AF.Identity, scale=w2, bias=P("bfdw", q)),
                                reads=[pzr, "pp"], writes=[cr])
                            S.op("act", lambda e, pz=pz, q=q: e.activation(out=upn[:, q, :], in_=pz[:, 0:NS], func=AF.Copy),
                                 reads=[pzr], writes=["upn"])
                            fv = fhs[:, q, :].rearrange("p (s j) -> p j s", j=2)
                            S.op("dve", lambda e, cb=cb, fv=fv, w1=w1: e.scalar_tensor_tensor(
                                out=cb[:, 0:N], in0=fv[:, 1, :], scalar=w1, in1=cb[:, 0:N], op0=ALU.mult, op1=ALU.add),
                                reads=["fhs", cr, "pp"], writes=[cr])
                            S.op("dve", lambda e, cb=cb, fv=fv, w0=w0: e.scalar_tensor_tensor(
                                out=cb[:, 0:N], in0=fv[:, 0, :], scalar=w0, in1=cb[:, 0:N], op0=ALU.mult, op1=ALU.add),
                                reads=["fhs", cr, "pp"], writes=[cr])
                        else:
                            S.op("dve", lambda e, ub=ub, q=q: e.tensor_copy(out=ub[:, 0:2], in_=fh[:, q, :]),
                                 reads=["fh"], writes=[ur])
                            S.op("act", lambda e, pz=pz, ub=ub: e.activation(out=ub[:, 2:2 + N], in_=pz[:, 0:N], func=AF.Copy),
                                 reads=[pzr], writes=[ur])
                            S.op("dve", lambda e, ub=ub, q=q: e.tensor_copy(out=fh[:, q, :], in_=ub[:, N:N + 2]),
                                 reads=[ur], writes=["fh"])
                            S.op("dve", lambda e, cb=cb, ub=ub, w2=w2, q=q: e.tensor_scalar(
                                out=cb[:, 0:N], in0=ub[:, 2:2 + N], scalar1=w2, scalar2=P("bfdw", q), op0=ALU.mult,
                                op1=ALU.add), reads=[ur, "pp"], writes=[cr])
                            S.op("dve", lambda e, cb=cb, ub=ub, w1=w1: e.scalar_tensor_tensor(
                                out=cb[:, 0:N], in0=ub[:, 1:1 + N], scalar=w1, in1=cb[:, 0:N], op0=ALU.mult, op1=ALU.add),
                                reads=[ur, cr, "pp"], writes=[cr])
                            S.op("dve", lambda e, cb=cb, ub=ub, w0=w0: e.scalar_tensor_tensor(
                                out=cb[:, 0:N], in0=ub[:, 0:N], scalar=w0, in1=cb[:, 0:N], op0=ALU.mult, op1=ALU.add),
                                reads=[ur, cr, "pp"], writes=[cr])
                    sgb, sgr = sg_[2], "sg2"
                    S.op("act", lambda e, sgb=sgb: e.activation(out=sgb[:, 0:N], in_=cgb[0][:, 0:N], func=AF.Silu),
                         reads=["cgb0"], writes=[sgr])
                    S.op("dve" if sample else "pool", lambda e, j=j, sgb=sgb: e.tensor_tensor(out=aT[:, j, 0:N], in0=sgb[:, 0:N],
                                                                                    in1=cgb[1][:, 0:N], op=ALU.mult),
                         reads=[sgr, "cgb1"], writes=["aT"])
                if sample or lastg:
                    nrow = NS if sample else 2
                    for q in range(44):
                        pst, pr = psum()
                        srcT = upn[:, q, :] if sample else fh[:, q, :]
                        S.op("pe", lambda e, pst=pst, srcT=srcT, nrow=nrow: e.transpose(pst[0:nrow, 0:128], srcT, identf[:, :]),
                             reads=["upn" if sample else "fh", "identf"], writes=[pr])
                        S.op("act", lambda e, q=q, pst=pst, nrow=nrow: e.activation(
                            out=orow[0:nrow, 128 * (q % 8):128 * (q % 8 + 1)], in_=pst[0:nrow, 0:128], func=AF.Copy),
                            reads=[pr], writes=["orow"])
                        if q % 8 == 7 or q == 43:
                            q0 = 8 * (q // 8)
                            wpc = 128 * (q - q0 + 1)
                            if sample:
                                dma("sp", ffn_s[:, 1, 128 * q0:128 * q0 + wpc], orow[0:NS, 0:wpc], reads=["orow"], key="oorow")
                            else:
                                dma("sp", ffn_p[:, 128 * q0:128 * q0 + wpc], orow[0:2, 0:wpc], reads=["orow"], key="oorow")
                if hook_a is not None:
                    hook_a()
                for (i, TT) in TT_list:
                    WN = 512 if TT == 128 else 256
                    for n in range(D // WN):
                        po, por = psum()
                        for kc in range(22):
                            S.op("pe", lambda e, kc=kc, po=po, i=i, TT=TT, n=n, WN=WN: e.matmul(
                                po[0:TT, 0:WN], lhsT=aT[:, kc, 128 * i:128 * i + TT], rhs=wdnb[:, kc, WN * n:WN * (n + 1)],
                                start=(kc == 0), stop=(kc == 21)), reads=["aT", "wdnb"], writes=[por], sig=(kc == 21))
                        S.op("dve", lambda e, po=po, i=i, TT=TT, n=n, WN=WN: e.tensor_tensor(
                            out=xg[0:TT, i, WN * n:WN * (n + 1)], in0=po[0:TT, 0:WN], in1=xg[0:TT, i, WN * n:WN * (n + 1)],
                            op=ALU.add), reads=[por, XG + str(i)], writes=[XG + str(i)])
                if hook_b is not None:
                    hook_b()
                for (i, TT) in TT_list:
                    col = stat_i[0]
                    stat_i[0] += 1
                    src = xg[0:TT, i, :]
                    S.op("act", lambda e, src=src, TT=TT, col=col: e.activation(
                        out=yt[0][0:TT, :], in_=src, func=AF.Square, accum_out=stat[0:TT, 0, col:col + 1]),
                        reads=[XG + str(i)], writes=["yt0", "stat%d" % col])
                    S.op("act", lambda e, TT=TT, col=col: e.activation(
                        out=stat[0:TT, 1, col:col + 1], in_=stat[0:TT, 0, col:col + 1], func=AF.Sqrt, scale=1.0 / D,
                        bias=epst[0:TT, 0:1]), reads=["stat%d" % col, "epst"], writes=["stat%d" % col])
                    S.op("dve", lambda e, TT=TT, col=col: e.reciprocal(out=stat[0:TT, 2, col:col + 1],
                                                                       in_=stat[0:TT, 1, col:col + 1]),
                         reads=["stat%d" % col], writes=["stat%d" % col])
                    yb = yt[0]
                    yr = "yt0"
                    S.op("dve", lambda e, src=src, TT=TT, col=col, yb=yb: e.scalar_tensor_tensor(
                        out=yb[0:TT, :], in0=src, scalar=stat[0:TT, 2, col:col + 1], in1=gfin[0:TT, :], op0=ALU.mult,
                        op1=ALU.mult), reads=[XG + str(i), "stat%d" % col, "gfin"], writes=[yr])
                    if sample:
                        dma("sp", y_s[:, :], yb[0:TT, :], reads=[yr], key="o" + yr)
                    elif out0 is not None:
                        dma("sp", y_p[out0 + 128 * i:out0 + 128 * i + TT, :], yb[0:TT, :], reads=[yr], key="o" + yr)

            hvt = sbuf(st2, "hvt", [128, 1], F32)
            dma("sp", hvt[:], hv_d, writes=["hvt"], key="hvt")
            specs = [dict(t0=HALO - 256, tiles=[(0, 128), (1, 128)], N=256, sample=False, first=True, lastg=False, out0=None)]
            for gi in range(NG):
                specs.append(dict(t0=HALO + gi * NT, tiles=[(i, 128) for i in range(NT // 128)], N=NT, sample=False,
                                  first=False, lastg=(gi == NG - 1), out0=gi * NT))
            specs.append(dict(t0=0, tiles=[(0, NS)], N=NS, sample=True, first=False, lastg=False, out0=None))
            head(specs[0], 0)
            head_b(specs[0], 0)
            for k, sp_ in enumerate(specs):
                if 1 <= k <= NG:
                    for k_, (dst_, src_) in enumerate(SHIFTS):
                        if k_ % NG == k - 1:
                            dma("pool", dst_, src_, key="cshift")
                nxt = specs[k + 1] if k + 1 < len(specs) else None
                body(sp_["t0"], sp_["tiles"], sp_["N"], sp_["sample"], first=sp_["first"], lastg=sp_["lastg"],
                     out0=sp_["out0"], kidx=k,
                     hook_a=(lambda nxt=nxt, k=k: head(nxt, k + 1)) if nxt else None,
                     hook_b=(lambda nxt=nxt, k=k: head_b(nxt, k + 1)) if nxt else None)
                if k == 0:
                    S.op("dve", lambda e: e.tensor_scalar(out=fh[:, :, :], in0=fh[:, :, :], scalar1=hvt[:, 0:1], scalar2=None,
                                                          op0=ALU.mult), reads=["fh", "hvt"], writes=["fh"])
            S.barrier()
            S.emit()
    return nc


def _perm_in():
    idx = []
    for c in range(8):
        idx += list(range(128 * c, 128 * (c + 1)))
        idx += list(range(1024 + 128 * c, 1024 + 128 * (c + 1)))
    idx += list(range(2048, 4352))
    for c in range(8):
        idx += list(range(4352 + 128 * c, 4352 + 128 * (c + 1)))
        idx += list(range(5376 + 128 * c, 5376 + 128 * (c + 1)))
    return np.array(idx)


def _perm_up():
    idx = []
    for j in range(22):
        idx += list(range(128 * j, 128 * (j + 1)))
        idx += list(range(DFF + 128 * j, DFF + 128 * (j + 1)))
    return np.array(idx)


def _fm(v, nch):
    return np.ascontiguousarray(v.reshape(nch, 128).T)


def make_shared(inp):
    f = np.float32
    pin = _perm_in()
    pup = _perm_up()
    ppv = np.zeros((128, NPP), f)

    def put(name, arr):
        o, w = PP[name]
        ppv[:, o:o + w] = arr.reshape(128, w)

    put("gmix", _fm(inp["g_mix"][0], 8))
    put("bdw", _fm(inp["b_dw"][0], 8))
    put("lng", _fm(inp["ln_g"][0], 8))
    put("lnb", _fm(inp["ln_b"][0], 8))
    put("gffn", _fm(inp["g_ffn"][0], 8))
    wdw = inp["w_dw"][0]
    put("wdw", np.ascontiguousarray(wdw.T.reshape(8, 128, 31).transpose(1, 0, 2)))
    wf = inp["w_fdw"][0][:, pup]
    put("wfdw", np.ascontiguousarray(wf.T.reshape(44, 128, 3).transpose(1, 0, 2)))
    put("bfdw", _fm(inp["b_fdw"][0][pup], 44))
    inv = (np.float32(500000.0) ** (-np.arange(8, dtype=f) / np.float32(8))).astype(f)
    angs = (np.full((NS, 1), 16384.0, f) * inv[None, :]).astype(f)
    css = np.concatenate([np.cos(angs), np.cos(angs)], 1).astype(f)
    sns = np.concatenate([-np.sin(angs), np.sin(angs)], 1).astype(f)
    j = np.arange(128)[:, None]
    i = np.arange(128)[None, :]
    mask2 = np.stack([(j >= i), (j <= i)], 1).astype(f)
    sel = np.zeros((NS, NS, 128), f)
    for s in range(NS):
        sel[s, s, :] = 1.0
    return {
        "w_in": np.ascontiguousarray(inp["w_in"][0][:, pin]),
        "w_co": np.ascontiguousarray(inp["w_conv_out"][0]),
        "w_ao": np.ascontiguousarray(inp["w_attn_out"][0]),
        "w_out": np.ascontiguousarray(inp["w_out"][0]),
        "w_up": np.ascontiguousarray(inp["w_up"][0][:, pup]),
        "w_dn": np.ascontiguousarray(inp["w_down"][0]),
        "pp": ppv,
        "gfin": np.ascontiguousarray(np.broadcast_to(inp["g_final"][None, :], (128, D))),
        "ident": np.eye(128, dtype=f),
        "mask2": mask2, "css": css, "sns": sns, "sel": sel,
    }


def make_core(inp, b, half, MAIN, s0, pup):
    f = np.float32
    LS = HALO + MAIN
    start = half * MAIN
    absp = start - HALO + np.arange(LS)
    valid = absp >= 0
    x = inp["x_prompt"][b]
    xl = np.zeros((LS, D), f)
    xl[valid] = x[absp[valid]]
    inv = (np.float32(500000.0) ** (-np.arange(8, dtype=f) / np.float32(8))).astype(f)
    pos = np.maximum(absp, 0).astype(f)
    ang = (pos[:, None] * inv[None, :]).astype(f)
    cos, sin = np.cos(ang).astype(f), np.sin(ang).astype(f)
    ntile = LS // 128
    csp = np.ascontiguousarray(np.concatenate([cos, cos], 1).reshape(ntile, 128, 16).transpose(1, 0, 2))
    snp = np.ascontiguousarray(np.concatenate([-sin, sin], 1).reshape(ntile, 128, 16).transpose(1, 0, 2))
    m = {
        "xp": xl, "csp": csp, "snp": snp,
        "hv": np.full((128, 1), 1.0 if start > 0 else 0.0, f),
        "xs": np.ascontiguousarray(inp["x_sample"][s0:s0 + NS, 0]),
        "sconv": np.ascontiguousarray(inp["state_conv"][0, s0:s0 + NS]),
        "sffn": np.ascontiguousarray(inp["state_ffn_conv"][0, s0:s0 + NS][:, :, pup]),
    }
    vf = valid.astype(f)
    nsb = LS // 2048
    for g, (W, d) in enumerate(GROUPS):
        m["vld%d" % g] = np.ascontiguousarray(vf.reshape(nsb, 16 // d, 128, d).transpose(2, 0, 3, 1))
    caches = ((inp["cache_k_w128"], inp["cache_v_w128"]), (inp["cache_k_w512"], inp["cache_v_w512"]),
              (inp["cache_k_w2048"], inp["cache_v_w2048"]))
    for g, W in enumerate((128, 512, 2048)):
        m["ck%d" % g] = np.ascontiguousarray(caches[g][0][0, s0:s0 + NS].reshape(NS, W, 256))
        m["cv%d" % g] = np.ascontiguousarray(caches[g][1][0, s0:s0 + NS].reshape(NS, W, 256))
    return m


_NC_CACHE = {}


def run(inp, n_cores=8):
    inp = {k: np.asarray(v) for k, v in inp.items()}
    B, S_FULL, _ = inp["x_prompt"].shape
    nsamp = inp["x_sample"].shape[0]
    MAIN = S_FULL // 2
    assert n_cores == 2 * B
    if MAIN not in _NC_CACHE:
        _NC_CACHE[MAIN] = build(MAIN)
    nc = _NC_CACHE[MAIN]
    shared = make_shared(inp)
    pup = _perm_up()
    in_maps = []
    for c in range(n_cores):
        m = dict(shared)
        m.update(make_core(inp, c // 2, c % 2, MAIN, (NS * c) % nsamp, pup))
        in_maps.append(m)
    res = run_bass_kernel_spmd(nc, in_maps, core_ids=list(range(n_cores))).results
    global LAST_RES
    LAST_RES = res
    ipup = np.argsort(pup)
    f = np.float32
    y_p = np.stack([np.concatenate([res[2 * b]["y_p"], res[2 * b + 1]["y_p"]], 0) for b in range(B)], 0)
    nsc = nsamp // NS
    y_s = np.concatenate([res[c]["y_s"] for c in range(nsc)], 0)[:, None, :]
    hi = [2 * b + 1 for b in range(B)]
    conv_p = np.stack([res[c]["conv_p"] for c in hi], 0)[None]
    conv_s = np.concatenate([res[c]["conv_s"] for c in range(nsc)], 0)[None]
    outs = [y_p.astype(f), y_s.astype(f), conv_p.astype(f), conv_s.astype(f)]
    for g, W in enumerate((128, 512, 2048)):
        outs.append(np.stack([res[c]["k%d_p" % g] for c in hi], 0).reshape(1, B, W, 4, 64).astype(f))
        outs.append(np.stack([res[c]["v%d_p" % g] for c in hi], 0).reshape(1, B, W, 4, 64).astype(f))
        outs.append(np.concatenate([res[c]["k%d_s" % g] for c in range(nsc)], 0).reshape(1, nsamp, W, 4, 64).astype(f))
        outs.append(np.concatenate([res[c]["v%d_s" % g] for c in range(nsc)], 0).reshape(1, nsamp, W, 4, 64).astype(f))
    ffn_p = np.stack([res[c]["ffn_p"] for c in hi], 0)[:, :, ipup][None]
    ffn_s = np.concatenate([res[c]["ffn_s"] for c in range(nsc)], 0)[:, :, ipup][None]
    outs += [ffn_p.astype(f), ffn_s.astype(f)]
    return tuple(outs)


def kernel(**inputs):
    return run(inputs, 8)
```

```python
import contextlib
import types
import numpy as np
import concourse.bass as bass
import concourse.mybir as mybir
from concourse.bass_utils import run_bass_kernel_spmd

F32 = mybir.dt.float32
BF16 = mybir.dt.bfloat16
ALU = mybir.AluOpType
AF = mybir.ActivationFunctionType
AX = mybir.AxisListType

D = 1024
DFF = 2816
NS = 4
NT = 512
GROUPS = ((128, 1), (512, 4), (2048, 16))
EPS = 1e-6
ENGS = ("pe", "act", "dve", "pool", "sp")


class Sched:
    def __init__(self, nc, st, nsem=100):
        self.nc = nc
        self.ops = {e: [] for e in ENGS}
        self.cnt = {}
        self.res_w = {}
        self.res_r = {}
        self.waited = {e: {} for e in ENGS}
        self.pool = [st.enter_context(nc.semaphore("sm%d" % i)) for i in range(nsem)]
        self.sem = {}
        self.alias = {}

    def _sk(self, k):
        if k not in self.cnt:
            self.cnt[k] = 0
            assert len(self.sem) < len(self.pool), "out of semaphores"
            self.sem[k] = self.pool[len(self.sem)]
        return k

    @staticmethod
    def _freeze(fn):
        if fn.__closure__ is None:
            return fn
        cells = []
        for c in fn.__closure__:
            try:
                cells.append(types.CellType(c.cell_contents))
            except ValueError:
                cells.append(c)
        return types.FunctionType(fn.__code__, fn.__globals__, fn.__name__, fn.__defaults__, tuple(cells))

    def op(self, eng, fn, reads=(), writes=(), dma=None, sig=True):
        fn = self._freeze(fn)
        waits = {}
        reads = [x for r in reads for x in [r] + self.alias.get(r, [])]
        writes = [x for r in writes for x in [r] + self.alias.get(r, [])]
        writes = writes + [r for r in reads if r.startswith("ps") or r.startswith("ptb")]

        def need(w):
            if w[1] > waits.get(w[0], 0):
                waits[w[0]] = w[1]

        for r in reads:
            if r in self.res_w:
                need(self.res_w[r])
        for r in writes:
            if r in self.res_w:
                need(self.res_w[r])
            for sk, v in self.res_r.get(r, {}).items():
                need((sk, v))
        if dma is None:
            sk = self._sk(eng)
            inc = 1 if sig else 0
        else:
            sk = self._sk("d:" + str(dma))
            inc = 16
        self.cnt[sk] += inc
        val = self.cnt[sk] if inc else self.cnt[sk] + 1
        wl = []
        for k, v in waits.items():
            if k == "pe" and eng == "pe" and dma is None:
                continue
            if self.waited[eng].get(k, 0) >= v:
                continue
            self.waited[eng][k] = v
            wl.append((k, v))
        for r in writes:
            self.res_w[r] = (sk, val)
            self.res_r[r] = {}
        for r in reads:
            d = self.res_r.setdefault(r, {})
            if d.get(sk, 0) < val:
                d[sk] = val
        self.ops[eng].append((wl, fn, sk, inc))

    def barrier(self, engs=ENGS):
        for e in engs:
            wl = []
            for k, v in self.cnt.items():
                if v > 0 and self.waited[e].get(k, 0) < v:
                    self.waited[e][k] = v
                    wl.append((k, v))
            if wl:
                self.ops[e].append((wl, None, None, 0))

    def emit(self):
        nc = self.nc
        with nc.Block() as block:
            def run(e, eng):
                for wl, fn, sk, inc in self.ops[eng]:
                    for k, v in wl:
                        e.wait_ge(self.sem[k], v)
                    if fn is not None:
                        ins = fn(e)
                        if inc:
                            ins.then_inc(self.sem[sk], inc)

            @block.tensor
            def _(e):
                run(e, "pe")

            @block.scalar
            def _(e):
                run(e, "act")

            @block.vector
            def _(e):
                run(e, "dve")

            @block.gpsimd
            def _(e):
                run(e, "pool")

            @block.sync
            def _(e):
                run(e, "sp")
        self.ops = {e: [] for e in ENGS}


PP = {}
_o = 0
for _n, _w in (("gmix", 8), ("bdw", 8), ("lng", 8), ("lnb", 8), ("gffn", 8), ("wdw", 8 * 31), ("wfdw", 44 * 3),
               ("bfdw", 44)):
    PP[_n] = (_o, _w)
    _o += _w
NPP = _o


HALO = 4096


def build(MAIN):
    S_TOK = HALO + MAIN
    NSB = S_TOK // 2048
    NTILE = S_TOK // 128
    NG = MAIN // NT
    nc = bass.Bass("TRN2", target_bir_lowering=False)

    def din(name, shape, dt=F32):
        return nc.dram_tensor(name, list(shape), dt, kind="ExternalInput").ap()

    def dout(name, shape, dt=F32):
        return nc.dram_tensor(name, list(shape), dt, kind="ExternalOutput").ap()

    def dscr(name, shape, dt):
        return nc.dram_tensor(name, list(shape), dt, kind="Internal").ap()

    xp = din("xp", [S_TOK, D])
    xs = din("xs", [NS, D])
    sconv = din("sconv", [NS, 30, D])
    cks = [din("ck%d" % g, [NS, GROUPS[g][0], 256]) for g in range(3)]
    cvs = [din("cv%d" % g, [NS, GROUPS[g][0], 256]) for g in range(3)]
    sffn = din("sffn", [NS, 2, 2 * DFF])
    w_in = din("w_in", [D, 6400])
    w_co = din("w_co", [D, D])
    w_ao = din("w_ao", [256, D])
    w_out = din("w_out", [D, D])
    w_up = din("w_up", [D, 2 * DFF])
    w_dn = din("w_dn", [DFF, D])
    pp_d = din("pp", [128, NPP])
    gfin_d = din("gfin", [128, D])
    ident_d = din("ident", [128, 128])
    mask_d = din("mask2", [128, 2, 128])
    csp_d = din("csp", [128, NTILE, 16])
    snp_d = din("snp", [128, NTILE, 16])
    css_d = din("css", [NS, 16])
    sns_d = din("sns", [NS, 16])
    sel_d = din("sel", [NS, NS, 128])
    vld_d = [din("vld%d" % g, [128, NSB, GROUPS[g][1], 16 // GROUPS[g][1]]) for g in range(3)]
    hv_d = din("hv", [128, 1])

    wb_in = dscr("wb_in", [D, 6400], BF16)
    wb_co = dscr("wb_co", [D, D], BF16)
    wb_ao = dscr("wb_ao", [256, D], BF16)
    wb_out = dscr("wb_out", [D, D], BF16)
    wb_up = dscr("wb_up", [D, 2 * DFF], BF16)
    wb_dn = dscr("wb_dn", [DFF, D], BF16)
    import os as _os0
    DBG = bool(_os0.environ.get("KDBG"))
    oT_d = (dout if DBG else dscr)("oT_d", [64, 4, S_TOK], BF16)
    if DBG:
        dbg_sT = dout("dbg_sT", [128, 8, NT], BF16)
        dbg_mix = dout("dbg_mix", [128, 8, NT], BF16)
        dbg_xmid = dout("dbg_xmid", [NT, D], F32)
        dbg_acc = dout("dbg_acc", [128, 8, NT], F32)

    y_p = dout("y_p", [MAIN, D])
    y_s = dout("y_s", [NS, D])
    conv_p = dout("conv_p", [30, D])
    conv_s = dout("conv_s", [NS, 30, D])
    kp = [dout("k%d_p" % g, [min(GROUPS[g][0], S_TOK), 256]) for g in range(3)]
    vp = [dout("v%d_p" % g, [min(GROUPS[g][0], S_TOK), 256]) for g in range(3)]
    ks_o = [dout("k%d_s" % g, [NS, GROUPS[g][0], 256]) for g in range(3)]
    vs_o = [dout("v%d_s" % g, [NS, GROUPS[g][0], 256]) for g in range(3)]
    ffn_p = dout("ffn_p", [2, 2 * DFF])
    ffn_s = dout("ffn_s", [NS, 2, 2 * DFF])

    with contextlib.ExitStack() as gst:
        S = Sched(nc, gst)

        def sbuf(st, name, shape, dt):
            return st.enter_context(nc.sbuf_tensor("sb_" + name, list(shape), dt))

        NPS = 6
        PSB = [gst.enter_context(nc.psum_tensor("psb%d" % i, [128, 512], F32)) for i in range(NPS)]
        PTB = gst.enter_context(nc.psum_tensor("ptb", [128, 1024], BF16))
        PTB2 = gst.enter_context(nc.psum_tensor("ptb2", [128, 1024], BF16))
        ps_i = [0]

        def psum():
            i = ps_i[0] % NPS
            ps_i[0] += 1
            return PSB[i], "ps%d" % i

        pp = sbuf(gst, "pp", [128, NPP], F32)
        identf = sbuf(gst, "identf", [128, 128], F32)
        identb = sbuf(gst, "identb", [128, 128], BF16)
        onesb = sbuf(gst, "onesb", [128, 128], BF16)
        onesf = sbuf(gst, "onesf", [128, 128], F32)
        epst = sbuf(gst, "epst", [128, 1], F32)
        stat = sbuf(gst, "stat", [128, 3, 4 * NTILE + 16], F32)
        xsb = [sbuf(gst, "xsb%d" % i, [128, D], BF16) for i in range(2)]
        oTs = sbuf(gst, "oTs", [128, 2, NS], BF16)
        sts = sbuf(gst, "sts", [NS, 2304], F32)
        stat_i = [0]

        ps45 = [0]

        def psum45():
            i = 4 + ps45[0] % 2
            ps45[0] += 1
            return PSB[i], "ps%d" % i

        def P(name, c=None):
            o, w = PP[name]
            if c is None:
                return pp[:, o:o + w]
            return pp[:, o + c:o + c + 1]

        def dma(eng, out, in_, reads=(), writes=(), key=None):
            S.op(eng, lambda e: e.dma_start(out=out, in_=in_), reads=reads, writes=writes, dma=key)

        dma("sp", pp[:], pp_d, writes=["pp"], key="pp")
        dma("sp", identf[:], ident_d, writes=["identf"], key="identf")
        S.op("dve", lambda e: e.tensor_copy(out=identb[:], in_=identf[:]), reads=["identf"], writes=["identb"])
        S.op("dve", lambda e: e.memset(onesb[:], 1.0), writes=["onesb"])
        S.op("dve", lambda e: e.memset(onesf[:], 1.0), writes=["onesf"])
        S.op("dve", lambda e: e.memset(epst[:], EPS), writes=["epst"])
        S.op("dve", lambda e: e.memset(stat[:], 0.0), writes=["stat"])
        def conv_w(dst, src, rows, c0, c1, key, after=()):
            for r0 in range(0, rows, 256):
                r1 = min(rows, r0 + 256)
                dma("pool", dst[r0:r1, c0:c1], src[r0:r1, c0:c1], reads=list(after), writes=[key], key=key)
        for cg in (2, 4, 0, 1, 3):
            conv_w(wb_in, w_in, D, 2048 + 512 * cg, 2048 + min(512 * (cg + 1), 2304), "wb_qkv%d" % cg)
        QKV_ALL = ["wb_qkv%d" % cg for cg in range(5)]
        LATE_CASTS = [
            [],
            [lambda: conv_w(wb_in, w_in, D, 0, 2048, "wb_in1"), lambda: conv_w(wb_in, w_in, D, 4352, 6400, "wb_in3")],
            [lambda: conv_w(wb_co, w_co, D, 0, D, "wb_co"), lambda: conv_w(wb_ao, w_ao, 256, 0, D, "wb_ao"),
             lambda: conv_w(wb_out, w_out, D, 0, D, "wb_out"), lambda: conv_w(wb_up, w_up, D, 0, 2 * DFF, "wb_up")],
            [lambda: conv_w(wb_dn, w_dn, DFF, 0, D, "wb_dn")],
        ]
        def flat16(ap):
            return ap.rearrange("w c -> (w c)").rearrange("(a b) -> a b", a=16)
        SHIFTS = []
        for g in range(3):
            W = GROUPS[g][0]
            for (src, dst) in ((cks[g], ks_o[g]), (cvs[g], vs_o[g])):
                for s in range(NS):
                    SHIFTS.append((flat16(dst[s, 0:W - 1, :]), flat16(src[s, 1:W, :])))
        for s in range(NS):
            SHIFTS.append((flat16(conv_s[s, 0:29, :]), flat16(sconv[s, 1:30, :])))
            SHIFTS.append((flat16(ffn_s[s, 0:1, :]), flat16(sffn[s, 1:2, :])))

        import os as _os
        KSTOP = int(_os.environ.get("KSTOP", "9"))
        if KSTOP == 0:
            S.barrier()
            S.emit()
            return nc
        def norm_a(src_ap, rd, TT, xb, xr):
            col = stat_i[0]
            stat_i[0] += 1
            S.op("act", lambda e: e.activation(out=xb[0:TT, :], in_=src_ap, func=AF.Square,
                                               accum_out=stat[0:TT, 0, col:col + 1]),
                 reads=[rd], writes=[xr, "stat%d" % col])
            S.op("act", lambda e: e.activation(out=stat[0:TT, 1, col:col + 1], in_=stat[0:TT, 0, col:col + 1],
                                               func=AF.Sqrt, scale=1.0 / D, bias=epst[0:TT, 0:1]),
                 reads=["stat%d" % col, "epst"], writes=["stat%d" % col])
            S.op("dve", lambda e: e.reciprocal(out=stat[0:TT, 2, col:col + 1], in_=stat[0:TT, 1, col:col + 1]),
                 reads=["stat%d" % col], writes=["stat%d" % col])
            S.op("act", lambda e: e.activation(out=xb[0:TT, :], in_=src_ap, func=AF.Copy,
                                               scale=stat[0:TT, 2, col:col + 1]),
                 reads=[rd, "stat%d" % col], writes=[xr])

        def norm_b(TT, xb, xr, dst3, dst_res, gain_name):
            for c in range(8):
                S.op("pe", lambda e, c=c: e.transpose(PTB[:, c * 128:c * 128 + TT], xb[0:TT, c * 128:(c + 1) * 128],
                                                      identb[0:TT, 0:TT]),
                     reads=[xr, "identb"], writes=["ptb"], sig=(c == 7))
            o, w = PP[gain_name]
            S.op("dve", lambda e: e.tensor_tensor(
                out=dst3, in0=PTB[:, :].rearrange("p (c t) -> p c t", c=8)[:, :, 0:TT],
                in1=pp[:, o:o + 8].unsqueeze(2).to_broadcast([128, 8, TT]), op=ALU.mult),
                reads=["ptb", "pp"], writes=[dst_res])

        def norm_T(src_ap, rd, TT, dst3, dst_res, gain_name, xi):
            xb = xsb[xi % 2]
            xr = "xsb%d" % (xi % 2)
            norm_a(src_ap, rd, TT, xb, xr)
            norm_b(TT, xb, xr, dst3, dst_res, gain_name)

        with contextlib.ExitStack() as st1:
            hT1 = sbuf(st1, "hT1", [128, 8, 2048], BF16)
            hTs = sbuf(st1, "hTs", [128, 8, NS], BF16)
            QT = [sbuf(st1, "QT%d" % g, [128, 2, GROUPS[g][1], 2048 // GROUPS[g][1]], BF16) for g in range(3)]
            KT = [sbuf(st1, "KT%d" % g, [128, 2, GROUPS[g][1], 128 + 2048 // GROUPS[g][1]], BF16) for g in range(3)]
            NB = [1 + 16 // GROUPS[g][1] for g in range(3)]
            VV = [sbuf(st1, "VV%d" % g, [128, GROUPS[g][1], NB[g], 260], BF16) for g in range(3)]
            Wg = [sbuf(st1, "Wg%d" % i, [128, 8, 512], BF16) for i in range(1)]
            xt = [sbuf(st1, "xt%d" % i, [128, D], F32) for i in range(2)]
            stg = [sbuf(st1, "stg%d" % i, [128, 512], F32) for i in range(2)]
            qkb = [sbuf(st1, "qkb%d" % i, [128, 512], BF16) for i in range(2)]
            rtmp = [sbuf(st1, "rtmp%d" % i, [128, 24, 16], F32) for i in range(2)]
            vst = [sbuf(st1, "vst%d" % i, [128, 256], F32) for i in range(2)]
            PT2 = sbuf(st1, "PT2", [128, 16, 2, 128], BF16)
            PTs = [sbuf(st1, "PTs%d" % i, [128, 2, 2, 128], BF16) for i in range(2)]
            csp = sbuf(st1, "csp", [128, 16, 16], F32)
            snp = sbuf(st1, "snp", [128, 16, 16], F32)
            css = sbuf(st1, "css", [NS, 16], F32)
            sns = sbuf(st1, "sns", [NS, 16], F32)
            maskf = sbuf(st1, "maskf", [128, 2, 128], F32)
            vld = [sbuf(st1, "vld%d" % g, [128, GROUPS[g][1], 16 // GROUPS[g][1]], F32) for g in range(3)]
            maskb = sbuf(st1, "maskb", [128, 2, 128], BF16)
            rec = [sbuf(st1, "rec%d" % i, [64, 512], F32) for i in range(1)]
            oTsb = sbuf(st1, "oTsb", [64, 2048], BF16)
            selb = sbuf(st1, "selb", [NS, NS, 128], F32)
            Kc = [sbuf(st1, "Kc%d" % i, [128, 256], F32) for i in range(3)]
            Vc = [sbuf(st1, "Vc%d" % i, [128, 256], F32) for i in range(3)]
            prod = sbuf(st1, "prod", [128, 256], F32)
            sT4 = sbuf(st1, "sT4", [128, 24], F32)
            sm = sbuf(st1, "sm", [NS, 768], F32)
            pcur = sbuf(st1, "pcur", [NS, 12], F32)
            pcm = sbuf(st1, "pcm", [NS, NS, 12], F32)
            id4 = sbuf(st1, "id4", [NS, NS], F32)
            oTf = sbuf(st1, "oTf", [128, 2, NS], F32)

            for g in range(3):
                S.op("pool", lambda e, g=g: e.memset(VV[g][:, :, :, :], 1.0), writes=["VV%d" % g])
            dma("sp", css[:], css_d, writes=["css"], key="css")
            dma("sp", sns[:], sns_d, writes=["sns"], key="sns")
            dma("sp", maskf[:], mask_d, writes=["maskf"], key="maskf")
            dma("sp", selb[:], sel_d, writes=["selb"], key="selb")
            S.op("dve", lambda e: e.tensor_copy(out=maskb[:], in_=maskf[:]), reads=["maskf"], writes=["maskb"])
            S.op("dve", lambda e: e.tensor_copy(out=id4[:], in_=identf[0:NS, 0:NS]), reads=["identf"], writes=["id4"])

            def rope(stv, rd, TT, nh, cs_ap, sn_ap, tab_res, ri):
                rt = rtmp[ri % 2]
                rr = "rtmp%d" % (ri % 2)
                t1 = rt[0:TT, 0:nh, :]
                csb = cs_ap.unsqueeze(1).to_broadcast([TT, nh, 16])
                S.op("dve", lambda e: e.tensor_tensor(out=t1, in0=stv[:, :, 0:16], in1=csb, op=ALU.mult),
                     reads=[rd] + tab_res, writes=[rr])
                S.op("dve", lambda e: e.tensor_tensor(out=stv[:, :, 0:8], in0=stv[:, :, 0:8],
                                                      in1=sn_ap[:, 8:16].unsqueeze(1).to_broadcast([TT, nh, 8]),
                                                      op=ALU.mult), reads=[rd] + tab_res, writes=[rd])
                S.op("dve", lambda e: e.tensor_tensor(out=stv[:, :, 8:16], in0=stv[:, :, 8:16],
                                                      in1=sn_ap[:, 0:8].unsqueeze(1).to_broadcast([TT, nh, 8]),
                                                      op=ALU.mult), reads=[rd] + tab_res, writes=[rd])
                S.op("dve", lambda e: e.tensor_tensor(out=t1[:, :, 0:8], in0=t1[:, :, 0:8], in1=stv[:, :, 8:16],
                                                      op=ALU.add), reads=[rd, rr], writes=[rr])
                S.op("dve", lambda e: e.tensor_tensor(out=t1[:, :, 8:16], in0=t1[:, :, 8:16], in1=stv[:, :, 0:8],
                                                      op=ALU.add), reads=[rd, rr], writes=[rr])
                S.op("dve", lambda e: e.tensor_copy(out=stv[:, :, 0:16], in_=t1), reads=[rr], writes=[rd])

            xi = 0
            ri = 0
            ev = 0
            bset = [0]
            PTBf = PTB[:, :].bitcast(F32)
            PTB2f = PTB2[:, :].bitcast(F32)
            for _k in range(int(_os.environ.get("KDUMMY", "0"))):
                if _os.environ.get("KDUMMYT") == "memset":
                    S.op("dve", lambda e: e.memset(prod[:, :], 0.0), writes=["prod"])
                else:
                    S.op("dve", lambda e: e.tensor_tensor(out=prod[:, :], in0=prod[:, :], in1=prod[:, :], op=ALU.mult),
                         writes=["prod"])
            for sb in range(NSB):
                T0 = sb * 2048
                last = (sb == NSB - 1)
                if sb < len(LATE_CASTS):
                    for f_ in LATE_CASTS[sb]:
                        f_()
                dma("sp", csp[:], csp_d[:, 16 * sb:16 * (sb + 1), :], writes=["csp"], key="csp")
                for g in range(3):
                    dma("sp", vld[g][:, :, :], vld_d[g][:, sb, :, :], writes=["vld%d" % g], key="vld%d" % g)
                dma("sp", snp[:], snp_d[:, 16 * sb:16 * (sb + 1), :], writes=["snp"], key="snp")
                for i in range(16):
                    b = xi % 2
                    dma("sp", xt[b][:], xp[T0 + 128 * i:T0 + 128 * (i + 1), :], writes=["xt%d" % b], key="xt%d" % b)
                    norm_T(xt[b][:], "xt%d" % b, 128, hT1[:, :, 128 * i:128 * (i + 1)], "hT1_%d" % i, "gmix", xi)
                    xi += 1
                if sb == 1:
                    b = xi % 2
                    dma("sp", xt[b][0:NS, :], xs, writes=["xt%d" % b], key="xt%d" % b)
                    norm_T(xt[b][0:NS, :], "xt%d" % b, NS, hTs[:, :, 0:NS], "hTs", "gmix", xi)
                    xi += 1
                if KSTOP == 10:
                    S.barrier()
                    S.emit()
                    return nc
                hT_all = ["hT1_%d" % i for i in range(16)]
                if sb > 0:
                    for g in range(3):
                        M = 2048 // GROUPS[g][1]
                        S.op("pool", lambda e, g=g, M=M: e.tensor_copy(out=KT[g][:, :, :, 0:128],
                                                                        in_=KT[g][:, :, :, M:M + 128]),
                             reads=["KT%d" % g], writes=["KT%d" % g])
                        S.op("pool", lambda e, g=g: e.tensor_copy(out=VV[g][:, :, 0, :], in_=VV[g][:, :, NB[g] - 1, :]),
                             reads=["VV%d" % g], writes=["VV%d" % g])
                for cg in range(5):
                    if sb == 0 and cg in (0, 1, 3):
                        continue
                    wcols = 512 if cg < 4 else 256
                    wb = Wg[0]
                    wr = "Wg0"
                    dma("sp", wb[:, :, 0:wcols],
                        wb_in[:, 2048 + 512 * cg:2048 + 512 * cg + wcols].rearrange("(kc p) n -> p kc n", p=128),
                        reads=["wb_qkv%d" % cg], writes=[wr], key=wr)
                    if sb == 1:
                        pst, pr = psum()
                        for h0 in range(0, wcols, 256):
                            for kc in range(8):
                                S.op("pe", lambda e, kc=kc, pst=pst, h0=h0: e.matmul(
                                    pst[0:NS, h0:h0 + 256], lhsT=hTs[:, kc, 0:NS], rhs=wb[:, kc, h0:h0 + 256],
                                    start=(kc == 0), stop=(kc == 7)), reads=["hTs", wr], writes=[pr], sig=(kc == 7))
                        for (c0, c1) in ((0, 256), (256, 512)):
                            if c0 >= wcols:
                                continue
                            gcol = 512 * cg + c0
                            sc = 0.125 if gcol < 768 else 1.0
                            S.op("act", lambda e, c0=c0, c1=c1, pst=pst, sc=sc, gcol=gcol: e.activation(
                                out=sts[0:NS, gcol:gcol + 256], in_=pst[0:NS, c0:c1], func=AF.Copy, scale=sc),
                                reads=[pr], writes=["sts"])
                    if KSTOP == 20:
                        S.barrier()
                        S.emit()
                        return nc
                    if cg < 3:
                        pendT = [None]
                        for i in range(16):
                            pst, pr = psum()
                            for kc in range(8):
                                S.op("pe", lambda e, kc=kc, pst=pst, i=i: e.matmul(
                                    pst[:, 0:512], lhsT=hT1[:, kc, 128 * i:128 * (i + 1)], rhs=wb[:, kc, 0:512],
                                    start=(kc == 0), stop=(kc == 7)), reads=["hT1_%d" % i, wr], writes=[pr], sig=(kc == 7))
                            sg = stg[ev % 2]
                            sr = "stg%d" % (ev % 2)
                            qb_ = qkb[ev % 2]
                            qr = "qkb%d" % (ev % 2)
                            ev += 1
                            for (c0, c1) in ((0, 256), (256, 512)):
                                sc = 0.125 if (512 * cg + c0) < 768 else 1.0
                                S.op("act", lambda e, c0=c0, c1=c1, pst=pst, sc=sc, sg=sg: e.activation(
                                    out=sg[:, c0:c1], in_=pst[:, c0:c1], func=AF.Copy, scale=sc),
                                    reads=[pr], writes=[sr])
                            ti = sb * 16 + i
                            if not _os.environ.get("KNOROPE"):
                              rope(sg[:, :].rearrange("p (h d) -> p h d", d=64), sr, 128, 8, csp[:, i, :], snp[:, i, :],
                                 ["csp", "snp"], ri)
                            ri += 1
                            S.op("dve", lambda e, sg=sg, qb_=qb_: e.tensor_copy(out=qb_[:, :], in_=sg[:, :]),
                                 reads=[sr], writes=[qr])
                            if last and KSTOP != 24:
                                for g in range(3):
                                    W = min(GROUPS[g][0], S_TOK)
                                    kcol = 768 + 256 * g
                                    if not (512 * cg <= kcol < 512 * cg + 512):
                                        continue
                                    tpos = 2048 - 128 * (16 - i)
                                    if S_TOK - (T0 + 128 * i) > W:
                                        continue
                                    row0 = W - (S_TOK - (T0 + 128 * i))
                                    lc = kcol - 512 * cg
                                    dma("sp", kp[g][row0:row0 + 128, :], sg[:, lc:lc + 256], reads=[sr], key="o" + sr)
                            def partB(i=i, qb_=qb_, qr=qr, cg=cg):
                                for j in range(4):
                                    S.op("pe", lambda e, j=j, qb_=qb_: e.transpose((PTB if j < 2 else PTB2)[:, j * 128:(j + 1) * 128],
                                                                                    qb_[:, j * 128:(j + 1) * 128], identb[:, :]),
                                         reads=[qr, "identb"], writes=["ptb" if j < 2 else "ptb2"], sig=(j % 2 == 1))
                                for j in range(4):
                                    if _os.environ.get("KNOEVAC"):
                                        continue
                                    cc = cg * 4 + j
                                    isq = cc < 6
                                    g = (cc % 6) // 2
                                    half = cc % 2
                                    d = GROUPS[g][1]
                                    mlo = 128 * i // d
                                    cnt = 128 // d
                                    if isq:
                                        dst = QT[g][:, half, :, mlo:mlo + cnt]
                                        dres = "QT%d" % g
                                    else:
                                        dst = KT[g][:, half, :, 128 + mlo:128 + mlo + cnt]
                                        dres = "KT%d" % g
                                    src = (PTB if j < 2 else PTB2)[:, j * 128:(j + 1) * 128].rearrange("p (m r) -> p r m", r=d)
                                    if j < 2:
                                        S.op("act", lambda e, dst=dst, src=src: e.activation(out=dst, in_=src, func=AF.Copy),
                                             reads=["ptb"], writes=[dres])
                                    else:
                                        S.op("dve", lambda e, dst=dst, src=src: e.tensor_copy(out=dst, in_=src),
                                             reads=["ptb2"], writes=[dres])
                            if pendT[0] is not None:
                                pendT[0]()
                            pendT[0] = partB
                        pendT[0]()
                        pendT[0] = None
                    else:
                        jobs = []
                        if cg == 3:
                            for blk in range(16):
                                jobs.append((0, 0, blk, 0, hT1[:, :, 128 * blk:128 * (blk + 1)], ["hT1_%d" % blk]))
                            for r in range(4):
                                for blk in range(4):
                                    jobs.append((1, r, blk, 256, hT1[:, :, 512 * blk + r:512 * (blk + 1):4],
                                                 ["hT1_%d" % t for t in range(4 * blk, 4 * blk + 4)]))
                        else:
                            for r in range(16):
                                jobs.append((2, r, 0, 0, hT1[:, :, r:2048:16], hT_all))
                        for (g, r, blk, c0, lh, lres) in jobs:
                            pst, pr = psum()
                            for kc in range(8):
                                S.op("pe", lambda e, kc=kc, pst=pst, lh=lh, c0=c0: e.matmul(
                                    pst[:, 0:256], lhsT=lh[:, kc, :], rhs=wb[:, kc, c0:c0 + 256],
                                    start=(kc == 0), stop=(kc == 7)), reads=lres + [wr], writes=[pr], sig=(kc == 7))
                            S.op("act", lambda e, pst=pst, g=g, r=r, blk=blk: e.activation(
                                out=VV[g][:, r, 1 + blk, :].rearrange("p (s e) -> p s e", e=65)[:, :, 0:64],
                                in_=pst[:, 0:256].rearrange("p (s e) -> p s e", e=64), func=AF.Copy,
                                scale=vld[g][:, r, blk:blk + 1]),
                                reads=[pr, "vld%d" % g], writes=["VV%d" % g])
                            S.op("pool", lambda e, g=g, r=r, blk=blk: e.tensor_copy(
                                out=VV[g][:, r, 1 + blk, :].rearrange("p (s e) -> p s e", e=65)[:, :, 64:65],
                                in_=vld[g][:, r, blk:blk + 1].unsqueeze(1).to_broadcast([128, 4, 1])),
                                reads=["vld%d" % g], writes=["VV%d" % g])
                            if last:
                                W = min(GROUPS[g][0], S_TOK)
                                d = GROUPS[g][1]
                                nblk = 16 // d
                                first_tok = T0 + r + d * 128 * blk
                                if S_TOK - (T0 + d * 128 * blk) <= W:
                                    row0 = W - (S_TOK - first_tok)
                                    vb = vst[ev % 2]
                                    vr = "vst%d" % (ev % 2)
                                    ev += 1
                                    S.op("dve", lambda e, pst=pst, vb=vb: e.tensor_copy(out=vb[:, :], in_=pst[:, 0:256]),
                                         reads=[pr], writes=[vr])
                                    dma("sp", vp[g][row0:W:d, :], vb[:, :], reads=[vr], key="o" + vr)
                if KSTOP == 23 and False:
                    S.barrier()
                    S.emit()
                    return nc
                if KSTOP == 11:
                    S.barrier()
                    S.emit()
                    return nc
                if sb == 1:
                    rope(sts[0:NS, 0:1536].rearrange("p (h d) -> p h d", d=64), "sts", NS, 24, css[:, :], sns[:, :],
                         ["css", "sns"], ri)
                    ri += 1
                    for g in range(3):
                        W = GROUPS[g][0]
                        dma("sp", ks_o[g][:, W - 1, :], sts[0:NS, 768 + 256 * g:768 + 256 * (g + 1)], reads=["sts"],
                            key="osts")
                        dma("sp", vs_o[g][:, W - 1, :], sts[0:NS, 1536 + 256 * g:1536 + 256 * (g + 1)], reads=["sts"],
                            key="osts")
                    S.op("dve", lambda e: e.tensor_tensor(out=sm[0:NS, 0:768], in0=sts[0:NS, 0:768],
                                                          in1=sts[0:NS, 768:1536], op=ALU.mult),
                         reads=["sts"], writes=["sm"])
                    S.op("dve", lambda e: e.tensor_reduce(out=pcur[:, :],
                                                          in_=sm[0:NS, 0:768].rearrange("p (h d) -> p h d", d=64),
                                                          axis=AX.X, op=ALU.add), reads=["sm"], writes=["pcur"])
                    S.op("act", lambda e: e.activation(out=pcur[:, :], in_=pcur[:, :], func=AF.Exp),
                         reads=["pcur"], writes=["pcur"])
                    S.op("dve", lambda e: e.tensor_tensor(out=pcm[:, :, :],
                                                          in0=pcur[:, :].unsqueeze(1).to_broadcast([NS, NS, 12]),
                                                          in1=id4[:, :].unsqueeze(2).to_broadcast([NS, NS, 12]),
                                                          op=ALU.mult), reads=["pcur", "id4"], writes=["pcm"])
                    for s in range(NS):
                        pnum, pnr = psum()
                        for g in range(3):
                            W, d = GROUPS[g]
                            kb_, vb_ = Kc[g], Vc[g]
                            kr, vr = "Kc%d" % g, "Vc%d" % g
                            dma("sp", kb_[:, :], cks[g][s, 0:W:d, :], writes=[kr], key=kr)
                            dma("sp", vb_[:, :], cvs[g][s, 0:W:d, :], writes=[vr], key=vr)
                            pq, pqr = psum()
                            S.op("pe", lambda e, pq=pq, s=s, g=g: e.matmul(pq[:, 0:256], lhsT=selb[0:NS, s, :],
                                                                           rhs=sts[0:NS, 256 * g:256 * (g + 1)],
                                                                           start=True, stop=True),
                                 reads=["selb", "sts"], writes=[pqr])
                            S.op("dve", lambda e, pq=pq, kb_=kb_: e.tensor_tensor(out=prod[:, :], in0=kb_[:, :],
                                                                                    in1=pq[:, 0:256], op=ALU.mult),
                                 reads=[kr, pqr], writes=["prod"])
                            S.op("dve", lambda e, g=g: e.tensor_reduce(out=sT4[:, 4 * g:4 * g + 4],
                                                                  in_=prod[:, :].rearrange("p (h d) -> p h d", d=64),
                                                                  axis=AX.X, op=ALU.add), reads=["prod"], writes=["sT4"])
                        S.op("act", lambda e: e.activation(out=sT4[:, 12:24], in_=sT4[:, 0:12], func=AF.Exp),
                             reads=["sT4"], writes=["sT4"])
                        for c in range(3):
                            for g in range(3):
                                pT_ = sT4[:, 12 + 4 * g:16 + 4 * g]
                                pc_ = pcm[0:NS, s, 4 * g:4 * g + 4]
                                if c < 2:
                                    l1 = Vc[g][:, 128 * c:128 * (c + 1)]
                                    l2 = sts[0:NS, 1536 + 256 * g + 128 * c:1536 + 256 * g + 128 * (c + 1)]
                                    r1 = ["Vc%d" % g, "sT4"]
                                else:
                                    l1 = onesf[:, :]
                                    l2 = onesf[0:NS, :]
                                    r1 = ["onesf", "sT4"]
                                S.op("pe", lambda e, c=c, g=g, pnum=pnum, l1=l1, pT_=pT_: e.matmul(
                                    pnum[:, 4 * c:4 * c + 4], lhsT=l1, rhs=pT_, start=(g == 0), stop=False),
                                    reads=r1, writes=[pnr])
                                S.op("pe", lambda e, c=c, g=g, pnum=pnum, l2=l2, pc_=pc_: e.matmul(
                                    pnum[:, 4 * c:4 * c + 4], lhsT=l2, rhs=pc_, start=False, stop=(g == 2)),
                                    reads=["sts", "pcm", "onesf"], writes=[pnr], sig=(g == 2))
                        S.op("dve", lambda e, pnum=pnum: e.reciprocal(out=sT4[:, 0:4], in_=pnum[:, 8:12]),
                             reads=[pnr], writes=["sT4"])
                        for c in range(2):
                            for hf in range(2):
                                slot = 2 * c + hf
                                S.op("dve", lambda e, c=c, hf=hf, slot=slot, pnum=pnum, s=s: e.tensor_tensor(
                                    out=oTf[64 * hf:64 * (hf + 1), c, s:s + 1],
                                    in0=pnum[64 * hf:64 * (hf + 1), 4 * c + slot:4 * c + slot + 1],
                                    in1=sT4[64 * hf:64 * (hf + 1), slot:slot + 1], op=ALU.mult),
                                    reads=[pnr, "sT4"], writes=["oTf"])
                    S.op("dve", lambda e: e.tensor_copy(out=oTs[:, :, :], in_=oTf[:, :, :]), reads=["oTf"], writes=["oTs"])
                    if DBG:
                        d1 = dout("dbg_oTf", [128, 2, NS], F32)
                        dma("sp", d1, oTf[:, :, :], reads=["oTf"], key="dbg")
                        d2 = dout("dbg_sts", [NS, 2304], F32)
                        dma("sp", d2, sts[:, :], reads=["sts"], key="dbg")
                        d3 = dout("dbg_pcur", [NS, 12], F32)
                        dma("sp", d3, pcur[:, :], reads=["pcur"], key="dbg")

                if KSTOP == 12:
                    S.barrier()
                    S.emit()
                    return nc
                if DBG and sb == 0:
                    for g in range(3):
                        dq = dout("dbg_QT%d" % g, [128, 2, GROUPS[g][1], 2048 // GROUPS[g][1]], BF16)
                        dk = dout("dbg_KT%d" % g, [128, 2, GROUPS[g][1], 128 + 2048 // GROUPS[g][1]], BF16)
                        dv = dout("dbg_VV%d" % g, [128, GROUPS[g][1], NB[g], 260], BF16)
                        dma("sp", dq, QT[g][:, :, :, :], reads=["QT%d" % g], key="dbg")
                        dma("sp", dk, KT[g][:, :, :, :], reads=["KT%d" % g], key="dbg")
                        dma("sp", dv, VV[g][:, :, :, :], reads=["VV%d" % g], key="dbg")
                pti = 0
                if sb == 0:
                    continue
                ch_list = (3,) if sb == 1 else (0, 1, 2, 3)
                for slot in range(4):
                    c2 = slot // 2
                    pb = 64 * (slot % 2)
                    for rp in range(8):
                        pss, psr = psum45()
                        for q in range(2):
                            r = 2 * rp + q
                            if sb > 0:
                                S.op("pe", lambda e, pss=pss, q=q, r=r: e.matmul(
                                    pss[:, 256 * q:256 * q + 128], lhsT=KT[2][pb:pb + 64, c2, r, 0:128],
                                    rhs=QT[2][pb:pb + 64, c2, r, 0:128], start=True, stop=True),
                                    reads=["KT2", "QT2"], writes=[psr])
                            S.op("pe", lambda e, pss=pss, q=q, r=r: e.matmul(
                                pss[:, 256 * q + 128:256 * q + 256], lhsT=KT[2][pb:pb + 64, c2, r, 128:256],
                                rhs=QT[2][pb:pb + 64, c2, r, 0:128], start=True, stop=True),
                                reads=["KT2", "QT2"], writes=[psr])
                        k0 = 0 if sb > 0 else 1
                        S.op("act", lambda e, pss=pss, rp=rp, k0=k0: e.activation(
                            out=PT2[:, 2 * rp:2 * rp + 2, k0:2, :],
                            in_=pss[:, :].rearrange("p (q k t) -> p q k t", q=2, k=2)[:, :, k0:2, :], func=AF.Exp),
                            reads=[psr], writes=["PT2"])
                        S.op("dve", lambda e, rp=rp, k0=k0: e.tensor_tensor(
                            out=PT2[:, 2 * rp:2 * rp + 2, k0:2, :], in0=PT2[:, 2 * rp:2 * rp + 2, k0:2, :],
                            in1=maskb[:, k0:2, :].unsqueeze(1).to_broadcast([128, 2, 2 - k0, 128]), op=ALU.mult),
                            reads=["PT2", "maskb"], writes=["PT2"])
                    if DBG and sb == 0 and slot == 0:
                        dp2 = dout("dbg_PT2", [128, 16, 2, 128], BF16)
                        dma("sp", dp2, PT2[:, :, :, :], reads=["PT2"], key="dbg")
                    pendB = [None]
                    for ch in ch_list:
                        bset[0] += 1
                        if bset[0] % 2:
                            Bk = [(PSB[i_], "ps%d" % i_) for i_ in range(3)]
                        else:
                            Bk = [(PSB[3], "ps3"), (PTBf, "ptb"), (PTB2f, "ptb2")]
                        for g in range(2):
                            for pair in range(2):
                                pss, psr = psum45()
                                ptb_ = PTs[pti % 2]
                                ptr = "PTs%d" % (pti % 2)
                                pti += 1
                                info = []
                                for q in range(2):
                                    u = 2 * pair + q
                                    if g == 0:
                                        qb_i = 4 * ch + u
                                        hasp = (sb > 0) or (qb_i > 0)
                                        kprev = KT[0][pb:pb + 64, c2, 0, 128 * qb_i:128 * qb_i + 128]
                                        kcur = KT[0][pb:pb + 64, c2, 0, 128 + 128 * qb_i:256 + 128 * qb_i]
                                        qq = QT[0][pb:pb + 64, c2, 0, 128 * qb_i:128 * qb_i + 128]
                                        vprev = VV[0][:, 0, qb_i, 65 * slot:65 * (slot + 1)]
                                        vcur = VV[0][:, 0, qb_i + 1, 65 * slot:65 * (slot + 1)]
                                        ocols = slice(128 * u, 128 * (u + 1))
                                    else:
                                        hasp = (sb > 0) or (ch > 0)
                                        kprev = KT[1][pb:pb + 64, c2, u, 128 * ch:128 * ch + 128]
                                        kcur = KT[1][pb:pb + 64, c2, u, 128 + 128 * ch:256 + 128 * ch]
                                        qq = QT[1][pb:pb + 64, c2, u, 128 * ch:128 * ch + 128]
                                        vprev = VV[1][:, u, ch, 65 * slot:65 * (slot + 1)]
                                        vcur = VV[1][:, u, ch + 1, 65 * slot:65 * (slot + 1)]
                                        ocols = slice(128 * u, 128 * (u + 1))
                                    if hasp:
                                        S.op("pe", lambda e, pss=pss, q=q, kprev=kprev, qq=qq: e.matmul(
                                            pss[:, 256 * q:256 * q + 128], lhsT=kprev, rhs=qq, start=True, stop=True),
                                            reads=["KT%d" % g, "QT%d" % g], writes=[psr])
                                    S.op("pe", lambda e, pss=pss, q=q, kcur=kcur, qq=qq: e.matmul(
                                        pss[:, 256 * q + 128:256 * q + 256], lhsT=kcur, rhs=qq, start=True, stop=True),
                                        reads=["KT%d" % g, "QT%d" % g], writes=[psr])
                                    info.append((hasp, vprev, vcur, ocols))
                                allp = info[0][0] and info[1][0]
                                k0 = 0 if allp else 1
                                if (not allp) and (info[0][0] or info[1][0]):
                                    S.op("act", lambda e, pss=pss, ptb_=ptb_: e.activation(
                                        out=ptb_[:, 1, 0, :], in_=pss[:, 256:384], func=AF.Exp), reads=[psr], writes=[ptr])
                                    S.op("dve", lambda e, ptb_=ptb_: e.tensor_tensor(
                                        out=ptb_[:, 1, 0, :], in0=ptb_[:, 1, 0, :], in1=maskb[:, 0, :], op=ALU.mult),
                                        reads=[ptr, "maskb"], writes=[ptr])
                                S.op("act", lambda e, pss=pss, ptb_=ptb_, k0=k0: e.activation(
                                    out=ptb_[:, :, k0:2, :],
                                    in_=pss[:, :].rearrange("p (q k t) -> p q k t", q=2, k=2)[:, :, k0:2, :], func=AF.Exp),
                                    reads=[psr], writes=[ptr])
                                S.op("dve", lambda e, ptb_=ptb_, k0=k0: e.tensor_tensor(
                                    out=ptb_[:, :, k0:2, :], in0=ptb_[:, :, k0:2, :],
                                    in1=maskb[:, k0:2, :].unsqueeze(1).to_broadcast([128, 2, 2 - k0, 128]), op=ALU.mult),
                                    reads=[ptr, "maskb"], writes=[ptr])
                                if DBG and sb == 0 and slot == 0 and ch == 0:
                                    dpt = dout("dbg_PT%d_%d" % (g, pair), [128, 2, 2, 128], BF16)
                                    dma("sp", dpt, ptb_[:, :, :, :], reads=[ptr], key="dbg")

                                def emitB(info=info, ptb_=ptb_, ptr=ptr, g=g, Bk=Bk):
                                    for q in range(2):
                                        hasp, vprev, vcur, ocols = info[q]
                                        pO, pOr = Bk[g]
                                        kbl = (0, 1) if hasp else (1,)
                                        for kb in kbl:
                                            vv = vprev if kb == 0 else vcur
                                            S.op("pe", lambda e, pO=pO, vv=vv, ptb_=ptb_, q=q, kb=kb, ocols=ocols, kbl=kbl: e.matmul(
                                                pO[0:65, ocols], lhsT=vv, rhs=ptb_[:, q, kb, :], start=(kb == kbl[0]),
                                                stop=(kb == 1)), reads=["VV%d" % g, ptr], writes=[pOr])
                                if pendB[0] is not None:
                                    pendB[0]()
                                pendB[0] = emitB
                        pendB[0]()
                        pendB[0] = None
                        pO, pOr = Bk[2]
                        for r in range(16):
                            kbs = (0, 1) if sb > 0 else (1,)
                            for kb in kbs:
                                S.op("pe", lambda e, pO=pO, r=r, kb=kb, kbs=kbs: e.matmul(
                                    pO[0:65, 32 * r:32 * (r + 1)], lhsT=VV[2][:, r, kb, 65 * slot:65 * (slot + 1)],
                                    rhs=PT2[:, r, kb, 32 * ch:32 * (ch + 1)], start=(kb == kbs[0]), stop=(kb == 1)),
                                    reads=["VV2", "PT2"], writes=[pOr], sig=(kb == 1))
                        if DBG and sb == 0 and slot == 0 and ch == 0:
                            for gq in range(3):
                                db = dout("dbg_B%d" % gq, [65, 512], F32)
                                S.op("dve", lambda e, gq=gq: e.tensor_copy(out=stg[1][0:65, :], in_=Bk[gq][0][0:65, :]),
                                     reads=[Bk[gq][1]], writes=["stg1"])
                                dma("sp", db, stg[1][0:65, :], reads=["stg1"], key="dbg")
                        tmpb = stg[0]
                        S.op("act", lambda e, tmpb=tmpb, b0=Bk[0][0]: e.activation(out=tmpb[0:65, :], in_=b0[0:65, :], func=AF.Copy),
                             reads=[Bk[0][1]], writes=["stg0"])
                        S.op("dve", lambda e, tmpb=tmpb, b1=Bk[1][0]: e.tensor_tensor(
                            out=tmpb[0:65, :].rearrange("p (j r) -> p j r", r=4),
                            in0=tmpb[0:65, :].rearrange("p (j r) -> p j r", r=4),
                            in1=b1[0:65, :].rearrange("p (r j) -> p j r", r=4), op=ALU.add),
                            reads=[Bk[1][1], "stg0"], writes=["stg0"])
                        S.op("dve", lambda e, tmpb=tmpb, b2=Bk[2][0]: e.tensor_tensor(
                            out=tmpb[0:65, :].rearrange("p (j r) -> p j r", r=16),
                            in0=tmpb[0:65, :].rearrange("p (j r) -> p j r", r=16),
                            in1=b2[0:65, :].rearrange("p (r j) -> p j r", r=16), op=ALU.add),
                            reads=[Bk[2][1], "stg0"], writes=["stg0"])
                        S.op("dve", lambda e, tmpb=tmpb: e.tensor_scalar(out=tmpb[64:65, :], in0=tmpb[64:65, :], scalar1=1e-30,
                                                                        scalar2=None, op0=ALU.add),
                             reads=["stg0"], writes=["stg0"])
                        S.op("dve", lambda e, tmpb=tmpb: e.reciprocal(out=stg[1][64:65, :], in_=tmpb[64:65, :]),
                             reads=["stg0"], writes=["stg1"])
                        pD, pDr = Bk[0]
                        S.op("pe", lambda e, pD=pD: e.matmul(pD[0:64, :], lhsT=onesf[64:65, 0:64], rhs=stg[1][64:65, :],
                                                             start=True, stop=True), reads=["onesf", "stg1"], writes=[pDr])
                        pO, pOr = None, "stg0"
                        rc = rec[0]
                        rcr = "rec0"
                        S.op("dve", lambda e, pD=pD, tmpb=tmpb, slot=slot, ch=ch: e.tensor_tensor(
                            out=oTsb[:, 512 * ch:512 * (ch + 1)], in0=tmpb[0:64, :], in1=pD[0:64, :], op=ALU.mult),
                            reads=[pDr, "stg0"], writes=["oTsb"])
                    dma("sp", oT_d[:, slot, T0:T0 + 2048], oTsb[:, :], reads=["oTsb"], writes=["oT_d"], key="oT_d")
            for sb_ in range(NSB, len(LATE_CASTS)):
                for f_ in LATE_CASTS[sb_]:
                    f_()
            S.barrier()
            S.emit()
        if KSTOP == 1:
            return nc

        with contextlib.ExitStack() as st2:
            xg2 = [sbuf(st2, "xg_%d" % i, [128, NT // 128, D], F32) for i in range(2)]
            xs4 = xsb + [sbuf(st2, "xsb%d" % i, [128, D], BF16) for i in (2, 3)]
            hT = sbuf(st2, "hT", [128, 8, NT], BF16)
            uext = sbuf(st2, "uext", [128, 8, 30 + NT], BF16)
            dgb = [sbuf(st2, "dgb%d" % i, [128, 31, 128], BF16) for i in range(2)]
            big2 = sbuf(st2, "big2", [128, 12 * NT], F32)
            acc = big2[:, 0:8 * NT].rearrange("p (c t) -> p c t", c=8)
            lnm = big2[:, 8 * NT:12 * NT].rearrange("p (c t) -> p c t", c=4)
            aT = big2[:, 0:11 * NT].bitcast(BF16).rearrange("p (c t) -> p c t", c=22)
            R3 = sbuf(st2, "R3", [128, 11264], F32)
            wdnb = R3[:, :].bitcast(BF16).rearrange("p (c t) -> p c t", c=22)
            sT = R3[:, 0:2048].bitcast(BF16).rearrange("p (c t) -> p c t", c=8)
            mixT = R3[:, 2048:4096].bitcast(BF16).rearrange("p (c t) -> p c t", c=8)
            woutb = R3[:, 4096:8192].bitcast(BF16).rearrange("p (c t) -> p c t", c=8)
            oTg = R3[0:64, 8192:9216].bitcast(BF16).rearrange("p (c t) -> p c t", c=4)
            cb16 = [R3[:, 9216 + 256 * i:9472 + 256 * i].bitcast(BF16) for i in range(2)]
            csq16 = [R3[:, 9728 + 256 * i:9984 + 256 * i].bitcast(BF16) for i in range(2)]
            tt = [R3[:, 10240 + 512 * i:10752 + 512 * i] for i in range(2)]
            S.alias["wdnb"] = ["sT", "mixT", "woutb", "oTg", "cb16_0", "cb16_1", "csq16_0", "csq16_1", "tt0", "tt1"]
            S.alias["aT"] = ["acc%d" % c for c in range(8)] + ["lnm"]
            S.alias["uh"] = ["uext"] + ["uext%d" % c for c in range(8)]
            S.alias["uprod"] = S.alias["uh"]
            wao64 = sbuf(st2, "wao64", [64, 4, D], BF16)
            wao128 = sbuf(st2, "wao128", [128, 2, D], BF16)
            NWB = 3
            wt = [sbuf(st2, "wt%d" % i, [128, 8, 256], BF16) for i in range(NWB)]
            sg_ = [sbuf(st2, "sg%d" % i, [128, NT], F32) for i in range(3)]
            upx = [sbuf(st2, "upx%d" % i, [128, NT + 2], F32) for i in range(2)]
            cgb = [sbuf(st2, "cgb%d" % i, [128, NT], F32) for i in range(2)]
            fh = sbuf(st2, "fh", [128, 44, 2], F32)
            gfin = sbuf(st2, "gfin", [128, D], F32)
            yt = [sbuf(st2, "yt%d" % i, [128, D], F32) for i in range(1)]
            uflat = uext[:, :, :].rearrange("p c t -> p (c t)")[:, 0:4336].bitcast(F32)
            uh = uflat[:, 0:8 * NS * 31].rearrange("p (c s j) -> p c s j", c=8, s=NS)
            uprod = uflat[:, 8 * NS * 31:16 * NS * 31].rearrange("p (c s j) -> p c s j", c=8, s=NS)
            fhs = sbuf(st2, "fhs", [128, 44, 2 * NS], F32)
            upn = sbuf(st2, "upn", [128, 44, NS], F32)
            orow = yt[0][0:30, :]
            S.alias["orow"] = ["yt0"]

            dma("sp", gfin[:], gfin_d, writes=["gfin"], key="gfin")
            dma("sp", wao64[:], wb_ao.rearrange("(s d) n -> d s n", d=64), reads=["wb_ao"], writes=["wao"], key="wao64")
            dma("sp", wao128[:], wb_ao.rearrange("(c p) n -> p c n", p=128), reads=["wb_ao"], writes=["wao"], key="wao128")
            S.op("pool", lambda e: e.memset(uext[:, :, 0:30], 0.0), writes=["uext"])
            S.op("pool", lambda e: e.memset(fh[:], 0.0), writes=["fh"])

            wi = [0]

            def wtile(src_ap, rd):
                i = wi[0] % NWB
                wi[0] += 1
                dma("sp", wt[i][:, :, :], src_ap.rearrange("(kc p) n -> p kc n", p=128), reads=[rd],
                    writes=["wt%d" % i], key="wt%d" % i)
                return wt[i], "wt%d" % i

            tgl = [0]

            def alt(a, b):
                tgl[0] += 1
                return a if tgl[0] % 2 else b

            ygi = [0]

            prevN = [NT]

            def head(sp_, k):
                xgk = xg2[k % 2]
                for (i, TT) in sp_["tiles"]:
                    src = xs if sp_["sample"] else xp[sp_["t0"] + 128 * i:sp_["t0"] + 128 * i + TT, :]
                    dma("sp", xgk[0:TT, i, :], src, writes=["xg%d_%d" % (k % 2, i)], key="xg%d_%d" % (k % 2, i))
                    norm_a(xgk[0:TT, i, :], "xg%d_%d" % (k % 2, i), TT, xs4[i], "xsb%d" % i)

            def head_b(sp_, k):
                for (i, TT) in sp_["tiles"]:
                    norm_b(TT, xs4[i], "xsb%d" % i, hT[:, :, 128 * i:128 * i + TT], "hT", "gmix")

            def body(t0, TT_list, N, sample, first=False, lastg=False, out0=None, kidx=0, hook_a=None, hook_b=None):
                gi = 1
                xg = xg2[kidx % 2]
                XG = "xg%d_" % (kidx % 2)
                if sample:
                    for s in range(NS):
                        dma("sp", orow[:, :], sconv[s, :, :], writes=["orow"], key="scv")
                        for c in range(8):
                            pst, pr = psum()
                            S.op("pe", lambda e, c=c, pst=pst: e.transpose(pst[:, 0:30], orow[0:30, 128 * c:128 * (c + 1)],
                                                                            identf[0:30, 0:30]),
                                 reads=["orow", "identf"], writes=[pr])
                            S.op("act", lambda e, c=c, pst=pst, s=s: e.activation(out=uh[:, c, s, 0:30], in_=pst[:, 0:30],
                                                                                  func=AF.Copy), reads=[pr], writes=["uh"])
                elif not first:
                    pN = prevN[0]
                    S.op("pool", lambda e: e.tensor_copy(out=uext[:, :, 0:30], in_=uext[:, :, pN:pN + 30]),
                         reads=["uext"], writes=["uext"])
                if not sample:
                    prevN[0] = N
                for c in range(8):
                    w, wr = wtile(wb_in[:, 256 * c:256 * (c + 1)], "wb_in1")
                    pl, plr = psum()
                    pg, pgr = psum()
                    for kc in range(8):
                        S.op("pe", lambda e, kc=kc, w=w, pl=pl: e.matmul(pl[:, 0:N], lhsT=w[:, kc, 0:128], rhs=hT[:, kc, 0:N],
                                                                         start=(kc == 0), stop=(kc == 7)),
                             reads=[wr, "hT"], writes=[plr], sig=(kc == 7))
                    for kc in range(8):
                        S.op("pe", lambda e, kc=kc, w=w, pg=pg: e.matmul(pg[:, 0:N], lhsT=w[:, kc, 128:256], rhs=hT[:, kc, 0:N],
                                                                         start=(kc == 0), stop=(kc == 7)),
                             reads=[wr, "hT"], writes=[pgr], sig=(kc == 7))
                    sgb = sg_[c % 2]
                    sgr = "sg%d" % (c % 2)
                    S.op("act", lambda e, pg=pg, sgb=sgb: e.activation(out=sgb[:, 0:N], in_=pg[:, 0:N], func=AF.Sigmoid),
                         reads=[pgr], writes=[sgr])
                    if sample:
                        S.op("dve", lambda e, c=c, pl=pl, sgb=sgb: e.tensor_tensor(out=uh[:, c, :, 30], in0=pl[:, 0:N],
                                                                                   in1=sgb[:, 0:N], op=ALU.mult),
                             reads=[plr, sgr], writes=["uh"])
                    else:
                        S.op("dve", lambda e, c=c, pl=pl, sgb=sgb: e.tensor_tensor(out=uext[:, c, 30:30 + N], in0=pl[:, 0:N],
                                                                                   in1=sgb[:, 0:N], op=ALU.mult),
                             reads=[plr, sgr], writes=["uext%d" % c])
                if not sample:
                    dma("sp", oTg[:, :, 0:N], oT_d[:, :, t0:t0 + N], reads=["oT_d"], writes=["oTg"], key="oTg")
                dma("sp", woutb[:, :, :], wb_out.rearrange("(kc p) n -> p kc n", p=128), reads=["wb_out"],
                    writes=["woutb"], key="woutb")
                o_w = PP["wdw"][0]
                if sample:
                    S.op("dve", lambda e: e.tensor_tensor(
                        out=uprod[:, :, :, :], in0=uh[:, :, :, :],
                        in1=pp[:, o_w:o_w + 248].rearrange("p (c j) -> p c j", j=31).unsqueeze(2).to_broadcast([128, 8, NS, 31]),
                        op=ALU.mult), reads=["uh", "pp"], writes=["uprod"])
                    S.op("dve", lambda e: e.tensor_reduce(out=acc[:, :, 0:NS], in_=uprod[:, :, :, :], axis=AX.X, op=ALU.add),
                         reads=["uprod"], writes=["acc%d" % c for c in range(8)])
                    o_b = PP["bdw"][0]
                    S.op("dve", lambda e: e.tensor_tensor(out=acc[:, :, 0:NS], in0=acc[:, :, 0:NS],
                                                          in1=pp[:, o_b:o_b + 8].unsqueeze(2).to_broadcast([128, 8, NS]),
                                                          op=ALU.add), reads=["acc%d" % c for c in range(8)] + ["pp"],
                         writes=["acc%d" % c for c in range(8)])
                else:
                    for c in range(8):
                        dg = dgb[c % 2]
                        dgr = "dgb%d" % (c % 2)
                        S.op("pool", lambda e, c=c, dg=dg: e.tensor_tensor(
                            out=dg[:, :, :], in0=identf[:, :].unsqueeze(1).to_broadcast([128, 31, 128]),
                            in1=pp[:, o_w + 31 * c:o_w + 31 * c + 31].unsqueeze(2).to_broadcast([128, 31, 128]),
                            op=ALU.mult), reads=["identf", "pp"], writes=[dgr])
                        pcv, pcr = psum()
                        for j in range(31):
                            S.op("pe", lambda e, c=c, j=j, dg=dg, pcv=pcv: e.matmul(
                                pcv[:, 0:N], lhsT=dg[:, j, :], rhs=uext[:, c, j:j + N], start=(j == 0), stop=(j == 30)),
                                reads=[dgr, "uext%d" % c, "uext"], writes=[pcr], sig=(j == 30))
                        S.op("act", lambda e, c=c, pcv=pcv: e.activation(out=acc[:, c, 0:N], in_=pcv[:, 0:N], func=AF.Identity,
                                                                         bias=P("bdw", c)),
                             reads=[pcr, "pp"], writes=["acc%d" % c])
                if sample or lastg:
                    for c in range(8):
                        pst, pr = psum()
                        if sample:
                            S.op("pe", lambda e, c=c, pst=pst: e.transpose(pst[0:NS, 0:128], uh[:, c, :, 30], identf[:, :]),
                                 reads=["uh", "identf"], writes=[pr])
                            S.op("act", lambda e, c=c, pst=pst: e.activation(out=orow[0:NS, 128 * c:128 * (c + 1)],
                                                                             in_=pst[0:NS, 0:128], func=AF.Copy),
                                 reads=[pr], writes=["orow"])
                        else:
                            S.op("pe", lambda e, c=c: e.transpose(PTB[0:30, 128 * c:128 * (c + 1)], uext[:, c, NT:NT + 30],
                                                                   identb[:, :]),
                                 reads=["uext%d" % c, "identb"], writes=["ptb"])
                            S.op("act", lambda e, c=c: e.activation(out=orow[0:30, 128 * c:128 * (c + 1)],
                                                                    in_=PTB[0:30, 128 * c:128 * (c + 1)], func=AF.Copy),
                                 reads=["ptb"], writes=["orow"])
                    if sample:
                        dma("sp", conv_s[:, 29, :], orow[0:NS, 0:D], reads=["orow"], key="oorow")
                    else:
                        dma("sp", conv_p[:, :], orow[0:30, 0:D], reads=["orow"], key="oorow")
                if DBG and t0 == HALO and not sample:
                    dma("sp", dbg_acc, acc[:, :, :], reads=["acc%d" % c for c in range(8)], key="dbg")
                p1, p1r = psum()
                p2, p2r = psum()
                for c in range(8):
                    b1 = cb16[c % 2]
                    b2 = csq16[c % 2]
                    S.op("dve", lambda e, c=c, b1=b1: e.tensor_copy(out=b1[:, 0:N], in_=acc[:, c, 0:N]),
                         reads=["acc%d" % c], writes=["cb16_%d" % (c % 2)])
                    S.op("act", lambda e, c=c, b2=b2: e.activation(out=b2[:, 0:N], in_=acc[:, c, 0:N], func=AF.Square),
                         reads=["acc%d" % c], writes=["csq16_%d" % (c % 2)])
                    S.op("pe", lambda e, c=c, b1=b1: e.matmul(p1[:, 0:N], lhsT=onesb[:, :], rhs=b1[:, 0:N], start=(c == 0),
                                                              stop=(c == 7)), reads=["onesb", "cb16_%d" % (c % 2)], writes=[p1r])
                    S.op("pe", lambda e, c=c, b2=b2: e.matmul(p2[:, 0:N], lhsT=onesb[:, :], rhs=b2[:, 0:N], start=(c == 0),
                                                              stop=(c == 7)), reads=["onesb", "csq16_%d" % (c % 2)], writes=[p2r])
                S.op("dve", lambda e: e.tensor_scalar(out=lnm[:, 0, 0:N], in0=p1[:, 0:N], scalar1=1.0 / D, scalar2=None,
                                                      op0=ALU.mult), reads=[p1r], writes=["lnm"])
                S.op("dve", lambda e: e.tensor_tensor(out=lnm[:, 1, 0:N], in0=lnm[:, 0, 0:N], in1=lnm[:, 0, 0:N], op=ALU.mult),
                     reads=["lnm"], writes=["lnm"])
                S.op("dve", lambda e: e.scalar_tensor_tensor(out=lnm[:, 1, 0:N], in0=p2[:, 0:N], scalar=1.0 / D,
                                                             in1=lnm[:, 1, 0:N], op0=ALU.mult, op1=ALU.subtract),
                     reads=[p2r, "lnm"], writes=["lnm"])
                S.op("act", lambda e: e.activation(out=lnm[:, 2, 0:N], in_=lnm[:, 1, 0:N], func=AF.Sqrt, bias=epst[:, 0:1]),
                     reads=["lnm", "epst"], writes=["lnm"])
                S.op("dve", lambda e: e.reciprocal(out=lnm[:, 3, 0:N], in_=lnm[:, 2, 0:N]), reads=["lnm"], writes=["lnm"])
                for c in range(8):
                    tb = tt[c % 2]
                    tr = "tt%d" % (c % 2)
                    S.op("dve", lambda e, c=c, tb=tb: e.tensor_tensor(out=tb[:, 0:N], in0=acc[:, c, 0:N], in1=lnm[:, 0, 0:N],
                                                                      op=ALU.subtract),
                         reads=["acc%d" % c, "lnm"], writes=[tr])
                    S.op("dve", lambda e, tb=tb: e.tensor_tensor(out=tb[:, 0:N], in0=tb[:, 0:N], in1=lnm[:, 3, 0:N],
                                                                  op=ALU.mult), reads=[tr, "lnm"], writes=[tr])
                    S.op("act", lambda e, c=c, tb=tb: e.activation(out=sT[:, c, 0:N], in_=tb[:, 0:N], func=AF.Silu,
                                                                   scale=P("lng", c), bias=P("lnb", c)),
                         reads=[tr, "pp"], writes=["sT"])
                if DBG and t0 == HALO and not sample:
                    dma("sp", dbg_sT, sT[:, :, :], reads=["sT"], key="dbg")
                for c in range(8):
                    if c % 2 == 0:
                        wco_t, wco_r = wtile(wb_co[:, 128 * c:128 * c + 256], "wb_co")
                    wg_t, wg_r = wtile(wb_in[:, 4352 + 256 * c:4352 + 256 * (c + 1)], "wb_in3")
                    pa, par = psum()
                    pbb, pbr = psum()
                    pga, pgar = psum()
                    pgb, pgbr = psum()
                    co = 128 * (c % 2)
                    for kc in range(8):
                        S.op("pe", lambda e, kc=kc, pa=pa, wco_t=wco_t, co=co: e.matmul(
                            pa[:, 0:N], lhsT=wco_t[:, kc, co:co + 128], rhs=sT[:, kc, 0:N], start=(kc == 0), stop=(kc == 7)),
                            reads=[wco_r, "sT"], writes=[par], sig=(kc == 7))
                    if sample:
                        for k2 in range(2):
                            S.op("pe", lambda e, k2=k2, pbb=pbb, c=c: e.matmul(
                                pbb[:, 0:N], lhsT=wao128[:, k2, 128 * c:128 * (c + 1)], rhs=oTs[:, k2, 0:N],
                                start=(k2 == 0), stop=(k2 == 1)), reads=["wao", "oTs"], writes=[pbr], sig=(k2 == 1))
                    else:
                        for sl in range(4):
                            S.op("pe", lambda e, sl=sl, pbb=pbb, c=c: e.matmul(
                                pbb[:, 0:N], lhsT=wao64[:, sl, 128 * c:128 * (c + 1)], rhs=oTg[:, sl, 0:N],
                                start=(sl == 0), stop=(sl == 3)), reads=["wao", "oTg"], writes=[pbr], sig=(sl == 3))
                    for kc in range(8):
                        S.op("pe", lambda e, kc=kc, pga=pga, wg_t=wg_t: e.matmul(
                            pga[:, 0:N], lhsT=wg_t[:, kc, 0:128], rhs=hT[:, kc, 0:N], start=(kc == 0), stop=(kc == 7)),
                            reads=[wg_r, "hT"], writes=[pgar], sig=(kc == 7))
                    for kc in range(8):
                        S.op("pe", lambda e, kc=kc, pgb=pgb, wg_t=wg_t: e.matmul(
                            pgb[:, 0:N], lhsT=wg_t[:, kc, 128:256], rhs=hT[:, kc, 0:N], start=(kc == 0), stop=(kc == 7)),
                            reads=[wg_r, "hT"], writes=[pgbr], sig=(kc == 7))
                    sa, sar = sg_[0], "sg0"
                    sb_, sbr = sg_[1], "sg1"
                    S.op("act", lambda e, pga=pga: e.activation(out=sa[:, 0:N], in_=pga[:, 0:N], func=AF.Sigmoid),
                         reads=[pgar], writes=[sar])
                    S.op("act", lambda e, pgb=pgb: e.activation(out=sb_[:, 0:N], in_=pgb[:, 0:N], func=AF.Sigmoid),
                         reads=[pgbr], writes=[sbr])
                    S.op("dve", lambda e, pa=pa: e.tensor_tensor(out=sa[:, 0:N], in0=pa[:, 0:N], in1=sa[:, 0:N], op=ALU.mult),
                         reads=[par, sar], writes=[sar])
                    S.op("dve", lambda e, pbb=pbb: e.tensor_tensor(out=sb_[:, 0:N], in0=pbb[:, 0:N], in1=sb_[:, 0:N],
                                                                   op=ALU.mult), reads=[pbr, sbr], writes=[sbr])
                    S.op("dve", lambda e, c=c: e.tensor_tensor(out=mixT[:, c, 0:N], in0=sa[:, 0:N], in1=sb_[:, 0:N],
                                                                op=ALU.add), reads=[sar, sbr], writes=["mixT"])
                if DBG and t0 == HALO and not sample:
                    dma("sp", dbg_mix, mixT[:, :, :], reads=["mixT"], key="dbg")
                for (i, TT) in TT_list:
                    WN = 512 if TT == 128 else 256
                    for n in range(D // WN):
                        po, por = psum()
                        for kc in range(8):
                            S.op("pe", lambda e, kc=kc, po=po, i=i, TT=TT, n=n, WN=WN: e.matmul(
                                po[0:TT, 0:WN], lhsT=mixT[:, kc, 128 * i:128 * i + TT], rhs=woutb[:, kc, WN * n:WN * (n + 1)],
                                start=(kc == 0), stop=(kc == 7)), reads=["mixT", "woutb"], writes=[por], sig=(kc == 7))
                        S.op("dve", lambda e, po=po, i=i, TT=TT, n=n, WN=WN: e.tensor_tensor(
                            out=xg[0:TT, i, WN * n:WN * (n + 1)], in0=po[0:TT, 0:WN], in1=xg[0:TT, i, WN * n:WN * (n + 1)],
                            op=ALU.add), reads=[por, XG + str(i)], writes=[XG + str(i)])
                if DBG and t0 == HALO and not sample:
                    for (i, TT) in TT_list:
                        dma("sp", dbg_xmid[128 * i:128 * (i + 1), :], xg[0:TT, i, :], reads=[XG + str(i)], key="dbg")
                dma("sp", wdnb[:, :, :], wb_dn.rearrange("(kc p) n -> p kc n", p=128), reads=["wb_dn"], writes=["wdnb"],
                    key="wdnb")
                for (i, TT) in TT_list:
                    norm_T(xg[0:TT, i, :], XG + str(i), TT, hT[:, :, 128 * i:128 * i + TT], "hT", "gffn", ygi[0])
                    ygi[0] += 1
                if sample:
                    for q in range(44):
                        if q % 8 == 0:
                            wpc = min(1024, 2 * DFF - 128 * q)
                            dma("sp", orow[0:2 * NS, 0:wpc], sffn.rearrange("s j n -> (s j) n")[:, 128 * q:128 * q + wpc],
                                writes=["orow"], key="scv")
                        pst, pr = psum()
                        S.op("pe", lambda e, q=q, pst=pst: e.transpose(
                            pst[:, 0:2 * NS], orow[0:2 * NS, 128 * (q % 8):128 * (q % 8 + 1)], identf[0:2 * NS, 0:2 * NS]),
                             reads=["orow", "identf"], writes=[pr])
                        S.op("act", lambda e, q=q, pst=pst: e.activation(out=fhs[:, q, :], in_=pst[:, 0:2 * NS], func=AF.Copy),
                             reads=[pr], writes=["fhs"])
                o_f = PP["wfdw"][0]
                for j in range(22):
                    w, wr = wtile(wb_up[:, 256 * j:256 * (j + 1)], "wb_up")
                    pgv = []
                    for hv in range(2):
                        pz, pzr = psum()
                        for kc in range(8):
                            S.op("pe", lambda e, kc=kc, w=w, pz=pz, hv=hv: e.matmul(
                                pz[:, 0:N], lhsT=w[:, kc, 128 * hv:128 * (hv + 1)], rhs=hT[:, kc, 0:N], start=(kc == 0),
                                stop=(kc == 7)), reads=[wr, "hT"], writes=[pzr], sig=(kc == 7))
                        pgv.append((pz, pzr))
                    for hv in range(2):
                        q = 2 * j + hv
                        pz, pzr = pgv[hv]
                        ub, ur = upx[hv], "upx%d" % hv
                        cb, cr = cgb[hv], "cgb%d" % hv
                        w0 = pp[:, o_f + 3 * q:o_f + 3 * q + 1]
                        w1 = pp[:, o_f + 3 * q + 1:o_f + 3 * q + 2]
                        w2 = pp[:, o_f + 3 * q + 2:o_f + 3 * q + 3]
                        if sample:
                            S.op("act", lambda e, pz=pz, cb=cb, w2=w2, q=q: e.activation(
                                out=cb[:, 0:N], in_=pz[:, 0:N], func=AF.Identity, scale=w2, bias=P("bfdw", q)),
                                reads=[pzr, "pp"], writes=[cr])
                            S.op("act", lambda e, pz=pz, q=q: e.activation(out=upn[:, q, :], in_=pz[:, 0:NS], func=AF.Copy),
                                 reads=[pzr], writes=["upn"])
                            fv = fhs[:, q, :].rearrange("p (s j) -> p j s", j=2)
                            S.op("dve", lambda e, cb=cb, fv=fv, w1=w1: e.scalar_tensor_tensor(
                                out=cb[:, 0:N], in0=fv[:, 1, :], scalar=w1, in1=cb[:, 0:N], op0=ALU.mult, op1=ALU.add),
                                reads=["fhs", cr, "pp"], writes=[cr])
                            S.op("dve", lambda e, cb=cb, fv=fv, w0=w0: e.scalar_tensor_tensor(
                                out=cb[:, 0:N], in0=fv[:, 0, :], scalar=w0, in1=cb[:, 0:N], op0=ALU.mult, op1=ALU.add),
                                reads=["fhs", cr, "pp"], writes=[cr])
                        else:
                            S.op("dve", lambda e, ub=ub, q=q: e.tensor_copy(out=ub[:, 0:2], in_=fh[:, q, :]),
                                 reads=["fh"], writes=[ur])
                            S.op("act", lambda e, pz=pz, ub=ub: e.activation(out=ub[:, 2:2 + N], in_=pz[:, 0:N], func=AF.Copy),
                                 reads=[pzr], writes=[ur])
                            S.op("dve", lambda e, ub=ub, q=q: e.tensor_copy(out=fh[:, q, :], in_=ub[:, N:N + 2]),
                                 reads=[ur], writes=["fh"])
                            S.op("dve", lambda e, cb=cb, ub=ub, w2=w2, q=q: e.tensor_scalar(
                                out=cb[:, 0:N], in0=ub[:, 2:2 + N], scalar1=w2, scalar2=P("bfdw", q), op0=ALU.mult,
                                op1=ALU.add), reads=[ur, "pp"], writes=[cr])
                            S.op("dve", lambda e, cb=cb, ub=ub, w1=w1: e.scalar_tensor_tensor(
                                out=cb[:, 0:N], in0=ub[:, 1:1 + N], scalar=w1, in1=cb[:, 0:N], op0=ALU.mult, op1=ALU.add),
                                reads=[ur, cr, "pp"], writes=[cr])
                            S.op("dve", lambda e, cb=cb, ub=ub, w0=w0: e.scalar_tensor_tensor(
                                out=cb[:, 0:N], in0=ub[:, 0:N], scalar=w0, in1=cb[:, 0:N], op0=ALU.mult, op1=ALU.add),
                                reads=[ur, cr, "pp"], writes=[cr])
                    sgb, sgr = sg_[2], "sg2"
                    S.op("act", lambda e, sgb=sgb: e.activation(out=sgb[:, 0:N], in_=cgb[0][:, 0:N], func=AF.Silu),
                         reads=["cgb0"], writes=[sgr])
                    S.op("dve" if sample else "pool", lambda e, j=j, sgb=sgb: e.tensor_tensor(out=aT[:, j, 0:N], in0=sgb[:, 0:N],
                                                                                    in1=cgb[1][:, 0:N], op=ALU.mult),
                         reads=[sgr, "cgb1"], writes=["aT"])
                if sample or lastg:
                    nrow = NS if sample else 2
                    for q in range(44):
                        pst, pr = psum()
                        srcT = upn[:, q, :] if sample else fh[:, q, :]
                        S.op("pe", lambda e, pst=pst, srcT=srcT, nrow=nrow: e.transpose(pst[0:nrow, 0:128], srcT, identf[:, :]),
                             reads=["upn" if sample else "fh", "identf"], writes=[pr])
                        S.op("act", lambda e, q=q, pst=pst, nrow=nrow: e.activation(
                            out=orow[0:nrow, 128 * (q % 8):128 * (q % 8 + 1)], in_=pst[0:nrow, 0:128], func=AF.Copy),
                            reads=[pr], writes=["orow"])
                        if q % 8 == 7 or q == 43:
                            q0 = 8 * (q // 8)
                            wpc = 128 * (q - q0 + 1)
                            if sample:
                                dma("sp", ffn_s[:, 1, 128 * q0:128 * q0 + wpc], orow[0:NS, 0:wpc], reads=["orow"], key="oorow")
                            else:
                                dma("sp", ffn_p[:, 128 * q0:128 * q0 + wpc], orow[0:2, 0:wpc], reads=["orow"], key="oorow")
                if hook_a is not None:
                    hook_a()
                for (i, TT) in TT_list:
                    WN = 512 if TT == 128 else 256
                    for n in range(D // WN):
                        po, por = psum()
                        for kc in range(22):
                            S.op("pe", lambda e, kc=kc, po=po, i=i, TT=TT, n=n, WN=WN: e.matmul(
                                po[0:TT, 0:WN], lhsT=aT[:, kc, 128 * i:128 * i + TT], rhs=wdnb[:, kc, WN * n:WN * (n + 1)],
                                start=(kc == 0), stop=(kc == 21)), reads=["aT", "wdnb"], writes=[por], sig=(kc == 21))
                        S.op("dve", lambda e, po=po, i=i, TT=TT, n=n, WN=WN: e.tensor_tensor(
                            out=xg[0:TT, i, WN * n:WN * (n + 1)], in0=po[0:TT, 0:WN], in1=xg[0:TT, i, WN * n:WN * (n + 1)],
                            op=ALU.add), reads=[por, XG + str(i)], writes=[XG + str(i)])
                if hook_b is not None:
                    hook_b()
                for (i, TT) in TT_list:
                    col = stat_i[0]
                    stat_i[0] += 1
                    src = xg[0:TT, i, :]
                    S.op("act", lambda e, src=src, TT=TT, col=col: e.activation(
                        out=yt[0][0:TT, :], in_=src, func=AF.Square, accum_out=stat[0:TT, 0, col:col + 1]),
                        reads=[XG + str(i)], writes=["yt0", "stat%d" % col])
                    S.op("act", lambda e, TT=TT, col=col: e.activation(
                        out=stat[0:TT, 1, col:col + 1], in_=stat[0:TT, 0, col:col + 1], func=AF.Sqrt, scale=1.0 / D,
                        bias=epst[0:TT, 0:1]), reads=["stat%d" % col, "epst"], writes=["stat%d" % col])
                    S.op("dve", lambda e, TT=TT, col=col: e.reciprocal(out=stat[0:TT, 2, col:col + 1],
                                                                       in_=stat[0:TT, 1, col:col + 1]),
                         reads=["stat%d" % col], writes=["stat%d" % col])
                    yb = yt[0]
                    yr = "yt0"
                    S.op("dve", lambda e, src=src, TT=TT, col=col, yb=yb: e.scalar_tensor_tensor(
                        out=yb[0:TT, :], in0=src, scalar=stat[0:TT, 2, col:col + 1], in1=gfin[0:TT, :], op0=ALU.mult,
                        op1=ALU.mult), reads=[XG + str(i), "stat%d" % col, "gfin"], writes=[yr])
                    if sample:
                        dma("sp", y_s[:, :], yb[0:TT, :], reads=[yr], key="o" + yr)
                    elif out0 is not None:
                        dma("sp", y_p[out0 + 128 * i:out0 + 128 * i + TT, :], yb[0:TT, :], reads=[yr], key="o" + yr)

            hvt = sbuf(st2, "hvt", [128, 1], F32)
            dma("sp", hvt[:], hv_d, writes=["hvt"], key="hvt")
            specs = [dict(t0=HALO - 256, tiles=[(0, 128), (1, 128)], N=256, sample=False, first=True, lastg=False, out0=None)]
            for gi in range(NG):
                specs.append(dict(t0=HALO + gi * NT, tiles=[(i, 128) for i in range(NT // 128)], N=NT, sample=False,
                                  first=False, lastg=(gi == NG - 1), out0=gi * NT))
            specs.append(dict(t0=0, tiles=[(0, NS)], N=NS, sample=True, first=False, lastg=False, out0=None))
            head(specs[0], 0)
            head_b(specs[0], 0)
            for k, sp_ in enumerate(specs):
                if 1 <= k <= NG:
                    for k_, (dst_, src_) in enumerate(SHIFTS):
                        if k_ % NG == k - 1:
                            dma("pool", dst_, src_, key="cshift")
                nxt = specs[k + 1] if k + 1 < len(specs) else None
                body(sp_["t0"], sp_["tiles"], sp_["N"], sp_["sample"], first=sp_["first"], lastg=sp_["lastg"],
                     out0=sp_["out0"], kidx=k,
                     hook_a=(lambda nxt=nxt, k=k: head(nxt, k + 1)) if nxt else None,
                     hook_b=(lambda nxt=nxt, k=k: head_b(nxt, k + 1)) if nxt else None)
                if k == 0:
                    S.op("dve", lambda e: e.tensor_scalar(out=fh[:, :, :], in0=fh[:, :, :], scalar1=hvt[:, 0:1], scalar2=None,
                                                          op0=ALU.mult), reads=["fh", "hvt"], writes=["fh"])
            S.barrier()
            S.emit()
    return nc


def _perm_in():
    idx = []
    for c in range(8):
        idx += list(range(128 * c, 128 * (c + 1)))
        idx += list(range(1024 + 128 * c, 1024 + 128 * (c + 1)))
    idx += list(range(2048, 4352))
    for c in range(8):
        idx += list(range(4352 + 128 * c, 4352 + 128 * (c + 1)))
        idx += list(range(5376 + 128 * c, 5376 + 128 * (c + 1)))
    return np.array(idx)


def _perm_up():
    idx = []
    for j in range(22):
        idx += list(range(128 * j, 128 * (j + 1)))
        idx += list(range(DFF + 128 * j, DFF + 128 * (j + 1)))
    return np.array(idx)


def _fm(v, nch):
    return np.ascontiguousarray(v.reshape(nch, 128).T)


def make_shared(inp):
    f = np.float32
    pin = _perm_in()
    pup = _perm_up()
    ppv = np.zeros((128, NPP), f)

    def put(name, arr):
        o, w = PP[name]
        ppv[:, o:o + w] = arr.reshape(128, w)

    put("gmix", _fm(inp["g_mix"][0], 8))
    put("bdw", _fm(inp["b_dw"][0], 8))
    put("lng", _fm(inp["ln_g"][0], 8))
    put("lnb", _fm(inp["ln_b"][0], 8))
    put("gffn", _fm(inp["g_ffn"][0], 8))
    wdw = inp["w_dw"][0]
    put("wdw", np.ascontiguousarray(wdw.T.reshape(8, 128, 31).transpose(1, 0, 2)))
    wf = inp["w_fdw"][0][:, pup]
    put("wfdw", np.ascontiguousarray(wf.T.reshape(44, 128, 3).transpose(1, 0, 2)))
    put("bfdw", _fm(inp["b_fdw"][0][pup], 44))
    inv = (np.float32(500000.0) ** (-np.arange(8, dtype=f) / np.float32(8))).astype(f)
    angs = (np.full((NS, 1), 16384.0, f) * inv[None, :]).astype(f)
    css = np.concatenate([np.cos(angs), np.cos(angs)], 1).astype(f)
    sns = np.concatenate([-np.sin(angs), np.sin(angs)], 1).astype(f)
    j = np.arange(128)[:, None]
    i = np.arange(128)[None, :]
    mask2 = np.stack([(j >= i), (j <= i)], 1).astype(f)
    sel = np.zeros((NS, NS, 128), f)
    for s in range(NS):
        sel[s, s, :] = 1.0
    return {
        "w_in": np.ascontiguousarray(inp["w_in"][0][:, pin]),
        "w_co": np.ascontiguousarray(inp["w_conv_out"][0]),
        "w_ao": np.ascontiguousarray(inp["w_attn_out"][0]),
        "w_out": np.ascontiguousarray(inp["w_out"][0]),
        "w_up": np.ascontiguousarray(inp["w_up"][0][:, pup]),
        "w_dn": np.ascontiguousarray(inp["w_down"][0]),
        "pp": ppv,
        "gfin": np.ascontiguousarray(np.broadcast_to(inp["g_final"][None, :], (128, D))),
        "ident": np.eye(128, dtype=f),
        "mask2": mask2, "css": css, "sns": sns, "sel": sel,
    }


def make_core(inp, b, half, MAIN, s0, pup):
    f = np.float32
    LS = HALO + MAIN
    start = half * MAIN
    absp = start - HALO + np.arange(LS)
    valid = absp >= 0
    x = inp["x_prompt"][b]
    xl = np.zeros((LS, D), f)
    xl[valid] = x[absp[valid]]
    inv = (np.float32(500000.0) ** (-np.arange(8, dtype=f) / np.float32(8))).astype(f)
    pos = np.maximum(absp, 0).astype(f)
    ang = (pos[:, None] * inv[None, :]).astype(f)
    cos, sin = np.cos(ang).astype(f), np.sin(ang).astype(f)
    ntile = LS // 128
    csp = np.ascontiguousarray(np.concatenate([cos, cos], 1).reshape(ntile, 128, 16).transpose(1, 0, 2))
    snp = np.ascontiguousarray(np.concatenate([-sin, sin], 1).reshape(ntile, 128, 16).transpose(1, 0, 2))
    m = {
        "xp": xl, "csp": csp, "snp": snp,
        "hv": np.full((128, 1), 1.0 if start > 0 else 0.0, f),
        "xs": np.ascontiguousarray(inp["x_sample"][s0:s0 + NS, 0]),
        "sconv": np.ascontiguousarray(inp["state_conv"][0, s0:s0 + NS]),
        "sffn": np.ascontiguousarray(inp["state_ffn_conv"][0, s0:s0 + NS][:, :, pup]),
    }
    vf = valid.astype(f)
    nsb = LS // 2048
    for g, (W, d) in enumerate(GROUPS):
        m["vld%d" % g] = np.ascontiguousarray(vf.reshape(nsb, 16 // d, 128, d).transpose(2, 0, 3, 1))
    caches = ((inp["cache_k_w128"], inp["cache_v_w128"]), (inp["cache_k_w512"], inp["cache_v_w512"]),
              (inp["cache_k_w2048"], inp["cache_v_w2048"]))
    for g, W in enumerate((128, 512, 2048)):
        m["ck%d" % g] = np.ascontiguousarray(caches[g][0][0, s0:s0 + NS].reshape(NS, W, 256))
        m["cv%d" % g] = np.ascontiguousarray(caches[g][1][0, s0:s0 + NS].reshape(NS, W, 256))
    return m


_NC_CACHE = {}


def run(inp, n_cores=8):
    inp = {k: np.asarray(v) for k, v in inp.items()}
    B, S_FULL, _ = inp["x_prompt"].shape
    nsamp = inp["x_sample"].shape[0]
    MAIN = S_FULL // 2
    assert n_cores == 2 * B
    if MAIN not in _NC_CACHE:
        _NC_CACHE[MAIN] = build(MAIN)
    nc = _NC_CACHE[MAIN]
    shared = make_shared(inp)
    pup = _perm_up()
    in_maps = []
    for c in range(n_cores):
        m = dict(shared)
        m.update(make_core(inp, c // 2, c % 2, MAIN, (NS * c) % nsamp, pup))
        in_maps.append(m)
    res = run_bass_kernel_spmd(nc, in_maps, core_ids=list(range(n_cores))).results
    global LAST_RES
    LAST_RES = res
    ipup = np.argsort(pup)
    f = np.float32
    y_p = np.stack([np.concatenate([res[2 * b]["y_p"], res[2 * b + 1]["y_p"]], 0) for b in range(B)], 0)
    nsc = nsamp // NS
    y_s = np.concatenate([res[c]["y_s"] for c in range(nsc)], 0)[:, None, :]
    hi = [2 * b + 1 for b in range(B)]
    conv_p = np.stack([res[c]["conv_p"] for c in hi], 0)[None]
    conv_s = np.concatenate([res[c]["conv_s"] for c in range(nsc)], 0)[None]
    outs = [y_p.astype(f), y_s.astype(f), conv_p.astype(f), conv_s.astype(f)]
    for g, W in enumerate((128, 512, 2048)):
        outs.append(np.stack([res[c]["k%d_p" % g] for c in hi], 0).reshape(1, B, W, 4, 64).astype(f))
        outs.append(np.stack([res[c]["v%d_p" % g] for c in hi], 0).reshape(1, B, W, 4, 64).astype(f))
        outs.append(np.concatenate([res[c]["k%d_s" % g] for c in range(nsc)], 0).reshape(1, nsamp, W, 4, 64).astype(f))
        outs.append(np.concatenate([res[c]["v%d_s" % g] for c in range(nsc)], 0).reshape(1, nsamp, W, 4, 64).astype(f))
    ffn_p = np.stack([res[c]["ffn_p"] for c in hi], 0)[:, :, ipup][None]
    ffn_s = np.concatenate([res[c]["ffn_s"] for c in range(nsc)], 0)[:, :, ipup][None]
    outs += [ffn_p.astype(f), ffn_s.astype(f)]
    return tuple(outs)


def kernel(**inputs):
    return run(inputs, 8)
```
